# Optimizing a Trainium2 kernel written in Bass

```python
import jax, jax.numpy as jnp
from jax import lax
import numpy as np

D_MODEL = 1024
BATCH = 16
SEQ = 256
DEPTH = 2
DEC_BATCH = 4
DEC_SEQ = 1024
PAST_LEN = 256

GRID_W = 64
CONV_W = 512
N_HEADS = 8
HEAD_DIM = 64
ATTN_W = N_HEADS * HEAD_DIM
POOL_W = 512
POOL_SIZES = (2, 4, 8, 16)
POOL_GROUP = POOL_W // len(POOL_SIZES)
N_BRANCH = 3
BRANCH_W = 512
WIN_H_MAX = 8
WIN_W = 16
Q_BLK_W = 16
K_BAND_W = Q_BLK_W + WIN_W
CTX_Q_BLOCK = 128
D_FF = 2816
CONV_K = 3
EPS = 1e-6
IN_W = 3 * CONV_W + 3 * ATTN_W + POOL_W + N_BRANCH * D_MODEL

kernel_name = "hybrid_diffusion_prefix_trunk_step"

F32 = jnp.float32


def rmsnorm(x, g):
    xf = x.astype(F32)
    y = xf * lax.rsqrt(jnp.mean(xf * xf, axis=-1, keepdims=True) + EPS)
    return (y * g.astype(F32)).astype(x.dtype)


def dwconv3(x, w):
    xp = jnp.pad(x, ((0, 0), (1, 1), (0, 0)))
    return xp[:, :-2] * w[0] + xp[:, 1:-1] * w[1] + xp[:, 2:] * w[2]


def adaln(cvec, w_mod, b_mod):
    m = jax.nn.silu(cvec) @ w_mod + b_mod
    return jnp.split(m[:, None, :], 6, axis=-1)


def multiscale_pool(p, pool_w, pool_scale):
    Bn, N, _ = p.shape
    pf = p.astype(F32)
    cs = jnp.concatenate([jnp.zeros((Bn, 1, POOL_W), F32), jnp.cumsum(pf, axis=1)], axis=1)
    t = np.arange(N)
    outs = []
    for gi, w in enumerate(POOL_SIZES):
        lo = np.clip(t - w // 2, 0, N)
        hi = np.clip(t - w // 2 + w, 0, N)
        cnt = (hi - lo).astype(np.float32)[None, :, None]
        sl = slice(gi * POOL_GROUP, (gi + 1) * POOL_GROUP)
        mean = (cs[:, hi, sl] - cs[:, lo, sl]) / cnt
        outs.append(mean - pf[:, :, sl])
    d = jnp.stack(outs, axis=2).astype(p.dtype)
    y = jnp.einsum('bngc,gce->bnge', d, pool_w).reshape(Bn, N, POOL_W)
    return y * pool_scale


def ctx_attention(q, k, v):
    Bn, L = q.shape[:2]
    nb = L // CTX_Q_BLOCK
    qb = q.reshape(Bn, nb, CTX_Q_BLOCK, N_HEADS, HEAD_DIM).transpose(1, 0, 2, 3, 4)
    scale = HEAD_DIM ** -0.5

    def one(qblk):
        s = jnp.einsum('bqhd,bkhd->bhqk', qblk, k).astype(F32) * scale
        pr = jax.nn.softmax(s, axis=-1).astype(v.dtype)
        return jnp.einsum('bhqk,bkhd->bqhd', pr, v)

    o = lax.map(one, qb)
    return o.transpose(1, 0, 2, 3, 4).reshape(Bn, L, ATTN_W)


def neighbourhood_attention(q, k, v, k_ctx, v_ctx, rpb):
    Bn, S = q.shape[:2]
    rows = S // GRID_W
    kh = min(WIN_H_MAX, rows)
    ncb = GRID_W // Q_BLK_W
    r = np.arange(rows)
    row_idx = np.clip(r - kh // 2, 0, rows - kh)[:, None] + np.arange(kh)
    j = np.arange(ncb)
    col_idx = np.clip(j * Q_BLK_W - WIN_W // 2, 0, GRID_W - K_BAND_W)[:, None] + np.arange(K_BAND_W)
    qc = j[:, None] * Q_BLK_W + np.arange(Q_BLK_W)
    col_start = np.clip(qc - WIN_W // 2, 0, GRID_W - WIN_W)[..., None]
    kc = col_idx[:, None, :]
    valid = (kc >= col_start) & (kc < col_start + WIN_W)
    dr = row_idx - r[:, None] + WIN_H_MAX - 1
    dc = np.clip(kc - qc[..., None] + WIN_W - 1, 0, 2 * WIN_W - 2)
    bias = rpb[:, dr[:, None, None, :, None], dc[None, :, :, None, :]].astype(F32)

    qg = q.reshape(Bn, rows, ncb, Q_BLK_W, N_HEADS, HEAD_DIM)
    kgrid = k.reshape(Bn, rows, GRID_W, N_HEADS, HEAD_DIM)
    vgrid = v.reshape(Bn, rows, GRID_W, N_HEADS, HEAD_DIM)
    kg = kgrid[:, row_idx][:, :, :, col_idx]
    vg = vgrid[:, row_idx][:, :, :, col_idx]
    scale = HEAD_DIM ** -0.5
    s_lat = jnp.einsum('brjqhd,brkjchd->bhrjqkc', qg, kg).astype(F32) * scale + bias[None]
    s_lat = jnp.where(valid[None, None, None, :, :, None, :], s_lat, -jnp.inf)
    s_ctx = jnp.einsum('brjqhd,bhld->bhrjql', qg, k_ctx).astype(F32) * scale
    nlat = kh * K_BAND_W
    s = jnp.concatenate([s_lat.reshape(s_lat.shape[:5] + (nlat,)), s_ctx], axis=-1)
    pr = jax.nn.softmax(s, axis=-1).astype(v.dtype)
    p_lat = pr[..., :nlat].reshape(s_lat.shape)
    p_ctx = pr[..., nlat:]
    o = (jnp.einsum('bhrjqkc,brkjchd->brjqhd', p_lat, vg)
         + jnp.einsum('bhrjql,bhld->brjqhd', p_ctx, v_ctx))
    return o.reshape(Bn, S, ATTN_W)


def token_mixer(h, w_in, conv_w, pool_w, pool_scale, w_branch, w_out, attend):
    Bn, N, _ = h.shape
    z = h @ w_in
    sizes = [CONV_W] * 3 + [ATTN_W] * 3 + [POOL_W]
    pts, acc = [], 0
    for sz in sizes:
        acc += sz
        pts.append(acc)
    b_g, c_g, hc, q, k, v, pz, gz = jnp.split(z, pts, axis=-1)
    y_conv = b_g * dwconv3(c_g * hc, conv_w)
    q = q.reshape(Bn, N, N_HEADS, HEAD_DIM)
    k = k.reshape(Bn, N, N_HEADS, HEAD_DIM)
    v = v.reshape(Bn, N, N_HEADS, HEAD_DIM)
    y_attn = attend(q, k, v)
    y_pool = multiscale_pool(pz, pool_w, pool_scale)
    ys = jnp.stack([y_conv, y_attn, y_pool], axis=2)
    proj = jnp.einsum('bnic,icd->bnid', ys, w_branch)
    gates = jax.nn.sigmoid(gz.astype(F32)).astype(h.dtype).reshape(Bn, N, N_BRANCH, D_MODEL)
    merged = jnp.sum(gates * proj, axis=2)
    return merged @ w_out, k.transpose(0, 2, 1, 3), v.transpose(0, 2, 1, 3)


def conv_glu(h, w_up, conv, w_down):
    u, val = jnp.split(h @ w_up, 2, axis=-1)
    u = dwconv3(u, conv)
    return (jax.nn.gelu(u, approximate=True) * val) @ w_down


def trunk_layer(x, mod, g1n, g2n, w_in, conv_w, pool_w, pool_scale, w_branch, w_out,
                w_up, f_conv, w_down, attend):
    sh1, sc1, gt1, sh2, sc2, gt2 = mod
    h = rmsnorm(x, g1n) * (1 + sc1) + sh1
    mix, k, v = token_mixer(h, w_in, conv_w, pool_w, pool_scale, w_branch, w_out, attend)
    x = x + gt1 * mix
    h = rmsnorm(x, g2n) * (1 + sc2) + sh2
    x = x + gt2 * conv_glu(h, w_up, f_conv, w_down)
    return x, k, v


def setup_inputs(seed: int = 0) -> dict:
    key = jax.random.key(seed)
    ks = jax.random.split(key, 24)
    nrm = jax.random.normal
    D = D_MODEL
    return {
        "x_prompt": nrm(ks[0], (BATCH, SEQ, D), F32),
        "x_sample": nrm(ks[1], (DEC_BATCH, DEC_SEQ, D), F32),
        "cache_kv": nrm(ks[2], (DEC_BATCH, DEPTH, 2, N_HEADS, PAST_LEN, HEAD_DIM), F32),
        "c": nrm(ks[3], (DEC_BATCH, D), F32),
        "c_ctx": nrm(ks[4], (D,), F32),
        "w_mod": nrm(ks[5], (DEPTH, D, 6 * D), F32) * (0.5 * D ** -0.5),
        "b_mod": nrm(ks[6], (DEPTH, 6 * D), F32) * 0.02,
        "g_norm1": 1.0 + 0.02 * nrm(ks[7], (DEPTH, D), F32),
        "g_norm2": 1.0 + 0.02 * nrm(ks[8], (DEPTH, D), F32),
        "w_in": nrm(ks[9], (DEPTH, D, IN_W), F32) * D ** -0.5,
        "conv_w": nrm(ks[10], (DEPTH, CONV_K, CONV_W), F32) * CONV_K ** -0.5,
        "rpb": nrm(ks[11], (DEPTH, N_HEADS, 2 * WIN_H_MAX - 1, 2 * WIN_W - 1), F32) * 0.1,
        "pool_w": nrm(ks[12], (DEPTH, len(POOL_SIZES), POOL_GROUP, POOL_GROUP), F32) * POOL_GROUP ** -0.5,
        "pool_scale": 1.0 + 0.02 * nrm(ks[13], (DEPTH, POOL_W), F32),
        "w_branch": nrm(ks[14], (DEPTH, N_BRANCH, BRANCH_W, D), F32) * BRANCH_W ** -0.5,
        "w_out": nrm(ks[15], (DEPTH, D, D), F32) * D ** -0.5,
        "ffn_w_up": nrm(ks[16], (DEPTH, D, 2 * D_FF), F32) * D ** -0.5,
        "ffn_conv": nrm(ks[17], (DEPTH, CONV_K, D_FF), F32) * CONV_K ** -0.5,
        "ffn_w_down": nrm(ks[18], (DEPTH, D_FF, D), F32) * D_FF ** -0.5,
        "g_final": 1.0 + 0.02 * nrm(ks[19], (D,), F32),
    }


def reference(x_prompt, x_sample, cache_kv, c, c_ctx, w_mod, b_mod, g_norm1, g_norm2, w_in,
              conv_w, rpb, pool_w, pool_scale, w_branch, w_out, ffn_w_up, ffn_conv, ffn_w_down,
              g_final):
    xp = x_prompt
    xs = x_sample
    kv_layers = []
    for l in range(DEPTH):
        shared = (g_norm1[l], g_norm2[l], w_in[l], conv_w[l], pool_w[l], pool_scale[l],
                  w_branch[l], w_out[l], ffn_w_up[l], ffn_conv[l], ffn_w_down[l])
        mod_ctx = adaln(c_ctx[None, :], w_mod[l], b_mod[l])
        xp, k_new, v_new = trunk_layer(xp, mod_ctx, *shared, attend=ctx_attention)
        kv_layers.append(jnp.stack([k_new, v_new], axis=1))
        mod_lat = adaln(c, w_mod[l], b_mod[l])
        k_ctx = cache_kv[:, l, 0]
        v_ctx = cache_kv[:, l, 1]
        rb = rpb[l]
        attend_lat = lambda q, k, v, kc=k_ctx, vc=v_ctx, rb=rb: neighbourhood_attention(q, k, v, kc, vc, rb)
        xs, _, _ = trunk_layer(xs, mod_lat, *shared, attend=attend_lat)
    y_prompt = rmsnorm(xp, g_final)
    y_sample = rmsnorm(xs, g_final)
    kv_state = jnp.stack(kv_layers, axis=1)
    return (y_prompt, y_sample, kv_state)
```

```python
import numpy as np
import ml_dtypes
from contextlib import ExitStack
import concourse.bass as bass
import concourse.mybir as mybir
from concourse.bass_utils import run_bass_kernel_spmd

F32 = mybir.dt.float32
BF16 = mybir.dt.bfloat16
AF = mybir.ActivationFunctionType
ALU = mybir.AluOpType

D = 1024
T = 1024
KC = 8
DFF = 2816
NJ = 22
INW = 6656
NL = 2
NH = 8
NE = 22
E0 = 10
NEG = -30000.0
SLOT = 4608
NSLOT = 5
ND = 16
SEM_LIMIT = 12000

V_BMOD = 0
V_G1 = 96
V_G2 = 112
V_GF = 128
V_CONVW = 136
V_FCONV = 160
V_PSCALE = 292
V_CV = 300
V_BFLAG = 308
NV = 320

LOCAL_TILES = {0: [0, 1, 2, 3, 4, 5], 1: [2, 3, 4, 5, 6, 7]}


class Sched:
    def __init__(self, nc, es):
        self.nc = nc
        self.es = es
        self.eh = {"pe": nc.tensor, "act": nc.scalar, "dve": nc.vector, "pool": nc.gpsimd, "sp": nc.sync}
        self.epoch = {e: 0 for e in self.eh}
        self.cnt = {e: 0 for e in self.eh}
        self.sems = {}
        for e in self.eh:
            self.sems[(e, 0)] = es.enter_context(nc.semaphore(f"s_{e}_0"))
        self.dsem = [es.enter_context(nc.semaphore(f"s_dma_{i}")) for i in range(ND)]
        self.dcnt = [0] * ND
        self.dpool = {"sp": list(range(0, 6)), "pool": list(range(6, ND))}
        self.dnext = {"sp": 0, "pool": 0}
        self.seen = {e: {} for e in self.eh}
        self.last_w = {}
        self.readers = {}
        self.out_dmas = []
        self.aux = {}
        self.nops = {e: 0 for e in self.eh}
        self.pending = {e: False for e in self.eh}

    def _sem_of(self, src):
        if src[0] == "d":
            return self.dsem[src[1]]
        return self.sems[src]

    def _wait(self, eng, src, c):
        if self.seen[eng].get(src, 0) >= c:
            return
        self.eh[eng].wait_ge(self._sem_of(src), c)
        self.seen[eng][src] = c

    def _deps(self, eng, reads, writes):
        deps = {}

        def add(src, c):
            if deps.get(src, 0) < c:
                deps[src] = c

        for k in reads:
            w = self.last_w.get(k)
            if w:
                add(*w)
        for k in writes:
            w = self.last_w.get(k)
            if w:
                add(*w)
            for src, c in self.readers.get(k, {}).items():
                add(src, c)
        for src, c in deps.items():
            if eng == "pe" and src[0] == "pe":
                continue
            self._wait(eng, src, c)

    def _record(self, src, c, reads, writes):
        for k in writes:
            self.last_w[k] = (src, c)
            self.readers[k] = {}
        for k in reads:
            r = self.readers.setdefault(k, {})
            if r.get(src, 0) < c:
                r[src] = c

    def op(self, eng, fn, reads=(), writes=(), signal=True):
        if (not self.pending[eng]) and self.cnt[eng] >= SEM_LIMIT:
            self.epoch[eng] += 1
            self.cnt[eng] = 0
            self.sems[(eng, self.epoch[eng])] = self.es.enter_context(
                self.nc.semaphore(f"s_{eng}_{self.epoch[eng]}"))
        self._deps(eng, reads, writes)
        ins = fn(self.eh[eng])
        src = (eng, self.epoch[eng])
        if signal:
            ins.then_inc(self.sems[src], 1)
            self.cnt[eng] += 1
            c = self.cnt[eng]
            self.pending[eng] = False
        else:
            c = self.cnt[eng] + 1
            self.pending[eng] = True
        self._record(src, c, reads, writes)
        self.nops[eng] += 1
        return ins

    def dma(self, q, out, in_, reads=(), writes=(), is_output=False, ring=False):
        self._deps(q, reads, writes)
        lst = self.dpool[q]
        s = lst[self.dnext[q] % len(lst)]
        self.dnext[q] += 1
        src = ("d", s)
        if self.dcnt[s] > 0:
            self._wait(q, src, self.dcnt[s])
        self.eh[q].dma_start(out=out, in_=in_).then_inc(self.dsem[s], 16)
        self.dcnt[s] += 16
        self._record(src, self.dcnt[s], reads, writes)
        if is_output:
            self.out_dmas.append((src, self.dcnt[s]))
        if not ring:
            self.aux[src] = self.dcnt[s]

    def barrier(self):
        cur = [((e, self.epoch[e]), self.cnt[e]) for e in self.eh if self.cnt[e] > 0]
        cur += list(self.aux.items())
        self.aux = {}
        for e in self.eh:
            for src, c in cur:
                if src[0] == e and e == "pe":
                    continue
                self._wait(e, src, c)

    def finish(self):
        for src, c in self.out_dmas:
            self._wait("sp", src, c)
        self.barrier()


class _Stop(Exception):
    pass


def build_nc(debug=None, stop=None):
    nc = bass.Bass("TRN2", target_bir_lowering=False)

    def din(name, shape):
        return nc.dram_tensor(name, list(shape), F32, kind="ExternalInput").ap()

    def dout(name, shape):
        return nc.dram_tensor(name, list(shape), F32, kind="ExternalOutput").ap()

    xT_d = din("xT", [D, T])
    vecs_d = din("vecs", [128, NV])
    w_mod_d = din("w_mod", [NL, D, 6 * D])
    w_in_d = din("w_in", [NL, D, INW])
    w_br_d = din("w_branch", [NL, 3, 512, D])
    w_out_d = din("w_out", [NL, D, D])
    w_up_d = din("w_up", [NL, D, 2 * DFF])
    w_dn_d = din("w_down", [NL, DFF, D])
    pool_w_d = din("pool_w", [NL, 4, 128, 128])
    ctxk_d = din("ctxkT", [NL, 128, 4 * 256])
    ctxv_d = din("ctxv", [NL, 128, 2 * 512])
    utab_d = din("utab", [NL, 128, NH * NE * 64])
    r01_d = din("r01", [128, 128])
    onesc_d = din("onesc", [128, 192])
    band_d = din("band", [128, 4 * 8 * 2 * 128])
    yT_d = dout("yT", [D, T])
    kvo_d = dout("kvo", [NL, 2, T, 512])
    dbg_d = {}
    if debug:
        for name, (shape, dt_) in debug.items():
            dbg_d[name] = nc.dram_tensor("dbg_" + name, list(shape), dt_, kind="ExternalOutput").ap()

    es = ExitStack()
    with es:
        def sb(name, shape, dt):
            return es.enter_context(nc.sbuf_tensor(name, list(shape), dt))

        X = sb("X", [128, KC, T], F32)
        Hh = sb("Hh", [128, KC, T], BF16)
        AR = sb("AR", [128, 28160], BF16)
        UT = sb("UT", [128, 2, NE * 64], BF16)
        WR = sb("WR", [128, NSLOT, SLOT], BF16)
        SCR = sb("SCR", [128, 5632], F32)
        BAND = sb("BAND", [128, 4 * 8 * 2 * 128], BF16)
        VEC = sb("VEC", [128, NV], F32)
        MOD = sb("MOD", [128, NL, 48], F32)
        GM = sb("GM", [128, NL * 2 * 8], F32)
        SMALL = sb("SMALL", [128, 256], F32)
        SCb = sb("SCb", [128, 8], BF16)
        R01 = sb("R01", [128, 128], BF16)
        ONES = sb("ONES", [128, 128], BF16)
        ONESL = sb("ONESL", [128, 192], BF16)
        ONESC = sb("ONESC", [128, 192], BF16)
        CK = sb("CK", [128, 4 * 256], BF16)
        PS = [es.enter_context(nc.psum_tensor(f"ps{i}", [128, 512], F32)) for i in range(8)]

        S = Sched(nc, es)

        QK = AR[:, 0:8192].rearrange("p (c t) -> p c t", c=8)
        MERGED = QK
        VP = AR[:, 8192:8192 + 7680].rearrange("p (k h c) -> p k h c", k=10, h=4)
        YS = AR[:, 15872:15872 + 12288].rearrange("p (c t) -> p c t", c=12)
        PZ = AR[:, 15872 + 4096:15872 + 8192].rearrange("p (b c) -> p b c", b=8)
        A = AR[:, 0:NJ * T].rearrange("p (j t) -> p j t", j=NJ)
        BANDv = BAND[:, :].rearrange("p (g v s c) -> p g v s c", g=4, v=8, s=2)
        CKv = CK[:, :].rearrange("p (h k) -> p h k", h=4)
        GMv = GM[:, :].rearrange("p (l j c) -> p l j c", l=NL, j=2)

        def vcol(base, idx=0, n=1):
            return VEC[:, base + idx: base + idx + n]

        ps_ctr = [0]

        def ps_next():
            i = ps_ctr[0] % 8
            ps_ctr[0] += 1
            return PS[i], ("ps", i)

        def HS(half):
            return slice(half * 512, half * 512 + 512)

        def scr_f32(off, n):
            return SCR[:, off:off + n]

        def scr_bf16(off, n):
            return SCR[:, off:off + n].bitcast(BF16)

        loads = []

        class Ring:
            issued = 0
            consumed = 0

        def ring_use_many(n):
            idx = Ring.consumed
            target = min(len(loads), idx + NSLOT)
            while Ring.issued < target:
                i = Ring.issued
                loads[i](i % NSLOT)
                Ring.issued += 1
            Ring.consumed += n
            return [(idx + k) % NSLOT for k in range(n)]

        def ring_use():
            return ring_use_many(1)[0]

        def wkeys(slot):
            return [("WR", slot, 0), ("WR", slot, 1)]

        def ld_cols(src2d, col0, ncols, nkc=KC):
            def f(slot):
                dst = WR[:, slot, 0:nkc * ncols].rearrange("p (k n) -> p k n", k=nkc)
                src = src2d.rearrange("(k p) n -> p k n", p=128)[:, :, col0:col0 + ncols]
                S.dma("pool", dst, src, writes=wkeys(slot), ring=True)
            return f

        def ld_gate_branch(l, c):
            def f(slot):
                for i in range(3):
                    dst = WR[:, slot, i * 1024:(i + 1) * 1024].rearrange("p (k n) -> p k n", k=KC)
                    col0 = 3584 + i * 1024 + c * 128
                    src = w_in_d[l].rearrange("(k p) n -> p k n", p=128)[:, :, col0:col0 + 128]
                    S.dma("pool", dst, src, writes=[("WR", slot, 0)] if i == 0 else [("WRx", slot, i)], ring=True)
                for i in range(3):
                    dst = WR[:, slot, 3072 + i * 512:3072 + (i + 1) * 512].rearrange("p (k n) -> p k n", k=4)
                    src = w_br_d[l, i].rearrange("(k p) n -> p k n", p=128)[:, :, c * 128:(c + 1) * 128]
                    S.dma("pool", dst, src, writes=[("WR", slot, 1)] if i == 0 else [("WRy", slot, i)], ring=True)
            return f

        def gate_keys(slot):
            return [("WR", slot, 0), ("WRx", slot, 1), ("WRx", slot, 2), ("WR", slot, 1), ("WRy", slot, 1), ("WRy", slot, 2)]

        def ld_up(l, jp):
            def f(slot):
                dstv = WR[:, slot, 0:4096].rearrange("p (k two n) -> p k two n", k=KC, two=2)
                srcv = w_up_d[l].rearrange("(k p) (two n) -> p k two n", p=128, two=2)
                for t_ in range(2):
                    S.dma("pool", dstv[:, :, t_, :], srcv[:, :, t_, jp * 256:(jp + 1) * 256], writes=[("WR", slot, t_)], ring=True)
            return f

        def ld_down(l, c):
            def f(slot):
                dst = WR[:, slot, 0:NJ * 128].rearrange("p (k n) -> p k n", k=NJ)
                src = w_dn_d[l].rearrange("(k p) n -> p k n", p=128)[:, :, c * 128:(c + 1) * 128]
                S.dma("pool", dst, src, writes=wkeys(slot), ring=True)
            return f

        def ld_poolw(l):
            def f(slot):
                dst = WR[:, slot, 0:512].rearrange("p (g e) -> p g e", g=4)
                src = pool_w_d[l].rearrange("g c e -> c g e")
                S.dma("pool", dst, src, writes=wkeys(slot), ring=True)
            return f

        for l in range(NL):
            for g in range(4):
                loads.append(ld_cols(w_mod_d[l], g * 512, 512))
            for g in (3, 4, 5, 6):
                loads.append(ld_cols(w_in_d[l], g * 512, 512))
            loads.append(ld_poolw(l))
            for g in (2, 1, 0):
                loads.append(ld_cols(w_in_d[l], g * 512, 512))
            for c in range(8):
                loads.append(ld_gate_branch(l, c))
            for g in range(4, 12):
                loads.append(ld_cols(w_mod_d[l], g * 512, 512))
            for g in range(2):
                loads.append(ld_cols(w_out_d[l], g * 512, 512))
            for jp in range(11):
                loads.append(ld_up(l, jp))
            for c in range(8):
                loads.append(ld_down(l, c))

        def mm_group(ps_ap, ps_key, pairs, reads):
            n = len(pairs)
            for i, (lt, rh) in enumerate(pairs):
                S.op("pe", lambda e, lt=lt, rh=rh, i=i: e.matmul(ps_ap, lt, rh, start=(i == 0), stop=(i == n - 1)),
                     reads=reads if i == 0 else (), writes=[ps_key] if i == 0 else (), signal=(i == n - 1))

        def Hkeys(half):
            return [("H", kc, half) for kc in range(KC)]

        S.dma("sp", VEC[:, :], vecs_d, writes=[("VEC",)])
        for dc in range(KC):
            S.dma("sp", X[:, dc, :], xT_d[dc * 128:(dc + 1) * 128, :], writes=[("X", dc, 0), ("X", dc, 1)])
        S.dma("pool", R01[:, :], r01_d, writes=[("R01",)])
        S.dma("pool", ONESC[:, :], onesc_d, writes=[("ONESC",)])
        S.dma("pool", BAND[:, :].rearrange("p (a b) -> p a b", b=1024), band_d.rearrange("p (a b) -> p a b", b=1024), writes=[("BAND",)])
        S.op("dve", lambda e: e.memset(ONES[:, :], 1.0), writes=[("ONES",)])
        S.op("dve", lambda e: e.memset(ONESL[:, :], 1.0), writes=[("ONESL",)])
        S.op("dve", lambda e: e.memset(ONESL[:, 64:128], 0.0), writes=[("ONESL",)])
        S.op("dve", lambda e: e.memset(AR[:, 8192:8192 + 7680], 0.0), writes=[("VPall",)])
        S.op("act", lambda e: e.activation(out=SCb[:, :], in_=vcol(V_CV, 0, 8), func=AF.Silu),
             reads=[("VEC",)], writes=[("SCb",)])
        S.op("dve", lambda e: e.tensor_scalar(out=SMALL[:, 0:1], in0=vcol(V_BFLAG), scalar1=-1.0, scalar2=None,
                                              op0=ALU.mult), reads=[("VEC",)], writes=[("NBF",)])

        def mod_part(l, groups):
            for g in groups:
                slot = ring_use()
                ps, pk = ps_next()
                wv = WR[:, slot, 0:4096].rearrange("p (k n) -> p k n", k=KC)
                first = True
                for c in range(4):
                    col = g * 4 + c
                    for kc in range(KC):
                        S.op("pe", lambda e, c=c, kc=kc, col=col: e.matmul(
                            ps[:, col:col + 1], wv[:, kc, c * 128:(c + 1) * 128], SCb[:, kc:kc + 1],
                            start=(kc == 0), stop=(kc == KC - 1)),
                            reads=(wkeys(slot) + [("SCb",)]) if first else (), writes=[pk] if first else (),
                            signal=(c == 3 and kc == KC - 1))
                        first = False
                S.op("dve", lambda e, g=g: e.tensor_tensor(
                    out=MOD[:, l, g * 4:(g + 1) * 4], in0=ps[:, g * 4:(g + 1) * 4],
                    in1=VEC[:, V_BMOD + l * 48 + g * 4: V_BMOD + l * 48 + (g + 1) * 4], op=ALU.add),
                    reads=[pk, ("VEC",)], writes=[("MOD", l, g)])

        def gm_compute(l, j):
            sc0 = 8 if j == 0 else 32
            gb = (V_G1 if j == 0 else V_G2) + l * 8
            S.op("dve", lambda e: e.scalar_tensor_tensor(
                out=GMv[:, l, j, :], in0=MOD[:, l, sc0:sc0 + 8], scalar=1.0, in1=VEC[:, gb:gb + 8],
                op0=ALU.add, op1=ALU.mult),
                reads=[("MOD", l, sc0 // 4), ("MOD", l, sc0 // 4 + 1), ("VEC",)], writes=[("GM", l, j)])

        def norm(scale_ap_fn, bias_ap_fn, extra_reads, out_fn, out_keys_fn):
            SQ = [scr_bf16(0, 256), scr_bf16(256, 256)]
            SS = [scr_f32(512, 512), scr_f32(1024, 512)]
            RS = [scr_f32(1536, 512), scr_f32(2048, 512)]
            TMP = [scr_f32(2560, 512), scr_f32(3072, 512)]
            for half in range(2):
                ps, pk = ps_next()
                for dc in range(KC):
                    sq = SQ[dc % 2]
                    S.op("act", lambda e, dc=dc, sq=sq: e.activation(out=sq, in_=X[:, dc, HS(half)], func=AF.Square),
                         reads=[("X", dc, half)], writes=[("SQ", dc % 2)])
                    S.op("pe", lambda e, dc=dc, sq=sq: e.matmul(ps[:, :], ONES[:, :], sq, start=(dc == 0), stop=(dc == KC - 1)),
                         reads=[("SQ", dc % 2), ("ONES",)], writes=[pk])
                S.op("act", lambda e: e.activation(out=SS[half], in_=ps[:, :], func=AF.Sqrt, scale=1.0 / D, bias=1e-6),
                     reads=[pk], writes=[("SS", half)])
                S.op("dve", lambda e: e.reciprocal(out=RS[half], in_=SS[half]), reads=[("SS", half)], writes=[("RS", half)])
                for dc in range(KC):
                    tmp = TMP[dc % 2]
                    S.op("dve", lambda e, dc=dc, tmp=tmp: e.tensor_tensor(out=tmp, in0=X[:, dc, HS(half)], in1=RS[half], op=ALU.mult),
                         reads=[("X", dc, half), ("RS", half)], writes=[("TMPn", dc % 2)])
                    b = bias_ap_fn(dc)
                    S.op("act", lambda e, dc=dc, tmp=tmp, b=b: e.activation(
                        out=out_fn(dc, half), in_=tmp, func=AF.Identity, scale=scale_ap_fn(dc),
                        **({"bias": b} if b is not None else {})),
                        reads=[("TMPn", dc % 2)] + extra_reads, writes=out_keys_fn(dc, half))

        def stop_at(tag):
            if stop == tag:
                raise _Stop()

        def dbg_dump(name, ap, keys):
            if debug and name in dbg_d:
                S.barrier()
                S.dma("sp", dbg_d[name], ap, reads=keys, is_output=True)
                S.barrier()

        def run_layers():
          mod_part(0, range(0, 4))
          stop_at("mod0")
          for l in range(NL):
            if l > 0:
                mod_part(l, range(0, 4))
            gm_compute(l, 0)
            S.dma("pool", CK[:, :], ctxk_d[l], writes=[("CK",)])
            for j in range(2):
                dst = VP[:, 8:10, :, j * 128:j * 128 + 64]
                src = ctxv_d[l].rearrange("p (k h j d) -> p k h j d", k=2, h=4, j=2)[:, :, :, j, :]
                S.dma("pool", dst, src, writes=[("VPc", j)], reads=[("VPall",)])

            norm(lambda dc: GMv[:, l, 0, dc:dc + 1], lambda dc: MOD[:, l, dc:dc + 1],
                 [("GM", l, 0), ("MOD", l, 0), ("MOD", l, 1)],
                 lambda dc, half: Hh[:, dc, HS(half)], lambda dc, half: [("H", dc, half)])
            S.barrier()
            if l == 0:
                stop_at("norm1")
            if l == 0:
                dbg_dump("h1", Hh[:, :, :], [("H", dc, hf) for dc in range(KC) for hf in range(2)])

            KST = [scr_f32(4096, 512), scr_f32(4608, 512)]
            for which in range(2):
                slot = ring_use()
                wv = WR[:, slot, 0:4096].rearrange("p (k n) -> p k n", k=KC)
                for c in range(4):
                    for half in range(2):
                        ps, pk = ps_next()
                        mm_group(ps[:, :], pk, [(wv[:, kc, c * 128:(c + 1) * 128], Hh[:, kc, HS(half)]) for kc in range(KC)],
                                 wkeys(slot) + Hkeys(half))
                        if which == 0:
                            S.op("act", lambda e, c=c, half=half, ps=ps: e.activation(
                                out=QK[:, c, HS(half)], in_=ps[:, :], func=AF.Copy, scale=0.125),
                                reads=[pk], writes=[("QK", c, half)])
                        else:
                            S.op("dve", lambda e, c=c, half=half, ps=ps: e.tensor_copy(out=QK[:, 4 + c, HS(half)], in_=ps[:, :]),
                                 reads=[pk], writes=[("QK", 4 + c, half)])
                if which == 1:
                    for tb in range(8):
                        ps, pk = ps_next()
                        mm_group(ps[:, :], pk, [(Hh[:, kc, tb * 128:(tb + 1) * 128], wv[:, kc, :]) for kc in range(KC)],
                                 wkeys(slot) + Hkeys(tb // 4))
                        st = KST[tb % 2]
                        S.op("act", lambda e, ps=ps, st=st: e.activation(out=st, in_=ps[:, :], func=AF.Copy),
                             reads=[pk], writes=[("KST", tb % 2)])
                        S.dma("sp", kvo_d[l, 0, tb * 128:(tb + 1) * 128, :], st, reads=[("KST", tb % 2)], is_output=True)
            if l == 0:
                stop_at("A_qk")
            slot = ring_use()
            wv = WR[:, slot, 0:4096].rearrange("p (k n) -> p k n", k=KC)
            for tb in range(8):
                ps, pk = ps_next()
                mm_group(ps[:, :], pk, [(Hh[:, kc, tb * 128:(tb + 1) * 128], wv[:, kc, :]) for kc in range(KC)],
                         wkeys(slot) + Hkeys(tb // 4))
                st = KST[tb % 2]
                S.op("act", lambda e, ps=ps, st=st: e.activation(out=st, in_=ps[:, :], func=AF.Copy),
                     reads=[pk], writes=[("KST", tb % 2)])
                S.dma("sp", kvo_d[l, 1, tb * 128:(tb + 1) * 128, :], st, reads=[("KST", tb % 2)], is_output=True)
                for j in range(2):
                    S.op("dve", lambda e, st=st, tb=tb, j=j: e.tensor_copy(
                        out=VP[:, tb, :, j * 128:j * 128 + 64],
                        in_=st.rearrange("p (h j d) -> p h j d", h=4, j=2)[:, :, j, :]),
                        reads=[("KST", tb % 2), ("VPall",)], writes=[("VP", tb, j)])
            if l == 0:
                stop_at("A_v")
            slot = ring_use()
            wv = WR[:, slot, 0:4096].rearrange("p (k n) -> p k n", k=KC)
            for tb in range(8):
                ps, pk = ps_next()
                mm_group(ps[:, :], pk, [(Hh[:, kc, tb * 128:(tb + 1) * 128], wv[:, kc, :]) for kc in range(KC)],
                         wkeys(slot) + Hkeys(tb // 4))
                S.op("act", lambda e, ps=ps, tb=tb: e.activation(out=PZ[:, tb, :], in_=ps[:, :], func=AF.Copy),
                     reads=[pk], writes=[("PZ", tb)])

            if l == 0:
                stop_at("A_pz")
            slot = ring_use()
            pwv = WR[:, slot, 0:512].rearrange("p (g e) -> p g e", g=4)
            DT = [scr_bf16(0, 256), scr_bf16(256, 256)]
            for g in range(4):
                for half in range(2):
                    ps, pk = ps_next()
                    first = True
                    for obi in range(4):
                        ob = half * 4 + obi
                        terms = []
                        dv = 0 if ob == 0 else (1 if ob == 7 else (2 if ob % 2 == 0 else 3))
                        terms.append((ob, dv))
                        if ob >= 1:
                            terms.append((ob - 1, 4 if ob % 2 == 1 else 5))
                        if ob <= 6:
                            terms.append((ob + 1, 6 if ob % 2 == 0 else 7))
                        mats = [(ib, v, s) for (ib, v) in terms for s in range(2)]
                        for i, (ib, v, s) in enumerate(mats):
                            last = (obi == 3 and i == len(mats) - 1)
                            S.op("pe", lambda e, ib=ib, v=v, s=s, i=i, obi=obi, n=len(mats): e.matmul(
                                ps[:, obi * 128:(obi + 1) * 128], PZ[:, ib, g * 128:(g + 1) * 128], BANDv[:, g, v, s, :],
                                start=(i == 0), stop=(i == n - 1)),
                                reads=([("PZ", t) for t in range(8)] + [("BAND",)]) if first else (),
                                writes=[pk] if first else (), signal=last)
                            first = False
                    dt = DT[(g * 2 + half) % 2]
                    S.op("act", lambda e, ps=ps, dt=dt: e.activation(out=dt, in_=ps[:, :], func=AF.Copy),
                         reads=[pk], writes=[("DT", (g * 2 + half) % 2)])
                    ps2, pk2 = ps_next()
                    mm_group(ps2[:, :], pk2, [(pwv[:, g, :], dt)], wkeys(slot) + [("DT", (g * 2 + half) % 2)])
                    S.op("dve", lambda e, ps2=ps2, g=g, half=half: e.tensor_scalar(
                        out=YS[:, 8 + g, HS(half)], in0=ps2[:, :], scalar1=vcol(V_PSCALE, l * 4 + g), scalar2=None, op0=ALU.mult),
                        reads=[pk2, ("VEC",)], writes=[("YS", 8 + g, half)])

            if l == 0:
                stop_at("A_pool")
            slot_h, slot_c, slot_b = ring_use_many(3)
            wh = WR[:, slot_h, 0:4096].rearrange("p (k n) -> p k n", k=KC)
            wc = WR[:, slot_c, 0:4096].rearrange("p (k n) -> p k n", k=KC)
            wb = WR[:, slot_b, 0:4096].rearrange("p (k n) -> p k n", k=KC)
            HC = [scr_f32(512, 512), scr_f32(1024, 512)]
            CH = scr_f32(1536, 1026)
            YC = scr_f32(2562, 1024)
            S.op("dve", lambda e: e.memset(CH[:, 0:1], 0.0), writes=[("CH", 0), ("CH", 1)])
            S.op("dve", lambda e: e.memset(CH[:, 1025:1026], 0.0), writes=[("CHp",)])
            for c in range(4):
                cwb = V_CONVW + (l * 4 + c) * 3
                S.op("dve", lambda e, cwb=cwb: e.tensor_scalar(out=SMALL[:, 1:2], in0=vcol(cwb, 0), scalar1=SMALL[:, 0:1],
                                                              scalar2=None, op0=ALU.mult),
                     reads=[("VEC",), ("NBF",)], writes=[("FX", 0)])
                S.op("dve", lambda e, cwb=cwb: e.tensor_scalar(out=SMALL[:, 2:3], in0=vcol(cwb, 2), scalar1=SMALL[:, 0:1],
                                                              scalar2=None, op0=ALU.mult),
                     reads=[("VEC",), ("NBF",)], writes=[("FX", 1)])
                for half in range(2):
                    ps1, pk1 = ps_next()
                    mm_group(ps1[:, :], pk1, [(wh[:, kc, c * 128:(c + 1) * 128], Hh[:, kc, HS(half)]) for kc in range(KC)],
                             wkeys(slot_h) + Hkeys(half))
                    hc = HC[half]
                    S.op("act", lambda e, ps1=ps1, hc=hc: e.activation(out=hc, in_=ps1[:, :], func=AF.Copy),
                         reads=[pk1], writes=[("HC", half)])
                    ps2, pk2 = ps_next()
                    mm_group(ps2[:, :], pk2, [(wc[:, kc, c * 128:(c + 1) * 128], Hh[:, kc, HS(half)]) for kc in range(KC)],
                             wkeys(slot_c) + Hkeys(half))
                    S.op("dve", lambda e, ps2=ps2, hc=hc, half=half: e.tensor_tensor(
                        out=CH[:, 1 + half * 512: 1 + half * 512 + 512], in0=ps2[:, :], in1=hc, op=ALU.mult),
                        reads=[pk2, ("HC", half)], writes=[("CH", half)])
                chk = [("CH", 0), ("CH", 1), ("CHp",)]
                S.op("act", lambda e, cwb=cwb: e.activation(out=YC[:, :], in_=CH[:, 1:1025], func=AF.Identity, scale=vcol(cwb, 1)),
                     reads=chk + [("VEC",)], writes=[("YC",)])
                S.op("dve", lambda e, cwb=cwb: e.scalar_tensor_tensor(
                    out=YC[:, :], in0=CH[:, 0:1024], scalar=vcol(cwb, 0), in1=YC[:, :], op0=ALU.mult, op1=ALU.add),
                    reads=chk + [("VEC",), ("YC",)], writes=[("YC",)])
                S.op("dve", lambda e, cwb=cwb: e.scalar_tensor_tensor(
                    out=YC[:, :], in0=CH[:, 2:1026], scalar=vcol(cwb, 2), in1=YC[:, :], op0=ALU.mult, op1=ALU.add),
                    reads=chk + [("VEC",), ("YC",)], writes=[("YC",)])
                ycv = YC[:, :].rearrange("p (s t) -> p s t", s=4)
                chv = CH[:, 1:1025].rearrange("p (s t) -> p s t", s=4)
                S.op("dve", lambda e, ycv=ycv, chv=chv: e.scalar_tensor_tensor(
                    out=ycv[:, 1:4, 0:1], in0=chv[:, 0:3, 255:256], scalar=SMALL[:, 1:2], in1=ycv[:, 1:4, 0:1],
                    op0=ALU.mult, op1=ALU.add), reads=chk + [("FX", 0), ("YC",)], writes=[("YC",)])
                S.op("dve", lambda e, ycv=ycv, chv=chv: e.scalar_tensor_tensor(
                    out=ycv[:, 0:3, 255:256], in0=chv[:, 1:4, 0:1], scalar=SMALL[:, 2:3], in1=ycv[:, 0:3, 255:256],
                    op0=ALU.mult, op1=ALU.add), reads=chk + [("FX", 1), ("YC",)], writes=[("YC",)])
                for half in range(2):
                    ps3, pk3 = ps_next()
                    mm_group(ps3[:, :], pk3, [(wb[:, kc, c * 128:(c + 1) * 128], Hh[:, kc, HS(half)]) for kc in range(KC)],
                             wkeys(slot_b) + Hkeys(half))
                    S.op("dve", lambda e, ps3=ps3, half=half, c=c: e.tensor_tensor(
                        out=YS[:, c, HS(half)], in0=ps3[:, :], in1=YC[:, HS(half)], op=ALU.mult),
                        reads=[pk3, ("YC",)], writes=[("YS", c, half)])

            S.barrier()
            if l == 0:
                stop_at("phaseA")
            if l == 0:
                dbg_dump("qk", AR[:, 0:8192], [])
                dbg_dump("ysA", AR[:, 15872:15872 + 12288], [])

            TT_ = [scr_f32(0 + i * 512, 512) for i in range(3)]
            EE = [scr_bf16(1536 + i * 256, 256) for i in range(3)]
            PP = [scr_bf16(2304 + i * 256, 256) for i in range(3)]
            RD = [scr_f32(3072 + i * 512, 512) for i in range(2)]
            tctr = [0]
            ectr = [0]
            unit = 0
            for hp in range(4):
                for qc in range(2):
                    NUM, nk = PS[4 + (unit % 2) * 2], ("ps", 4 + (unit % 2) * 2)
                    DEN, dk = PS[5 + (unit % 2) * 2], ("ps", 5 + (unit % 2) * 2)
                    unit += 1
                    tiles = [("c", 8), ("c", 9)] + [("l", b) for b in LOCAL_TILES[qc]]
                    ntot = 2 * len(tiles)
                    it = 0
                    for j in range(2):
                        h = 2 * hp + j
                        if qc == 0:
                            ub = h % 2
                            S.dma("pool", UT[:, ub, :], utab_d[l][:, h * NE * 64:(h + 1) * NE * 64], writes=[("UT", ub)])
                        else:
                            ub = h % 2
                        for kind, b in tiles:
                            si = tctr[0] % 4
                            tctr[0] += 1
                            Sps, sk = PS[si], ("ps", si)
                            if kind == "l":
                                ksrc = QK[j * 64:(j + 1) * 64, 4 + hp, b * 128:(b + 1) * 128]
                                kkeys = [("QK", 4 + hp, b // 4)]
                            else:
                                ksrc = CKv[j * 64:(j + 1) * 64, hp, (b - 8) * 128:(b - 7) * 128]
                                kkeys = [("CK",)]
                            qsrc = QK[j * 64:(j + 1) * 64, hp, HS(qc)]
                            S.op("pe", lambda e, Sps=Sps, ksrc=ksrc, qsrc=qsrc: e.matmul(Sps[:, :], ksrc, qsrc, start=True, stop=True),
                                 reads=kkeys + [("QK", hp, qc)], writes=[sk])
                            ei = ectr[0] % 3
                            ectr[0] += 1
                            pt = PP[ei]
                            if kind == "l":
                                e0 = 8 * qc - 2 * b + E0
                                tt = TT_[ei]
                                S.op("dve", lambda e, Sps=Sps, tt=tt, ub=ub, e0=e0: e.tensor_tensor(
                                    out=tt, in0=Sps[:, :], in1=UT[:, ub, e0 * 64:(e0 + 8) * 64], op=ALU.add),
                                    reads=[sk, ("UT", ub)], writes=[("TT", ei)])
                                et = EE[ei]
                                S.op("act", lambda e, tt=tt, et=et: e.activation(out=et, in_=tt, func=AF.Exp),
                                     reads=[("TT", ei)], writes=[("EE", ei)])
                                S.op("pool", lambda e, et=et, pt=pt, b=b: e.tensor_tensor(
                                    out=pt.rearrange("p (r c) -> p r c", r=8),
                                    in0=et.rearrange("p (r c) -> p r c", r=8),
                                    in1=R01[:, b * 16 + 8 * qc: b * 16 + 8 * qc + 8].unsqueeze(2).to_broadcast([128, 8, 64]),
                                    op=ALU.mult),
                                    reads=[("EE", ei), ("R01",)], writes=[("PP", ei)])
                                vsrc = VP[:, b, hp, j * 64:j * 64 + 128]
                                vkeys = [("VP", b, 0), ("VP", b, 1), ("VPall",)]
                                osrc = ONESL[:, j * 64:j * 64 + 128]
                                okeys = [("ONESL",)]
                            else:
                                S.op("act", lambda e, Sps=Sps, pt=pt: e.activation(out=pt, in_=Sps[:, :], func=AF.Exp),
                                     reads=[sk], writes=[("PP", ei)])
                                vsrc = VP[:, b, hp, j * 64:j * 64 + 128]
                                vkeys = [("VPc", 0), ("VPc", 1), ("VPall",)]
                                osrc = ONESC[:, j * 64:j * 64 + 128]
                                okeys = [("ONESC",)]
                            S.op("pe", lambda e, vsrc=vsrc, pt=pt, it=it: e.matmul(NUM[:, :], vsrc, pt, start=(it == 0), stop=(it == ntot - 1)),
                                 reads=[("PP", ei)] + vkeys, writes=[nk])
                            S.op("pe", lambda e, osrc=osrc, pt=pt, it=it: e.matmul(DEN[:, :], osrc, pt, start=(it == 0), stop=(it == ntot - 1)),
                                 reads=[("PP", ei)] + okeys, writes=[dk])
                            it += 1
                    rd = RD[unit % 2]
                    S.op("dve", lambda e, rd=rd, DEN=DEN: e.reciprocal(out=rd, in_=DEN[:, :]), reads=[dk], writes=[("RD", unit % 2)])
                    S.op("dve", lambda e, rd=rd, NUM=NUM, hp=hp, qc=qc: e.tensor_tensor(
                        out=YS[:, 4 + hp, HS(qc)], in0=NUM[:, :], in1=rd, op=ALU.mult),
                        reads=[nk, ("RD", unit % 2)] + [("PZ", t) for t in range(8)], writes=[("YS", 4 + hp, qc)])
            S.barrier()
            ps_ctr[0] = 0
            if l == 0:
                stop_at("attn")
            if l == 0:
                dbg_dump("ysB", AR[:, 15872:15872 + 12288], [])

            GT = [scr_f32(i * 512, 512) for i in range(6)]
            MM = [scr_f32(3072 + i * 512, 512) for i in range(4)]
            gctr = [0]
            mctr = [0]
            for c in range(8):
                slot = ring_use()
                gw = WR[:, slot, 0:3072].rearrange("p (i k n) -> p i k n", i=3, k=KC)
                bw = WR[:, slot, 3072:3072 + 1536].rearrange("p (i k n) -> p i k n", i=3, k=4)
                for half in range(2):
                    prods = []
                    for i in range(3):
                        psg, pkg = ps_next()
                        mm_group(psg[:, :], pkg, [(gw[:, i, kc, :], Hh[:, kc, HS(half)]) for kc in range(KC)],
                                 gate_keys(slot) + Hkeys(half))
                        gi = gctr[0] % 6
                        gctr[0] += 1
                        gt = GT[gi]
                        S.op("act", lambda e, psg=psg, gt=gt: e.activation(out=gt, in_=psg[:, :], func=AF.Sigmoid),
                             reads=[pkg], writes=[("GT", gi)])
                        psp, pkp = ps_next()
                        mm_group(psp[:, :], pkp, [(bw[:, i, kc, :], YS[:, 4 * i + kc, HS(half)]) for kc in range(4)],
                                 gate_keys(slot) + [("YS", 4 * i + kc, half) for kc in range(4)])
                        mi = mctr[0] % 4
                        mctr[0] += 1
                        mt = MM[mi]
                        S.op("dve", lambda e, psp=psp, gt=gt, mt=mt: e.tensor_tensor(out=mt, in0=psp[:, :], in1=gt, op=ALU.mult),
                             reads=[pkp, ("GT", gi)], writes=[("MM", mi)])
                        prods.append((mt, mi))
                    (m0, k0), (m1, k1), (m2, k2) = prods
                    S.op("pool", lambda e, m0=m0, m1=m1: e.tensor_tensor(out=m0, in0=m0, in1=m1, op=ALU.add),
                         reads=[("MM", k0), ("MM", k1)], writes=[("MM", k0)])
                    S.op("pool", lambda e, m0=m0, m2=m2, c=c, half=half: e.tensor_tensor(out=MERGED[:, c, HS(half)], in0=m0, in1=m2, op=ALU.add),
                         reads=[("MM", k0), ("MM", k2)], writes=[("QK", c, half)])
            if l == 0:
                stop_at("phaseC")
            if l == 0:
                dbg_dump("merged", AR[:, 0:8192], [("QK", c, hf) for c in range(8) for hf in range(2)])

            mod_part(l, range(4, 12))
            gm_compute(l, 1)
            for g in range(2):
                slot = ring_use()
                wv = WR[:, slot, 0:4096].rearrange("p (k n) -> p k n", k=KC)
                for cc in range(4):
                    c = g * 4 + cc
                    for half in range(2):
                        ps, pk = ps_next()
                        mm_group(ps[:, :], pk, [(wv[:, kc, cc * 128:(cc + 1) * 128], MERGED[:, kc, HS(half)]) for kc in range(KC)],
                                 wkeys(slot) + [("QK", kc, half) for kc in range(KC)])
                        S.op("dve", lambda e, ps=ps, c=c, half=half: e.scalar_tensor_tensor(
                            out=X[:, c, HS(half)], in0=ps[:, :], scalar=MOD[:, l, 16 + c:17 + c], in1=X[:, c, HS(half)],
                            op0=ALU.mult, op1=ALU.add),
                            reads=[pk, ("MOD", l, 4 + c // 4), ("X", c, half)], writes=[("X", c, half)])
            if l == 0:
                stop_at("x1")
            if l == 0:
                dbg_dump("x1", X[:, :, :], [("X", dc, hf) for dc in range(KC) for hf in range(2)])

            S.barrier()
            norm(lambda dc: GMv[:, l, 1, dc:dc + 1], lambda dc: MOD[:, l, 24 + dc:25 + dc],
                 [("GM", l, 1), ("MOD", l, 6), ("MOD", l, 7)],
                 lambda dc, half: Hh[:, dc, HS(half)], lambda dc, half: [("H", dc, half)])
            S.barrier()

            U32 = scr_f32(0, 1026)
            YF = scr_f32(1026, 1024)
            GF = scr_f32(2050, 1024)
            S.op("dve", lambda e: e.memset(U32[:, 0:1], 0.0), writes=[("U32", 0), ("U32", 1)])
            S.op("dve", lambda e: e.memset(U32[:, 1025:1026], 0.0), writes=[("U32p",)])
            for jp in range(11):
                slot = ring_use()
                wv = WR[:, slot, 0:4096].rearrange("p (k two n) -> p k two n", k=KC, two=2)
                for jj in range(2):
                    jx = jp * 2 + jj
                    fwb = V_FCONV + (l * NJ + jx) * 3
                    S.op("dve", lambda e, fwb=fwb: e.tensor_scalar(out=SMALL[:, 1:2], in0=vcol(fwb, 0), scalar1=SMALL[:, 0:1],
                                                                  scalar2=None, op0=ALU.mult),
                         reads=[("VEC",), ("NBF",)], writes=[("FX", 0)])
                    S.op("dve", lambda e, fwb=fwb: e.tensor_scalar(out=SMALL[:, 2:3], in0=vcol(fwb, 2), scalar1=SMALL[:, 0:1],
                                                                  scalar2=None, op0=ALU.mult),
                         reads=[("VEC",), ("NBF",)], writes=[("FX", 1)])
                    for half in range(2):
                        ps, pk = ps_next()
                        mm_group(ps[:, :], pk, [(wv[:, kc, 0, jj * 128:(jj + 1) * 128], Hh[:, kc, HS(half)]) for kc in range(KC)],
                                 wkeys(slot) + Hkeys(half))
                        S.op("act", lambda e, ps=ps, half=half: e.activation(out=U32[:, 1 + half * 512:1 + half * 512 + 512], in_=ps[:, :], func=AF.Copy),
                             reads=[pk], writes=[("U32", half)])
                    uk = [("U32", 0), ("U32", 1), ("U32p",)]
                    S.op("act", lambda e, fwb=fwb: e.activation(out=YF[:, :], in_=U32[:, 1:1025], func=AF.Identity, scale=vcol(fwb, 1)),
                         reads=uk + [("VEC",)], writes=[("YF",)])
                    S.op("dve", lambda e, fwb=fwb: e.scalar_tensor_tensor(
                        out=YF[:, :], in0=U32[:, 0:1024], scalar=vcol(fwb, 0), in1=YF[:, :], op0=ALU.mult, op1=ALU.add),
                        reads=uk + [("VEC",), ("YF",)], writes=[("YF",)])
                    S.op("dve", lambda e, fwb=fwb: e.scalar_tensor_tensor(
                        out=YF[:, :], in0=U32[:, 2:1026], scalar=vcol(fwb, 2), in1=YF[:, :], op0=ALU.mult, op1=ALU.add),
                        reads=uk + [("VEC",), ("YF",)], writes=[("YF",)])
                    yfv = YF[:, :].rearrange("p (s t) -> p s t", s=4)
                    uv = U32[:, 1:1025].rearrange("p (s t) -> p s t", s=4)
                    S.op("dve", lambda e, yfv=yfv, uv=uv: e.scalar_tensor_tensor(
                        out=yfv[:, 1:4, 0:1], in0=uv[:, 0:3, 255:256], scalar=SMALL[:, 1:2], in1=yfv[:, 1:4, 0:1],
                        op0=ALU.mult, op1=ALU.add), reads=uk + [("FX", 0), ("YF",)], writes=[("YF",)])
                    S.op("dve", lambda e, yfv=yfv, uv=uv: e.scalar_tensor_tensor(
                        out=yfv[:, 0:3, 255:256], in0=uv[:, 1:4, 0:1], scalar=SMALL[:, 2:3], in1=yfv[:, 0:3, 255:256],
                        op0=ALU.mult, op1=ALU.add), reads=uk + [("FX", 1), ("YF",)], writes=[("YF",)])
                    S.op("act", lambda e: e.activation(out=GF[:, :], in_=YF[:, :], func=AF.Gelu_apprx_tanh),
                         reads=[("YF",)], writes=[("GF",)])
                    for half in range(2):
                        ps, pk = ps_next()
                        mm_group(ps[:, :], pk, [(wv[:, kc, 1, jj * 128:(jj + 1) * 128], Hh[:, kc, HS(half)]) for kc in range(KC)],
                                 wkeys(slot) + Hkeys(half))
                        S.op("dve", lambda e, ps=ps, half=half, jx=jx: e.tensor_tensor(
                            out=A[:, jx, HS(half)], in0=ps[:, :], in1=GF[:, HS(half)], op=ALU.mult),
                            reads=[pk, ("GF",)], writes=[("A", jx, half)])
            if l == 0:
                stop_at("ffnup")
            if l == 0:
                dbg_dump("a", AR[:, 0:NJ * T], [("A", j, hf) for j in range(NJ) for hf in range(2)])

            for c in range(8):
                slot = ring_use()
                wv = WR[:, slot, 0:NJ * 128].rearrange("p (k n) -> p k n", k=NJ)
                for half in range(2):
                    ps, pk = ps_next()
                    mm_group(ps[:, :], pk, [(wv[:, j, :], A[:, j, HS(half)]) for j in range(NJ)],
                             wkeys(slot) + [("A", j, half) for j in range(NJ)])
                    S.op("dve", lambda e, ps=ps, c=c, half=half: e.scalar_tensor_tensor(
                        out=X[:, c, HS(half)], in0=ps[:, :], scalar=MOD[:, l, 40 + c:41 + c], in1=X[:, c, HS(half)],
                        op0=ALU.mult, op1=ALU.add),
                        reads=[pk, ("MOD", l, 10 + c // 4), ("X", c, half)], writes=[("X", c, half)])
            S.barrier()
            if l + 1 < NL:
                S.op("dve", lambda e: e.memset(AR[:, 8192:8192 + 7680], 0.0), writes=[("VPall",)])
            if l == 0:
                stop_at("layer0")
            if l == 0:
                dbg_dump("x2", X[:, :, :], [("X", dc, hf) for dc in range(KC) for hf in range(2)])

        try:
            run_layers()
        except _Stop:
            pass
        S.barrier()
        OUTB = [scr_f32(3584, 512), scr_f32(4096, 512), scr_f32(4608, 512), scr_f32(5120, 512)]
        octr = [0]

        def out_fn(dc, half):
            i = octr[0] % 4
            return OUTB[i]

        def out_keys(dc, half):
            return [("OUTB", octr[0] % 4)]

        SQ = [scr_bf16(0, 256), scr_bf16(256, 256)]
        SS = [scr_f32(512, 512), scr_f32(1024, 512)]
        RS = [scr_f32(1536, 512), scr_f32(2048, 512)]
        TMP = [scr_f32(2560, 512), scr_f32(3072, 512)]
        for half in range(2):
            ps, pk = ps_next()
            for dc in range(KC):
                sq = SQ[dc % 2]
                S.op("act", lambda e, dc=dc, sq=sq: e.activation(out=sq, in_=X[:, dc, HS(half)], func=AF.Square),
                     reads=[("X", dc, half)], writes=[("SQ", dc % 2)])
                S.op("pe", lambda e, dc=dc, sq=sq: e.matmul(ps[:, :], ONES[:, :], sq, start=(dc == 0), stop=(dc == KC - 1)),
                     reads=[("SQ", dc % 2), ("ONES",)], writes=[pk])
            S.op("act", lambda e: e.activation(out=SS[half], in_=ps[:, :], func=AF.Sqrt, scale=1.0 / D, bias=1e-6),
                 reads=[pk], writes=[("SS", half)])
            S.op("dve", lambda e: e.reciprocal(out=RS[half], in_=SS[half]), reads=[("SS", half)], writes=[("RS", half)])
            for dc in range(KC):
                tmp = TMP[dc % 2]
                S.op("dve", lambda e, dc=dc, tmp=tmp: e.tensor_tensor(out=tmp, in0=X[:, dc, HS(half)], in1=RS[half], op=ALU.mult),
                     reads=[("X", dc, half), ("RS", half)], writes=[("TMPn", dc % 2)])
                oi = octr[0] % 4
                octr[0] += 1
                ob = OUTB[oi]
                S.op("act", lambda e, dc=dc, tmp=tmp, ob=ob: e.activation(out=ob, in_=tmp, func=AF.Identity, scale=vcol(V_GF, dc)),
                     reads=[("TMPn", dc % 2), ("VEC",)], writes=[("OUTB", oi)])
                S.dma("sp", yT_d[dc * 128:(dc + 1) * 128, HS(half)], ob, reads=[("OUTB", oi)], is_output=True)
        S.finish()
        build_nc.stats = dict(S.nops)
    return nc


def _bf16_split(a):
    hi = a.astype(ml_dtypes.bfloat16).astype(np.float32)
    lo = (a - hi).astype(ml_dtypes.bfloat16).astype(np.float32)
    return hi, lo


def _band_tables(seq_len):
    out = np.zeros((128, 4, 8, 2, 128), np.float32)
    t = np.arange(T)
    for g, w in enumerate((2, 4, 8, 16)):
        B = np.zeros((T, T), np.float64)
        s0 = (t // seq_len) * seq_len
        lo = np.clip(t - w // 2, s0, s0 + seq_len)
        hi = np.clip(t - w // 2 + w, s0, s0 + seq_len)
        for tt in range(T):
            B[lo[tt]:hi[tt], tt] = 1.0 / (hi[tt] - lo[tt])
            B[tt, tt] -= 1.0
        B = B.astype(np.float32)

        def blk(ib, ob):
            return B[ib * 128:(ib + 1) * 128, ob * 128:(ob + 1) * 128]
        variants = [blk(0, 0), blk(7, 7), blk(2, 2), blk(1, 1), blk(0, 1), blk(1, 2), blk(1, 0), blk(2, 1)]
        for ob in range(8):
            dv = 0 if ob == 0 else (1 if ob == 7 else (2 if ob % 2 == 0 else 3))
            assert np.array_equal(blk(ob, ob), variants[dv])
            if ob >= 1:
                assert np.array_equal(blk(ob - 1, ob), variants[4 if ob % 2 == 1 else 5])
            if ob <= 6:
                assert np.array_equal(blk(ob + 1, ob), variants[6 if ob % 2 == 0 else 7])
        for v, m in enumerate(variants):
            h_, l_ = _bf16_split(m)
            out[:, g, v, 0, :] = h_
            out[:, g, v, 1, :] = l_
    return out.reshape(128, -1)


def _r01_table(sample):
    r = np.zeros((128, 8, 16), np.float32)
    for b in range(8):
        for jj in range(2):
            rk = 2 * b + jj
            for q in range(16):
                if sample:
                    s = min(max(q - 4, 0), 8)
                    ok = s <= rk < s + 8
                else:
                    ok = (rk // 4) == (q // 4)
                r[jj * 64:(jj + 1) * 64, b, q] = 1.0 if ok else 0.0
    return r.reshape(128, 128)


def _u_table(rpb_l):
    U = np.full((128, NH, NE, 64), NEG, np.float32)
    cq = np.arange(64)
    cs = np.clip(cq - 8, 0, 48)
    for jj in range(2):
        for ei in range(NE):
            dr = jj - (ei - E0)
            if abs(dr) > 7:
                continue
            for ck in range(64):
                valid = (ck >= cs) & (ck < cs + 16)
                dc = np.clip(ck - cq + 15, 0, 30)
                vals = rpb_l[:, dr + 7, :][:, dc]
                U[jj * 64 + ck, :, ei, :] = np.where(valid[None, :], vals, NEG)
    return U.reshape(128, -1)


def _prep(inputs):
    f = lambda a: np.ascontiguousarray(np.asarray(a, dtype=np.float32))
    x_prompt, x_sample = f(inputs["x_prompt"]), f(inputs["x_sample"])
    cache_kv, c, c_ctx = f(inputs["cache_kv"]), f(inputs["c"]), f(inputs["c_ctx"])
    rpb = f(inputs["rpb"])
    shared = {
        "w_mod": f(inputs["w_mod"]), "w_in": f(inputs["w_in"]), "w_branch": f(inputs["w_branch"]),
        "w_out": f(inputs["w_out"]), "w_up": f(inputs["ffn_w_up"]), "w_down": f(inputs["ffn_w_down"]),
        "pool_w": f(inputs["pool_w"]),
    }

    def pk(v):
        return np.ascontiguousarray(v.reshape(-1, 128).T)

    vec_common = np.zeros((128, NV), np.float32)
    b_mod, g1, g2, gf = f(inputs["b_mod"]), f(inputs["g_norm1"]), f(inputs["g_norm2"]), f(inputs["g_final"])
    conv_w, fconv, pscale = f(inputs["conv_w"]), f(inputs["ffn_conv"]), f(inputs["pool_scale"])
    for l in range(NL):
        vec_common[:, V_BMOD + l * 48: V_BMOD + (l + 1) * 48] = pk(b_mod[l])
        vec_common[:, V_G1 + l * 8: V_G1 + (l + 1) * 8] = pk(g1[l])
        vec_common[:, V_G2 + l * 8: V_G2 + (l + 1) * 8] = pk(g2[l])
        for cc in range(4):
            for k in range(3):
                vec_common[:, V_CONVW + (l * 4 + cc) * 3 + k] = conv_w[l, k, cc * 128:(cc + 1) * 128]
        for j in range(NJ):
            for k in range(3):
                vec_common[:, V_FCONV + (l * NJ + j) * 3 + k] = fconv[l, k, j * 128:(j + 1) * 128]
        vec_common[:, V_PSCALE + l * 4: V_PSCALE + (l + 1) * 4] = pk(pscale[l])
    vec_common[:, V_GF:V_GF + 8] = pk(gf)

    ones_pat = np.concatenate([np.ones((128, 64)), np.zeros((128, 64)), np.ones((128, 64))], axis=1).astype(np.float32)
    band_p, band_s = _band_tables(256), _band_tables(1024)
    r01_p, r01_s = _r01_table(False), _r01_table(True)
    utab_s = np.stack([_u_table(rpb[l]) for l in range(NL)])
    utab_p = np.zeros_like(utab_s)
    in_maps = []
    for i in range(8):
        sample = i >= 4
        vec = vec_common.copy()
        if sample:
            b = i - 4
            xT = np.ascontiguousarray(x_sample[b].T)
            vec[:, V_CV:V_CV + 8] = pk(c[b])
            vec[:, V_BFLAG] = 0.0
            ck = cache_kv[b, :, 0]
            ctxk = ck.reshape(NL, 4, 2, 256, 64).transpose(0, 2, 4, 1, 3).reshape(NL, 128, 4 * 256)
            cvv = cache_kv[b, :, 1]
            ctxv = cvv.transpose(0, 2, 1, 3).reshape(NL, 2, 128, 512).transpose(0, 2, 1, 3).reshape(NL, 128, 1024)
            m = {"ctxkT": np.ascontiguousarray(ctxk), "ctxv": np.ascontiguousarray(ctxv), "utab": utab_s,
                 "r01": r01_s, "onesc": ones_pat, "band": band_s}
        else:
            xT = np.ascontiguousarray(x_prompt[4 * i:4 * i + 4].reshape(T, D).T)
            vec[:, V_CV:V_CV + 8] = pk(c_ctx)
            vec[:, V_BFLAG] = 1.0
            m = {"ctxkT": np.zeros((NL, 128, 1024), np.float32), "ctxv": np.zeros((NL, 128, 1024), np.float32),
                 "utab": utab_p, "r01": r01_p, "onesc": np.zeros_like(ones_pat), "band": band_p}
        m.update(shared)
        m["xT"] = xT
        m["vecs"] = vec
        in_maps.append(m)
    return in_maps


def _assemble(results):
    y_prompt = np.empty((16, 256, D), np.float32)
    y_sample = np.empty((4, T, D), np.float32)
    kv_state = np.empty((16, NL, 2, NH, 256, 64), np.float32)
    for i in range(8):
        r = results[i]
        y = np.asarray(r["yT"], dtype=np.float32).T
        if i < 4:
            y_prompt[4 * i:4 * i + 4] = y.reshape(4, 256, D)
            kvo = np.asarray(r["kvo"], dtype=np.float32).reshape(NL, 2, 4, 256, NH, 64)
            kv_state[4 * i:4 * i + 4] = kvo.transpose(2, 0, 1, 4, 3, 5)
        else:
            y_sample[i - 4] = y
    return y_prompt, y_sample, kv_state


_NC_CACHE = {}


def kernel(**inputs):
    in_maps = _prep(inputs)
    if "nc" not in _NC_CACHE:
        _NC_CACHE["nc"] = build_nc()
    res = run_bass_kernel_spmd(_NC_CACHE["nc"], in_maps, core_ids=list(range(8)))
    return _assemble(res.results)
```

```python
import numpy as np
import ml_dtypes
from contextlib import ExitStack
import concourse.bass as bass
import concourse.mybir as mybir
from concourse.bass_utils import run_bass_kernel_spmd

F32 = mybir.dt.float32
BF16 = mybir.dt.bfloat16
AF = mybir.ActivationFunctionType
ALU = mybir.AluOpType

D = 1024
T = 1024
KC = 8
DFF = 2816
NJ = 22
INW = 6656
NL = 2
NH = 8
NE = 22
E0 = 10
NEG = -30000.0
SLOT = 4608
NSLOT = 5
ND = 16
SEM_LIMIT = 12000

V_BMOD = 0
V_G1 = 96
V_G2 = 112
V_GF = 128
V_CONVW = 136
V_FCONV = 160
V_PSCALE = 292
V_CV = 300
V_BFLAG = 308
NV = 320

LOCAL_TILES = {0: [0, 1, 2, 3, 4, 5], 1: [2, 3, 4, 5, 6, 7]}


class Sched:
    def __init__(self, nc, es):
        self.nc = nc
        self.es = es
        self.eh = {"pe": nc.tensor, "act": nc.scalar, "dve": nc.vector, "pool": nc.gpsimd, "sp": nc.sync}
        self.epoch = {e: 0 for e in self.eh}
        self.cnt = {e: 0 for e in self.eh}
        self.sems = {}
        for e in self.eh:
            self.sems[(e, 0)] = es.enter_context(nc.semaphore(f"s_{e}_0"))
        self.dsem = [es.enter_context(nc.semaphore(f"s_dma_{i}")) for i in range(ND)]
        self.dcnt = [0] * ND
        self.dpool = {"sp": list(range(0, 6)), "pool": list(range(6, ND))}
        self.dnext = {"sp": 0, "pool": 0}
        self.seen = {e: {} for e in self.eh}
        self.last_w = {}
        self.readers = {}
        self.out_dmas = []
        self.aux = {}
        self.nops = {e: 0 for e in self.eh}
        self.pending = {e: False for e in self.eh}

    def _sem_of(self, src):
        if src[0] == "d":
            return self.dsem[src[1]]
        return self.sems[src]

    def _wait(self, eng, src, c):
        if self.seen[eng].get(src, 0) >= c:
            return
        self.eh[eng].wait_ge(self._sem_of(src), c)
        self.seen[eng][src] = c

    def _deps(self, eng, reads, writes):
        deps = {}

        def add(src, c):
            if deps.get(src, 0) < c:
                deps[src] = c

        for k in reads:
            w = self.last_w.get(k)
            if w:
                add(*w)
        for k in writes:
            w = self.last_w.get(k)
            if w:
                add(*w)
            for src, c in self.readers.get(k, {}).items():
                add(src, c)
        for src, c in deps.items():
            if eng == "pe" and src[0] == "pe":
                continue
            self._wait(eng, src, c)

    def _record(self, src, c, reads, writes):
        for k in writes:
            self.last_w[k] = (src, c)
            self.readers[k] = {}
        for k in reads:
            r = self.readers.setdefault(k, {})
            if r.get(src, 0) < c:
                r[src] = c

    def op(self, eng, fn, reads=(), writes=(), signal=True):
        if (not self.pending[eng]) and self.cnt[eng] >= SEM_LIMIT:
            self.epoch[eng] += 1
            self.cnt[eng] = 0
            self.sems[(eng, self.epoch[eng])] = self.es.enter_context(
                self.nc.semaphore(f"s_{eng}_{self.epoch[eng]}"))
        self._deps(eng, reads, writes)
        ins = fn(self.eh[eng])
        src = (eng, self.epoch[eng])
        if signal:
            ins.then_inc(self.sems[src], 1)
            self.cnt[eng] += 1
            c = self.cnt[eng]
            self.pending[eng] = False
        else:
            c = self.cnt[eng] + 1
            self.pending[eng] = True
        self._record(src, c, reads, writes)
        self.nops[eng] += 1
        return ins

    def dma(self, q, out, in_, reads=(), writes=(), is_output=False, ring=False):
        self._deps(q, reads, writes)
        lst = self.dpool[q]
        s = lst[self.dnext[q] % len(lst)]
        self.dnext[q] += 1
        src = ("d", s)
        if self.dcnt[s] > 0:
            self._wait(q, src, self.dcnt[s])
        self.eh[q].dma_start(out=out, in_=in_).then_inc(self.dsem[s], 16)
        self.dcnt[s] += 16
        self._record(src, self.dcnt[s], reads, writes)
        if is_output:
            self.out_dmas.append((src, self.dcnt[s]))
        if not ring:
            self.aux[src] = self.dcnt[s]

    def barrier(self):
        cur = [((e, self.epoch[e]), self.cnt[e]) for e in self.eh if self.cnt[e] > 0]
        cur += list(self.aux.items())
        self.aux = {}
        for e in self.eh:
            for src, c in cur:
                if src[0] == e and e == "pe":
                    continue
                self._wait(e, src, c)

    def finish(self):
        for src, c in self.out_dmas:
            self._wait("sp", src, c)
        self.barrier()


class _Stop(Exception):
    pass


def build_nc(debug=None, stop=None):
    nc = bass.Bass("TRN2", target_bir_lowering=False)

    def din(name, shape):
        return nc.dram_tensor(name, list(shape), F32, kind="ExternalInput").ap()

    def dout(name, shape):
        return nc.dram_tensor(name, list(shape), F32, kind="ExternalOutput").ap()

    xT_d = din("xT", [D, T])
    vecs_d = din("vecs", [128, NV])
    w_mod_d = din("w_mod", [NL, D, 6 * D])
    w_in_d = din("w_in", [NL, D, INW])
    w_br_d = din("w_branch", [NL, 3, 512, D])
    w_out_d = din("w_out", [NL, D, D])
    w_up_d = din("w_up", [NL, D, 2 * DFF])
    w_dn_d = din("w_down", [NL, DFF, D])
    pool_w_d = din("pool_w", [NL, 4, 128, 128])
    ctxk_d = din("ctxkT", [NL, 128, 4 * 256])
    ctxv_d = din("ctxv", [NL, 128, 2 * 512])
    utab_d = din("utab", [NL, 128, NH * NE * 64])
    rneg_d = din("rneg", [128, T])
    oh_d = din("oh", [128, 8 * 128])
    onesc_d = din("onesc", [128, 192])
    band_d = din("band", [128, 4 * 8 * 2 * 128])
    yT_d = dout("yT", [D, T])
    kvo_d = dout("kvo", [NL, 2, T, 512])
    dbg_d = {}
    if debug:
        for name, (shape, dt_) in debug.items():
            dbg_d[name] = nc.dram_tensor("dbg_" + name, list(shape), dt_, kind="ExternalOutput").ap()

    es = ExitStack()
    with es:
        def sb(name, shape, dt):
            return es.enter_context(nc.sbuf_tensor(name, list(shape), dt))

        X = sb("X", [128, KC, T], F32)
        Hh = sb("Hh", [128, KC, T], BF16)
        AR = sb("AR", [128, 28160], BF16)
        UT = sb("UT", [128, 2, NE * 64], BF16)
        WR = sb("WR", [128, NSLOT, SLOT], BF16)
        SCR = sb("SCR", [128, 5632], F32)
        BAND = sb("BAND", [128, 4 * 8 * 2 * 128], BF16)
        VEC = sb("VEC", [128, NV], F32)
        MOD = sb("MOD", [128, NL, 48], F32)
        GM = sb("GM", [128, NL * 2 * 8], F32)
        SMALL = sb("SMALL", [128, 256], F32)
        SCb = sb("SCb", [128, 8], BF16)
        RNEG = sb("RNEG", [128, T], BF16)
        OH = sb("OH", [128, 8 * 128], BF16)
        ONES = sb("ONES", [128, 128], BF16)
        ONESL = sb("ONESL", [128, 192], BF16)
        ONESC = sb("ONESC", [128, 192], BF16)
        CK = sb("CK", [128, 4 * 256], BF16)
        PS = [es.enter_context(nc.psum_tensor(f"ps{i}", [128, 512], F32)) for i in range(8)]

        S = Sched(nc, es)

        QK = AR[:, 0:8192].rearrange("p (c t) -> p c t", c=8)
        MERGED = QK
        VP = AR[:, 8192:8192 + 7680].rearrange("p (k h c) -> p k h c", k=10, h=4)
        YS = AR[:, 15872:15872 + 12288].rearrange("p (c t) -> p c t", c=12)
        PZ = AR[:, 15872 + 4096:15872 + 8192].rearrange("p (b c) -> p b c", b=8)
        A = AR[:, 0:NJ * T].rearrange("p (j t) -> p j t", j=NJ)
        BANDv = BAND[:, :].rearrange("p (g v s c) -> p g v s c", g=4, v=8, s=2)
        CKv = CK[:, :].rearrange("p (h k) -> p h k", h=4)
        OHv = OH[:, :].rearrange("p (b k) -> p b k", b=8)
        GMv = GM[:, :].rearrange("p (l j c) -> p l j c", l=NL, j=2)

        def vcol(base, idx=0, n=1):
            return VEC[:, base + idx: base + idx + n]

        ps_ctr = [0]

        def ps_next():
            i = ps_ctr[0] % 8
            ps_ctr[0] += 1
            return PS[i], ("ps", i)

        def HS(half):
            return slice(half * 512, half * 512 + 512)

        def scr_f32(off, n):
            return SCR[:, off:off + n]

        def scr_bf16(off, n):
            return SCR[:, off:off + n].bitcast(BF16)

        loads = []

        class Ring:
            issued = 0
            consumed = 0

        def ring_use_many(n):
            idx = Ring.consumed
            target = min(len(loads), idx + NSLOT)
            while Ring.issued < target:
                i = Ring.issued
                loads[i](i % NSLOT)
                Ring.issued += 1
            Ring.consumed += n
            return [(idx + k) % NSLOT for k in range(n)]

        def ring_use():
            return ring_use_many(1)[0]

        def wkeys(slot):
            return [("WR", slot, 0), ("WR", slot, 1)]

        def ld_cols(src2d, col0, ncols, nkc=KC):
            def f(slot):
                dst = WR[:, slot, 0:nkc * ncols].rearrange("p (k n) -> p k n", k=nkc)
                src = src2d.rearrange("(k p) n -> p k n", p=128)[:, :, col0:col0 + ncols]
                S.dma("pool", dst, src, writes=wkeys(slot), ring=True)
            return f

        def ld_gate_branch(l, c):
            def f(slot):
                for i in range(3):
                    dst = WR[:, slot, i * 1024:(i + 1) * 1024].rearrange("p (k n) -> p k n", k=KC)
                    col0 = 3584 + i * 1024 + c * 128
                    src = w_in_d[l].rearrange("(k p) n -> p k n", p=128)[:, :, col0:col0 + 128]
                    S.dma("pool", dst, src, writes=[("WR", slot, 0)] if i == 0 else [("WRx", slot, i)], ring=True)
                for i in range(3):
                    dst = WR[:, slot, 3072 + i * 512:3072 + (i + 1) * 512].rearrange("p (k n) -> p k n", k=4)
                    src = w_br_d[l, i].rearrange("(k p) n -> p k n", p=128)[:, :, c * 128:(c + 1) * 128]
                    S.dma("pool", dst, src, writes=[("WR", slot, 1)] if i == 0 else [("WRy", slot, i)], ring=True)
            return f

        def gate_keys(slot):
            return [("WR", slot, 0), ("WRx", slot, 1), ("WRx", slot, 2), ("WR", slot, 1), ("WRy", slot, 1), ("WRy", slot, 2)]

        def ld_up(l, jp):
            def f(slot):
                dstv = WR[:, slot, 0:4096].rearrange("p (k two n) -> p k two n", k=KC, two=2)
                srcv = w_up_d[l].rearrange("(k p) (two n) -> p k two n", p=128, two=2)
                for t_ in range(2):
                    S.dma("pool", dstv[:, :, t_, :], srcv[:, :, t_, jp * 256:(jp + 1) * 256], writes=[("WR", slot, t_)], ring=True)
            return f

        def ld_down(l, c):
            def f(slot):
                dst = WR[:, slot, 0:NJ * 128].rearrange("p (k n) -> p k n", k=NJ)
                src = w_dn_d[l].rearrange("(k p) n -> p k n", p=128)[:, :, c * 128:(c + 1) * 128]
                S.dma("pool", dst, src, writes=wkeys(slot), ring=True)
            return f

        def ld_poolw(l):
            def f(slot):
                dst = WR[:, slot, 0:512].rearrange("p (g e) -> p g e", g=4)
                src = pool_w_d[l].rearrange("g c e -> c g e")
                S.dma("pool", dst, src, writes=wkeys(slot), ring=True)
            return f

        for l in range(NL):
            for g in range(4):
                loads.append(ld_cols(w_mod_d[l], g * 512, 512))
            for g in (3, 4, 5, 6):
                loads.append(ld_cols(w_in_d[l], g * 512, 512))
            loads.append(ld_poolw(l))
            for g in (2, 1, 0):
                loads.append(ld_cols(w_in_d[l], g * 512, 512))
            for c in range(8):
                loads.append(ld_gate_branch(l, c))
            for g in range(4, 12):
                loads.append(ld_cols(w_mod_d[l], g * 512, 512))
            for g in range(2):
                loads.append(ld_cols(w_out_d[l], g * 512, 512))
            for jp in range(11):
                loads.append(ld_up(l, jp))
            for c in range(8):
                loads.append(ld_down(l, c))

        def mm_group(ps_ap, ps_key, pairs, reads):
            n = len(pairs)
            for i, (lt, rh) in enumerate(pairs):
                S.op("pe", lambda e, lt=lt, rh=rh, i=i: e.matmul(ps_ap, lt, rh, start=(i == 0), stop=(i == n - 1)),
                     reads=reads if i == 0 else (), writes=[ps_key] if i == 0 else (), signal=(i == n - 1))

        def Hkeys(half):
            return [("H", kc, half) for kc in range(KC)]

        S.dma("sp", VEC[:, :], vecs_d, writes=[("VEC",)])
        for dc in range(KC):
            S.dma("sp", X[:, dc, :], xT_d[dc * 128:(dc + 1) * 128, :], writes=[("X", dc, 0), ("X", dc, 1)])
        S.dma("pool", RNEG[:, :], rneg_d, writes=[("RNEG",)])
        S.dma("pool", OH[:, :], oh_d, writes=[("OH",)])
        S.dma("pool", ONESC[:, :], onesc_d, writes=[("ONESC",)])
        S.dma("pool", BAND[:, :].rearrange("p (a b) -> p a b", b=1024), band_d.rearrange("p (a b) -> p a b", b=1024), writes=[("BAND",)])
        S.op("dve", lambda e: e.memset(ONES[:, :], 1.0), writes=[("ONES",)])
        S.op("dve", lambda e: e.memset(ONESL[:, :], 1.0), writes=[("ONESL",)])
        S.op("dve", lambda e: e.memset(ONESL[:, 64:128], 0.0), writes=[("ONESL",)])
        S.op("dve", lambda e: e.memset(AR[:, 8192:8192 + 7680], 0.0), writes=[("VPall",)])
        S.op("act", lambda e: e.activation(out=SCb[:, :], in_=vcol(V_CV, 0, 8), func=AF.Silu),
             reads=[("VEC",)], writes=[("SCb",)])
        S.op("dve", lambda e: e.tensor_scalar(out=SMALL[:, 0:1], in0=vcol(V_BFLAG), scalar1=-1.0, scalar2=None,
                                              op0=ALU.mult), reads=[("VEC",)], writes=[("NBF",)])

        def mod_part(l, groups):
            for g in groups:
                slot = ring_use()
                ps, pk = ps_next()
                wv = WR[:, slot, 0:4096].rearrange("p (k n) -> p k n", k=KC)
                first = True
                for c in range(4):
                    col = g * 4 + c
                    for kc in range(KC):
                        S.op("pe", lambda e, c=c, kc=kc, col=col: e.matmul(
                            ps[:, col:col + 1], wv[:, kc, c * 128:(c + 1) * 128], SCb[:, kc:kc + 1],
                            start=(kc == 0), stop=(kc == KC - 1)),
                            reads=(wkeys(slot) + [("SCb",)]) if first else (), writes=[pk] if first else (),
                            signal=(c == 3 and kc == KC - 1))
                        first = False
                S.op("dve", lambda e, g=g: e.tensor_tensor(
                    out=MOD[:, l, g * 4:(g + 1) * 4], in0=ps[:, g * 4:(g + 1) * 4],
                    in1=VEC[:, V_BMOD + l * 48 + g * 4: V_BMOD + l * 48 + (g + 1) * 4], op=ALU.add),
                    reads=[pk, ("VEC",)], writes=[("MOD", l, g)])

        def gm_compute(l, j):
            sc0 = 8 if j == 0 else 32
            gb = (V_G1 if j == 0 else V_G2) + l * 8
            S.op("dve", lambda e: e.scalar_tensor_tensor(
                out=GMv[:, l, j, :], in0=MOD[:, l, sc0:sc0 + 8], scalar=1.0, in1=VEC[:, gb:gb + 8],
                op0=ALU.add, op1=ALU.mult),
                reads=[("MOD", l, sc0 // 4), ("MOD", l, sc0 // 4 + 1), ("VEC",)], writes=[("GM", l, j)])

        def norm(scale_ap_fn, bias_ap_fn, extra_reads, out_fn, out_keys_fn):
            SQ = [scr_bf16(0, 256), scr_bf16(256, 256)]
            SS = [scr_f32(512, 512), scr_f32(1024, 512)]
            RS = [scr_f32(1536, 512), scr_f32(2048, 512)]
            TMP = [scr_f32(2560, 512), scr_f32(3072, 512)]
            for half in range(2):
                ps, pk = ps_next()
                for dc in range(KC):
                    sq = SQ[dc % 2]
                    S.op("act", lambda e, dc=dc, sq=sq: e.activation(out=sq, in_=X[:, dc, HS(half)], func=AF.Square),
                         reads=[("X", dc, half)], writes=[("SQ", dc % 2)])
                    S.op("pe", lambda e, dc=dc, sq=sq: e.matmul(ps[:, :], ONES[:, :], sq, start=(dc == 0), stop=(dc == KC - 1)),
                         reads=[("SQ", dc % 2), ("ONES",)], writes=[pk])
                S.op("act", lambda e: e.activation(out=SS[half], in_=ps[:, :], func=AF.Sqrt, scale=1.0 / D, bias=1e-6),
                     reads=[pk], writes=[("SS", half)])
                S.op("dve", lambda e: e.reciprocal(out=RS[half], in_=SS[half]), reads=[("SS", half)], writes=[("RS", half)])
                for dc in range(KC):
                    tmp = TMP[dc % 2]
                    S.op("dve", lambda e, dc=dc, tmp=tmp: e.tensor_tensor(out=tmp, in0=X[:, dc, HS(half)], in1=RS[half], op=ALU.mult),
                         reads=[("X", dc, half), ("RS", half)], writes=[("TMPn", dc % 2)])
                    b = bias_ap_fn(dc)
                    S.op("act", lambda e, dc=dc, tmp=tmp, b=b: e.activation(
                        out=out_fn(dc, half), in_=tmp, func=AF.Identity, scale=scale_ap_fn(dc),
                        **({"bias": b} if b is not None else {})),
                        reads=[("TMPn", dc % 2)] + extra_reads, writes=out_keys_fn(dc, half))

        def stop_at(tag):
            if stop == tag:
                raise _Stop()

        def dbg_dump(name, ap, keys):
            if debug and name in dbg_d:
                S.barrier()
                S.dma("sp", dbg_d[name], ap, reads=keys, is_output=True)
                S.barrier()

        def run_layers():
          mod_part(0, range(0, 4))
          stop_at("mod0")
          for l in range(NL):
            if l > 0:
                mod_part(l, range(0, 4))
            gm_compute(l, 0)
            S.dma("pool", CK[:, :], ctxk_d[l], writes=[("CK",)])
            for j in range(2):
                dst = VP[:, 8:10, :, j * 128:j * 128 + 64]
                src = ctxv_d[l].rearrange("p (k h j d) -> p k h j d", k=2, h=4, j=2)[:, :, :, j, :]
                S.dma("pool", dst, src, writes=[("VPc", j)], reads=[("VPall",)])

            norm(lambda dc: GMv[:, l, 0, dc:dc + 1], lambda dc: MOD[:, l, dc:dc + 1],
                 [("GM", l, 0), ("MOD", l, 0), ("MOD", l, 1)],
                 lambda dc, half: Hh[:, dc, HS(half)], lambda dc, half: [("H", dc, half)])
            S.barrier()
            if l == 0:
                stop_at("norm1")
            if l == 0:
                dbg_dump("h1", Hh[:, :, :], [("H", dc, hf) for dc in range(KC) for hf in range(2)])

            KST = [scr_f32(4096, 512), scr_f32(4608, 512)]
            for which in range(2):
                slot = ring_use()
                wv = WR[:, slot, 0:4096].rearrange("p (k n) -> p k n", k=KC)
                for c in range(4):
                    for half in range(2):
                        ps, pk = ps_next()
                        mm_group(ps[:, :], pk, [(wv[:, kc, c * 128:(c + 1) * 128], Hh[:, kc, HS(half)]) for kc in range(KC)],
                                 wkeys(slot) + Hkeys(half))
                        if which == 0:
                            S.op("act", lambda e, c=c, half=half, ps=ps: e.activation(
                                out=QK[:, c, HS(half)], in_=ps[:, :], func=AF.Copy, scale=0.125),
                                reads=[pk], writes=[("QK", c, half)])
                        else:
                            S.op("dve", lambda e, c=c, half=half, ps=ps: e.tensor_copy(out=QK[:, 4 + c, HS(half)], in_=ps[:, :]),
                                 reads=[pk], writes=[("QK", 4 + c, half)])
                if which == 1:
                    for tb in range(8):
                        ps, pk = ps_next()
                        mm_group(ps[:, :], pk, [(Hh[:, kc, tb * 128:(tb + 1) * 128], wv[:, kc, :]) for kc in range(KC)],
                                 wkeys(slot) + Hkeys(tb // 4))
                        st = KST[tb % 2]
                        S.op("act", lambda e, ps=ps, st=st: e.activation(out=st, in_=ps[:, :], func=AF.Copy),
                             reads=[pk], writes=[("KST", tb % 2)])
                        S.dma("sp", kvo_d[l, 0, tb * 128:(tb + 1) * 128, :], st, reads=[("KST", tb % 2)], is_output=True)
            if l == 0:
                stop_at("A_qk")
            slot = ring_use()
            wv = WR[:, slot, 0:4096].rearrange("p (k n) -> p k n", k=KC)
            for tb in range(8):
                ps, pk = ps_next()
                mm_group(ps[:, :], pk, [(Hh[:, kc, tb * 128:(tb + 1) * 128], wv[:, kc, :]) for kc in range(KC)],
                         wkeys(slot) + Hkeys(tb // 4))
                st = KST[tb % 2]
                S.op("act", lambda e, ps=ps, st=st: e.activation(out=st, in_=ps[:, :], func=AF.Copy),
                     reads=[pk], writes=[("KST", tb % 2)])
                S.dma("sp", kvo_d[l, 1, tb * 128:(tb + 1) * 128, :], st, reads=[("KST", tb % 2)], is_output=True)
                for j in range(2):
                    S.op("dve", lambda e, st=st, tb=tb, j=j: e.tensor_copy(
                        out=VP[:, tb, :, j * 128:j * 128 + 64],
                        in_=st.rearrange("p (h j d) -> p h j d", h=4, j=2)[:, :, j, :]),
                        reads=[("KST", tb % 2), ("VPall",)], writes=[("VP", tb, j)])
            if l == 0:
                stop_at("A_v")
            slot = ring_use()
            wv = WR[:, slot, 0:4096].rearrange("p (k n) -> p k n", k=KC)
            for tb in range(8):
                ps, pk = ps_next()
                mm_group(ps[:, :], pk, [(Hh[:, kc, tb * 128:(tb + 1) * 128], wv[:, kc, :]) for kc in range(KC)],
                         wkeys(slot) + Hkeys(tb // 4))
                S.op("act", lambda e, ps=ps, tb=tb: e.activation(out=PZ[:, tb, :], in_=ps[:, :], func=AF.Copy),
                     reads=[pk], writes=[("PZ", tb)])

            if l == 0:
                stop_at("A_pz")
            slot = ring_use()
            pwv = WR[:, slot, 0:512].rearrange("p (g e) -> p g e", g=4)
            DT = [scr_bf16(0, 256), scr_bf16(256, 256)]
            for g in range(4):
                for half in range(2):
                    ps, pk = ps_next()
                    first = True
                    for obi in range(4):
                        ob = half * 4 + obi
                        terms = []
                        dv = 0 if ob == 0 else (1 if ob == 7 else (2 if ob % 2 == 0 else 3))
                        terms.append((ob, dv))
                        if ob >= 1:
                            terms.append((ob - 1, 4 if ob % 2 == 1 else 5))
                        if ob <= 6:
                            terms.append((ob + 1, 6 if ob % 2 == 0 else 7))
                        mats = [(ib, v, s) for (ib, v) in terms for s in range(2)]
                        for i, (ib, v, s) in enumerate(mats):
                            last = (obi == 3 and i == len(mats) - 1)
                            S.op("pe", lambda e, ib=ib, v=v, s=s, i=i, obi=obi, n=len(mats): e.matmul(
                                ps[:, obi * 128:(obi + 1) * 128], PZ[:, ib, g * 128:(g + 1) * 128], BANDv[:, g, v, s, :],
                                start=(i == 0), stop=(i == n - 1)),
                                reads=([("PZ", t) for t in range(8)] + [("BAND",)]) if first else (),
                                writes=[pk] if first else (), signal=last)
                            first = False
                    dt = DT[(g * 2 + half) % 2]
                    S.op("act", lambda e, ps=ps, dt=dt: e.activation(out=dt, in_=ps[:, :], func=AF.Copy),
                         reads=[pk], writes=[("DT", (g * 2 + half) % 2)])
                    ps2, pk2 = ps_next()
                    mm_group(ps2[:, :], pk2, [(pwv[:, g, :], dt)], wkeys(slot) + [("DT", (g * 2 + half) % 2)])
                    S.op("dve", lambda e, ps2=ps2, g=g, half=half: e.tensor_scalar(
                        out=YS[:, 8 + g, HS(half)], in0=ps2[:, :], scalar1=vcol(V_PSCALE, l * 4 + g), scalar2=None, op0=ALU.mult),
                        reads=[pk2, ("VEC",)], writes=[("YS", 8 + g, half)])

            if l == 0:
                stop_at("A_pool")
            slot_h, slot_c, slot_b = ring_use_many(3)
            wh = WR[:, slot_h, 0:4096].rearrange("p (k n) -> p k n", k=KC)
            wc = WR[:, slot_c, 0:4096].rearrange("p (k n) -> p k n", k=KC)
            wb = WR[:, slot_b, 0:4096].rearrange("p (k n) -> p k n", k=KC)
            HC = [scr_f32(512, 512), scr_f32(1024, 512)]
            CH = scr_f32(1536, 1026)
            YC = scr_f32(2562, 1024)
            S.op("dve", lambda e: e.memset(CH[:, 0:1], 0.0), writes=[("CH", 0), ("CH", 1)])
            S.op("dve", lambda e: e.memset(CH[:, 1025:1026], 0.0), writes=[("CHp",)])
            for c in range(4):
                cwb = V_CONVW + (l * 4 + c) * 3
                S.op("dve", lambda e, cwb=cwb: e.tensor_scalar(out=SMALL[:, 1:2], in0=vcol(cwb, 0), scalar1=SMALL[:, 0:1],
                                                              scalar2=None, op0=ALU.mult),
                     reads=[("VEC",), ("NBF",)], writes=[("FX", 0)])
                S.op("dve", lambda e, cwb=cwb: e.tensor_scalar(out=SMALL[:, 2:3], in0=vcol(cwb, 2), scalar1=SMALL[:, 0:1],
                                                              scalar2=None, op0=ALU.mult),
                     reads=[("VEC",), ("NBF",)], writes=[("FX", 1)])
                for half in range(2):
                    ps1, pk1 = ps_next()
                    mm_group(ps1[:, :], pk1, [(wh[:, kc, c * 128:(c + 1) * 128], Hh[:, kc, HS(half)]) for kc in range(KC)],
                             wkeys(slot_h) + Hkeys(half))
                    hc = HC[half]
                    S.op("act", lambda e, ps1=ps1, hc=hc: e.activation(out=hc, in_=ps1[:, :], func=AF.Copy),
                         reads=[pk1], writes=[("HC", half)])
                    ps2, pk2 = ps_next()
                    mm_group(ps2[:, :], pk2, [(wc[:, kc, c * 128:(c + 1) * 128], Hh[:, kc, HS(half)]) for kc in range(KC)],
                             wkeys(slot_c) + Hkeys(half))
                    S.op("dve", lambda e, ps2=ps2, hc=hc, half=half: e.tensor_tensor(
                        out=CH[:, 1 + half * 512: 1 + half * 512 + 512], in0=ps2[:, :], in1=hc, op=ALU.mult),
                        reads=[pk2, ("HC", half)], writes=[("CH", half)])
                chk = [("CH", 0), ("CH", 1), ("CHp",)]
                S.op("act", lambda e, cwb=cwb: e.activation(out=YC[:, :], in_=CH[:, 1:1025], func=AF.Identity, scale=vcol(cwb, 1)),
                     reads=chk + [("VEC",)], writes=[("YC",)])
                S.op("dve", lambda e, cwb=cwb: e.scalar_tensor_tensor(
                    out=YC[:, :], in0=CH[:, 0:1024], scalar=vcol(cwb, 0), in1=YC[:, :], op0=ALU.mult, op1=ALU.add),
                    reads=chk + [("VEC",), ("YC",)], writes=[("YC",)])
                S.op("dve", lambda e, cwb=cwb: e.scalar_tensor_tensor(
                    out=YC[:, :], in0=CH[:, 2:1026], scalar=vcol(cwb, 2), in1=YC[:, :], op0=ALU.mult, op1=ALU.add),
                    reads=chk + [("VEC",), ("YC",)], writes=[("YC",)])
                ycv = YC[:, :].rearrange("p (s t) -> p s t", s=4)
                chv = CH[:, 1:1025].rearrange("p (s t) -> p s t", s=4)
                S.op("dve", lambda e, ycv=ycv, chv=chv: e.scalar_tensor_tensor(
                    out=ycv[:, 1:4, 0:1], in0=chv[:, 0:3, 255:256], scalar=SMALL[:, 1:2], in1=ycv[:, 1:4, 0:1],
                    op0=ALU.mult, op1=ALU.add), reads=chk + [("FX", 0), ("YC",)], writes=[("YC",)])
                S.op("dve", lambda e, ycv=ycv, chv=chv: e.scalar_tensor_tensor(
                    out=ycv[:, 0:3, 255:256], in0=chv[:, 1:4, 0:1], scalar=SMALL[:, 2:3], in1=ycv[:, 0:3, 255:256],
                    op0=ALU.mult, op1=ALU.add), reads=chk + [("FX", 1), ("YC",)], writes=[("YC",)])
                for half in range(2):
                    ps3, pk3 = ps_next()
                    mm_group(ps3[:, :], pk3, [(wb[:, kc, c * 128:(c + 1) * 128], Hh[:, kc, HS(half)]) for kc in range(KC)],
                             wkeys(slot_b) + Hkeys(half))
                    S.op("dve", lambda e, ps3=ps3, half=half, c=c: e.tensor_tensor(
                        out=YS[:, c, HS(half)], in0=ps3[:, :], in1=YC[:, HS(half)], op=ALU.mult),
                        reads=[pk3, ("YC",)], writes=[("YS", c, half)])

            S.barrier()
            if l == 0:
                stop_at("phaseA")
            if l == 0:
                dbg_dump("qk", AR[:, 0:8192], [])
                dbg_dump("ysA", AR[:, 15872:15872 + 12288], [])

            TT_ = [scr_f32(0 + i * 512, 512) for i in range(3)]
            PP = [scr_bf16(1536 + i * 256, 256) for i in range(4)]
            RD = [scr_f32(3072 + i * 512, 512) for i in range(2)]
            work = []
            unit = 0
            for hp in range(4):
                for qc in range(2):
                    tiles = [("c", 8), ("c", 9)] + [("l", b_) for b_ in LOCAL_TILES[qc]]
                    ntot = 2 * len(tiles)
                    it = 0
                    for j in range(2):
                        for kind, b_ in tiles:
                            work.append(dict(hp=hp, qc=qc, j=j, kind=kind, b=b_, unit=unit, it=it, ntot=ntot,
                                             first_of_head=(b_ == 8 and kind == "c")))
                            it += 1
                    unit += 1
            LA = 3

            def ut_load(h):
                S.dma("pool", UT[:, h % 2, :], utab_d[l][:, h * NE * 64:(h + 1) * NE * 64], writes=[("UT", h % 2)])

            def emit_S(k):
                w = work[k]
                hp, qc, j, b_ = w["hp"], w["qc"], w["j"], w["b"]
                si = k % 4
                Sps, sk = PS[si], ("ps", si)
                qsrc = QK[j * 64:(j + 1) * 64, hp, HS(qc)]
                if w["kind"] == "l":
                    ksrc = QK[j * 64:(j + 1) * 64, 4 + hp, b_ * 128:(b_ + 1) * 128]
                    S.op("pe", lambda e: e.matmul(Sps[:, :], ksrc, qsrc, start=True, stop=False),
                         reads=[("QK", 4 + hp, b_ // 4), ("QK", hp, qc)], writes=[sk], signal=False)
                    S.op("pe", lambda e: e.matmul(Sps[:, :], OHv[j * 64:(j + 1) * 64, b_, :], RNEG[j * 64:(j + 1) * 64, HS(qc)],
                                                  start=False, stop=True),
                         reads=[("OH",), ("RNEG",)], writes=[sk])
                else:
                    ksrc = CKv[j * 64:(j + 1) * 64, hp, (b_ - 8) * 128:(b_ - 7) * 128]
                    S.op("pe", lambda e: e.matmul(Sps[:, :], ksrc, qsrc, start=True, stop=True),
                         reads=[("CK",), ("QK", hp, qc)], writes=[sk])

            def emit_rest(k):
                w = work[k]
                hp, qc, j, b_, it, ntot = w["hp"], w["qc"], w["j"], w["b"], w["it"], w["ntot"]
                u = w["unit"]
                h = 2 * hp + j
                NUM, nk = PS[4 + (u % 2) * 2], ("ps", 4 + (u % 2) * 2)
                DEN, dk = PS[5 + (u % 2) * 2], ("ps", 5 + (u % 2) * 2)
                si = k % 4
                Sps, sk = PS[si], ("ps", si)
                pi = k % 4
                pt = PP[pi]
                if w["first_of_head"] and qc == 1 and j == 1 and hp < 3:
                    ut_load(2 * (hp + 1))
                if w["first_of_head"] and qc == 0 and j == 0 and hp > 0:
                    ut_load(2 * hp + 1)
                if w["kind"] == "l":
                    e0 = 8 * qc - 2 * b_ + E0
                    ti = k % 3
                    tt = TT_[ti]
                    S.op("dve", lambda e: e.tensor_tensor(out=tt, in0=Sps[:, :], in1=UT[:, h % 2, e0 * 64:(e0 + 8) * 64], op=ALU.add),
                         reads=[sk, ("UT", h % 2)], writes=[("TT", ti)])
                    S.op("act", lambda e: e.activation(out=pt, in_=tt, func=AF.Exp),
                         reads=[("TT", ti)], writes=[("PP", pi)])
                    vkeys = [("VP", b_, 0), ("VP", b_, 1), ("VPall",)]
                    osrc, okeys = ONESL[:, j * 64:j * 64 + 128], [("ONESL",)]
                else:
                    S.op("act", lambda e: e.activation(out=pt, in_=Sps[:, :], func=AF.Exp),
                         reads=[sk], writes=[("PP", pi)])
                    vkeys = [("VPc", 0), ("VPc", 1), ("VPall",)]
                    osrc, okeys = ONESC[:, j * 64:j * 64 + 128], [("ONESC",)]
                vsrc = VP[:, b_, hp, j * 64:j * 64 + 128]
                S.op("pe", lambda e: e.matmul(NUM[:, :], vsrc, pt, start=(it == 0), stop=(it == ntot - 1)),
                     reads=[("PP", pi)] + vkeys, writes=[nk], signal=False)
                S.op("pe", lambda e: e.matmul(DEN[:, :], osrc, pt, start=(it == 0), stop=(it == ntot - 1)),
                     reads=[("PP", pi)] + okeys, writes=[dk])
                if it == ntot - 1:
                    rd = RD[u % 2]
                    S.op("dve", lambda e: e.reciprocal(out=rd, in_=DEN[:, :]), reads=[dk], writes=[("RD", u % 2)])
                    S.op("dve", lambda e: e.tensor_tensor(out=YS[:, 4 + hp, HS(qc)], in0=NUM[:, :], in1=rd, op=ALU.mult),
                         reads=[nk, dk, ("RD", u % 2)] + [("PZ", t) for t in range(8)], writes=[("YS", 4 + hp, qc)])

            ut_load(0)
            ut_load(1)
            for k in range(min(LA, len(work))):
                emit_S(k)
            for k in range(len(work)):
                if k + LA < len(work):
                    emit_S(k + LA)
                emit_rest(k)
            S.barrier()
            ps_ctr[0] = 0
            if l == 0:
                stop_at("attn")
            if l == 0:
                dbg_dump("ysB", AR[:, 15872:15872 + 12288], [])

            GT = [scr_f32(i * 512, 512) for i in range(6)]
            MM = [scr_f32(3072 + i * 512, 512) for i in range(4)]
            gctr = [0]
            mctr = [0]
            for c in range(8):
                slot = ring_use()
                gw = WR[:, slot, 0:3072].rearrange("p (i k n) -> p i k n", i=3, k=KC)
                bw = WR[:, slot, 3072:3072 + 1536].rearrange("p (i k n) -> p i k n", i=3, k=4)
                for half in range(2):
                    prods = []
                    for i in range(3):
                        psg, pkg = ps_next()
                        mm_group(psg[:, :], pkg, [(gw[:, i, kc, :], Hh[:, kc, HS(half)]) for kc in range(KC)],
                                 gate_keys(slot) + Hkeys(half))
                        gi = gctr[0] % 6
                        gctr[0] += 1
                        gt = GT[gi]
                        S.op("act", lambda e, psg=psg, gt=gt: e.activation(out=gt, in_=psg[:, :], func=AF.Sigmoid),
                             reads=[pkg], writes=[("GT", gi)])
                        psp, pkp = ps_next()
                        mm_group(psp[:, :], pkp, [(bw[:, i, kc, :], YS[:, 4 * i + kc, HS(half)]) for kc in range(4)],
                                 gate_keys(slot) + [("YS", 4 * i + kc, half) for kc in range(4)])
                        mi = mctr[0] % 4
                        mctr[0] += 1
                        mt = MM[mi]
                        S.op("dve", lambda e, psp=psp, gt=gt, mt=mt: e.tensor_tensor(out=mt, in0=psp[:, :], in1=gt, op=ALU.mult),
                             reads=[pkp, ("GT", gi)], writes=[("MM", mi)])
                        prods.append((mt, mi))
                    (m0, k0), (m1, k1), (m2, k2) = prods
                    S.op("pool", lambda e, m0=m0, m1=m1: e.tensor_tensor(out=m0, in0=m0, in1=m1, op=ALU.add),
                         reads=[("MM", k0), ("MM", k1)], writes=[("MM", k0)])
                    S.op("pool", lambda e, m0=m0, m2=m2, c=c, half=half: e.tensor_tensor(out=MERGED[:, c, HS(half)], in0=m0, in1=m2, op=ALU.add),
                         reads=[("MM", k0), ("MM", k2)], writes=[("QK", c, half)])
            if l == 0:
                stop_at("phaseC")
            if l == 0:
                dbg_dump("merged", AR[:, 0:8192], [("QK", c, hf) for c in range(8) for hf in range(2)])

            mod_part(l, range(4, 12))
            gm_compute(l, 1)
            for g in range(2):
                slot = ring_use()
                wv = WR[:, slot, 0:4096].rearrange("p (k n) -> p k n", k=KC)
                for cc in range(4):
                    c = g * 4 + cc
                    for half in range(2):
                        ps, pk = ps_next()
                        mm_group(ps[:, :], pk, [(wv[:, kc, cc * 128:(cc + 1) * 128], MERGED[:, kc, HS(half)]) for kc in range(KC)],
                                 wkeys(slot) + [("QK", kc, half) for kc in range(KC)])
                        S.op("dve", lambda e, ps=ps, c=c, half=half: e.scalar_tensor_tensor(
                            out=X[:, c, HS(half)], in0=ps[:, :], scalar=MOD[:, l, 16 + c:17 + c], in1=X[:, c, HS(half)],
                            op0=ALU.mult, op1=ALU.add),
                            reads=[pk, ("MOD", l, 4 + c // 4), ("X", c, half)], writes=[("X", c, half)])
            if l == 0:
                stop_at("x1")
            if l == 0:
                dbg_dump("x1", X[:, :, :], [("X", dc, hf) for dc in range(KC) for hf in range(2)])

            S.barrier()
            norm(lambda dc: GMv[:, l, 1, dc:dc + 1], lambda dc: MOD[:, l, 24 + dc:25 + dc],
                 [("GM", l, 1), ("MOD", l, 6), ("MOD", l, 7)],
                 lambda dc, half: Hh[:, dc, HS(half)], lambda dc, half: [("H", dc, half)])
            S.barrier()

            U32 = scr_f32(0, 1026)
            YF = scr_f32(1026, 1024)
            GF = scr_f32(2050, 1024)
            S.op("dve", lambda e: e.memset(U32[:, 0:1], 0.0), writes=[("U32", 0), ("U32", 1)])
            S.op("dve", lambda e: e.memset(U32[:, 1025:1026], 0.0), writes=[("U32p",)])
            for jp in range(11):
                slot = ring_use()
                wv = WR[:, slot, 0:4096].rearrange("p (k two n) -> p k two n", k=KC, two=2)
                for jj in range(2):
                    jx = jp * 2 + jj
                    fwb = V_FCONV + (l * NJ + jx) * 3
                    S.op("dve", lambda e, fwb=fwb: e.tensor_scalar(out=SMALL[:, 1:2], in0=vcol(fwb, 0), scalar1=SMALL[:, 0:1],
                                                                  scalar2=None, op0=ALU.mult),
                         reads=[("VEC",), ("NBF",)], writes=[("FX", 0)])
                    S.op("dve", lambda e, fwb=fwb: e.tensor_scalar(out=SMALL[:, 2:3], in0=vcol(fwb, 2), scalar1=SMALL[:, 0:1],
                                                                  scalar2=None, op0=ALU.mult),
                         reads=[("VEC",), ("NBF",)], writes=[("FX", 1)])
                    for half in range(2):
                        ps, pk = ps_next()
                        mm_group(ps[:, :], pk, [(wv[:, kc, 0, jj * 128:(jj + 1) * 128], Hh[:, kc, HS(half)]) for kc in range(KC)],
                                 wkeys(slot) + Hkeys(half))
                        S.op("act", lambda e, ps=ps, half=half: e.activation(out=U32[:, 1 + half * 512:1 + half * 512 + 512], in_=ps[:, :], func=AF.Copy),
                             reads=[pk], writes=[("U32", half)])
                    uk = [("U32", 0), ("U32", 1), ("U32p",)]
                    S.op("act", lambda e, fwb=fwb: e.activation(out=YF[:, :], in_=U32[:, 1:1025], func=AF.Identity, scale=vcol(fwb, 1)),
                         reads=uk + [("VEC",)], writes=[("YF",)])
                    S.op("dve", lambda e, fwb=fwb: e.scalar_tensor_tensor(
                        out=YF[:, :], in0=U32[:, 0:1024], scalar=vcol(fwb, 0), in1=YF[:, :], op0=ALU.mult, op1=ALU.add),
                        reads=uk + [("VEC",), ("YF",)], writes=[("YF",)])
                    S.op("dve", lambda e, fwb=fwb: e.scalar_tensor_tensor(
                        out=YF[:, :], in0=U32[:, 2:1026], scalar=vcol(fwb, 2), in1=YF[:, :], op0=ALU.mult, op1=ALU.add),
                        reads=uk + [("VEC",), ("YF",)], writes=[("YF",)])
                    yfv = YF[:, :].rearrange("p (s t) -> p s t", s=4)
                    uv = U32[:, 1:1025].rearrange("p (s t) -> p s t", s=4)
                    S.op("dve", lambda e, yfv=yfv, uv=uv: e.scalar_tensor_tensor(
                        out=yfv[:, 1:4, 0:1], in0=uv[:, 0:3, 255:256], scalar=SMALL[:, 1:2], in1=yfv[:, 1:4, 0:1],
                        op0=ALU.mult, op1=ALU.add), reads=uk + [("FX", 0), ("YF",)], writes=[("YF",)])
                    S.op("dve", lambda e, yfv=yfv, uv=uv: e.scalar_tensor_tensor(
                        out=yfv[:, 0:3, 255:256], in0=uv[:, 1:4, 0:1], scalar=SMALL[:, 2:3], in1=yfv[:, 0:3, 255:256],
                        op0=ALU.mult, op1=ALU.add), reads=uk + [("FX", 1), ("YF",)], writes=[("YF",)])
                    S.op("act", lambda e: e.activation(out=GF[:, :], in_=YF[:, :], func=AF.Gelu_apprx_tanh),
                         reads=[("YF",)], writes=[("GF",)])
                    for half in range(2):
                        ps, pk = ps_next()
                        mm_group(ps[:, :], pk, [(wv[:, kc, 1, jj * 128:(jj + 1) * 128], Hh[:, kc, HS(half)]) for kc in range(KC)],
                                 wkeys(slot) + Hkeys(half))
                        S.op("dve", lambda e, ps=ps, half=half, jx=jx: e.tensor_tensor(
                            out=A[:, jx, HS(half)], in0=ps[:, :], in1=GF[:, HS(half)], op=ALU.mult),
                            reads=[pk, ("GF",)], writes=[("A", jx, half)])
            if l == 0:
                stop_at("ffnup")
            if l == 0:
                dbg_dump("a", AR[:, 0:NJ * T], [("A", j, hf) for j in range(NJ) for hf in range(2)])

            for c in range(8):
                slot = ring_use()
                wv = WR[:, slot, 0:NJ * 128].rearrange("p (k n) -> p k n", k=NJ)
                for half in range(2):
                    ps, pk = ps_next()
                    mm_group(ps[:, :], pk, [(wv[:, j, :], A[:, j, HS(half)]) for j in range(NJ)],
                             wkeys(slot) + [("A", j, half) for j in range(NJ)])
                    S.op("dve", lambda e, ps=ps, c=c, half=half: e.scalar_tensor_tensor(
                        out=X[:, c, HS(half)], in0=ps[:, :], scalar=MOD[:, l, 40 + c:41 + c], in1=X[:, c, HS(half)],
                        op0=ALU.mult, op1=ALU.add),
                        reads=[pk, ("MOD", l, 10 + c // 4), ("X", c, half)], writes=[("X", c, half)])
            S.barrier()
            if l + 1 < NL:
                S.op("dve", lambda e: e.memset(AR[:, 8192:8192 + 7680], 0.0), writes=[("VPall",)])
            if l == 0:
                stop_at("layer0")
            if l == 0:
                dbg_dump("x2", X[:, :, :], [("X", dc, hf) for dc in range(KC) for hf in range(2)])

        try:
            run_layers()
        except _Stop:
            pass
        S.barrier()
        OUTB = [scr_f32(3584, 512), scr_f32(4096, 512), scr_f32(4608, 512), scr_f32(5120, 512)]
        octr = [0]

        def out_fn(dc, half):
            i = octr[0] % 4
            return OUTB[i]

        def out_keys(dc, half):
            return [("OUTB", octr[0] % 4)]

        SQ = [scr_bf16(0, 256), scr_bf16(256, 256)]
        SS = [scr_f32(512, 512), scr_f32(1024, 512)]
        RS = [scr_f32(1536, 512), scr_f32(2048, 512)]
        TMP = [scr_f32(2560, 512), scr_f32(3072, 512)]
        for half in range(2):
            ps, pk = ps_next()
            for dc in range(KC):
                sq = SQ[dc % 2]
                S.op("act", lambda e, dc=dc, sq=sq: e.activation(out=sq, in_=X[:, dc, HS(half)], func=AF.Square),
                     reads=[("X", dc, half)], writes=[("SQ", dc % 2)])
                S.op("pe", lambda e, dc=dc, sq=sq: e.matmul(ps[:, :], ONES[:, :], sq, start=(dc == 0), stop=(dc == KC - 1)),
                     reads=[("SQ", dc % 2), ("ONES",)], writes=[pk])
            S.op("act", lambda e: e.activation(out=SS[half], in_=ps[:, :], func=AF.Sqrt, scale=1.0 / D, bias=1e-6),
                 reads=[pk], writes=[("SS", half)])
            S.op("dve", lambda e: e.reciprocal(out=RS[half], in_=SS[half]), reads=[("SS", half)], writes=[("RS", half)])
            for dc in range(KC):
                tmp = TMP[dc % 2]
                S.op("dve", lambda e, dc=dc, tmp=tmp: e.tensor_tensor(out=tmp, in0=X[:, dc, HS(half)], in1=RS[half], op=ALU.mult),
                     reads=[("X", dc, half), ("RS", half)], writes=[("TMPn", dc % 2)])
                oi = octr[0] % 4
                octr[0] += 1
                ob = OUTB[oi]
                S.op("act", lambda e, dc=dc, tmp=tmp, ob=ob: e.activation(out=ob, in_=tmp, func=AF.Identity, scale=vcol(V_GF, dc)),
                     reads=[("TMPn", dc % 2), ("VEC",)], writes=[("OUTB", oi)])
                S.dma("sp", yT_d[dc * 128:(dc + 1) * 128, HS(half)], ob, reads=[("OUTB", oi)], is_output=True)
        S.finish()
        build_nc.stats = dict(S.nops)
    return nc


def _bf16_split(a):
    hi = a.astype(ml_dtypes.bfloat16).astype(np.float32)
    lo = (a - hi).astype(ml_dtypes.bfloat16).astype(np.float32)
    return hi, lo


def _band_tables(seq_len):
    out = np.zeros((128, 4, 8, 2, 128), np.float32)
    t = np.arange(T)
    for g, w in enumerate((2, 4, 8, 16)):
        B = np.zeros((T, T), np.float64)
        s0 = (t // seq_len) * seq_len
        lo = np.clip(t - w // 2, s0, s0 + seq_len)
        hi = np.clip(t - w // 2 + w, s0, s0 + seq_len)
        for tt in range(T):
            B[lo[tt]:hi[tt], tt] = 1.0 / (hi[tt] - lo[tt])
            B[tt, tt] -= 1.0
        B = B.astype(np.float32)

        def blk(ib, ob):
            return B[ib * 128:(ib + 1) * 128, ob * 128:(ob + 1) * 128]
        variants = [blk(0, 0), blk(7, 7), blk(2, 2), blk(1, 1), blk(0, 1), blk(1, 2), blk(1, 0), blk(2, 1)]
        for ob in range(8):
            dv = 0 if ob == 0 else (1 if ob == 7 else (2 if ob % 2 == 0 else 3))
            assert np.array_equal(blk(ob, ob), variants[dv])
            if ob >= 1:
                assert np.array_equal(blk(ob - 1, ob), variants[4 if ob % 2 == 1 else 5])
            if ob <= 6:
                assert np.array_equal(blk(ob + 1, ob), variants[6 if ob % 2 == 0 else 7])
        for v, m in enumerate(variants):
            h_, l_ = _bf16_split(m)
            out[:, g, v, 0, :] = h_
            out[:, g, v, 1, :] = l_
    return out.reshape(128, -1)


def _rneg_table(sample):
    r = np.full((16, 16), NEG, np.float32)
    for rk in range(16):
        for q in range(16):
            if sample:
                s = min(max(q - 4, 0), 8)
                ok = s <= rk < s + 8
            else:
                ok = (rk // 4) == (q // 4)
            if ok:
                r[rk, q] = 0.0
    out = np.zeros((128, T), np.float32)
    out[0:16] = np.repeat(r, 64, axis=1)
    out[64:80] = out[0:16]
    return out


def _onehot_rows():
    oh = np.zeros((16, 8, 128), np.float32)
    for b in range(8):
        oh[2 * b, b, 0:64] = 1.0
        oh[2 * b + 1, b, 64:128] = 1.0
    out = np.zeros((128, 1024), np.float32)
    out[0:16] = oh.reshape(16, 1024)
    out[64:80] = out[0:16]
    return out


def _u_table(rpb_l):
    U = np.full((128, NH, NE, 64), NEG, np.float32)
    cq = np.arange(64)
    cs = np.clip(cq - 8, 0, 48)
    for jj in range(2):
        for ei in range(NE):
            dr = jj - (ei - E0)
            if abs(dr) > 7:
                continue
            for ck in range(64):
                valid = (ck >= cs) & (ck < cs + 16)
                dc = np.clip(ck - cq + 15, 0, 30)
                vals = rpb_l[:, dr + 7, :][:, dc]
                U[jj * 64 + ck, :, ei, :] = np.where(valid[None, :], vals, NEG)
    return U.reshape(128, -1)


def _prep(inputs):
    f = lambda a: np.ascontiguousarray(np.asarray(a, dtype=np.float32))
    x_prompt, x_sample = f(inputs["x_prompt"]), f(inputs["x_sample"])
    cache_kv, c, c_ctx = f(inputs["cache_kv"]), f(inputs["c"]), f(inputs["c_ctx"])
    rpb = f(inputs["rpb"])
    shared = {
        "w_mod": f(inputs["w_mod"]), "w_in": f(inputs["w_in"]), "w_branch": f(inputs["w_branch"]),
        "w_out": f(inputs["w_out"]), "w_up": f(inputs["ffn_w_up"]), "w_down": f(inputs["ffn_w_down"]),
        "pool_w": f(inputs["pool_w"]),
    }

    def pk(v):
        return np.ascontiguousarray(v.reshape(-1, 128).T)

    vec_common = np.zeros((128, NV), np.float32)
    b_mod, g1, g2, gf = f(inputs["b_mod"]), f(inputs["g_norm1"]), f(inputs["g_norm2"]), f(inputs["g_final"])
    conv_w, fconv, pscale = f(inputs["conv_w"]), f(inputs["ffn_conv"]), f(inputs["pool_scale"])
    for l in range(NL):
        vec_common[:, V_BMOD + l * 48: V_BMOD + (l + 1) * 48] = pk(b_mod[l])
        vec_common[:, V_G1 + l * 8: V_G1 + (l + 1) * 8] = pk(g1[l])
        vec_common[:, V_G2 + l * 8: V_G2 + (l + 1) * 8] = pk(g2[l])
        for cc in range(4):
            for k in range(3):
                vec_common[:, V_CONVW + (l * 4 + cc) * 3 + k] = conv_w[l, k, cc * 128:(cc + 1) * 128]
        for j in range(NJ):
            for k in range(3):
                vec_common[:, V_FCONV + (l * NJ + j) * 3 + k] = fconv[l, k, j * 128:(j + 1) * 128]
        vec_common[:, V_PSCALE + l * 4: V_PSCALE + (l + 1) * 4] = pk(pscale[l])
    vec_common[:, V_GF:V_GF + 8] = pk(gf)

    ones_pat = np.concatenate([np.ones((128, 64)), np.zeros((128, 64)), np.ones((128, 64))], axis=1).astype(np.float32)
    band_p, band_s = _band_tables(256), _band_tables(1024)
    rneg_p, rneg_s = _rneg_table(False), _rneg_table(True)
    oh = _onehot_rows()
    utab_s = np.stack([_u_table(rpb[l]) for l in range(NL)])
    utab_p = np.zeros_like(utab_s)
    in_maps = []
    for i in range(8):
        sample = i >= 4
        vec = vec_common.copy()
        if sample:
            b = i - 4
            xT = np.ascontiguousarray(x_sample[b].T)
            vec[:, V_CV:V_CV + 8] = pk(c[b])
            vec[:, V_BFLAG] = 0.0
            ck = cache_kv[b, :, 0]
            ctxk = ck.reshape(NL, 4, 2, 256, 64).transpose(0, 2, 4, 1, 3).reshape(NL, 128, 4 * 256)
            cvv = cache_kv[b, :, 1]
            ctxv = cvv.transpose(0, 2, 1, 3).reshape(NL, 2, 128, 512).transpose(0, 2, 1, 3).reshape(NL, 128, 1024)
            m = {"ctxkT": np.ascontiguousarray(ctxk), "ctxv": np.ascontiguousarray(ctxv), "utab": utab_s,
                 "rneg": rneg_s, "oh": oh, "onesc": ones_pat, "band": band_s}
        else:
            xT = np.ascontiguousarray(x_prompt[4 * i:4 * i + 4].reshape(T, D).T)
            vec[:, V_CV:V_CV + 8] = pk(c_ctx)
            vec[:, V_BFLAG] = 1.0
            m = {"ctxkT": np.zeros((NL, 128, 1024), np.float32), "ctxv": np.zeros((NL, 128, 1024), np.float32),
                 "utab": utab_p, "rneg": rneg_p, "oh": oh, "onesc": np.zeros_like(ones_pat), "band": band_p}
        m.update(shared)
        m["xT"] = xT
        m["vecs"] = vec
        in_maps.append(m)
    return in_maps


def _assemble(results):
    y_prompt = np.empty((16, 256, D), np.float32)
    y_sample = np.empty((4, T, D), np.float32)
    kv_state = np.empty((16, NL, 2, NH, 256, 64), np.float32)
    for i in range(8):
        r = results[i]
        y = np.asarray(r["yT"], dtype=np.float32).T
        if i < 4:
            y_prompt[4 * i:4 * i + 4] = y.reshape(4, 256, D)
            kvo = np.asarray(r["kvo"], dtype=np.float32).reshape(NL, 2, 4, 256, NH, 64)
            kv_state[4 * i:4 * i + 4] = kvo.transpose(2, 0, 1, 4, 3, 5)
        else:
            y_sample[i - 4] = y
    return y_prompt, y_sample, kv_state


_NC_CACHE = {}


def kernel(**inputs):
    in_maps = _prep(inputs)
    if "nc" not in _NC_CACHE:
        _NC_CACHE["nc"] = build_nc()
    res = run_bass_kernel_spmd(_NC_CACHE["nc"], in_maps, core_ids=list(range(8)))
    return _assemble(res.results)
```

```python
import numpy as np
import ml_dtypes
from contextlib import ExitStack
import concourse.bass as bass
import concourse.mybir as mybir
from concourse.bass_utils import run_bass_kernel_spmd

F32 = mybir.dt.float32
BF16 = mybir.dt.bfloat16
AF = mybir.ActivationFunctionType
ALU = mybir.AluOpType

D = 1024
T = 1024
KC = 8
DFF = 2816
NJ = 22
INW = 6656
NL = 2
NH = 8
NE = 22
E0 = 10
NEG = -30000.0
SLOT = 4608
NSLOT = 5
ND = 16
SEM_LIMIT = 12000

V_BMOD = 0
V_G1 = 96
V_G2 = 112
V_GF = 128
V_CONVW = 136
V_FCONV = 160
V_PSCALE = 292
V_CV = 300
V_BFLAG = 308
NV = 320

LOCAL_TILES = {0: [0, 1, 2, 3, 4, 5], 1: [2, 3, 4, 5, 6, 7]}


class Sched:
    def __init__(self, nc, es):
        self.nc = nc
        self.es = es
        self.eh = {"pe": nc.tensor, "act": nc.scalar, "dve": nc.vector, "pool": nc.gpsimd, "sp": nc.sync}
        self.epoch = {e: 0 for e in self.eh}
        self.cnt = {e: 0 for e in self.eh}
        self.sems = {}
        for e in self.eh:
            self.sems[(e, 0)] = es.enter_context(nc.semaphore(f"s_{e}_0"))
        self.dsem = [es.enter_context(nc.semaphore(f"s_dma_{i}")) for i in range(ND)]
        self.dcnt = [0] * ND
        self.dpool = {"sp": list(range(0, 6)), "pool": list(range(6, ND))}
        self.dnext = {"sp": 0, "pool": 0}
        self.seen = {e: {} for e in self.eh}
        self.last_w = {}
        self.readers = {}
        self.out_dmas = []
        self.aux = {}
        self.nops = {e: 0 for e in self.eh}
        self.pending = {e: False for e in self.eh}
        self.phase = "init"
        self.phase_of = {}

    def _sem_of(self, src):
        if src[0] == "d":
            return self.dsem[src[1]]
        return self.sems[src]

    def _wait(self, eng, src, c):
        if self.seen[eng].get(src, 0) >= c:
            return
        self.eh[eng].wait_ge(self._sem_of(src), c)
        self.seen[eng][src] = c

    def _deps(self, eng, reads, writes):
        deps = {}

        def add(src, c):
            if deps.get(src, 0) < c:
                deps[src] = c

        for k in reads:
            w = self.last_w.get(k)
            if w:
                add(*w)
        for k in writes:
            w = self.last_w.get(k)
            if w:
                add(*w)
            for src, c in self.readers.get(k, {}).items():
                add(src, c)
        for src, c in deps.items():
            if eng == "pe" and src[0] == "pe":
                continue
            self._wait(eng, src, c)

    def _record(self, src, c, reads, writes):
        for k in writes:
            self.last_w[k] = (src, c)
            self.readers[k] = {}
        for k in reads:
            r = self.readers.setdefault(k, {})
            if r.get(src, 0) < c:
                r[src] = c

    def op(self, eng, fn, reads=(), writes=(), signal=True):
        if (not self.pending[eng]) and self.cnt[eng] >= SEM_LIMIT:
            self.epoch[eng] += 1
            self.cnt[eng] = 0
            self.sems[(eng, self.epoch[eng])] = self.es.enter_context(
                self.nc.semaphore(f"s_{eng}_{self.epoch[eng]}"))
        self._deps(eng, reads, writes)
        ins = fn(self.eh[eng])
        src = (eng, self.epoch[eng])
        if signal:
            ins.then_inc(self.sems[src], 1)
            self.cnt[eng] += 1
            c = self.cnt[eng]
            self.pending[eng] = False
        else:
            c = self.cnt[eng] + 1
            self.pending[eng] = True
        self._record(src, c, reads, writes)
        self.nops[eng] += 1
        try:
            self.phase_of[ins.ins.name] = self.phase
        except Exception:
            pass
        return ins

    def dma(self, q, out, in_, reads=(), writes=(), is_output=False, ring=False):
        self._deps(q, reads, writes)
        lst = self.dpool[q]
        s = lst[self.dnext[q] % len(lst)]
        self.dnext[q] += 1
        src = ("d", s)
        if self.dcnt[s] > 0:
            self._wait(q, src, self.dcnt[s])
        self.eh[q].dma_start(out=out, in_=in_).then_inc(self.dsem[s], 16)
        self.dcnt[s] += 16
        self._record(src, self.dcnt[s], reads, writes)
        if is_output:
            self.out_dmas.append((src, self.dcnt[s]))
        if not ring:
            self.aux[src] = self.dcnt[s]

    def barrier(self):
        cur = [((e, self.epoch[e]), self.cnt[e]) for e in self.eh if self.cnt[e] > 0]
        cur += list(self.aux.items())
        self.aux = {}
        for e in self.eh:
            for src, c in cur:
                if src[0] == e and e == "pe":
                    continue
                self._wait(e, src, c)

    def finish(self):
        for src, c in self.out_dmas:
            self._wait("sp", src, c)
        self.barrier()


class _Stop(Exception):
    pass


def build_nc(debug=None, stop=None):
    nc = bass.Bass("TRN2", target_bir_lowering=False)

    def din(name, shape):
        return nc.dram_tensor(name, list(shape), F32, kind="ExternalInput").ap()

    def dout(name, shape):
        return nc.dram_tensor(name, list(shape), F32, kind="ExternalOutput").ap()

    xT_d = din("xT", [D, T])
    vecs_d = din("vecs", [128, NV])
    w_mod_d = din("w_mod", [NL, D, 6 * D])
    w_in_d = din("w_in", [NL, D, INW])
    w_br_d = din("w_branch", [NL, 3, 512, D])
    w_out_d = din("w_out", [NL, D, D])
    w_up_d = din("w_up", [NL, D, 2 * DFF])
    w_dn_d = din("w_down", [NL, DFF, D])
    pool_w_d = din("pool_w", [NL, 4, 128, 128])
    ctxk_d = din("ctxkT", [NL, 128, 8 * 256])
    ctxv_d = din("ctxv", [NL, 128, 2 * 512])
    utab_d = din("utab", [NL, 128, NH * NE * 64])
    rneg_d = din("rneg", [128, T])
    oh_d = din("oh", [128, 8 * 128])
    onesc_d = din("onesc", [128, 192])
    band_d = din("band", [128, 4 * 8 * 2 * 128])
    yT_d = dout("yT", [D, T])
    kvo_d = dout("kvo", [NL, 2, T, 512])
    dbg_d = {}
    if debug:
        for name, (shape, dt_) in debug.items():
            dbg_d[name] = nc.dram_tensor("dbg_" + name, list(shape), dt_, kind="ExternalOutput").ap()

    es = ExitStack()
    with es:
        def sb(name, shape, dt):
            return es.enter_context(nc.sbuf_tensor(name, list(shape), dt))

        X = sb("X", [128, KC, T], F32)
        Hh = sb("Hh", [128, KC, T], BF16)
        AR = sb("AR", [128, 32256], BF16)
        UT = sb("UT", [128, 2, NE * 64], BF16)
        WR = sb("WR", [128, NSLOT, SLOT], BF16)
        SCR = sb("SCR", [128, 5632], F32)
        VEC = sb("VEC", [128, NV], F32)
        MOD = sb("MOD", [128, NL, 48], F32)
        GM = sb("GM", [128, NL * 2 * 8], F32)
        SMALL = sb("SMALL", [128, 256], F32)
        SCb = sb("SCb", [128, 8], BF16)
        RNEG = sb("RNEG", [128, T], BF16)
        OH = sb("OH", [128, 8 * 128], BF16)
        ONES = sb("ONES", [128, 128], BF16)
        ONESL = sb("ONESL", [128, 192], BF16)
        ONESC = sb("ONESC", [128, 192], BF16)
        CK = sb("CK", [128, 8 * 256], BF16)
        PS = [es.enter_context(nc.psum_tensor(f"ps{i}", [128, 512], F32)) for i in range(8)]

        S = Sched(nc, es)

        QK = AR[:, 0:4096].rearrange("p (c t) -> p c t", c=4)
        KZ = AR[:, 4096:12288].rearrange("p (h t) -> p h t", h=8)
        MERGED = AR[:, 0:8192].rearrange("p (c t) -> p c t", c=8)
        VP = AR[:, 12288:12288 + 7680].rearrange("p (k h c) -> p k h c", k=10, h=4)
        YS = AR[:, 19968:19968 + 12288].rearrange("p (c t) -> p c t", c=12)
        PZ = AR[:, 19968 + 4096:19968 + 8192].rearrange("p (b c) -> p b c", b=8)
        A = AR[:, 0:NJ * T].rearrange("p (j t) -> p j t", j=NJ)
        BAND = AR[:, 4096:12288]
        BANDv = BAND.rearrange("p (g v s c) -> p g v s c", g=4, v=8, s=2)
        CKv = CK[:, :].rearrange("p (h k) -> p h k", h=8)
        OHv = OH[:, :].rearrange("p (b k) -> p b k", b=8)
        GMv = GM[:, :].rearrange("p (l j c) -> p l j c", l=NL, j=2)

        def vcol(base, idx=0, n=1):
            return VEC[:, base + idx: base + idx + n]

        ps_ctr = [0]

        def ps_next():
            i = ps_ctr[0] % 8
            ps_ctr[0] += 1
            return PS[i], ("ps", i)

        def HS(half):
            return slice(half * 512, half * 512 + 512)

        def scr_f32(off, n):
            return SCR[:, off:off + n]

        def scr_bf16(off, n):
            return SCR[:, off:off + n].bitcast(BF16)

        loads = []

        class Ring:
            issued = 0
            consumed = 0

        def ring_use_many(n):
            idx = Ring.consumed
            target = min(len(loads), idx + NSLOT)
            while Ring.issued < target:
                i = Ring.issued
                loads[i](i % NSLOT)
                Ring.issued += 1
            Ring.consumed += n
            return [(idx + k) % NSLOT for k in range(n)]

        def ring_use():
            return ring_use_many(1)[0]

        def wkeys(slot):
            return [("WR", slot, 0), ("WR", slot, 1)]

        def ld_cols(src2d, col0, ncols, nkc=KC):
            def f(slot):
                dst = WR[:, slot, 0:nkc * ncols].rearrange("p (k n) -> p k n", k=nkc)
                src = src2d.rearrange("(k p) n -> p k n", p=128)[:, :, col0:col0 + ncols]
                S.dma("pool", dst, src, writes=wkeys(slot), ring=True)
            return f

        def ld_gate_branch(l, c):
            def f(slot):
                for i in range(3):
                    dst = WR[:, slot, i * 1024:(i + 1) * 1024].rearrange("p (k n) -> p k n", k=KC)
                    col0 = 3584 + i * 1024 + c * 128
                    src = w_in_d[l].rearrange("(k p) n -> p k n", p=128)[:, :, col0:col0 + 128]
                    S.dma("pool", dst, src, writes=[("WR", slot, 0)] if i == 0 else [("WRx", slot, i)], ring=True)
                for i in range(3):
                    dst = WR[:, slot, 3072 + i * 512:3072 + (i + 1) * 512].rearrange("p (k n) -> p k n", k=4)
                    src = w_br_d[l, i].rearrange("(k p) n -> p k n", p=128)[:, :, c * 128:(c + 1) * 128]
                    S.dma("pool", dst, src, writes=[("WR", slot, 1)] if i == 0 else [("WRy", slot, i)], ring=True)
            return f

        def gate_keys(slot):
            return [("WR", slot, 0), ("WRx", slot, 1), ("WRx", slot, 2), ("WR", slot, 1), ("WRy", slot, 1), ("WRy", slot, 2)]

        def ld_up(l, jp):
            def f(slot):
                dstv = WR[:, slot, 0:4096].rearrange("p (k two n) -> p k two n", k=KC, two=2)
                srcv = w_up_d[l].rearrange("(k p) (two n) -> p k two n", p=128, two=2)
                for t_ in range(2):
                    S.dma("pool", dstv[:, :, t_, :], srcv[:, :, t_, jp * 256:(jp + 1) * 256], writes=[("WR", slot, t_)], ring=True)
            return f

        def ld_down(l, c):
            def f(slot):
                dst = WR[:, slot, 0:NJ * 128].rearrange("p (k n) -> p k n", k=NJ)
                src = w_dn_d[l].rearrange("(k p) n -> p k n", p=128)[:, :, c * 128:(c + 1) * 128]
                S.dma("pool", dst, src, writes=wkeys(slot), ring=True)
            return f

        def ld_poolw(l):
            def f(slot):
                dst = WR[:, slot, 0:512].rearrange("p (g e) -> p g e", g=4)
                src = pool_w_d[l].rearrange("g c e -> c g e")
                S.dma("pool", dst, src, writes=wkeys(slot), ring=True)
            return f

        for l in range(NL):
            for g in range(4):
                loads.append(ld_cols(w_mod_d[l], g * 512, 512))
            loads.append(ld_cols(w_in_d[l], 6 * 512, 512))
            loads.append(ld_poolw(l))
            for g in (3, 4, 5):
                loads.append(ld_cols(w_in_d[l], g * 512, 512))
            for g in (2, 1, 0):
                loads.append(ld_cols(w_in_d[l], g * 512, 512))
            for c in range(8):
                loads.append(ld_gate_branch(l, c))
            for g in range(4, 12):
                loads.append(ld_cols(w_mod_d[l], g * 512, 512))
            for g in range(2):
                loads.append(ld_cols(w_out_d[l], g * 512, 512))
            for jp in range(11):
                loads.append(ld_up(l, jp))
            for c in range(8):
                loads.append(ld_down(l, c))

        def mm_group(ps_ap, ps_key, pairs, reads):
            n = len(pairs)
            for i, (lt, rh) in enumerate(pairs):
                S.op("pe", lambda e, lt=lt, rh=rh, i=i: e.matmul(ps_ap, lt, rh, start=(i == 0), stop=(i == n - 1)),
                     reads=reads if i == 0 else (), writes=[ps_key] if i == 0 else (), signal=(i == n - 1))

        def Hkeys(half):
            return [("H", kc, half) for kc in range(KC)]

        S.dma("sp", VEC[:, :], vecs_d, writes=[("VEC",)])
        for dc in range(KC):
            S.dma("sp", X[:, dc, :], xT_d[dc * 128:(dc + 1) * 128, :], writes=[("X", dc, 0), ("X", dc, 1)])
        S.dma("pool", RNEG[:, :], rneg_d, writes=[("RNEG",)])
        S.dma("pool", OH[:, :], oh_d, writes=[("OH",)])
        S.dma("pool", ONESC[:, :], onesc_d, writes=[("ONESC",)])
        S.op("dve", lambda e: e.memset(ONES[:, :], 1.0), writes=[("ONES",)])
        S.op("dve", lambda e: e.memset(ONESL[:, :], 1.0), writes=[("ONESL",)])
        S.op("dve", lambda e: e.memset(ONESL[:, 64:128], 0.0), writes=[("ONESL",)])
        S.op("dve", lambda e: e.memset(AR[:, 12288:12288 + 7680], 0.0), writes=[("VPall",)])
        S.op("act", lambda e: e.activation(out=SCb[:, :], in_=vcol(V_CV, 0, 8), func=AF.Silu),
             reads=[("VEC",)], writes=[("SCb",)])
        S.op("dve", lambda e: e.tensor_scalar(out=SMALL[:, 0:1], in0=vcol(V_BFLAG), scalar1=-1.0, scalar2=None,
                                              op0=ALU.mult), reads=[("VEC",)], writes=[("NBF",)])

        def mod_part(l, groups):
            for g in groups:
                slot = ring_use()
                ps, pk = ps_next()
                wv = WR[:, slot, 0:4096].rearrange("p (k n) -> p k n", k=KC)
                first = True
                for c in range(4):
                    col = g * 4 + c
                    for kc in range(KC):
                        S.op("pe", lambda e, c=c, kc=kc, col=col: e.matmul(
                            ps[:, col:col + 1], wv[:, kc, c * 128:(c + 1) * 128], SCb[:, kc:kc + 1],
                            start=(kc == 0), stop=(kc == KC - 1)),
                            reads=(wkeys(slot) + [("SCb",)]) if first else (), writes=[pk] if first else (),
                            signal=(c == 3 and kc == KC - 1))
                        first = False
                S.op("dve", lambda e, g=g: e.tensor_tensor(
                    out=MOD[:, l, g * 4:(g + 1) * 4], in0=ps[:, g * 4:(g + 1) * 4],
                    in1=VEC[:, V_BMOD + l * 48 + g * 4: V_BMOD + l * 48 + (g + 1) * 4], op=ALU.add),
                    reads=[pk, ("VEC",)], writes=[("MOD", l, g)])

        def gm_compute(l, j):
            sc0 = 8 if j == 0 else 32
            gb = (V_G1 if j == 0 else V_G2) + l * 8
            S.op("dve", lambda e: e.scalar_tensor_tensor(
                out=GMv[:, l, j, :], in0=MOD[:, l, sc0:sc0 + 8], scalar=1.0, in1=VEC[:, gb:gb + 8],
                op0=ALU.add, op1=ALU.mult),
                reads=[("MOD", l, sc0 // 4), ("MOD", l, sc0 // 4 + 1), ("VEC",)], writes=[("GM", l, j)])

        def norm(scale_ap_fn, bias_ap_fn, extra_reads, out_fn, out_keys_fn):
            SQ = [scr_bf16(0, 256), scr_bf16(256, 256)]
            SS = [scr_f32(512, 512), scr_f32(1024, 512)]
            RS = [scr_f32(1536, 512), scr_f32(2048, 512)]
            TMP = [scr_f32(2560, 512), scr_f32(3072, 512)]
            for half in range(2):
                ps, pk = ps_next()
                for dc in range(KC):
                    sq = SQ[dc % 2]
                    S.op("act", lambda e, dc=dc, sq=sq: e.activation(out=sq, in_=X[:, dc, HS(half)], func=AF.Square),
                         reads=[("X", dc, half)], writes=[("SQ", dc % 2)])
                    S.op("pe", lambda e, dc=dc, sq=sq: e.matmul(ps[:, :], ONES[:, :], sq, start=(dc == 0), stop=(dc == KC - 1)),
                         reads=[("SQ", dc % 2), ("ONES",)], writes=[pk])
                S.op("act", lambda e: e.activation(out=SS[half], in_=ps[:, :], func=AF.Sqrt, scale=1.0 / D, bias=1e-6),
                     reads=[pk], writes=[("SS", half)])
                S.op("dve", lambda e: e.reciprocal(out=RS[half], in_=SS[half]), reads=[("SS", half)], writes=[("RS", half)])
                for dc in range(KC):
                    tmp = TMP[dc % 2]
                    S.op("dve", lambda e, dc=dc, tmp=tmp: e.tensor_tensor(out=tmp, in0=X[:, dc, HS(half)], in1=RS[half], op=ALU.mult),
                         reads=[("X", dc, half), ("RS", half)], writes=[("TMPn", dc % 2)])
                    b = bias_ap_fn(dc)
                    S.op("act", lambda e, dc=dc, tmp=tmp, b=b: e.activation(
                        out=out_fn(dc, half), in_=tmp, func=AF.Identity, scale=scale_ap_fn(dc),
                        **({"bias": b} if b is not None else {})),
                        reads=[("TMPn", dc % 2)] + extra_reads, writes=out_keys_fn(dc, half))

        def stop_at(tag):
            if stop == tag:
                raise _Stop()

        def dbg_dump(name, ap, keys):
            if debug and name in dbg_d:
                S.barrier()
                S.dma("sp", dbg_d[name], ap, reads=keys, is_output=True)
                S.barrier()

        def run_layers():
          mod_part(0, range(0, 4))
          stop_at("mod0")
          for l in range(NL):
            if l > 0:
                mod_part(l, range(0, 4))
            gm_compute(l, 0)
            S.dma("pool", CK[:, :].rearrange("p (a b) -> p a b", b=1024), ctxk_d[l].rearrange("p (a b) -> p a b", b=1024),
                  writes=[("CK",)])
            S.dma("pool", BAND.rearrange("p (a b) -> p a b", b=1024), band_d.rearrange("p (a b) -> p a b", b=1024),
                  writes=[("BAND",)])
            for j in range(2):
                dst = VP[:, 8:10, :, j * 128:j * 128 + 64]
                src = ctxv_d[l].rearrange("p (k h j d) -> p k h j d", k=2, h=4, j=2)[:, :, :, j, :]
                S.dma("pool", dst, src, writes=[("VPc", j)], reads=[("VPall",)])

            S.phase = f"L{l}:norm1"
            norm(lambda dc: GMv[:, l, 0, dc:dc + 1], lambda dc: MOD[:, l, dc:dc + 1],
                 [("GM", l, 0), ("MOD", l, 0), ("MOD", l, 1)],
                 lambda dc, half: Hh[:, dc, HS(half)], lambda dc, half: [("H", dc, half)])
            S.barrier()
            if l == 0:
                stop_at("norm1")
            if l == 0:
                dbg_dump("h1", Hh[:, :, :], [("H", dc, hf) for dc in range(KC) for hf in range(2)])

            if l == 0:
                stop_at("A_pz0")
            slot = ring_use()
            wv = WR[:, slot, 0:4096].rearrange("p (k n) -> p k n", k=KC)
            for tb in range(8):
                ps, pk = ps_next()
                mm_group(ps[:, :], pk, [(Hh[:, kc, tb * 128:(tb + 1) * 128], wv[:, kc, :]) for kc in range(KC)],
                         wkeys(slot) + Hkeys(tb // 4))
                S.op("act", lambda e, ps=ps, tb=tb: e.activation(out=PZ[:, tb, :], in_=ps[:, :], func=AF.Copy),
                     reads=[pk], writes=[("PZ", tb)])

            if l == 0:
                stop_at("A_pz")
            S.phase = f"L{l}:A_pool"
            slot = ring_use()
            pwv = WR[:, slot, 0:512].rearrange("p (g e) -> p g e", g=4)
            DT = [scr_bf16(0, 256), scr_bf16(256, 256)]
            for g in range(4):
                for half in range(2):
                    ps, pk = ps_next()
                    first = True
                    for obi in range(4):
                        ob = half * 4 + obi
                        terms = []
                        dv = 0 if ob == 0 else (1 if ob == 7 else (2 if ob % 2 == 0 else 3))
                        terms.append((ob, dv))
                        if ob >= 1:
                            terms.append((ob - 1, 4 if ob % 2 == 1 else 5))
                        if ob <= 6:
                            terms.append((ob + 1, 6 if ob % 2 == 0 else 7))
                        mats = [(ib, v, s) for (ib, v) in terms for s in range(2)]
                        for i, (ib, v, s) in enumerate(mats):
                            last = (obi == 3 and i == len(mats) - 1)
                            S.op("pe", lambda e, ib=ib, v=v, s=s, i=i, obi=obi, n=len(mats): e.matmul(
                                ps[:, obi * 128:(obi + 1) * 128], PZ[:, ib, g * 128:(g + 1) * 128], BANDv[:, g, v, s, :],
                                start=(i == 0), stop=(i == n - 1)),
                                reads=([("PZ", t) for t in range(8)] + [("BAND",)]) if first else (),
                                writes=[pk] if first else (), signal=last)
                            first = False
                    dt = DT[(g * 2 + half) % 2]
                    S.op("act", lambda e, ps=ps, dt=dt: e.activation(out=dt, in_=ps[:, :], func=AF.Copy),
                         reads=[pk], writes=[("DT", (g * 2 + half) % 2)])
                    ps2, pk2 = ps_next()
                    mm_group(ps2[:, :], pk2, [(pwv[:, g, :], dt)], wkeys(slot) + [("DT", (g * 2 + half) % 2)])
                    S.op("dve", lambda e, ps2=ps2, g=g, half=half: e.tensor_scalar(
                        out=YS[:, 8 + g, HS(half)], in0=ps2[:, :], scalar1=vcol(V_PSCALE, l * 4 + g), scalar2=None, op0=ALU.mult),
                        reads=[pk2, ("VEC",)], writes=[("YS", 8 + g, half)])

            if l == 0:
                stop_at("A_pool")
            S.op("dve", lambda e: e.memset(AR[:, 4096:12288], 0.0), writes=[("KZall",), ("BAND",)])
            S.phase = f"L{l}:A_qkv"
            KST = [scr_f32(4096, 512), scr_f32(4608, 512)]
            for which in range(2):
                slot = ring_use()
                wv = WR[:, slot, 0:4096].rearrange("p (k n) -> p k n", k=KC)
                for c in range(4):
                    for half in range(2):
                        ps, pk = ps_next()
                        mm_group(ps[:, :], pk, [(wv[:, kc, c * 128:(c + 1) * 128], Hh[:, kc, HS(half)]) for kc in range(KC)],
                                 wkeys(slot) + Hkeys(half))
                        if which == 0:
                            S.op("act", lambda e, c=c, half=half, ps=ps: e.activation(
                                out=QK[:, c, HS(half)], in_=ps[:, :], func=AF.Copy, scale=0.125),
                                reads=[pk], writes=[("QK", c, half)])
                        else:
                            for jj in range(2):
                                S.op("dve", lambda e, c=c, half=half, ps=ps, jj=jj: e.tensor_copy(
                                    out=KZ[jj * 64:(jj + 1) * 64, 2 * c + jj, HS(half)], in_=ps[jj * 64:(jj + 1) * 64, :]),
                                    reads=[pk, ("KZall",)], writes=[("KZ", 2 * c + jj, half)])
                if which == 1:
                    for tb in range(8):
                        ps, pk = ps_next()
                        mm_group(ps[:, :], pk, [(Hh[:, kc, tb * 128:(tb + 1) * 128], wv[:, kc, :]) for kc in range(KC)],
                                 wkeys(slot) + Hkeys(tb // 4))
                        st = KST[tb % 2]
                        S.op("act", lambda e, ps=ps, st=st: e.activation(out=st, in_=ps[:, :], func=AF.Copy),
                             reads=[pk], writes=[("KST", tb % 2)])
                        S.dma("sp", kvo_d[l, 0, tb * 128:(tb + 1) * 128, :], st, reads=[("KST", tb % 2)], is_output=True)
            if l == 0:
                stop_at("A_qk")
            slot = ring_use()
            wv = WR[:, slot, 0:4096].rearrange("p (k n) -> p k n", k=KC)
            for tb in range(8):
                ps, pk = ps_next()
                mm_group(ps[:, :], pk, [(Hh[:, kc, tb * 128:(tb + 1) * 128], wv[:, kc, :]) for kc in range(KC)],
                         wkeys(slot) + Hkeys(tb // 4))
                st = KST[tb % 2]
                S.op("act", lambda e, ps=ps, st=st: e.activation(out=st, in_=ps[:, :], func=AF.Copy),
                     reads=[pk], writes=[("KST", tb % 2)])
                S.dma("sp", kvo_d[l, 1, tb * 128:(tb + 1) * 128, :], st, reads=[("KST", tb % 2)], is_output=True)
                for j in range(2):
                    S.op("dve", lambda e, st=st, tb=tb, j=j: e.tensor_copy(
                        out=VP[:, tb, :, j * 128:j * 128 + 64],
                        in_=st.rearrange("p (h j d) -> p h j d", h=4, j=2)[:, :, j, :]),
                        reads=[("KST", tb % 2), ("VPall",)], writes=[("VP", tb, j)])
            S.phase = f"L{l}:A_conv"
            slot_h, slot_c, slot_b = ring_use_many(3)
            wh = WR[:, slot_h, 0:4096].rearrange("p (k n) -> p k n", k=KC)
            wc = WR[:, slot_c, 0:4096].rearrange("p (k n) -> p k n", k=KC)
            wb = WR[:, slot_b, 0:4096].rearrange("p (k n) -> p k n", k=KC)
            HC = [scr_f32(512, 512), scr_f32(1024, 512)]
            CH = scr_f32(1536, 1026)
            YC = scr_f32(2562, 1024)
            S.op("dve", lambda e: e.memset(CH[:, 0:1], 0.0), writes=[("CH", 0), ("CH", 1)])
            S.op("dve", lambda e: e.memset(CH[:, 1025:1026], 0.0), writes=[("CHp",)])
            for c in range(4):
                cwb = V_CONVW + (l * 4 + c) * 3
                S.op("dve", lambda e, cwb=cwb: e.tensor_scalar(out=SMALL[:, 1:2], in0=vcol(cwb, 0), scalar1=SMALL[:, 0:1],
                                                              scalar2=None, op0=ALU.mult),
                     reads=[("VEC",), ("NBF",)], writes=[("FX", 0)])
                S.op("dve", lambda e, cwb=cwb: e.tensor_scalar(out=SMALL[:, 2:3], in0=vcol(cwb, 2), scalar1=SMALL[:, 0:1],
                                                              scalar2=None, op0=ALU.mult),
                     reads=[("VEC",), ("NBF",)], writes=[("FX", 1)])
                for half in range(2):
                    ps1, pk1 = ps_next()
                    mm_group(ps1[:, :], pk1, [(wh[:, kc, c * 128:(c + 1) * 128], Hh[:, kc, HS(half)]) for kc in range(KC)],
                             wkeys(slot_h) + Hkeys(half))
                    hc = HC[half]
                    S.op("act", lambda e, ps1=ps1, hc=hc: e.activation(out=hc, in_=ps1[:, :], func=AF.Copy),
                         reads=[pk1], writes=[("HC", half)])
                    ps2, pk2 = ps_next()
                    mm_group(ps2[:, :], pk2, [(wc[:, kc, c * 128:(c + 1) * 128], Hh[:, kc, HS(half)]) for kc in range(KC)],
                             wkeys(slot_c) + Hkeys(half))
                    S.op("dve", lambda e, ps2=ps2, hc=hc, half=half: e.tensor_tensor(
                        out=CH[:, 1 + half * 512: 1 + half * 512 + 512], in0=ps2[:, :], in1=hc, op=ALU.mult),
                        reads=[pk2, ("HC", half)], writes=[("CH", half)])
                chk = [("CH", 0), ("CH", 1), ("CHp",)]
                S.op("act", lambda e, cwb=cwb: e.activation(out=YC[:, :], in_=CH[:, 1:1025], func=AF.Identity, scale=vcol(cwb, 1)),
                     reads=chk + [("VEC",)], writes=[("YC",)])
                S.op("dve", lambda e, cwb=cwb: e.scalar_tensor_tensor(
                    out=YC[:, :], in0=CH[:, 0:1024], scalar=vcol(cwb, 0), in1=YC[:, :], op0=ALU.mult, op1=ALU.add),
                    reads=chk + [("VEC",), ("YC",)], writes=[("YC",)])
                S.op("dve", lambda e, cwb=cwb: e.scalar_tensor_tensor(
                    out=YC[:, :], in0=CH[:, 2:1026], scalar=vcol(cwb, 2), in1=YC[:, :], op0=ALU.mult, op1=ALU.add),
                    reads=chk + [("VEC",), ("YC",)], writes=[("YC",)])
                ycv = YC[:, :].rearrange("p (s t) -> p s t", s=4)
                chv = CH[:, 1:1025].rearrange("p (s t) -> p s t", s=4)
                S.op("dve", lambda e, ycv=ycv, chv=chv: e.scalar_tensor_tensor(
                    out=ycv[:, 1:4, 0:1], in0=chv[:, 0:3, 255:256], scalar=SMALL[:, 1:2], in1=ycv[:, 1:4, 0:1],
                    op0=ALU.mult, op1=ALU.add), reads=chk + [("FX", 0), ("YC",)], writes=[("YC",)])
                S.op("dve", lambda e, ycv=ycv, chv=chv: e.scalar_tensor_tensor(
                    out=ycv[:, 0:3, 255:256], in0=chv[:, 1:4, 0:1], scalar=SMALL[:, 2:3], in1=ycv[:, 0:3, 255:256],
                    op0=ALU.mult, op1=ALU.add), reads=chk + [("FX", 1), ("YC",)], writes=[("YC",)])
                for half in range(2):
                    ps3, pk3 = ps_next()
                    mm_group(ps3[:, :], pk3, [(wb[:, kc, c * 128:(c + 1) * 128], Hh[:, kc, HS(half)]) for kc in range(KC)],
                             wkeys(slot_b) + Hkeys(half))
                    S.op("dve", lambda e, ps3=ps3, half=half, c=c: e.tensor_tensor(
                        out=YS[:, c, HS(half)], in0=ps3[:, :], in1=YC[:, HS(half)], op=ALU.mult),
                        reads=[pk3, ("YC",)], writes=[("YS", c, half)])

            S.barrier()
            if l == 0:
                stop_at("phaseA")
            if l == 0:
                dbg_dump("qk", AR[:, 0:8192], [])
                dbg_dump("ysA", AR[:, 19968:19968 + 12288], [])

            S.phase = f"L{l}:B_attn"
            TT_ = [scr_f32(0 + i * 512, 512) for i in range(3)]
            PP = [scr_bf16(1536 + i * 256, 256) for i in range(4)]
            RD = [scr_f32(3072 + i * 512, 512) for i in range(2)]
            work = []
            unit = 0
            for hp in range(4):
                for qc in range(2):
                    tiles = [("c", 8), ("c", 9)] + [("l", b_) for b_ in LOCAL_TILES[qc]]
                    ntot = 2 * len(tiles)
                    it = 0
                    for j in range(2):
                        for kind, b_ in tiles:
                            work.append(dict(hp=hp, qc=qc, j=j, kind=kind, b=b_, unit=unit, it=it, ntot=ntot,
                                             first_of_head=(b_ == 8 and kind == "c")))
                            it += 1
                    unit += 1
            LA = 3

            def ut_load(h):
                S.dma("pool", UT[:, h % 2, :], utab_d[l][:, h * NE * 64:(h + 1) * NE * 64], writes=[("UT", h % 2)])

            def emit_S(k):
                w = work[k]
                hp, qc, j, b_ = w["hp"], w["qc"], w["j"], w["b"]
                si = k % 4
                Sps, sk = PS[si], ("ps", si)
                h_ = 2 * hp + j
                qsrc = QK[:, hp, HS(qc)]
                if w["kind"] == "l":
                    ksrc = KZ[:, h_, b_ * 128:(b_ + 1) * 128]
                    S.op("pe", lambda e: e.matmul(Sps[:, :], ksrc, qsrc, start=True, stop=False),
                         reads=[("KZ", h_, b_ // 4), ("KZall",), ("QK", hp, qc)], writes=[sk], signal=False)
                    S.op("pe", lambda e: e.matmul(Sps[:, :], OHv[:, b_, :], RNEG[:, HS(qc)], start=False, stop=True),
                         reads=[("OH",), ("RNEG",)], writes=[sk])
                else:
                    ksrc = CKv[:, h_, (b_ - 8) * 128:(b_ - 7) * 128]
                    S.op("pe", lambda e: e.matmul(Sps[:, :], ksrc, qsrc, start=True, stop=True),
                         reads=[("CK",), ("QK", hp, qc)], writes=[sk])

            def emit_rest(k):
                w = work[k]
                hp, qc, j, b_, it, ntot = w["hp"], w["qc"], w["j"], w["b"], w["it"], w["ntot"]
                u = w["unit"]
                h = 2 * hp + j
                NUM, nk = PS[4 + (u % 2) * 2], ("ps", 4 + (u % 2) * 2)
                DEN, dk = PS[5 + (u % 2) * 2], ("ps", 5 + (u % 2) * 2)
                si = k % 4
                Sps, sk = PS[si], ("ps", si)
                pi = k % 4
                pt = PP[pi]
                if w["first_of_head"] and qc == 1 and j == 1 and hp < 3:
                    ut_load(2 * (hp + 1))
                if w["first_of_head"] and qc == 0 and j == 0 and hp > 0:
                    ut_load(2 * hp + 1)
                if w["kind"] == "l":
                    e0 = 8 * qc - 2 * b_ + E0
                    ti = k % 3
                    tt = TT_[ti]
                    S.op("dve", lambda e: e.tensor_tensor(out=tt, in0=Sps[:, :], in1=UT[:, h % 2, e0 * 64:(e0 + 8) * 64], op=ALU.add),
                         reads=[sk, ("UT", h % 2)], writes=[("TT", ti)])
                    S.op("act", lambda e: e.activation(out=pt, in_=tt, func=AF.Exp),
                         reads=[("TT", ti)], writes=[("PP", pi)])
                    vkeys = [("VP", b_, 0), ("VP", b_, 1), ("VPall",)]
                    osrc, okeys = ONESL[:, j * 64:j * 64 + 128], [("ONESL",)]
                else:
                    S.op("act", lambda e: e.activation(out=pt, in_=Sps[:, :], func=AF.Exp),
                         reads=[sk], writes=[("PP", pi)])
                    vkeys = [("VPc", 0), ("VPc", 1), ("VPall",)]
                    osrc, okeys = ONESC[:, j * 64:j * 64 + 128], [("ONESC",)]
                vsrc = VP[:, b_, hp, j * 64:j * 64 + 128]
                S.op("pe", lambda e: e.matmul(NUM[:, :], vsrc, pt, start=(it == 0), stop=(it == ntot - 1)),
                     reads=[("PP", pi)] + vkeys, writes=[nk], signal=False)
                S.op("pe", lambda e: e.matmul(DEN[:, :], osrc, pt, start=(it == 0), stop=(it == ntot - 1)),
                     reads=[("PP", pi)] + okeys, writes=[dk])
                if it == ntot - 1:
                    rd = RD[u % 2]
                    S.op("dve", lambda e: e.reciprocal(out=rd, in_=DEN[:, :]), reads=[dk], writes=[("RD", u % 2)])
                    S.op("dve", lambda e: e.tensor_tensor(out=YS[:, 4 + hp, HS(qc)], in0=NUM[:, :], in1=rd, op=ALU.mult),
                         reads=[nk, dk, ("RD", u % 2)] + [("PZ", t) for t in range(8)], writes=[("YS", 4 + hp, qc)])

            ut_load(0)
            ut_load(1)
            for k in range(min(LA, len(work))):
                emit_S(k)
            for k in range(len(work)):
                if k + LA < len(work):
                    emit_S(k + LA)
                emit_rest(k)
            S.barrier()
            ps_ctr[0] = 0
            if l == 0:
                stop_at("attn")
            if l == 0:
                dbg_dump("ysB", AR[:, 19968:19968 + 12288], [])

            S.phase = f"L{l}:C_gates"
            GT = [scr_f32(i * 512, 512) for i in range(6)]
            MM = [scr_f32(3072 + i * 512, 512) for i in range(4)]
            gctr = [0]
            mctr = [0]
            for c in range(8):
                slot = ring_use()
                gw = WR[:, slot, 0:3072].rearrange("p (i k n) -> p i k n", i=3, k=KC)
                bw = WR[:, slot, 3072:3072 + 1536].rearrange("p (i k n) -> p i k n", i=3, k=4)
                for half in range(2):
                    prods = []
                    for i in range(3):
                        psg, pkg = ps_next()
                        mm_group(psg[:, :], pkg, [(gw[:, i, kc, :], Hh[:, kc, HS(half)]) for kc in range(KC)],
                                 gate_keys(slot) + Hkeys(half))
                        gi = gctr[0] % 6
                        gctr[0] += 1
                        gt = GT[gi]
                        S.op("act", lambda e, psg=psg, gt=gt: e.activation(out=gt, in_=psg[:, :], func=AF.Sigmoid),
                             reads=[pkg], writes=[("GT", gi)])
                        psp, pkp = ps_next()
                        mm_group(psp[:, :], pkp, [(bw[:, i, kc, :], YS[:, 4 * i + kc, HS(half)]) for kc in range(4)],
                                 gate_keys(slot) + [("YS", 4 * i + kc, half) for kc in range(4)])
                        mi = mctr[0] % 4
                        mctr[0] += 1
                        mt = MM[mi]
                        S.op("dve", lambda e, psp=psp, gt=gt, mt=mt: e.tensor_tensor(out=mt, in0=psp[:, :], in1=gt, op=ALU.mult),
                             reads=[pkp, ("GT", gi)], writes=[("MM", mi)])
                        prods.append((mt, mi))
                    (m0, k0), (m1, k1), (m2, k2) = prods
                    S.op("pool", lambda e, m0=m0, m1=m1: e.tensor_tensor(out=m0, in0=m0, in1=m1, op=ALU.add),
                         reads=[("MM", k0), ("MM", k1)], writes=[("MM", k0)])
                    S.op("pool", lambda e, m0=m0, m2=m2, c=c, half=half: e.tensor_tensor(out=MERGED[:, c, HS(half)], in0=m0, in1=m2, op=ALU.add),
                         reads=[("MM", k0), ("MM", k2)], writes=[("MG", c, half)])
            if l == 0:
                stop_at("phaseC")
            if l == 0:
                dbg_dump("merged", AR[:, 0:8192], [("MG", c, hf) for c in range(8) for hf in range(2)])

            S.phase = f"L{l}:D_mod_wout"
            mod_part(l, range(4, 12))
            gm_compute(l, 1)
            for g in range(2):
                slot = ring_use()
                wv = WR[:, slot, 0:4096].rearrange("p (k n) -> p k n", k=KC)
                for cc in range(4):
                    c = g * 4 + cc
                    for half in range(2):
                        ps, pk = ps_next()
                        mm_group(ps[:, :], pk, [(wv[:, kc, cc * 128:(cc + 1) * 128], MERGED[:, kc, HS(half)]) for kc in range(KC)],
                                 wkeys(slot) + [("MG", kc, half) for kc in range(KC)])
                        S.op("dve", lambda e, ps=ps, c=c, half=half: e.scalar_tensor_tensor(
                            out=X[:, c, HS(half)], in0=ps[:, :], scalar=MOD[:, l, 16 + c:17 + c], in1=X[:, c, HS(half)],
                            op0=ALU.mult, op1=ALU.add),
                            reads=[pk, ("MOD", l, 4 + c // 4), ("X", c, half)], writes=[("X", c, half)])
            if l == 0:
                stop_at("x1")
            if l == 0:
                dbg_dump("x1", X[:, :, :], [("X", dc, hf) for dc in range(KC) for hf in range(2)])

            S.phase = f"L{l}:E_norm2"
            S.barrier()
            norm(lambda dc: GMv[:, l, 1, dc:dc + 1], lambda dc: MOD[:, l, 24 + dc:25 + dc],
                 [("GM", l, 1), ("MOD", l, 6), ("MOD", l, 7)],
                 lambda dc, half: Hh[:, dc, HS(half)], lambda dc, half: [("H", dc, half)])
            S.barrier()

            S.phase = f"L{l}:F_up"
            U32 = scr_f32(0, 1026)
            YF = scr_f32(1026, 1024)
            GF = scr_f32(2050, 1024)
            S.op("dve", lambda e: e.memset(U32[:, 0:1], 0.0), writes=[("U32", 0), ("U32", 1)])
            S.op("dve", lambda e: e.memset(U32[:, 1025:1026], 0.0), writes=[("U32p",)])
            for jp in range(11):
                slot = ring_use()
                wv = WR[:, slot, 0:4096].rearrange("p (k two n) -> p k two n", k=KC, two=2)
                for jj in range(2):
                    jx = jp * 2 + jj
                    fwb = V_FCONV + (l * NJ + jx) * 3
                    S.op("dve", lambda e, fwb=fwb: e.tensor_scalar(out=SMALL[:, 1:2], in0=vcol(fwb, 0), scalar1=SMALL[:, 0:1],
                                                                  scalar2=None, op0=ALU.mult),
                         reads=[("VEC",), ("NBF",)], writes=[("FX", 0)])
                    S.op("dve", lambda e, fwb=fwb: e.tensor_scalar(out=SMALL[:, 2:3], in0=vcol(fwb, 2), scalar1=SMALL[:, 0:1],
                                                                  scalar2=None, op0=ALU.mult),
                         reads=[("VEC",), ("NBF",)], writes=[("FX", 1)])
                    for half in range(2):
                        ps, pk = ps_next()
                        mm_group(ps[:, :], pk, [(wv[:, kc, 0, jj * 128:(jj + 1) * 128], Hh[:, kc, HS(half)]) for kc in range(KC)],
                                 wkeys(slot) + Hkeys(half))
                        S.op("act", lambda e, ps=ps, half=half: e.activation(out=U32[:, 1 + half * 512:1 + half * 512 + 512], in_=ps[:, :], func=AF.Copy),
                             reads=[pk], writes=[("U32", half)])
                    uk = [("U32", 0), ("U32", 1), ("U32p",)]
                    S.op("act", lambda e, fwb=fwb: e.activation(out=YF[:, :], in_=U32[:, 1:1025], func=AF.Identity, scale=vcol(fwb, 1)),
                         reads=uk + [("VEC",)], writes=[("YF",)])
                    S.op("dve", lambda e, fwb=fwb: e.scalar_tensor_tensor(
                        out=YF[:, :], in0=U32[:, 0:1024], scalar=vcol(fwb, 0), in1=YF[:, :], op0=ALU.mult, op1=ALU.add),
                        reads=uk + [("VEC",), ("YF",)], writes=[("YF",)])
                    S.op("dve", lambda e, fwb=fwb: e.scalar_tensor_tensor(
                        out=YF[:, :], in0=U32[:, 2:1026], scalar=vcol(fwb, 2), in1=YF[:, :], op0=ALU.mult, op1=ALU.add),
                        reads=uk + [("VEC",), ("YF",)], writes=[("YF",)])
                    yfv = YF[:, :].rearrange("p (s t) -> p s t", s=4)
                    uv = U32[:, 1:1025].rearrange("p (s t) -> p s t", s=4)
                    S.op("dve", lambda e, yfv=yfv, uv=uv: e.scalar_tensor_tensor(
                        out=yfv[:, 1:4, 0:1], in0=uv[:, 0:3, 255:256], scalar=SMALL[:, 1:2], in1=yfv[:, 1:4, 0:1],
                        op0=ALU.mult, op1=ALU.add), reads=uk + [("FX", 0), ("YF",)], writes=[("YF",)])
                    S.op("dve", lambda e, yfv=yfv, uv=uv: e.scalar_tensor_tensor(
                        out=yfv[:, 0:3, 255:256], in0=uv[:, 1:4, 0:1], scalar=SMALL[:, 2:3], in1=yfv[:, 0:3, 255:256],
                        op0=ALU.mult, op1=ALU.add), reads=uk + [("FX", 1), ("YF",)], writes=[("YF",)])
                    S.op("act", lambda e: e.activation(out=GF[:, :], in_=YF[:, :], func=AF.Gelu_apprx_tanh),
                         reads=[("YF",)], writes=[("GF",)])
                    for half in range(2):
                        ps, pk = ps_next()
                        mm_group(ps[:, :], pk, [(wv[:, kc, 1, jj * 128:(jj + 1) * 128], Hh[:, kc, HS(half)]) for kc in range(KC)],
                                 wkeys(slot) + Hkeys(half))
                        S.op("dve", lambda e, ps=ps, half=half, jx=jx: e.tensor_tensor(
                            out=A[:, jx, HS(half)], in0=ps[:, :], in1=GF[:, HS(half)], op=ALU.mult),
                            reads=[pk, ("GF",)], writes=[("A", jx, half)])
            if l == 0:
                stop_at("ffnup")
            if l == 0:
                dbg_dump("a", AR[:, 0:NJ * T], [("A", j, hf) for j in range(NJ) for hf in range(2)])

            S.phase = f"L{l}:G_down"
            for c in range(8):
                slot = ring_use()
                wv = WR[:, slot, 0:NJ * 128].rearrange("p (k n) -> p k n", k=NJ)
                for half in range(2):
                    ps, pk = ps_next()
                    mm_group(ps[:, :], pk, [(wv[:, j, :], A[:, j, HS(half)]) for j in range(NJ)],
                             wkeys(slot) + [("A", j, half) for j in range(NJ)])
                    S.op("dve", lambda e, ps=ps, c=c, half=half: e.scalar_tensor_tensor(
                        out=X[:, c, HS(half)], in0=ps[:, :], scalar=MOD[:, l, 40 + c:41 + c], in1=X[:, c, HS(half)],
                        op0=ALU.mult, op1=ALU.add),
                        reads=[pk, ("MOD", l, 10 + c // 4), ("X", c, half)], writes=[("X", c, half)])
            S.barrier()
            if l + 1 < NL:
                S.op("dve", lambda e: e.memset(AR[:, 12288:12288 + 7680], 0.0), writes=[("VPall",)])
            if l == 0:
                stop_at("layer0")
            if l == 0:
                dbg_dump("x2", X[:, :, :], [("X", dc, hf) for dc in range(KC) for hf in range(2)])

        try:
            run_layers()
        except _Stop:
            pass
        S.barrier()
        S.phase = 'final'
        OUTB = [scr_f32(3584, 512), scr_f32(4096, 512), scr_f32(4608, 512), scr_f32(5120, 512)]
        octr = [0]

        def out_fn(dc, half):
            i = octr[0] % 4
            return OUTB[i]

        def out_keys(dc, half):
            return [("OUTB", octr[0] % 4)]

        SQ = [scr_bf16(0, 256), scr_bf16(256, 256)]
        SS = [scr_f32(512, 512), scr_f32(1024, 512)]
        RS = [scr_f32(1536, 512), scr_f32(2048, 512)]
        TMP = [scr_f32(2560, 512), scr_f32(3072, 512)]
        for half in range(2):
            ps, pk = ps_next()
            for dc in range(KC):
                sq = SQ[dc % 2]
                S.op("act", lambda e, dc=dc, sq=sq: e.activation(out=sq, in_=X[:, dc, HS(half)], func=AF.Square),
                     reads=[("X", dc, half)], writes=[("SQ", dc % 2)])
                S.op("pe", lambda e, dc=dc, sq=sq: e.matmul(ps[:, :], ONES[:, :], sq, start=(dc == 0), stop=(dc == KC - 1)),
                     reads=[("SQ", dc % 2), ("ONES",)], writes=[pk])
            S.op("act", lambda e: e.activation(out=SS[half], in_=ps[:, :], func=AF.Sqrt, scale=1.0 / D, bias=1e-6),
                 reads=[pk], writes=[("SS", half)])
            S.op("dve", lambda e: e.reciprocal(out=RS[half], in_=SS[half]), reads=[("SS", half)], writes=[("RS", half)])
            for dc in range(KC):
                tmp = TMP[dc % 2]
                S.op("dve", lambda e, dc=dc, tmp=tmp: e.tensor_tensor(out=tmp, in0=X[:, dc, HS(half)], in1=RS[half], op=ALU.mult),
                     reads=[("X", dc, half), ("RS", half)], writes=[("TMPn", dc % 2)])
                oi = octr[0] % 4
                octr[0] += 1
                ob = OUTB[oi]
                S.op("act", lambda e, dc=dc, tmp=tmp, ob=ob: e.activation(out=ob, in_=tmp, func=AF.Identity, scale=vcol(V_GF, dc)),
                     reads=[("TMPn", dc % 2), ("VEC",)], writes=[("OUTB", oi)])
                S.dma("sp", yT_d[dc * 128:(dc + 1) * 128, HS(half)], ob, reads=[("OUTB", oi)], is_output=True)
        S.finish()
        build_nc.stats = dict(S.nops)
        build_nc.phase_of = dict(S.phase_of)
    return nc


def _bf16_split(a):
    hi = a.astype(ml_dtypes.bfloat16).astype(np.float32)
    lo = (a - hi).astype(ml_dtypes.bfloat16).astype(np.float32)
    return hi, lo


def _band_tables(seq_len):
    out = np.zeros((128, 4, 8, 2, 128), np.float32)
    t = np.arange(T)
    for g, w in enumerate((2, 4, 8, 16)):
        B = np.zeros((T, T), np.float64)
        s0 = (t // seq_len) * seq_len
        lo = np.clip(t - w // 2, s0, s0 + seq_len)
        hi = np.clip(t - w // 2 + w, s0, s0 + seq_len)
        for tt in range(T):
            B[lo[tt]:hi[tt], tt] = 1.0 / (hi[tt] - lo[tt])
            B[tt, tt] -= 1.0
        B = B.astype(np.float32)

        def blk(ib, ob):
            return B[ib * 128:(ib + 1) * 128, ob * 128:(ob + 1) * 128]
        variants = [blk(0, 0), blk(7, 7), blk(2, 2), blk(1, 1), blk(0, 1), blk(1, 2), blk(1, 0), blk(2, 1)]
        for ob in range(8):
            dv = 0 if ob == 0 else (1 if ob == 7 else (2 if ob % 2 == 0 else 3))
            assert np.array_equal(blk(ob, ob), variants[dv])
            if ob >= 1:
                assert np.array_equal(blk(ob - 1, ob), variants[4 if ob % 2 == 1 else 5])
            if ob <= 6:
                assert np.array_equal(blk(ob + 1, ob), variants[6 if ob % 2 == 0 else 7])
        for v, m in enumerate(variants):
            h_, l_ = _bf16_split(m)
            out[:, g, v, 0, :] = h_
            out[:, g, v, 1, :] = l_
    return out.reshape(128, -1)


def _rneg_table(sample):
    r = np.full((16, 16), NEG, np.float32)
    for rk in range(16):
        for q in range(16):
            if sample:
                s = min(max(q - 4, 0), 8)
                ok = s <= rk < s + 8
            else:
                ok = (rk // 4) == (q // 4)
            if ok:
                r[rk, q] = 0.0
    out = np.zeros((128, T), np.float32)
    out[0:16] = np.repeat(r, 64, axis=1)
    return out


def _onehot_rows():
    oh = np.zeros((16, 8, 128), np.float32)
    for b in range(8):
        oh[2 * b, b, 0:64] = 1.0
        oh[2 * b + 1, b, 64:128] = 1.0
    out = np.zeros((128, 1024), np.float32)
    out[0:16] = oh.reshape(16, 1024)
    return out


def _u_table(rpb_l):
    U = np.full((128, NH, NE, 64), NEG, np.float32)
    cq = np.arange(64)
    cs = np.clip(cq - 8, 0, 48)
    for jj in range(2):
        for ei in range(NE):
            dr = jj - (ei - E0)
            if abs(dr) > 7:
                continue
            for ck in range(64):
                valid = (ck >= cs) & (ck < cs + 16)
                dc = np.clip(ck - cq + 15, 0, 30)
                vals = rpb_l[:, dr + 7, :][:, dc]
                U[jj * 64 + ck, :, ei, :] = np.where(valid[None, :], vals, NEG)
    return U.reshape(128, -1)


def _prep(inputs):
    f = lambda a: np.ascontiguousarray(np.asarray(a, dtype=np.float32))
    x_prompt, x_sample = f(inputs["x_prompt"]), f(inputs["x_sample"])
    cache_kv, c, c_ctx = f(inputs["cache_kv"]), f(inputs["c"]), f(inputs["c_ctx"])
    rpb = f(inputs["rpb"])
    shared = {
        "w_mod": f(inputs["w_mod"]), "w_in": f(inputs["w_in"]), "w_branch": f(inputs["w_branch"]),
        "w_out": f(inputs["w_out"]), "w_up": f(inputs["ffn_w_up"]), "w_down": f(inputs["ffn_w_down"]),
        "pool_w": f(inputs["pool_w"]),
    }

    def pk(v):
        return np.ascontiguousarray(v.reshape(-1, 128).T)

    vec_common = np.zeros((128, NV), np.float32)
    b_mod, g1, g2, gf = f(inputs["b_mod"]), f(inputs["g_norm1"]), f(inputs["g_norm2"]), f(inputs["g_final"])
    conv_w, fconv, pscale = f(inputs["conv_w"]), f(inputs["ffn_conv"]), f(inputs["pool_scale"])
    for l in range(NL):
        vec_common[:, V_BMOD + l * 48: V_BMOD + (l + 1) * 48] = pk(b_mod[l])
        vec_common[:, V_G1 + l * 8: V_G1 + (l + 1) * 8] = pk(g1[l])
        vec_common[:, V_G2 + l * 8: V_G2 + (l + 1) * 8] = pk(g2[l])
        for cc in range(4):
            for k in range(3):
                vec_common[:, V_CONVW + (l * 4 + cc) * 3 + k] = conv_w[l, k, cc * 128:(cc + 1) * 128]
        for j in range(NJ):
            for k in range(3):
                vec_common[:, V_FCONV + (l * NJ + j) * 3 + k] = fconv[l, k, j * 128:(j + 1) * 128]
        vec_common[:, V_PSCALE + l * 4: V_PSCALE + (l + 1) * 4] = pk(pscale[l])
    vec_common[:, V_GF:V_GF + 8] = pk(gf)

    ones_pat = np.concatenate([np.ones((128, 64)), np.zeros((128, 64)), np.ones((128, 64))], axis=1).astype(np.float32)
    band_p, band_s = _band_tables(256), _band_tables(1024)
    rneg_p, rneg_s = _rneg_table(False), _rneg_table(True)
    oh = _onehot_rows()
    utab_s = np.stack([_u_table(rpb[l]) for l in range(NL)])
    utab_p = np.zeros_like(utab_s)
    in_maps = []
    for i in range(8):
        sample = i >= 4
        vec = vec_common.copy()
        if sample:
            b = i - 4
            xT = np.ascontiguousarray(x_sample[b].T)
            vec[:, V_CV:V_CV + 8] = pk(c[b])
            vec[:, V_BFLAG] = 0.0
            ck = cache_kv[b, :, 0]
            ctxk = np.zeros((NL, 2, 64, NH, 256), np.float32)
            for h_ in range(NH):
                ctxk[:, h_ % 2, :, h_, :] = ck[:, h_].transpose(0, 2, 1)
            ctxk = ctxk.reshape(NL, 128, NH * 256)
            cvv = cache_kv[b, :, 1]
            ctxv = cvv.transpose(0, 2, 1, 3).reshape(NL, 2, 128, 512).transpose(0, 2, 1, 3).reshape(NL, 128, 1024)
            m = {"ctxkT": np.ascontiguousarray(ctxk), "ctxv": np.ascontiguousarray(ctxv), "utab": utab_s,
                 "rneg": rneg_s, "oh": oh, "onesc": ones_pat, "band": band_s}
        else:
            xT = np.ascontiguousarray(x_prompt[4 * i:4 * i + 4].reshape(T, D).T)
            vec[:, V_CV:V_CV + 8] = pk(c_ctx)
            vec[:, V_BFLAG] = 1.0
            m = {"ctxkT": np.zeros((NL, 128, 2048), np.float32), "ctxv": np.zeros((NL, 128, 1024), np.float32),
                 "utab": utab_p, "rneg": rneg_p, "oh": oh, "onesc": np.zeros_like(ones_pat), "band": band_p}
        m.update(shared)
        m["xT"] = xT
        m["vecs"] = vec
        in_maps.append(m)
    return in_maps


def _assemble(results):
    y_prompt = np.empty((16, 256, D), np.float32)
    y_sample = np.empty((4, T, D), np.float32)
    kv_state = np.empty((16, NL, 2, NH, 256, 64), np.float32)
    for i in range(8):
        r = results[i]
        y = np.asarray(r["yT"], dtype=np.float32).T
        if i < 4:
            y_prompt[4 * i:4 * i + 4] = y.reshape(4, 256, D)
            kvo = np.asarray(r["kvo"], dtype=np.float32).reshape(NL, 2, 4, 256, NH, 64)
            kv_state[4 * i:4 * i + 4] = kvo.transpose(2, 0, 1, 4, 3, 5)
        else:
            y_sample[i - 4] = y
    return y_prompt, y_sample, kv_state


_NC_CACHE = {}


def kernel(**inputs):
    in_maps = _prep(inputs)
    if "nc" not in _NC_CACHE:
        _NC_CACHE["nc"] = build_nc()
    res = run_bass_kernel_spmd(_NC_CACHE["nc"], in_maps, core_ids=list(range(8)))
    return _assemble(res.results)
```

```python
import numpy as np
import ml_dtypes
from contextlib import ExitStack
import concourse.bass as bass
import concourse.mybir as mybir
from concourse.bass_utils import run_bass_kernel_spmd

F32 = mybir.dt.float32
BF16 = mybir.dt.bfloat16
AF = mybir.ActivationFunctionType
ALU = mybir.AluOpType

D = 1024
T = 1024
KC = 8
DFF = 2816
NJ = 22
INW = 6656
NL = 2
NH = 8
NE = 22
E0 = 10
NEG = -30000.0
SLOT = 4608
NSLOT = 5
ND = 16
SEM_LIMIT = 12000

V_BMOD = 0
V_G1 = 96
V_G2 = 112
V_GF = 128
V_CONVW = 136
V_FCONV = 160
V_PSCALE = 292
V_CV = 300
V_BFLAG = 308
NV = 320

LOCAL_TILES = {0: [0, 1, 2, 3, 4, 5], 1: [2, 3, 4, 5, 6, 7]}


class Sched:
    def __init__(self, nc, es):
        self.nc = nc
        self.es = es
        self.eh = {"pe": nc.tensor, "act": nc.scalar, "dve": nc.vector, "pool": nc.gpsimd, "sp": nc.sync}
        self.epoch = {e: 0 for e in self.eh}
        self.cnt = {e: 0 for e in self.eh}
        self.sems = {}
        for e in self.eh:
            self.sems[(e, 0)] = es.enter_context(nc.semaphore(f"s_{e}_0"))
        self.dsem = [es.enter_context(nc.semaphore(f"s_dma_{i}")) for i in range(ND)]
        self.dcnt = [0] * ND
        self.dpool = {"sp": list(range(0, 6)), "pool": list(range(6, ND))}
        self.dnext = {"sp": 0, "pool": 0}
        self.seen = {e: {} for e in self.eh}
        self.last_w = {}
        self.readers = {}
        self.out_dmas = []
        self.aux = {}
        self.nops = {e: 0 for e in self.eh}
        self.pending = {e: False for e in self.eh}
        self.phase = "init"
        self.phase_of = {}

    def _sem_of(self, src):
        if src[0] == "d":
            return self.dsem[src[1]]
        return self.sems[src]

    def _wait(self, eng, src, c):
        if self.seen[eng].get(src, 0) >= c:
            return
        self.eh[eng].wait_ge(self._sem_of(src), c)
        self.seen[eng][src] = c

    def _deps(self, eng, reads, writes):
        deps = {}

        def add(src, c):
            if deps.get(src, 0) < c:
                deps[src] = c

        for k in reads:
            w = self.last_w.get(k)
            if w:
                add(*w)
        for k in writes:
            w = self.last_w.get(k)
            if w:
                add(*w)
            for src, c in self.readers.get(k, {}).items():
                add(src, c)
        for src, c in deps.items():
            if eng == "pe" and src[0] == "pe":
                continue
            self._wait(eng, src, c)

    def _record(self, src, c, reads, writes):
        for k in writes:
            self.last_w[k] = (src, c)
            self.readers[k] = {}
        for k in reads:
            r = self.readers.setdefault(k, {})
            if r.get(src, 0) < c:
                r[src] = c

    def op(self, eng, fn, reads=(), writes=(), signal=True):
        if (not self.pending[eng]) and self.cnt[eng] >= SEM_LIMIT:
            self.epoch[eng] += 1
            self.cnt[eng] = 0
            self.sems[(eng, self.epoch[eng])] = self.es.enter_context(
                self.nc.semaphore(f"s_{eng}_{self.epoch[eng]}"))
        self._deps(eng, reads, writes)
        ins = fn(self.eh[eng])
        src = (eng, self.epoch[eng])
        if signal:
            ins.then_inc(self.sems[src], 1)
            self.cnt[eng] += 1
            c = self.cnt[eng]
            self.pending[eng] = False
        else:
            c = self.cnt[eng] + 1
            self.pending[eng] = True
        self._record(src, c, reads, writes)
        self.nops[eng] += 1
        try:
            self.phase_of[ins.ins.name] = self.phase
        except Exception:
            pass
        return ins

    def dma(self, q, out, in_, reads=(), writes=(), is_output=False, ring=False):
        self._deps(q, reads, writes)
        lst = self.dpool[q]
        s = lst[self.dnext[q] % len(lst)]
        self.dnext[q] += 1
        src = ("d", s)
        if self.dcnt[s] > 0:
            self._wait(q, src, self.dcnt[s])
        self.eh[q].dma_start(out=out, in_=in_).then_inc(self.dsem[s], 16)
        self.dcnt[s] += 16
        self._record(src, self.dcnt[s], reads, writes)
        if is_output:
            self.out_dmas.append((src, self.dcnt[s]))
        if not ring:
            self.aux[src] = self.dcnt[s]

    def barrier(self):
        cur = [((e, self.epoch[e]), self.cnt[e]) for e in self.eh if self.cnt[e] > 0]
        cur += list(self.aux.items())
        self.aux = {}
        for e in self.eh:
            for src, c in cur:
                if src[0] == e and e == "pe":
                    continue
                self._wait(e, src, c)

    def finish(self):
        for q in ("sp", "pool"):
            for s_, c in enumerate(self.dcnt):
                if c > 0:
                    self._wait(q, ("d", s_), c)
        self.barrier()


class _Stop(Exception):
    pass


def build_nc(debug=None, stop=None):
    nc = bass.Bass("TRN2", target_bir_lowering=False)

    def din(name, shape):
        return nc.dram_tensor(name, list(shape), F32, kind="ExternalInput").ap()

    def dout(name, shape):
        return nc.dram_tensor(name, list(shape), F32, kind="ExternalOutput").ap()

    xT_d = din("xT", [D, T])
    vecs_d = din("vecs", [128, NV])
    w_mod_d = din("w_mod", [NL, D, 6 * D])
    w_in_d = din("w_in", [NL, D, INW])
    w_br_d = din("w_branch", [NL, 3, 512, D])
    w_out_d = din("w_out", [NL, D, D])
    w_up_d = din("w_up", [NL, D, 2 * DFF])
    w_dn_d = din("w_down", [NL, DFF, D])
    pool_w_d = din("pool_w", [NL, 4, 128, 128])
    ctxk_d = din("ctxkT", [NL, 128, 8 * 256])
    ctxv_d = din("ctxv", [NL, 128, 2 * 512])
    utab_d = din("utab", [NL, 128, NH * NE * 64])
    rneg_d = din("rneg", [128, T])
    oh_d = din("oh", [128, 8 * 128])
    onesc_d = din("onesc", [128, 192])
    band_d = din("band", [128, 4 * 8 * 2 * 128])
    yT_d = dout("yT", [D, T])
    kvo_d = dout("kvo", [NL, 2, T, 512])
    dbg_d = {}
    if debug:
        for name, (shape, dt_) in debug.items():
            dbg_d[name] = nc.dram_tensor("dbg_" + name, list(shape), dt_, kind="ExternalOutput").ap()

    es = ExitStack()
    with es:
        def sb(name, shape, dt):
            return es.enter_context(nc.sbuf_tensor(name, list(shape), dt))

        X = sb("X", [128, KC, T], F32)
        Hh = sb("Hh", [128, KC, T], BF16)
        AR = sb("AR", [128, 32256], BF16)
        UT = sb("UT", [128, 2, NE * 64], BF16)
        WR = sb("WR", [128, NSLOT, SLOT], BF16)
        SCR = sb("SCR", [128, 5632], F32)
        VEC = sb("VEC", [128, NV], F32)
        MOD = sb("MOD", [128, NL, 48], F32)
        GM = sb("GM", [128, NL * 2 * 8], F32)
        SMALL = sb("SMALL", [128, 256], F32)
        SCb = sb("SCb", [128, 8], BF16)
        RNEG = sb("RNEG", [128, T], BF16)
        OH = sb("OH", [128, 8 * 128], BF16)
        ONES = sb("ONES", [128, 128], BF16)
        ONESL = sb("ONESL", [128, 192], BF16)
        ONESC = sb("ONESC", [128, 192], BF16)
        CK = sb("CK", [128, 8 * 256], BF16)
        PS = [es.enter_context(nc.psum_tensor(f"ps{i}", [128, 512], F32)) for i in range(8)]

        S = Sched(nc, es)

        QK = AR[:, 0:4096].rearrange("p (c t) -> p c t", c=4)
        KZ = AR[:, 4096:12288].rearrange("p (h t) -> p h t", h=8)
        MERGED = AR[:, 0:8192].rearrange("p (c t) -> p c t", c=8)
        VP = AR[:, 12288:12288 + 7680].rearrange("p (k h c) -> p k h c", k=10, h=4)
        YS = AR[:, 19968:19968 + 12288].rearrange("p (c t) -> p c t", c=12)
        PZ = AR[:, 19968 + 4096:19968 + 8192].rearrange("p (b c) -> p b c", b=8)
        A = AR[:, 0:NJ * T].rearrange("p (j t) -> p j t", j=NJ)
        BAND = AR[:, 4096:12288]
        BANDv = BAND.rearrange("p (g v s c) -> p g v s c", g=4, v=8, s=2)
        CKv = CK[:, :].rearrange("p (h k) -> p h k", h=8)
        OHv = OH[:, :].rearrange("p (b k) -> p b k", b=8)
        GMv = GM[:, :].rearrange("p (l j c) -> p l j c", l=NL, j=2)

        def vcol(base, idx=0, n=1):
            return VEC[:, base + idx: base + idx + n]

        ps_ctr = [0]

        def ps_next():
            i = ps_ctr[0] % 8
            ps_ctr[0] += 1
            return PS[i], ("ps", i)

        def HS(half):
            return slice(half * 512, half * 512 + 512)

        def scr_f32(off, n):
            return SCR[:, off:off + n]

        def scr_bf16(off, n):
            return SCR[:, off:off + n].bitcast(BF16)

        loads = []

        class Ring:
            issued = 0
            consumed = 0

        def ring_use_many(n):
            idx = Ring.consumed
            target = min(len(loads), idx + NSLOT)
            while Ring.issued < target:
                i = Ring.issued
                loads[i](i % NSLOT)
                Ring.issued += 1
            Ring.consumed += n
            return [(idx + k) % NSLOT for k in range(n)]

        def ring_use():
            return ring_use_many(1)[0]

        def wkeys(slot):
            return [("WR", slot, 0), ("WR", slot, 1)]

        def ld_cols(src2d, col0, ncols, nkc=KC):
            def f(slot):
                dst = WR[:, slot, 0:nkc * ncols].rearrange("p (k n) -> p k n", k=nkc)
                src = src2d.rearrange("(k p) n -> p k n", p=128)[:, :, col0:col0 + ncols]
                S.dma("pool", dst, src, writes=wkeys(slot), ring=True)
            return f

        def ld_gate_branch(l, c):
            def f(slot):
                for i in range(3):
                    dst = WR[:, slot, i * 1024:(i + 1) * 1024].rearrange("p (k n) -> p k n", k=KC)
                    col0 = 3584 + i * 1024 + c * 128
                    src = w_in_d[l].rearrange("(k p) n -> p k n", p=128)[:, :, col0:col0 + 128]
                    S.dma("pool", dst, src, writes=[("WR", slot, 0)] if i == 0 else [("WRx", slot, i)], ring=True)
                for i in range(3):
                    dst = WR[:, slot, 3072 + i * 512:3072 + (i + 1) * 512].rearrange("p (k n) -> p k n", k=4)
                    src = w_br_d[l, i].rearrange("(k p) n -> p k n", p=128)[:, :, c * 128:(c + 1) * 128]
                    S.dma("pool", dst, src, writes=[("WR", slot, 1)] if i == 0 else [("WRy", slot, i)], ring=True)
            return f

        def gate_keys(slot):
            return [("WR", slot, 0), ("WRx", slot, 1), ("WRx", slot, 2), ("WR", slot, 1), ("WRy", slot, 1), ("WRy", slot, 2)]

        def ld_up(l, jp):
            def f(slot):
                dstv = WR[:, slot, 0:4096].rearrange("p (k two n) -> p k two n", k=KC, two=2)
                srcv = w_up_d[l].rearrange("(k p) (two n) -> p k two n", p=128, two=2)
                for t_ in range(2):
                    S.dma("pool", dstv[:, :, t_, :], srcv[:, :, t_, jp * 256:(jp + 1) * 256], writes=[("WR", slot, t_)], ring=True)
            return f

        def ld_down(l, c):
            def f(slot):
                dst = WR[:, slot, 0:NJ * 128].rearrange("p (k n) -> p k n", k=NJ)
                src = w_dn_d[l].rearrange("(k p) n -> p k n", p=128)[:, :, c * 128:(c + 1) * 128]
                S.dma("pool", dst, src, writes=wkeys(slot), ring=True)
            return f

        def ld_poolw(l):
            def f(slot):
                dst = WR[:, slot, 0:512].rearrange("p (g e) -> p g e", g=4)
                src = pool_w_d[l].rearrange("g c e -> c g e")
                S.dma("pool", dst, src, writes=wkeys(slot), ring=True)
            return f

        for l in range(NL):
            if l == 0:
                for g in range(4):
                    loads.append(ld_cols(w_mod_d[l], g * 512, 512))
            loads.append(ld_cols(w_in_d[l], 6 * 512, 512))
            loads.append(ld_poolw(l))
            for g in (3, 4, 5):
                loads.append(ld_cols(w_in_d[l], g * 512, 512))
            for g in (2, 1, 0):
                loads.append(ld_cols(w_in_d[l], g * 512, 512))
            for c in range(8):
                loads.append(ld_gate_branch(l, c))
                loads.append(ld_cols(w_mod_d[l], (4 + c) * 512, 512))
            for g in range(2):
                loads.append(ld_cols(w_out_d[l], g * 512, 512))
            for jp in range(11):
                loads.append(ld_up(l, jp))
            for c in range(8):
                loads.append(ld_down(l, c))
                if c < 4 and l + 1 < NL:
                    loads.append(ld_cols(w_mod_d[l + 1], c * 512, 512))

        def mm_group(ps_ap, ps_key, pairs, reads):
            n = len(pairs)
            for i, (lt, rh) in enumerate(pairs):
                S.op("pe", lambda e, lt=lt, rh=rh, i=i: e.matmul(ps_ap, lt, rh, start=(i == 0), stop=(i == n - 1)),
                     reads=reads if i == 0 else (), writes=[ps_key] if i == 0 else (), signal=(i == n - 1))

        def Hkeys(half):
            return [("H", kc, half) for kc in range(KC)]

        S.dma("sp", VEC[:, :], vecs_d, writes=[("VEC",)])
        for dc in range(KC):
            S.dma("sp", X[:, dc, :], xT_d[dc * 128:(dc + 1) * 128, :], writes=[("X", dc, 0), ("X", dc, 1)])
        S.dma("pool", RNEG[:, :], rneg_d, writes=[("RNEG",)])
        S.dma("pool", OH[:, :], oh_d, writes=[("OH",)])
        S.dma("pool", ONESC[:, :], onesc_d, writes=[("ONESC",)])
        S.op("dve", lambda e: e.memset(ONES[:, :], 1.0), writes=[("ONES",)])
        S.op("dve", lambda e: e.memset(ONESL[:, :], 1.0), writes=[("ONESL",)])
        S.op("dve", lambda e: e.memset(ONESL[:, 64:128], 0.0), writes=[("ONESL",)])
        S.op("dve", lambda e: e.memset(AR[:, 12288:12288 + 7680], 0.0), writes=[("VPall",)])
        S.op("act", lambda e: e.activation(out=SCb[:, :], in_=vcol(V_CV, 0, 8), func=AF.Silu),
             reads=[("VEC",)], writes=[("SCb",)])
        S.op("dve", lambda e: e.tensor_scalar(out=SMALL[:, 0:1], in0=vcol(V_BFLAG), scalar1=-1.0, scalar2=None,
                                              op0=ALU.mult), reads=[("VEC",)], writes=[("NBF",)])

        def mod_part(l, groups):
            for g in groups:
                slot = ring_use()
                ps, pk = ps_next()
                wv = WR[:, slot, 0:4096].rearrange("p (k n) -> p k n", k=KC)
                first = True
                for c in range(4):
                    col = g * 4 + c
                    for kc in range(KC):
                        S.op("pe", lambda e, c=c, kc=kc, col=col: e.matmul(
                            ps[:, col:col + 1], wv[:, kc, c * 128:(c + 1) * 128], SCb[:, kc:kc + 1],
                            start=(kc == 0), stop=(kc == KC - 1)),
                            reads=(wkeys(slot) + [("SCb",)]) if first else (), writes=[pk] if first else (),
                            signal=(c == 3 and kc == KC - 1))
                        first = False
                S.op("dve", lambda e, g=g: e.tensor_tensor(
                    out=MOD[:, l, g * 4:(g + 1) * 4], in0=ps[:, g * 4:(g + 1) * 4],
                    in1=VEC[:, V_BMOD + l * 48 + g * 4: V_BMOD + l * 48 + (g + 1) * 4], op=ALU.add),
                    reads=[pk, ("VEC",)], writes=[("MOD", l, g)])

        def gm_compute(l, j):
            sc0 = 8 if j == 0 else 32
            gb = (V_G1 if j == 0 else V_G2) + l * 8
            S.op("dve", lambda e: e.scalar_tensor_tensor(
                out=GMv[:, l, j, :], in0=MOD[:, l, sc0:sc0 + 8], scalar=1.0, in1=VEC[:, gb:gb + 8],
                op0=ALU.add, op1=ALU.mult),
                reads=[("MOD", l, sc0 // 4), ("MOD", l, sc0 // 4 + 1), ("VEC",)], writes=[("GM", l, j)])

        def norm(scale_ap_fn, bias_ap_fn, extra_reads, out_fn, out_keys_fn, hook=None):
            SQ = [scr_bf16(0, 256), scr_bf16(256, 256)]
            SS = [scr_f32(512, 512), scr_f32(1024, 512)]
            RS = [scr_f32(1536, 512), scr_f32(2048, 512)]
            TMP = [scr_f32(2560, 512), scr_f32(3072, 512)]
            for half in range(2):
                ps, pk = ps_next()
                for dc in range(KC):
                    sq = SQ[dc % 2]
                    if dc % 2 == 0:
                        S.op("act", lambda e, dc=dc, sq=sq: e.activation(out=sq, in_=X[:, dc, HS(half)], func=AF.Square),
                             reads=[("X", dc, half)], writes=[("SQ", dc % 2)])
                    else:
                        S.op("dve", lambda e, dc=dc, sq=sq: e.tensor_tensor(out=sq, in0=X[:, dc, HS(half)], in1=X[:, dc, HS(half)], op=ALU.mult),
                             reads=[("X", dc, half)], writes=[("SQ", dc % 2)])
                    S.op("pe", lambda e, dc=dc, sq=sq: e.matmul(ps[:, :], ONES[:, :], sq, start=(dc == 0), stop=(dc == KC - 1)),
                         reads=[("SQ", dc % 2), ("ONES",)], writes=[pk])
                S.op("act", lambda e: e.activation(out=SS[half], in_=ps[:, :], func=AF.Sqrt, scale=1.0 / D, bias=1e-6),
                     reads=[pk], writes=[("SS", half)])
                S.op("dve", lambda e: e.reciprocal(out=RS[half], in_=SS[half]), reads=[("SS", half)], writes=[("RS", half)])
            if hook is not None:
                hook()
            for half in range(2):
                for dc in range(KC):
                    tmp = TMP[dc % 2]
                    S.op("dve", lambda e, dc=dc, tmp=tmp: e.tensor_tensor(out=tmp, in0=X[:, dc, HS(half)], in1=RS[half], op=ALU.mult),
                         reads=[("X", dc, half), ("RS", half)], writes=[("TMPn", dc % 2)])
                    b = bias_ap_fn(dc)
                    S.op("act", lambda e, dc=dc, tmp=tmp, b=b: e.activation(
                        out=out_fn(dc, half), in_=tmp, func=AF.Identity, scale=scale_ap_fn(dc),
                        **({"bias": b} if b is not None else {})),
                        reads=[("TMPn", dc % 2)] + extra_reads(), writes=out_keys_fn(dc, half))

        def stop_at(tag):
            if stop == tag:
                raise _Stop()

        def dbg_dump(name, ap, keys):
            if debug and name in dbg_d:
                S.barrier()
                S.dma("sp", dbg_d[name], ap, reads=keys, is_output=True)
                S.barrier()

        def run_layers():
          for l in range(NL):
            if l > 0:
                gm_compute(l, 0)
            S.dma("pool", CK[:, :].rearrange("p (a b) -> p a b", b=1024), ctxk_d[l].rearrange("p (a b) -> p a b", b=1024),
                  writes=[("CK",)])
            S.dma("pool", BAND.rearrange("p (a b) -> p a b", b=1024), band_d.rearrange("p (a b) -> p a b", b=1024),
                  writes=[("BAND",)])
            for j in range(2):
                dst = VP[:, 8:10, :, j * 128:j * 128 + 64]
                src = ctxv_d[l].rearrange("p (k h j d) -> p k h j d", k=2, h=4, j=2)[:, :, :, j, :]
                S.dma("pool", dst, src, writes=[("VPc", j)], reads=[("VPall",)])

            S.phase = f"L{l}:norm1"
            def l0_hook():
                mod_part(0, range(0, 4))
                gm_compute(0, 0)
            norm(lambda dc: GMv[:, l, 0, dc:dc + 1], lambda dc: MOD[:, l, dc:dc + 1],
                 lambda: [("GM", l, 0), ("MOD", l, 0), ("MOD", l, 1)],
                 lambda dc, half: Hh[:, dc, HS(half)], lambda dc, half: [("H", dc, half)],
                 hook=l0_hook if l == 0 else None)
            S.barrier()
            if l == 0:
                stop_at("norm1")
            if l == 0:
                dbg_dump("h1", Hh[:, :, :], [("H", dc, hf) for dc in range(KC) for hf in range(2)])

            if l == 0:
                stop_at("A_pz0")
            slot = ring_use()
            wv = WR[:, slot, 0:4096].rearrange("p (k n) -> p k n", k=KC)
            for tb in range(8):
                ps, pk = ps_next()
                mm_group(ps[:, :], pk, [(Hh[:, kc, tb * 128:(tb + 1) * 128], wv[:, kc, :]) for kc in range(KC)],
                         wkeys(slot) + Hkeys(tb // 4))
                S.op("act", lambda e, ps=ps, tb=tb: e.activation(out=PZ[:, tb, :], in_=ps[:, :], func=AF.Copy),
                     reads=[pk], writes=[("PZ", tb)])

            if l == 0:
                stop_at("A_pz")
            S.phase = f"L{l}:A_pool"
            slot = ring_use()
            pwv = WR[:, slot, 0:512].rearrange("p (g e) -> p g e", g=4)
            DT = [scr_bf16(0, 256), scr_bf16(256, 256)]
            for g in range(4):
                for half in range(2):
                    ps, pk = ps_next()
                    first = True
                    for obi in range(4):
                        ob = half * 4 + obi
                        terms = []
                        dv = 0 if ob == 0 else (1 if ob == 7 else (2 if ob % 2 == 0 else 3))
                        terms.append((ob, dv))
                        if ob >= 1:
                            terms.append((ob - 1, 4 if ob % 2 == 1 else 5))
                        if ob <= 6:
                            terms.append((ob + 1, 6 if ob % 2 == 0 else 7))
                        mats = [(ib, v, s) for (ib, v) in terms for s in range(2)]
                        for i, (ib, v, s) in enumerate(mats):
                            last = (obi == 3 and i == len(mats) - 1)
                            S.op("pe", lambda e, ib=ib, v=v, s=s, i=i, obi=obi, n=len(mats): e.matmul(
                                ps[:, obi * 128:(obi + 1) * 128], PZ[:, ib, g * 128:(g + 1) * 128], BANDv[:, g, v, s, :],
                                start=(i == 0), stop=(i == n - 1)),
                                reads=([("PZ", t) for t in range(8)] + [("BAND",)]) if first else (),
                                writes=[pk] if first else (), signal=last)
                            first = False
                    dt = DT[(g * 2 + half) % 2]
                    S.op("act", lambda e, ps=ps, dt=dt: e.activation(out=dt, in_=ps[:, :], func=AF.Copy),
                         reads=[pk], writes=[("DT", (g * 2 + half) % 2)])
                    ps2, pk2 = ps_next()
                    mm_group(ps2[:, :], pk2, [(pwv[:, g, :], dt)], wkeys(slot) + [("DT", (g * 2 + half) % 2)])
                    S.op("dve", lambda e, ps2=ps2, g=g, half=half: e.tensor_scalar(
                        out=YS[:, 8 + g, HS(half)], in0=ps2[:, :], scalar1=vcol(V_PSCALE, l * 4 + g), scalar2=None, op0=ALU.mult),
                        reads=[pk2, ("VEC",)], writes=[("YS", 8 + g, half)])

            if l == 0:
                stop_at("A_pool")
            S.op("dve", lambda e: e.memset(AR[:, 4096:12288], 0.0), writes=[("KZall",), ("BAND",)])
            S.phase = f"L{l}:A_qkv"
            KST = [scr_f32(4096, 512), scr_f32(4608, 512)]
            for which in range(2):
                slot = ring_use()
                wv = WR[:, slot, 0:4096].rearrange("p (k n) -> p k n", k=KC)
                for c in range(4):
                    for half in range(2):
                        ps, pk = ps_next()
                        mm_group(ps[:, :], pk, [(wv[:, kc, c * 128:(c + 1) * 128], Hh[:, kc, HS(half)]) for kc in range(KC)],
                                 wkeys(slot) + Hkeys(half))
                        if which == 0:
                            S.op("act", lambda e, c=c, half=half, ps=ps: e.activation(
                                out=QK[:, c, HS(half)], in_=ps[:, :], func=AF.Copy, scale=0.125),
                                reads=[pk], writes=[("QK", c, half)])
                        else:
                            for jj in range(2):
                                S.op("dve", lambda e, c=c, half=half, ps=ps, jj=jj: e.tensor_copy(
                                    out=KZ[jj * 64:(jj + 1) * 64, 2 * c + jj, HS(half)], in_=ps[jj * 64:(jj + 1) * 64, :]),
                                    reads=[pk, ("KZall",)], writes=[("KZ", 2 * c + jj, half)])
                if which == 1:
                    for tb in range(8):
                        ps, pk = ps_next()
                        mm_group(ps[:, :], pk, [(Hh[:, kc, tb * 128:(tb + 1) * 128], wv[:, kc, :]) for kc in range(KC)],
                                 wkeys(slot) + Hkeys(tb // 4))
                        st = KST[tb % 2]
                        S.op("act", lambda e, ps=ps, st=st: e.activation(out=st, in_=ps[:, :], func=AF.Copy),
                             reads=[pk], writes=[("KST", tb % 2)])
                        S.dma("sp", kvo_d[l, 0, tb * 128:(tb + 1) * 128, :], st, reads=[("KST", tb % 2)], is_output=True)
            if l == 0:
                stop_at("A_qk")
            slot = ring_use()
            wv = WR[:, slot, 0:4096].rearrange("p (k n) -> p k n", k=KC)
            for tb in range(8):
                ps, pk = ps_next()
                mm_group(ps[:, :], pk, [(Hh[:, kc, tb * 128:(tb + 1) * 128], wv[:, kc, :]) for kc in range(KC)],
                         wkeys(slot) + Hkeys(tb // 4))
                st = KST[tb % 2]
                S.op("act", lambda e, ps=ps, st=st: e.activation(out=st, in_=ps[:, :], func=AF.Copy),
                     reads=[pk], writes=[("KST", tb % 2)])
                S.dma("sp", kvo_d[l, 1, tb * 128:(tb + 1) * 128, :], st, reads=[("KST", tb % 2)], is_output=True)
                for j in range(2):
                    S.op("dve", lambda e, st=st, tb=tb, j=j: e.tensor_copy(
                        out=VP[:, tb, :, j * 128:j * 128 + 64],
                        in_=st.rearrange("p (h j d) -> p h j d", h=4, j=2)[:, :, j, :]),
                        reads=[("KST", tb % 2), ("VPall",)], writes=[("VP", tb, j)])
            S.phase = f"L{l}:A_conv"
            slot_h, slot_c, slot_b = ring_use_many(3)
            wh = WR[:, slot_h, 0:4096].rearrange("p (k n) -> p k n", k=KC)
            wc = WR[:, slot_c, 0:4096].rearrange("p (k n) -> p k n", k=KC)
            wb = WR[:, slot_b, 0:4096].rearrange("p (k n) -> p k n", k=KC)
            HC = [scr_f32(512, 512), scr_f32(1024, 512)]
            CH = scr_f32(1536, 1026)
            YC = scr_f32(2562, 1024)
            S.op("dve", lambda e: e.memset(CH[:, 0:1], 0.0), writes=[("CH", 0), ("CH", 1)])
            S.op("dve", lambda e: e.memset(CH[:, 1025:1026], 0.0), writes=[("CHp",)])
            for c in range(4):
                cwb = V_CONVW + (l * 4 + c) * 3
                S.op("dve", lambda e, cwb=cwb: e.tensor_scalar(out=SMALL[:, 1:2], in0=vcol(cwb, 0), scalar1=SMALL[:, 0:1],
                                                              scalar2=None, op0=ALU.mult),
                     reads=[("VEC",), ("NBF",)], writes=[("FX", 0)])
                S.op("dve", lambda e, cwb=cwb: e.tensor_scalar(out=SMALL[:, 2:3], in0=vcol(cwb, 2), scalar1=SMALL[:, 0:1],
                                                              scalar2=None, op0=ALU.mult),
                     reads=[("VEC",), ("NBF",)], writes=[("FX", 1)])
                for half in range(2):
                    ps1, pk1 = ps_next()
                    mm_group(ps1[:, :], pk1, [(wh[:, kc, c * 128:(c + 1) * 128], Hh[:, kc, HS(half)]) for kc in range(KC)],
                             wkeys(slot_h) + Hkeys(half))
                    hc = HC[half]
                    S.op("act", lambda e, ps1=ps1, hc=hc: e.activation(out=hc, in_=ps1[:, :], func=AF.Copy),
                         reads=[pk1], writes=[("HC", half)])
                    ps2, pk2 = ps_next()
                    mm_group(ps2[:, :], pk2, [(wc[:, kc, c * 128:(c + 1) * 128], Hh[:, kc, HS(half)]) for kc in range(KC)],
                             wkeys(slot_c) + Hkeys(half))
                    S.op("dve", lambda e, ps2=ps2, hc=hc, half=half: e.tensor_tensor(
                        out=CH[:, 1 + half * 512: 1 + half * 512 + 512], in0=ps2[:, :], in1=hc, op=ALU.mult),
                        reads=[pk2, ("HC", half)], writes=[("CH", half)])
                chk = [("CH", 0), ("CH", 1), ("CHp",)]
                S.op("act", lambda e, cwb=cwb: e.activation(out=YC[:, :], in_=CH[:, 1:1025], func=AF.Identity, scale=vcol(cwb, 1)),
                     reads=chk + [("VEC",)], writes=[("YC",)])
                S.op("dve", lambda e, cwb=cwb: e.scalar_tensor_tensor(
                    out=YC[:, :], in0=CH[:, 0:1024], scalar=vcol(cwb, 0), in1=YC[:, :], op0=ALU.mult, op1=ALU.add),
                    reads=chk + [("VEC",), ("YC",)], writes=[("YC",)])
                S.op("dve", lambda e, cwb=cwb: e.scalar_tensor_tensor(
                    out=YC[:, :], in0=CH[:, 2:1026], scalar=vcol(cwb, 2), in1=YC[:, :], op0=ALU.mult, op1=ALU.add),
                    reads=chk + [("VEC",), ("YC",)], writes=[("YC",)])
                ycv = YC[:, :].rearrange("p (s t) -> p s t", s=4)
                chv = CH[:, 1:1025].rearrange("p (s t) -> p s t", s=4)
                S.op("dve", lambda e, ycv=ycv, chv=chv: e.scalar_tensor_tensor(
                    out=ycv[:, 1:4, 0:1], in0=chv[:, 0:3, 255:256], scalar=SMALL[:, 1:2], in1=ycv[:, 1:4, 0:1],
                    op0=ALU.mult, op1=ALU.add), reads=chk + [("FX", 0), ("YC",)], writes=[("YC",)])
                S.op("dve", lambda e, ycv=ycv, chv=chv: e.scalar_tensor_tensor(
                    out=ycv[:, 0:3, 255:256], in0=chv[:, 1:4, 0:1], scalar=SMALL[:, 2:3], in1=ycv[:, 0:3, 255:256],
                    op0=ALU.mult, op1=ALU.add), reads=chk + [("FX", 1), ("YC",)], writes=[("YC",)])
                for half in range(2):
                    ps3, pk3 = ps_next()
                    mm_group(ps3[:, :], pk3, [(wb[:, kc, c * 128:(c + 1) * 128], Hh[:, kc, HS(half)]) for kc in range(KC)],
                             wkeys(slot_b) + Hkeys(half))
                    S.op("dve", lambda e, ps3=ps3, half=half, c=c: e.tensor_tensor(
                        out=YS[:, c, HS(half)], in0=ps3[:, :], in1=YC[:, HS(half)], op=ALU.mult),
                        reads=[pk3, ("YC",)], writes=[("YS", c, half)])

            S.barrier()
            if l == 0:
                stop_at("phaseA")
            if l == 0:
                dbg_dump("qk", AR[:, 0:8192], [])
                dbg_dump("ysA", AR[:, 19968:19968 + 12288], [])

            S.phase = f"L{l}:B_attn"
            TT_ = [scr_f32(0 + i * 512, 512) for i in range(3)]
            PP = [scr_bf16(1536 + i * 256, 256) for i in range(4)]
            RD = [scr_f32(3072 + i * 512, 512) for i in range(2)]
            work = []
            unit = 0
            for hp in range(4):
                for qc in range(2):
                    tiles = [("c", 8), ("c", 9)] + [("l", b_) for b_ in LOCAL_TILES[qc]]
                    ntot = 2 * len(tiles)
                    it = 0
                    for j in range(2):
                        for kind, b_ in tiles:
                            work.append(dict(hp=hp, qc=qc, j=j, kind=kind, b=b_, unit=unit, it=it, ntot=ntot,
                                             first_of_head=(b_ == 8 and kind == "c")))
                            it += 1
                    unit += 1
            LA = 3

            def ut_load(h):
                S.dma("pool", UT[:, h % 2, :], utab_d[l][:, h * NE * 64:(h + 1) * NE * 64], writes=[("UT", h % 2)])

            def emit_S(k):
                w = work[k]
                hp, qc, j, b_ = w["hp"], w["qc"], w["j"], w["b"]
                si = k % 4
                Sps, sk = PS[si], ("ps", si)
                h_ = 2 * hp + j
                qsrc = QK[:, hp, HS(qc)]
                if w["kind"] == "l":
                    ksrc = KZ[:, h_, b_ * 128:(b_ + 1) * 128]
                    S.op("pe", lambda e: e.matmul(Sps[:, :], ksrc, qsrc, start=True, stop=False),
                         reads=[("KZ", h_, b_ // 4), ("KZall",), ("QK", hp, qc)], writes=[sk], signal=False)
                    S.op("pe", lambda e: e.matmul(Sps[:, :], OHv[:, b_, :], RNEG[:, HS(qc)], start=False, stop=True),
                         reads=[("OH",), ("RNEG",)], writes=[sk])
                else:
                    ksrc = CKv[:, h_, (b_ - 8) * 128:(b_ - 7) * 128]
                    S.op("pe", lambda e: e.matmul(Sps[:, :], ksrc, qsrc, start=True, stop=True),
                         reads=[("CK",), ("QK", hp, qc)], writes=[sk])

            def emit_rest(k):
                w = work[k]
                hp, qc, j, b_, it, ntot = w["hp"], w["qc"], w["j"], w["b"], w["it"], w["ntot"]
                u = w["unit"]
                h = 2 * hp + j
                NUM, nk = PS[4 + (u % 2) * 2], ("ps", 4 + (u % 2) * 2)
                DEN, dk = PS[5 + (u % 2) * 2], ("ps", 5 + (u % 2) * 2)
                si = k % 4
                Sps, sk = PS[si], ("ps", si)
                pi = k % 4
                pt = PP[pi]
                if w["first_of_head"] and qc == 1 and j == 1 and hp < 3:
                    ut_load(2 * (hp + 1))
                if w["first_of_head"] and qc == 0 and j == 0 and hp > 0:
                    ut_load(2 * hp + 1)
                if w["kind"] == "l":
                    e0 = 8 * qc - 2 * b_ + E0
                    ti = k % 3
                    tt = TT_[ti]
                    S.op("dve", lambda e: e.tensor_tensor(out=tt, in0=Sps[:, :], in1=UT[:, h % 2, e0 * 64:(e0 + 8) * 64], op=ALU.add),
                         reads=[sk, ("UT", h % 2)], writes=[("TT", ti)])
                    S.op("act", lambda e: e.activation(out=pt, in_=tt, func=AF.Exp),
                         reads=[("TT", ti)], writes=[("PP", pi)])
                    vkeys = [("VP", b_, 0), ("VP", b_, 1), ("VPall",)]
                    osrc, okeys = ONESL[:, j * 64:j * 64 + 128], [("ONESL",)]
                else:
                    S.op("act", lambda e: e.activation(out=pt, in_=Sps[:, :], func=AF.Exp),
                         reads=[sk], writes=[("PP", pi)])
                    vkeys = [("VPc", 0), ("VPc", 1), ("VPall",)]
                    osrc, okeys = ONESC[:, j * 64:j * 64 + 128], [("ONESC",)]
                vsrc = VP[:, b_, hp, j * 64:j * 64 + 128]
                S.op("pe", lambda e: e.matmul(NUM[:, :], vsrc, pt, start=(it == 0), stop=(it == ntot - 1)),
                     reads=[("PP", pi)] + vkeys, writes=[nk], signal=False)
                S.op("pe", lambda e: e.matmul(DEN[:, :], osrc, pt, start=(it == 0), stop=(it == ntot - 1)),
                     reads=[("PP", pi)] + okeys, writes=[dk])
                if it == ntot - 1:
                    rd = RD[u % 2]
                    S.op("dve", lambda e: e.reciprocal(out=rd, in_=DEN[:, :]), reads=[dk], writes=[("RD", u % 2)])
                    S.op("dve", lambda e: e.tensor_tensor(out=YS[:, 4 + hp, HS(qc)], in0=NUM[:, :], in1=rd, op=ALU.mult),
                         reads=[nk, dk, ("RD", u % 2)] + [("PZ", t) for t in range(8)], writes=[("YS", 4 + hp, qc)])

            ut_load(0)
            ut_load(1)
            for k in range(min(LA, len(work))):
                emit_S(k)
            for k in range(len(work)):
                if k + LA < len(work):
                    emit_S(k + LA)
                emit_rest(k)
            S.barrier()
            ps_ctr[0] = 0
            if l == 0:
                stop_at("attn")
            if l == 0:
                dbg_dump("ysB", AR[:, 19968:19968 + 12288], [])

            S.phase = f"L{l}:C_gates"
            GT = [scr_f32(i * 512, 512) for i in range(6)]
            MM = [scr_f32(3072 + i * 512, 512) for i in range(4)]
            gctr = [0]
            mctr = [0]
            for c in range(8):
                slot = ring_use()
                gw = WR[:, slot, 0:3072].rearrange("p (i k n) -> p i k n", i=3, k=KC)
                bw = WR[:, slot, 3072:3072 + 1536].rearrange("p (i k n) -> p i k n", i=3, k=4)
                for half in range(2):
                    prods = []
                    for i in range(3):
                        psg, pkg = ps_next()
                        mm_group(psg[:, :], pkg, [(gw[:, i, kc, :], Hh[:, kc, HS(half)]) for kc in range(KC)],
                                 gate_keys(slot) + Hkeys(half))
                        gi = gctr[0] % 6
                        gctr[0] += 1
                        gt = GT[gi]
                        S.op("act", lambda e, psg=psg, gt=gt: e.activation(out=gt, in_=psg[:, :], func=AF.Sigmoid),
                             reads=[pkg], writes=[("GT", gi)])
                        psp, pkp = ps_next()
                        mm_group(psp[:, :], pkp, [(bw[:, i, kc, :], YS[:, 4 * i + kc, HS(half)]) for kc in range(4)],
                                 gate_keys(slot) + [("YS", 4 * i + kc, half) for kc in range(4)])
                        mi = mctr[0] % 4
                        mctr[0] += 1
                        mt = MM[mi]
                        S.op("dve", lambda e, psp=psp, gt=gt, mt=mt: e.tensor_tensor(out=mt, in0=psp[:, :], in1=gt, op=ALU.mult),
                             reads=[pkp, ("GT", gi)], writes=[("MM", mi)])
                        prods.append((mt, mi))
                    (m0, k0), (m1, k1), (m2, k2) = prods
                    S.op("pool", lambda e, m0=m0, m1=m1: e.tensor_tensor(out=m0, in0=m0, in1=m1, op=ALU.add),
                         reads=[("MM", k0), ("MM", k1)], writes=[("MM", k0)])
                    S.op("pool", lambda e, m0=m0, m2=m2, c=c, half=half: e.tensor_tensor(out=MERGED[:, c, HS(half)], in0=m0, in1=m2, op=ALU.add),
                         reads=[("MM", k0), ("MM", k2)], writes=[("MG", c, half)])
                mod_part(l, [4 + c])
            if l == 0:
                stop_at("phaseC")
            if l == 0:
                dbg_dump("merged", AR[:, 0:8192], [("MG", c, hf) for c in range(8) for hf in range(2)])

            S.phase = f"L{l}:D_mod_wout"
            gm_compute(l, 1)
            for g in range(2):
                slot = ring_use()
                wv = WR[:, slot, 0:4096].rearrange("p (k n) -> p k n", k=KC)
                for cc in range(4):
                    c = g * 4 + cc
                    for half in range(2):
                        ps, pk = ps_next()
                        mm_group(ps[:, :], pk, [(wv[:, kc, cc * 128:(cc + 1) * 128], MERGED[:, kc, HS(half)]) for kc in range(KC)],
                                 wkeys(slot) + [("MG", kc, half) for kc in range(KC)])
                        S.op("dve", lambda e, ps=ps, c=c, half=half: e.scalar_tensor_tensor(
                            out=X[:, c, HS(half)], in0=ps[:, :], scalar=MOD[:, l, 16 + c:17 + c], in1=X[:, c, HS(half)],
                            op0=ALU.mult, op1=ALU.add),
                            reads=[pk, ("MOD", l, 4 + c // 4), ("X", c, half)], writes=[("X", c, half)])
            if l == 0:
                stop_at("x1")
            if l == 0:
                dbg_dump("x1", X[:, :, :], [("X", dc, hf) for dc in range(KC) for hf in range(2)])

            S.phase = f"L{l}:E_norm2"
            S.barrier()
            norm(lambda dc: GMv[:, l, 1, dc:dc + 1], lambda dc: MOD[:, l, 24 + dc:25 + dc],
                 lambda: [("GM", l, 1), ("MOD", l, 6), ("MOD", l, 7)],
                 lambda dc, half: Hh[:, dc, HS(half)], lambda dc, half: [("H", dc, half)])
            S.barrier()

            S.phase = f"L{l}:F_up"
            U32 = scr_f32(0, 1026)
            YF = scr_f32(1026, 1024)
            GF = scr_f32(2050, 1024)
            S.op("dve", lambda e: e.memset(U32[:, 0:1], 0.0), writes=[("U32", 0), ("U32", 1)])
            S.op("dve", lambda e: e.memset(U32[:, 1025:1026], 0.0), writes=[("U32p",)])
            for jp in range(11):
                slot = ring_use()
                wv = WR[:, slot, 0:4096].rearrange("p (k two n) -> p k two n", k=KC, two=2)
                for jj in range(2):
                    jx = jp * 2 + jj
                    fwb = V_FCONV + (l * NJ + jx) * 3
                    S.op("dve", lambda e, fwb=fwb: e.tensor_scalar(out=SMALL[:, 1:2], in0=vcol(fwb, 0), scalar1=SMALL[:, 0:1],
                                                                  scalar2=None, op0=ALU.mult),
                         reads=[("VEC",), ("NBF",)], writes=[("FX", 0)])
                    S.op("dve", lambda e, fwb=fwb: e.tensor_scalar(out=SMALL[:, 2:3], in0=vcol(fwb, 2), scalar1=SMALL[:, 0:1],
                                                                  scalar2=None, op0=ALU.mult),
                         reads=[("VEC",), ("NBF",)], writes=[("FX", 1)])
                    for half in range(2):
                        ps, pk = ps_next()
                        mm_group(ps[:, :], pk, [(wv[:, kc, 0, jj * 128:(jj + 1) * 128], Hh[:, kc, HS(half)]) for kc in range(KC)],
                                 wkeys(slot) + Hkeys(half))
                        S.op("act", lambda e, ps=ps, half=half: e.activation(out=U32[:, 1 + half * 512:1 + half * 512 + 512], in_=ps[:, :], func=AF.Copy),
                             reads=[pk], writes=[("U32", half)])
                    uk = [("U32", 0), ("U32", 1), ("U32p",)]
                    S.op("act", lambda e, fwb=fwb: e.activation(out=YF[:, :], in_=U32[:, 1:1025], func=AF.Identity, scale=vcol(fwb, 1)),
                         reads=uk + [("VEC",)], writes=[("YF",)])
                    S.op("dve", lambda e, fwb=fwb: e.scalar_tensor_tensor(
                        out=YF[:, :], in0=U32[:, 0:1024], scalar=vcol(fwb, 0), in1=YF[:, :], op0=ALU.mult, op1=ALU.add),
                        reads=uk + [("VEC",), ("YF",)], writes=[("YF",)])
                    S.op("dve", lambda e, fwb=fwb: e.scalar_tensor_tensor(
                        out=YF[:, :], in0=U32[:, 2:1026], scalar=vcol(fwb, 2), in1=YF[:, :], op0=ALU.mult, op1=ALU.add),
                        reads=uk + [("VEC",), ("YF",)], writes=[("YF",)])
                    yfv = YF[:, :].rearrange("p (s t) -> p s t", s=4)
                    uv = U32[:, 1:1025].rearrange("p (s t) -> p s t", s=4)
                    S.op("dve", lambda e, yfv=yfv, uv=uv: e.scalar_tensor_tensor(
                        out=yfv[:, 1:4, 0:1], in0=uv[:, 0:3, 255:256], scalar=SMALL[:, 1:2], in1=yfv[:, 1:4, 0:1],
                        op0=ALU.mult, op1=ALU.add), reads=uk + [("FX", 0), ("YF",)], writes=[("YF",)])
                    S.op("dve", lambda e, yfv=yfv, uv=uv: e.scalar_tensor_tensor(
                        out=yfv[:, 0:3, 255:256], in0=uv[:, 1:4, 0:1], scalar=SMALL[:, 2:3], in1=yfv[:, 0:3, 255:256],
                        op0=ALU.mult, op1=ALU.add), reads=uk + [("FX", 1), ("YF",)], writes=[("YF",)])
                    S.op("act", lambda e: e.activation(out=GF[:, :], in_=YF[:, :], func=AF.Gelu_apprx_tanh),
                         reads=[("YF",)], writes=[("GF",)])
                    for half in range(2):
                        ps, pk = ps_next()
                        mm_group(ps[:, :], pk, [(wv[:, kc, 1, jj * 128:(jj + 1) * 128], Hh[:, kc, HS(half)]) for kc in range(KC)],
                                 wkeys(slot) + Hkeys(half))
                        S.op("dve", lambda e, ps=ps, half=half, jx=jx: e.tensor_tensor(
                            out=A[:, jx, HS(half)], in0=ps[:, :], in1=GF[:, HS(half)], op=ALU.mult),
                            reads=[pk, ("GF",)], writes=[("A", jx, half)])
            if l == 0:
                stop_at("ffnup")
            if l == 0:
                dbg_dump("a", AR[:, 0:NJ * T], [("A", j, hf) for j in range(NJ) for hf in range(2)])

            S.phase = f"L{l}:G_down"
            for c in range(8):
                slot = ring_use()
                wv = WR[:, slot, 0:NJ * 128].rearrange("p (k n) -> p k n", k=NJ)
                for half in range(2):
                    ps, pk = ps_next()
                    mm_group(ps[:, :], pk, [(wv[:, j, :], A[:, j, HS(half)]) for j in range(NJ)],
                             wkeys(slot) + [("A", j, half) for j in range(NJ)])
                    S.op("dve", lambda e, ps=ps, c=c, half=half: e.scalar_tensor_tensor(
                        out=X[:, c, HS(half)], in0=ps[:, :], scalar=MOD[:, l, 40 + c:41 + c], in1=X[:, c, HS(half)],
                        op0=ALU.mult, op1=ALU.add),
                        reads=[pk, ("MOD", l, 10 + c // 4), ("X", c, half)], writes=[("X", c, half)])
                if c < 4 and l + 1 < NL:
                    mod_part(l + 1, [c])
            S.barrier()
            if l + 1 < NL:
                S.op("dve", lambda e: e.memset(AR[:, 12288:12288 + 7680], 0.0), writes=[("VPall",)])
            if l == 0:
                stop_at("layer0")
            if l == 0:
                dbg_dump("x2", X[:, :, :], [("X", dc, hf) for dc in range(KC) for hf in range(2)])

        try:
            run_layers()
        except _Stop:
            pass
        S.barrier()
        S.phase = 'final'
        OUTB = [scr_f32(3584, 512), scr_f32(4096, 512), scr_f32(4608, 512), scr_f32(5120, 512)]
        octr = [0]

        def out_fn(dc, half):
            i = octr[0] % 4
            return OUTB[i]

        def out_keys(dc, half):
            return [("OUTB", octr[0] % 4)]

        SQ = [scr_bf16(0, 256), scr_bf16(256, 256)]
        SS = [scr_f32(512, 512), scr_f32(1024, 512)]
        RS = [scr_f32(1536, 512), scr_f32(2048, 512)]
        TMP = [scr_f32(2560, 512), scr_f32(3072, 512)]
        for half in range(2):
            ps, pk = ps_next()
            for dc in range(KC):
                sq = SQ[dc % 2]
                S.op("act", lambda e, dc=dc, sq=sq: e.activation(out=sq, in_=X[:, dc, HS(half)], func=AF.Square),
                     reads=[("X", dc, half)], writes=[("SQ", dc % 2)])
                S.op("pe", lambda e, dc=dc, sq=sq: e.matmul(ps[:, :], ONES[:, :], sq, start=(dc == 0), stop=(dc == KC - 1)),
                     reads=[("SQ", dc % 2), ("ONES",)], writes=[pk])
            S.op("act", lambda e: e.activation(out=SS[half], in_=ps[:, :], func=AF.Sqrt, scale=1.0 / D, bias=1e-6),
                 reads=[pk], writes=[("SS", half)])
            S.op("dve", lambda e: e.reciprocal(out=RS[half], in_=SS[half]), reads=[("SS", half)], writes=[("RS", half)])
            for dc in range(KC):
                tmp = TMP[dc % 2]
                S.op("dve", lambda e, dc=dc, tmp=tmp: e.tensor_tensor(out=tmp, in0=X[:, dc, HS(half)], in1=RS[half], op=ALU.mult),
                     reads=[("X", dc, half), ("RS", half)], writes=[("TMPn", dc % 2)])
                oi = octr[0] % 4
                octr[0] += 1
                ob = OUTB[oi]
                S.op("act", lambda e, dc=dc, tmp=tmp, ob=ob: e.activation(out=ob, in_=tmp, func=AF.Identity, scale=vcol(V_GF, dc)),
                     reads=[("TMPn", dc % 2), ("VEC",)], writes=[("OUTB", oi)])
                S.dma("sp", yT_d[dc * 128:(dc + 1) * 128, HS(half)], ob, reads=[("OUTB", oi)], is_output=True)
        S.finish()
        build_nc.stats = dict(S.nops)
        build_nc.phase_of = dict(S.phase_of)
    return nc


def _bf16_split(a):
    hi = a.astype(ml_dtypes.bfloat16).astype(np.float32)
    lo = (a - hi).astype(ml_dtypes.bfloat16).astype(np.float32)
    return hi, lo


def _band_tables(seq_len):
    out = np.zeros((128, 4, 8, 2, 128), np.float32)
    t = np.arange(T)
    for g, w in enumerate((2, 4, 8, 16)):
        B = np.zeros((T, T), np.float64)
        s0 = (t // seq_len) * seq_len
        lo = np.clip(t - w // 2, s0, s0 + seq_len)
        hi = np.clip(t - w // 2 + w, s0, s0 + seq_len)
        for tt in range(T):
            B[lo[tt]:hi[tt], tt] = 1.0 / (hi[tt] - lo[tt])
            B[tt, tt] -= 1.0
        B = B.astype(np.float32)

        def blk(ib, ob):
            return B[ib * 128:(ib + 1) * 128, ob * 128:(ob + 1) * 128]
        variants = [blk(0, 0), blk(7, 7), blk(2, 2), blk(1, 1), blk(0, 1), blk(1, 2), blk(1, 0), blk(2, 1)]
        for ob in range(8):
            dv = 0 if ob == 0 else (1 if ob == 7 else (2 if ob % 2 == 0 else 3))
            assert np.array_equal(blk(ob, ob), variants[dv])
            if ob >= 1:
                assert np.array_equal(blk(ob - 1, ob), variants[4 if ob % 2 == 1 else 5])
            if ob <= 6:
                assert np.array_equal(blk(ob + 1, ob), variants[6 if ob % 2 == 0 else 7])
        for v, m in enumerate(variants):
            h_, l_ = _bf16_split(m)
            out[:, g, v, 0, :] = h_
            out[:, g, v, 1, :] = l_
    return out.reshape(128, -1)


def _rneg_table(sample):
    r = np.full((16, 16), NEG, np.float32)
    for rk in range(16):
        for q in range(16):
            if sample:
                s = min(max(q - 4, 0), 8)
                ok = s <= rk < s + 8
            else:
                ok = (rk // 4) == (q // 4)
            if ok:
                r[rk, q] = 0.0
    out = np.zeros((128, T), np.float32)
    out[0:16] = np.repeat(r, 64, axis=1)
    return out


def _onehot_rows():
    oh = np.zeros((16, 8, 128), np.float32)
    for b in range(8):
        oh[2 * b, b, 0:64] = 1.0
        oh[2 * b + 1, b, 64:128] = 1.0
    out = np.zeros((128, 1024), np.float32)
    out[0:16] = oh.reshape(16, 1024)
    return out


def _u_table(rpb_l):
    U = np.full((128, NH, NE, 64), NEG, np.float32)
    cq = np.arange(64)
    cs = np.clip(cq - 8, 0, 48)
    for jj in range(2):
        for ei in range(NE):
            dr = jj - (ei - E0)
            if abs(dr) > 7:
                continue
            for ck in range(64):
                valid = (ck >= cs) & (ck < cs + 16)
                dc = np.clip(ck - cq + 15, 0, 30)
                vals = rpb_l[:, dr + 7, :][:, dc]
                U[jj * 64 + ck, :, ei, :] = np.where(valid[None, :], vals, NEG)
    return U.reshape(128, -1)


def _prep(inputs):
    f = lambda a: np.ascontiguousarray(np.asarray(a, dtype=np.float32))
    x_prompt, x_sample = f(inputs["x_prompt"]), f(inputs["x_sample"])
    cache_kv, c, c_ctx = f(inputs["cache_kv"]), f(inputs["c"]), f(inputs["c_ctx"])
    rpb = f(inputs["rpb"])
    shared = {
        "w_mod": f(inputs["w_mod"]), "w_in": f(inputs["w_in"]), "w_branch": f(inputs["w_branch"]),
        "w_out": f(inputs["w_out"]), "w_up": f(inputs["ffn_w_up"]), "w_down": f(inputs["ffn_w_down"]),
        "pool_w": f(inputs["pool_w"]),
    }

    def pk(v):
        return np.ascontiguousarray(v.reshape(-1, 128).T)

    vec_common = np.zeros((128, NV), np.float32)
    b_mod, g1, g2, gf = f(inputs["b_mod"]), f(inputs["g_norm1"]), f(inputs["g_norm2"]), f(inputs["g_final"])
    conv_w, fconv, pscale = f(inputs["conv_w"]), f(inputs["ffn_conv"]), f(inputs["pool_scale"])
    for l in range(NL):
        vec_common[:, V_BMOD + l * 48: V_BMOD + (l + 1) * 48] = pk(b_mod[l])
        vec_common[:, V_G1 + l * 8: V_G1 + (l + 1) * 8] = pk(g1[l])
        vec_common[:, V_G2 + l * 8: V_G2 + (l + 1) * 8] = pk(g2[l])
        for cc in range(4):
            for k in range(3):
                vec_common[:, V_CONVW + (l * 4 + cc) * 3 + k] = conv_w[l, k, cc * 128:(cc + 1) * 128]
        for j in range(NJ):
            for k in range(3):
                vec_common[:, V_FCONV + (l * NJ + j) * 3 + k] = fconv[l, k, j * 128:(j + 1) * 128]
        vec_common[:, V_PSCALE + l * 4: V_PSCALE + (l + 1) * 4] = pk(pscale[l])
    vec_common[:, V_GF:V_GF + 8] = pk(gf)

    ones_pat = np.concatenate([np.ones((128, 64)), np.zeros((128, 64)), np.ones((128, 64))], axis=1).astype(np.float32)
    band_p, band_s = _band_tables(256), _band_tables(1024)
    rneg_p, rneg_s = _rneg_table(False), _rneg_table(True)
    oh = _onehot_rows()
    utab_s = np.stack([_u_table(rpb[l]) for l in range(NL)])
    utab_p = np.zeros_like(utab_s)
    in_maps = []
    for i in range(8):
        sample = i >= 4
        vec = vec_common.copy()
        if sample:
            b = i - 4
            xT = np.ascontiguousarray(x_sample[b].T)
            vec[:, V_CV:V_CV + 8] = pk(c[b])
            vec[:, V_BFLAG] = 0.0
            ck = cache_kv[b, :, 0]
            ctxk = np.zeros((NL, 2, 64, NH, 256), np.float32)
            for h_ in range(NH):
                ctxk[:, h_ % 2, :, h_, :] = ck[:, h_].transpose(0, 2, 1)
            ctxk = ctxk.reshape(NL, 128, NH * 256)
            cvv = cache_kv[b, :, 1]
            ctxv = cvv.transpose(0, 2, 1, 3).reshape(NL, 2, 128, 512).transpose(0, 2, 1, 3).reshape(NL, 128, 1024)
            m = {"ctxkT": np.ascontiguousarray(ctxk), "ctxv": np.ascontiguousarray(ctxv), "utab": utab_s,
                 "rneg": rneg_s, "oh": oh, "onesc": ones_pat, "band": band_s}
        else:
            xT = np.ascontiguousarray(x_prompt[4 * i:4 * i + 4].reshape(T, D).T)
            vec[:, V_CV:V_CV + 8] = pk(c_ctx)
            vec[:, V_BFLAG] = 1.0
            m = {"ctxkT": np.zeros((NL, 128, 2048), np.float32), "ctxv": np.zeros((NL, 128, 1024), np.float32),
                 "utab": utab_p, "rneg": rneg_p, "oh": oh, "onesc": np.zeros_like(ones_pat), "band": band_p}
        m.update(shared)
        m["xT"] = xT
        m["vecs"] = vec
        in_maps.append(m)
    return in_maps


def _assemble(results):
    y_prompt = np.empty((16, 256, D), np.float32)
    y_sample = np.empty((4, T, D), np.float32)
    kv_state = np.empty((16, NL, 2, NH, 256, 64), np.float32)
    for i in range(8):
        r = results[i]
        y = np.asarray(r["yT"], dtype=np.float32).T
        if i < 4:
            y_prompt[4 * i:4 * i + 4] = y.reshape(4, 256, D)
            kvo = np.asarray(r["kvo"], dtype=np.float32).reshape(NL, 2, 4, 256, NH, 64)
            kv_state[4 * i:4 * i + 4] = kvo.transpose(2, 0, 1, 4, 3, 5)
        else:
            y_sample[i - 4] = y
    return y_prompt, y_sample, kv_state


_NC_CACHE = {}


def kernel(**inputs):
    in_maps = _prep(inputs)
    if "nc" not in _NC_CACHE:
        _NC_CACHE["nc"] = build_nc()
    res = run_bass_kernel_spmd(_NC_CACHE["nc"], in_maps, core_ids=list(range(8)))
    return _assemble(res.results)
```

```python
import numpy as np
import ml_dtypes
from contextlib import ExitStack
import concourse.bass as bass
import concourse.mybir as mybir
from concourse.bass_utils import run_bass_kernel_spmd

F32 = mybir.dt.float32
BF16 = mybir.dt.bfloat16
AF = mybir.ActivationFunctionType
ALU = mybir.AluOpType

D = 1024
T = 1024
KC = 8
DFF = 2816
NJ = 22
INW = 6656
NL = 2
NH = 8
NE = 22
E0 = 10
NEG = -30000.0
SLOT = 4608
NSLOT = 5
ND = 16
SEM_LIMIT = 12000

V_BMOD = 0
V_G1 = 96
V_G2 = 112
V_GF = 128
V_CONVW = 136
V_FCONV = 160
V_PSCALE = 292
V_CV = 300
V_BFLAG = 308
NV = 320

LOCAL_TILES = {0: [0, 1, 2, 3, 4, 5], 1: [2, 3, 4, 5, 6, 7]}


class Sched:
    def __init__(self, nc, es):
        self.nc = nc
        self.es = es
        self.eh = {"pe": nc.tensor, "act": nc.scalar, "dve": nc.vector, "pool": nc.gpsimd, "sp": nc.sync}
        self.epoch = {e: 0 for e in self.eh}
        self.cnt = {e: 0 for e in self.eh}
        self.sems = {}
        for e in self.eh:
            self.sems[(e, 0)] = es.enter_context(nc.semaphore(f"s_{e}_0"))
        self.dsem = [es.enter_context(nc.semaphore(f"s_dma_{i}")) for i in range(ND)]
        self.dcnt = [0] * ND
        self.dpool = {"sp": list(range(0, 6)), "pool": list(range(6, ND))}
        self.dnext = {"sp": 0, "pool": 0}
        self.seen = {e: {} for e in self.eh}
        self.last_w = {}
        self.readers = {}
        self.out_dmas = []
        self.aux = {}
        self.nops = {e: 0 for e in self.eh}
        self.pending = {e: False for e in self.eh}
        self.phase = "init"
        self.phase_of = {}

    def _sem_of(self, src):
        if src[0] == "d":
            return self.dsem[src[1]]
        return self.sems[src]

    def _wait(self, eng, src, c):
        if self.seen[eng].get(src, 0) >= c:
            return
        self.eh[eng].wait_ge(self._sem_of(src), c)
        self.seen[eng][src] = c

    def _deps(self, eng, reads, writes):
        deps = {}

        def add(src, c):
            if deps.get(src, 0) < c:
                deps[src] = c

        for k in reads:
            w = self.last_w.get(k)
            if w:
                add(*w)
        for k in writes:
            w = self.last_w.get(k)
            if w:
                add(*w)
            for src, c in self.readers.get(k, {}).items():
                add(src, c)
        for src, c in deps.items():
            if eng == "pe" and src[0] == "pe":
                continue
            self._wait(eng, src, c)

    def _record(self, src, c, reads, writes):
        for k in writes:
            self.last_w[k] = (src, c)
            self.readers[k] = {}
        for k in reads:
            r = self.readers.setdefault(k, {})
            if r.get(src, 0) < c:
                r[src] = c

    def op(self, eng, fn, reads=(), writes=(), signal=True):
        if (not self.pending[eng]) and self.cnt[eng] >= SEM_LIMIT:
            self.epoch[eng] += 1
            self.cnt[eng] = 0
            self.sems[(eng, self.epoch[eng])] = self.es.enter_context(
                self.nc.semaphore(f"s_{eng}_{self.epoch[eng]}"))
        self._deps(eng, reads, writes)
        ins = fn(self.eh[eng])
        src = (eng, self.epoch[eng])
        if signal:
            ins.then_inc(self.sems[src], 1)
            self.cnt[eng] += 1
            c = self.cnt[eng]
            self.pending[eng] = False
        else:
            c = self.cnt[eng] + 1
            self.pending[eng] = True
        self._record(src, c, reads, writes)
        self.nops[eng] += 1
        try:
            self.phase_of[ins.ins.name] = self.phase
        except Exception:
            pass
        return ins

    def dma(self, q, out, in_, reads=(), writes=(), is_output=False, ring=False):
        self._deps(q, reads, writes)
        lst = self.dpool[q]
        s = lst[self.dnext[q] % len(lst)]
        self.dnext[q] += 1
        src = ("d", s)
        if self.dcnt[s] > 0:
            self._wait(q, src, self.dcnt[s])
        self.eh[q].dma_start(out=out, in_=in_).then_inc(self.dsem[s], 16)
        self.dcnt[s] += 16
        self._record(src, self.dcnt[s], reads, writes)
        if is_output:
            self.out_dmas.append((src, self.dcnt[s]))
        if not ring:
            self.aux[src] = self.dcnt[s]

    def barrier(self):
        cur = [((e, self.epoch[e]), self.cnt[e]) for e in self.eh if self.cnt[e] > 0]
        cur += list(self.aux.items())
        self.aux = {}
        for e in self.eh:
            for src, c in cur:
                if src[0] == e and e == "pe":
                    continue
                self._wait(e, src, c)

    def finish(self):
        for q in ("sp", "pool"):
            for s_, c in enumerate(self.dcnt):
                if c > 0:
                    self._wait(q, ("d", s_), c)
        self.barrier()


class _Stop(Exception):
    pass


def build_nc(debug=None, stop=None):
    nc = bass.Bass("TRN2", target_bir_lowering=False)

    def din(name, shape):
        return nc.dram_tensor(name, list(shape), F32, kind="ExternalInput").ap()

    def dout(name, shape):
        return nc.dram_tensor(name, list(shape), F32, kind="ExternalOutput").ap()

    xT_d = din("xT", [D, T])
    vecs_d = din("vecs", [128, NV])
    w_mod_d = din("w_mod", [NL, D, 6 * D])
    w_in_d = din("w_in", [NL, D, INW])
    w_br_d = din("w_branch", [NL, 3, 512, D])
    w_out_d = din("w_out", [NL, D, D])
    w_up_d = din("w_up", [NL, D, 2 * DFF])
    w_dn_d = din("w_down", [NL, DFF, D])
    pool_w_d = din("pool_w", [NL, 4, 128, 128])
    ctxk_d = din("ctxkT", [NL, 128, 8 * 256])
    ctxv_d = din("ctxv", [NL, 128, 2 * 512])
    utab_d = din("utab", [NL, 128, NH * NE * 64])
    rneg_d = din("rneg", [128, T])
    oh_d = din("oh", [128, 8 * 128])
    onesc_d = din("onesc", [128, 192])
    band_d = din("band", [128, 4 * 8 * 2 * 128])
    yT_d = dout("yT", [D, T])
    kvo_d = dout("kvo", [NL, T, 512])
    kvoT_d = dout("kvoT", [NL, 512, T])
    dbg_d = {}
    if debug:
        for name, (shape, dt_) in debug.items():
            dbg_d[name] = nc.dram_tensor("dbg_" + name, list(shape), dt_, kind="ExternalOutput").ap()

    es = ExitStack()
    with es:
        def sb(name, shape, dt):
            return es.enter_context(nc.sbuf_tensor(name, list(shape), dt))

        X = sb("X", [128, KC, T], F32)
        Hh = sb("Hh", [128, KC, T], BF16)
        AR = sb("AR", [128, 32256], BF16)
        UT = sb("UT", [128, 2, NE * 64], BF16)
        WR = sb("WR", [128, NSLOT, SLOT], BF16)
        SCR = sb("SCR", [128, 5632], F32)
        VEC = sb("VEC", [128, NV], F32)
        MOD = sb("MOD", [128, NL, 48], F32)
        GM = sb("GM", [128, NL * 2 * 8], F32)
        SMALL = sb("SMALL", [128, 256], F32)
        SCb = sb("SCb", [128, 8], BF16)
        RNEG = sb("RNEG", [128, T], BF16)
        OH = sb("OH", [128, 8 * 128], BF16)
        ONES = sb("ONES", [128, 128], BF16)
        ONESL = sb("ONESL", [128, 192], BF16)
        ONESC = sb("ONESC", [128, 192], BF16)
        CK = sb("CK", [128, 8 * 256], BF16)
        PS = [es.enter_context(nc.psum_tensor(f"ps{i}", [128, 512], F32)) for i in range(8)]

        S = Sched(nc, es)

        QK = AR[:, 0:4096].rearrange("p (c t) -> p c t", c=4)
        KZ = AR[:, 4096:12288].rearrange("p (h t) -> p h t", h=8)
        MERGED = AR[:, 0:8192].rearrange("p (c t) -> p c t", c=8)
        VP = AR[:, 12288:12288 + 7680].rearrange("p (k h c) -> p k h c", k=10, h=4)
        YS = AR[:, 19968:19968 + 12288].rearrange("p (c t) -> p c t", c=12)
        PZ = AR[:, 19968 + 4096:19968 + 8192].rearrange("p (b c) -> p b c", b=8)
        A = AR[:, 0:NJ * T].rearrange("p (j t) -> p j t", j=NJ)
        BAND = AR[:, 4096:12288]
        BANDv = BAND.rearrange("p (g v s c) -> p g v s c", g=4, v=8, s=2)
        CKv = CK[:, :].rearrange("p (h k) -> p h k", h=8)
        OHv = OH[:, :].rearrange("p (b k) -> p b k", b=8)
        GMv = GM[:, :].rearrange("p (l j c) -> p l j c", l=NL, j=2)

        def vcol(base, idx=0, n=1):
            return VEC[:, base + idx: base + idx + n]

        ps_ctr = [0]

        def ps_next():
            i = ps_ctr[0] % 8
            ps_ctr[0] += 1
            return PS[i], ("ps", i)

        def HS(half):
            return slice(half * 512, half * 512 + 512)

        def scr_f32(off, n):
            return SCR[:, off:off + n]

        def scr_bf16(off, n):
            return SCR[:, off:off + n].bitcast(BF16)

        loads = []

        class Ring:
            issued = 0
            consumed = 0

        def ring_use_many(n):
            idx = Ring.consumed
            target = min(len(loads), idx + NSLOT)
            while Ring.issued < target:
                i = Ring.issued
                loads[i](i % NSLOT)
                Ring.issued += 1
            Ring.consumed += n
            return [(idx + k) % NSLOT for k in range(n)]

        def ring_use():
            return ring_use_many(1)[0]

        def ring_prefetch():
            target = min(len(loads), Ring.consumed + NSLOT)
            while Ring.issued < target:
                i = Ring.issued
                loads[i](i % NSLOT)
                Ring.issued += 1

        def wkeys(slot):
            return [("WR", slot, 0), ("WR", slot, 1)]

        def ld_cols(src2d, col0, ncols, nkc=KC):
            def f(slot):
                dst = WR[:, slot, 0:nkc * ncols].rearrange("p (k n) -> p k n", k=nkc)
                src = src2d.rearrange("(k p) n -> p k n", p=128)[:, :, col0:col0 + ncols]
                S.dma("pool", dst, src, writes=wkeys(slot), ring=True)
            return f

        def ld_gate_branch(l, c):
            def f(slot):
                for i in range(3):
                    dst = WR[:, slot, i * 1024:(i + 1) * 1024].rearrange("p (k n) -> p k n", k=KC)
                    col0 = 3584 + i * 1024 + c * 128
                    src = w_in_d[l].rearrange("(k p) n -> p k n", p=128)[:, :, col0:col0 + 128]
                    S.dma("pool", dst, src, writes=[("WR", slot, 0)] if i == 0 else [("WRx", slot, i)], ring=True)
                for i in range(3):
                    dst = WR[:, slot, 3072 + i * 512:3072 + (i + 1) * 512].rearrange("p (k n) -> p k n", k=4)
                    src = w_br_d[l, i].rearrange("(k p) n -> p k n", p=128)[:, :, c * 128:(c + 1) * 128]
                    S.dma("pool", dst, src, writes=[("WR", slot, 1)] if i == 0 else [("WRy", slot, i)], ring=True)
            return f

        def gate_keys(slot):
            return [("WR", slot, 0), ("WRx", slot, 1), ("WRx", slot, 2), ("WR", slot, 1), ("WRy", slot, 1), ("WRy", slot, 2)]

        def ld_up(l, jp):
            def f(slot):
                dstv = WR[:, slot, 0:4096].rearrange("p (k two n) -> p k two n", k=KC, two=2)
                srcv = w_up_d[l].rearrange("(k p) (two n) -> p k two n", p=128, two=2)
                for t_ in range(2):
                    S.dma("pool", dstv[:, :, t_, :], srcv[:, :, t_, jp * 256:(jp + 1) * 256], writes=[("WR", slot, t_)], ring=True)
            return f

        def ld_down(l, c):
            def f(slot):
                dst = WR[:, slot, 0:NJ * 128].rearrange("p (k n) -> p k n", k=NJ)
                src = w_dn_d[l].rearrange("(k p) n -> p k n", p=128)[:, :, c * 128:(c + 1) * 128]
                S.dma("pool", dst, src, writes=wkeys(slot), ring=True)
            return f

        def ld_poolw(l):
            def f(slot):
                dst = WR[:, slot, 0:512].rearrange("p (g e) -> p g e", g=4)
                src = pool_w_d[l].rearrange("g c e -> c g e")
                S.dma("pool", dst, src, writes=wkeys(slot), ring=True)
            return f

        for l in range(NL):
            if l == 0:
                for g in range(4):
                    loads.append(ld_cols(w_mod_d[l], g * 512, 512))
            loads.append(ld_cols(w_in_d[l], 6 * 512, 512))
            loads.append(ld_poolw(l))
            for g in (3, 4, 5):
                loads.append(ld_cols(w_in_d[l], g * 512, 512))
            for g in (2, 1, 0):
                loads.append(ld_cols(w_in_d[l], g * 512, 512))
            for c in range(8):
                loads.append(ld_gate_branch(l, c))
                loads.append(ld_cols(w_mod_d[l], (4 + c) * 512, 512))
            for g in range(2):
                loads.append(ld_cols(w_out_d[l], g * 512, 512))
            for jp in range(11):
                loads.append(ld_up(l, jp))
            for c in range(8):
                loads.append(ld_down(l, c))
                if c < 4 and l + 1 < NL:
                    loads.append(ld_cols(w_mod_d[l + 1], c * 512, 512))

        def mm_group(ps_ap, ps_key, pairs, reads):
            n = len(pairs)
            for i, (lt, rh) in enumerate(pairs):
                S.op("pe", lambda e, lt=lt, rh=rh, i=i: e.matmul(ps_ap, lt, rh, start=(i == 0), stop=(i == n - 1)),
                     reads=reads if i == 0 else (), writes=[ps_key] if i == 0 else (), signal=(i == n - 1))

        def Hkeys(half):
            return [("H", kc, half) for kc in range(KC)]

        S.dma("sp", VEC[:, :], vecs_d, writes=[("VEC",)])
        for dc in range(KC):
            S.dma("sp", X[:, dc, :], xT_d[dc * 128:(dc + 1) * 128, :], writes=[("X", dc, 0), ("X", dc, 1)])
        ring_prefetch()
        S.dma("pool", RNEG[:, :], rneg_d, writes=[("RNEG",)])
        S.dma("pool", OH[:, :], oh_d, writes=[("OH",)])
        S.dma("pool", ONESC[:, :], onesc_d, writes=[("ONESC",)])
        S.op("dve", lambda e: e.memset(ONES[:, :], 1.0), writes=[("ONES",)])
        S.op("dve", lambda e: e.memset(ONESL[:, :], 1.0), writes=[("ONESL",)])
        S.op("dve", lambda e: e.memset(ONESL[:, 64:128], 0.0), writes=[("ONESL",)])
        S.op("dve", lambda e: e.memset(AR[:, 12288:12288 + 7680], 0.0), writes=[("VPall",)])
        S.op("act", lambda e: e.activation(out=SCb[:, :], in_=vcol(V_CV, 0, 8), func=AF.Silu),
             reads=[("VEC",)], writes=[("SCb",)])
        S.op("dve", lambda e: e.tensor_scalar(out=SMALL[:, 0:1], in0=vcol(V_BFLAG), scalar1=-1.0, scalar2=None,
                                              op0=ALU.mult), reads=[("VEC",)], writes=[("NBF",)])

        def mod_part(l, groups):
            for g in groups:
                slot = ring_use()
                ps, pk = ps_next()
                wv = WR[:, slot, 0:4096].rearrange("p (k n) -> p k n", k=KC)
                first = True
                for c in range(4):
                    col = g * 4 + c
                    for kc in range(KC):
                        S.op("pe", lambda e, c=c, kc=kc, col=col: e.matmul(
                            ps[:, col:col + 1], wv[:, kc, c * 128:(c + 1) * 128], SCb[:, kc:kc + 1],
                            start=(kc == 0), stop=(kc == KC - 1)),
                            reads=(wkeys(slot) + [("SCb",)]) if first else (), writes=[pk] if first else (),
                            signal=(c == 3 and kc == KC - 1))
                        first = False
                S.op("dve", lambda e, g=g: e.tensor_tensor(
                    out=MOD[:, l, g * 4:(g + 1) * 4], in0=ps[:, g * 4:(g + 1) * 4],
                    in1=VEC[:, V_BMOD + l * 48 + g * 4: V_BMOD + l * 48 + (g + 1) * 4], op=ALU.add),
                    reads=[pk, ("VEC",)], writes=[("MOD", l, g)])

        def gm_compute(l, j):
            sc0 = 8 if j == 0 else 32
            gb = (V_G1 if j == 0 else V_G2) + l * 8
            S.op("dve", lambda e: e.scalar_tensor_tensor(
                out=GMv[:, l, j, :], in0=MOD[:, l, sc0:sc0 + 8], scalar=1.0, in1=VEC[:, gb:gb + 8],
                op0=ALU.add, op1=ALU.mult),
                reads=[("MOD", l, sc0 // 4), ("MOD", l, sc0 // 4 + 1), ("VEC",)], writes=[("GM", l, j)])

        def norm(scale_ap_fn, bias_ap_fn, extra_reads, out_fn, out_keys_fn, hook=None):
            SQ = [scr_bf16(0, 256), scr_bf16(256, 256)]
            SS = [scr_f32(512, 512), scr_f32(1024, 512)]
            RS = [scr_f32(1536, 512), scr_f32(2048, 512)]
            TMP = [scr_f32(2560, 512), scr_f32(3072, 512)]
            for half in range(2):
                ps, pk = ps_next()
                for dc in range(KC):
                    sq = SQ[dc % 2]
                    if dc % 2 == 0:
                        S.op("act", lambda e, dc=dc, sq=sq: e.activation(out=sq, in_=X[:, dc, HS(half)], func=AF.Square),
                             reads=[("X", dc, half)], writes=[("SQ", dc % 2)])
                    else:
                        S.op("dve", lambda e, dc=dc, sq=sq: e.tensor_tensor(out=sq, in0=X[:, dc, HS(half)], in1=X[:, dc, HS(half)], op=ALU.mult),
                             reads=[("X", dc, half)], writes=[("SQ", dc % 2)])
                    S.op("pe", lambda e, dc=dc, sq=sq: e.matmul(ps[:, :], ONES[:, :], sq, start=(dc == 0), stop=(dc == KC - 1)),
                         reads=[("SQ", dc % 2), ("ONES",)], writes=[pk])
                S.op("act", lambda e: e.activation(out=SS[half], in_=ps[:, :], func=AF.Sqrt, scale=1.0 / D, bias=1e-6),
                     reads=[pk], writes=[("SS", half)])
                S.op("dve", lambda e: e.reciprocal(out=RS[half], in_=SS[half]), reads=[("SS", half)], writes=[("RS", half)])
            if hook is not None:
                hook()
            for half in range(2):
                for dc in range(KC):
                    tmp = TMP[dc % 2]
                    S.op("dve", lambda e, dc=dc, tmp=tmp: e.tensor_tensor(out=tmp, in0=X[:, dc, HS(half)], in1=RS[half], op=ALU.mult),
                         reads=[("X", dc, half), ("RS", half)], writes=[("TMPn", dc % 2)])
                    b = bias_ap_fn(dc)
                    S.op("act", lambda e, dc=dc, tmp=tmp, b=b: e.activation(
                        out=out_fn(dc, half), in_=tmp, func=AF.Identity, scale=scale_ap_fn(dc),
                        **({"bias": b} if b is not None else {})),
                        reads=[("TMPn", dc % 2)] + extra_reads(), writes=out_keys_fn(dc, half))

        def stop_at(tag):
            if stop == tag:
                raise _Stop()

        def dbg_dump(name, ap, keys):
            if debug and name in dbg_d:
                S.barrier()
                S.dma("sp", dbg_d[name], ap, reads=keys, is_output=True)
                S.barrier()

        def run_layers():
          for l in range(NL):
            if l > 0:
                gm_compute(l, 0)
            S.dma("pool", CK[:, :].rearrange("p (a b) -> p a b", b=1024), ctxk_d[l].rearrange("p (a b) -> p a b", b=1024),
                  writes=[("CK",)])
            S.dma("pool", BAND.rearrange("p (a b) -> p a b", b=1024), band_d.rearrange("p (a b) -> p a b", b=1024),
                  writes=[("BAND",)])
            for j in range(2):
                dst = VP[:, 8:10, :, j * 128:j * 128 + 64]
                src = ctxv_d[l].rearrange("p (k h j d) -> p k h j d", k=2, h=4, j=2)[:, :, :, j, :]
                S.dma("pool", dst, src, writes=[("VPc", j)], reads=[("VPall",)])

            S.phase = f"L{l}:norm1"
            def l0_hook():
                mod_part(0, range(0, 4))
                gm_compute(0, 0)
            norm(lambda dc: GMv[:, l, 0, dc:dc + 1], lambda dc: MOD[:, l, dc:dc + 1],
                 lambda: [("GM", l, 0), ("MOD", l, 0), ("MOD", l, 1)],
                 lambda dc, half: Hh[:, dc, HS(half)], lambda dc, half: [("H", dc, half)],
                 hook=l0_hook if l == 0 else None)
            S.barrier()
            if l == 0:
                stop_at("norm1")
            if l == 0:
                dbg_dump("h1", Hh[:, :, :], [("H", dc, hf) for dc in range(KC) for hf in range(2)])

            if l == 0:
                stop_at("A_pz0")
            slot = ring_use()
            wv = WR[:, slot, 0:4096].rearrange("p (k n) -> p k n", k=KC)
            for tb in range(8):
                ps, pk = ps_next()
                mm_group(ps[:, :], pk, [(Hh[:, kc, tb * 128:(tb + 1) * 128], wv[:, kc, :]) for kc in range(KC)],
                         wkeys(slot) + Hkeys(tb // 4))
                S.op("act", lambda e, ps=ps, tb=tb: e.activation(out=PZ[:, tb, :], in_=ps[:, :], func=AF.Copy),
                     reads=[pk], writes=[("PZ", tb)])

            if l == 0:
                stop_at("A_pz")
            S.phase = f"L{l}:A_pool"
            slot = ring_use()
            pwv = WR[:, slot, 0:512].rearrange("p (g e) -> p g e", g=4)
            DT = [scr_bf16(0, 256), scr_bf16(256, 256)]
            for g in range(4):
                for half in range(2):
                    ps, pk = ps_next()
                    first = True
                    for obi in range(4):
                        ob = half * 4 + obi
                        terms = []
                        dv = 0 if ob == 0 else (1 if ob == 7 else (2 if ob % 2 == 0 else 3))
                        terms.append((ob, dv))
                        if ob >= 1:
                            terms.append((ob - 1, 4 if ob % 2 == 1 else 5))
                        if ob <= 6:
                            terms.append((ob + 1, 6 if ob % 2 == 0 else 7))
                        mats = [(ib, v, s) for (ib, v) in terms for s in range(2)]
                        for i, (ib, v, s) in enumerate(mats):
                            last = (obi == 3 and i == len(mats) - 1)
                            S.op("pe", lambda e, ib=ib, v=v, s=s, i=i, obi=obi, n=len(mats): e.matmul(
                                ps[:, obi * 128:(obi + 1) * 128], PZ[:, ib, g * 128:(g + 1) * 128], BANDv[:, g, v, s, :],
                                start=(i == 0), stop=(i == n - 1)),
                                reads=([("PZ", t) for t in range(8)] + [("BAND",)]) if first else (),
                                writes=[pk] if first else (), signal=last)
                            first = False
                    dt = DT[(g * 2 + half) % 2]
                    S.op("act", lambda e, ps=ps, dt=dt: e.activation(out=dt, in_=ps[:, :], func=AF.Copy),
                         reads=[pk], writes=[("DT", (g * 2 + half) % 2)])
                    ps2, pk2 = ps_next()
                    mm_group(ps2[:, :], pk2, [(pwv[:, g, :], dt)], wkeys(slot) + [("DT", (g * 2 + half) % 2)])
                    S.op("dve", lambda e, ps2=ps2, g=g, half=half: e.tensor_scalar(
                        out=YS[:, 8 + g, HS(half)], in0=ps2[:, :], scalar1=vcol(V_PSCALE, l * 4 + g), scalar2=None, op0=ALU.mult),
                        reads=[pk2, ("VEC",)], writes=[("YS", 8 + g, half)])

            if l == 0:
                stop_at("A_pool")
            S.op("dve", lambda e: e.memset(AR[:, 4096:12288], 0.0), writes=[("KZall",), ("BAND",)])
            S.phase = f"L{l}:A_qkv"
            KST = [scr_f32(4096, 512), scr_f32(4608, 512)]
            for which in range(2):
                slot = ring_use()
                wv = WR[:, slot, 0:4096].rearrange("p (k n) -> p k n", k=KC)
                for c in range(4):
                    for half in range(2):
                        ps, pk = ps_next()
                        mm_group(ps[:, :], pk, [(wv[:, kc, c * 128:(c + 1) * 128], Hh[:, kc, HS(half)]) for kc in range(KC)],
                                 wkeys(slot) + Hkeys(half))
                        if which == 0:
                            S.op("act", lambda e, c=c, half=half, ps=ps: e.activation(
                                out=QK[:, c, HS(half)], in_=ps[:, :], func=AF.Copy, scale=0.125),
                                reads=[pk], writes=[("QK", c, half)])
                        else:
                            st = KST[(c * 2 + half) % 2]
                            sk_ = ("KST", (c * 2 + half) % 2)
                            S.op("act", lambda e, ps=ps, st=st: e.activation(out=st, in_=ps[:, :], func=AF.Copy),
                                 reads=[pk], writes=[sk_])
                            S.dma("sp", kvoT_d[l, c * 128:(c + 1) * 128, HS(half)], st, reads=[sk_], is_output=True)
                            for jj in range(2):
                                S.op("dve", lambda e, c=c, half=half, st=st, jj=jj: e.tensor_copy(
                                    out=KZ[jj * 64:(jj + 1) * 64, 2 * c + jj, HS(half)], in_=st[jj * 64:(jj + 1) * 64, :]),
                                    reads=[sk_, ("KZall",)], writes=[("KZ", 2 * c + jj, half)])
            if l == 0:
                stop_at("A_qk")
            slot = ring_use()
            wv = WR[:, slot, 0:4096].rearrange("p (k n) -> p k n", k=KC)
            for tb in range(8):
                ps, pk = ps_next()
                mm_group(ps[:, :], pk, [(Hh[:, kc, tb * 128:(tb + 1) * 128], wv[:, kc, :]) for kc in range(KC)],
                         wkeys(slot) + Hkeys(tb // 4))
                st = KST[tb % 2]
                S.op("act", lambda e, ps=ps, st=st: e.activation(out=st, in_=ps[:, :], func=AF.Copy),
                     reads=[pk], writes=[("KST", tb % 2)])
                S.dma("sp", kvo_d[l, tb * 128:(tb + 1) * 128, :], st, reads=[("KST", tb % 2)], is_output=True)
                for j in range(2):
                    S.op("dve", lambda e, st=st, tb=tb, j=j: e.tensor_copy(
                        out=VP[:, tb, :, j * 128:j * 128 + 64],
                        in_=st.rearrange("p (h j d) -> p h j d", h=4, j=2)[:, :, j, :]),
                        reads=[("KST", tb % 2), ("VPall",)], writes=[("VP", tb, j)])
            S.phase = f"L{l}:A_conv"
            slot_h, slot_c, slot_b = ring_use_many(3)
            wh = WR[:, slot_h, 0:4096].rearrange("p (k n) -> p k n", k=KC)
            wc = WR[:, slot_c, 0:4096].rearrange("p (k n) -> p k n", k=KC)
            wb = WR[:, slot_b, 0:4096].rearrange("p (k n) -> p k n", k=KC)
            HC = [scr_f32(512, 512), scr_f32(1024, 512)]
            CH = scr_f32(1536, 1026)
            YC = scr_f32(2562, 1024)
            S.op("dve", lambda e: e.memset(CH[:, 0:1], 0.0), writes=[("CH", 0), ("CH", 1)])
            S.op("dve", lambda e: e.memset(CH[:, 1025:1026], 0.0), writes=[("CHp",)])
            for c in range(4):
                cwb = V_CONVW + (l * 4 + c) * 3
                S.op("dve", lambda e, cwb=cwb: e.tensor_scalar(out=SMALL[:, 1:2], in0=vcol(cwb, 0), scalar1=SMALL[:, 0:1],
                                                              scalar2=None, op0=ALU.mult),
                     reads=[("VEC",), ("NBF",)], writes=[("FX", 0)])
                S.op("dve", lambda e, cwb=cwb: e.tensor_scalar(out=SMALL[:, 2:3], in0=vcol(cwb, 2), scalar1=SMALL[:, 0:1],
                                                              scalar2=None, op0=ALU.mult),
                     reads=[("VEC",), ("NBF",)], writes=[("FX", 1)])
                for half in range(2):
                    ps1, pk1 = ps_next()
                    mm_group(ps1[:, :], pk1, [(wh[:, kc, c * 128:(c + 1) * 128], Hh[:, kc, HS(half)]) for kc in range(KC)],
                             wkeys(slot_h) + Hkeys(half))
                    hc = HC[half]
                    S.op("act", lambda e, ps1=ps1, hc=hc: e.activation(out=hc, in_=ps1[:, :], func=AF.Copy),
                         reads=[pk1], writes=[("HC", half)])
                    ps2, pk2 = ps_next()
                    mm_group(ps2[:, :], pk2, [(wc[:, kc, c * 128:(c + 1) * 128], Hh[:, kc, HS(half)]) for kc in range(KC)],
                             wkeys(slot_c) + Hkeys(half))
                    S.op("dve", lambda e, ps2=ps2, hc=hc, half=half: e.tensor_tensor(
                        out=CH[:, 1 + half * 512: 1 + half * 512 + 512], in0=ps2[:, :], in1=hc, op=ALU.mult),
                        reads=[pk2, ("HC", half)], writes=[("CH", half)])
                chk = [("CH", 0), ("CH", 1), ("CHp",)]
                S.op("act", lambda e, cwb=cwb: e.activation(out=YC[:, :], in_=CH[:, 1:1025], func=AF.Identity, scale=vcol(cwb, 1)),
                     reads=chk + [("VEC",)], writes=[("YC",)])
                S.op("dve", lambda e, cwb=cwb: e.scalar_tensor_tensor(
                    out=YC[:, :], in0=CH[:, 0:1024], scalar=vcol(cwb, 0), in1=YC[:, :], op0=ALU.mult, op1=ALU.add),
                    reads=chk + [("VEC",), ("YC",)], writes=[("YC",)])
                S.op("dve", lambda e, cwb=cwb: e.scalar_tensor_tensor(
                    out=YC[:, :], in0=CH[:, 2:1026], scalar=vcol(cwb, 2), in1=YC[:, :], op0=ALU.mult, op1=ALU.add),
                    reads=chk + [("VEC",), ("YC",)], writes=[("YC",)])
                ycv = YC[:, :].rearrange("p (s t) -> p s t", s=4)
                chv = CH[:, 1:1025].rearrange("p (s t) -> p s t", s=4)
                S.op("dve", lambda e, ycv=ycv, chv=chv: e.scalar_tensor_tensor(
                    out=ycv[:, 1:4, 0:1], in0=chv[:, 0:3, 255:256], scalar=SMALL[:, 1:2], in1=ycv[:, 1:4, 0:1],
                    op0=ALU.mult, op1=ALU.add), reads=chk + [("FX", 0), ("YC",)], writes=[("YC",)])
                S.op("dve", lambda e, ycv=ycv, chv=chv: e.scalar_tensor_tensor(
                    out=ycv[:, 0:3, 255:256], in0=chv[:, 1:4, 0:1], scalar=SMALL[:, 2:3], in1=ycv[:, 0:3, 255:256],
                    op0=ALU.mult, op1=ALU.add), reads=chk + [("FX", 1), ("YC",)], writes=[("YC",)])
                for half in range(2):
                    ps3, pk3 = ps_next()
                    mm_group(ps3[:, :], pk3, [(wb[:, kc, c * 128:(c + 1) * 128], Hh[:, kc, HS(half)]) for kc in range(KC)],
                             wkeys(slot_b) + Hkeys(half))
                    S.op("dve", lambda e, ps3=ps3, half=half, c=c: e.tensor_tensor(
                        out=YS[:, c, HS(half)], in0=ps3[:, :], in1=YC[:, HS(half)], op=ALU.mult),
                        reads=[pk3, ("YC",)], writes=[("YS", c, half)])

            S.barrier()
            if l == 0:
                stop_at("phaseA")
            if l == 0:
                dbg_dump("qk", AR[:, 0:8192], [])
                dbg_dump("ysA", AR[:, 19968:19968 + 12288], [])

            S.phase = f"L{l}:B_attn"
            TT_ = [scr_f32(0 + i * 512, 512) for i in range(3)]
            PP = [scr_bf16(1536 + i * 256, 256) for i in range(4)]
            RD = [scr_f32(3072 + i * 512, 512) for i in range(2)]
            work = []
            unit = 0
            for hp in range(4):
                for qc in range(2):
                    tiles = [("c", 8), ("c", 9)] + [("l", b_) for b_ in LOCAL_TILES[qc]]
                    ntot = 2 * len(tiles)
                    it = 0
                    for j in range(2):
                        for kind, b_ in tiles:
                            work.append(dict(hp=hp, qc=qc, j=j, kind=kind, b=b_, unit=unit, it=it, ntot=ntot,
                                             first_of_head=(b_ == 8 and kind == "c")))
                            it += 1
                    unit += 1
            LA = 3

            def ut_load(h):
                S.dma("pool", UT[:, h % 2, :], utab_d[l][:, h * NE * 64:(h + 1) * NE * 64], writes=[("UT", h % 2)])

            def emit_S(k):
                w = work[k]
                hp, qc, j, b_ = w["hp"], w["qc"], w["j"], w["b"]
                si = k % 4
                Sps, sk = PS[si], ("ps", si)
                h_ = 2 * hp + j
                qsrc = QK[:, hp, HS(qc)]
                if w["kind"] == "l":
                    ksrc = KZ[:, h_, b_ * 128:(b_ + 1) * 128]
                    S.op("pe", lambda e: e.matmul(Sps[:, :], ksrc, qsrc, start=True, stop=False),
                         reads=[("KZ", h_, b_ // 4), ("KZall",), ("QK", hp, qc)], writes=[sk], signal=False)
                    S.op("pe", lambda e: e.matmul(Sps[:, :], OHv[:, b_, :], RNEG[:, HS(qc)], start=False, stop=True),
                         reads=[("OH",), ("RNEG",)], writes=[sk])
                else:
                    ksrc = CKv[:, h_, (b_ - 8) * 128:(b_ - 7) * 128]
                    S.op("pe", lambda e: e.matmul(Sps[:, :], ksrc, qsrc, start=True, stop=True),
                         reads=[("CK",), ("QK", hp, qc)], writes=[sk])

            def emit_rest(k):
                w = work[k]
                hp, qc, j, b_, it, ntot = w["hp"], w["qc"], w["j"], w["b"], w["it"], w["ntot"]
                u = w["unit"]
                h = 2 * hp + j
                NUM, nk = PS[4 + (u % 2) * 2], ("ps", 4 + (u % 2) * 2)
                DEN, dk = PS[5 + (u % 2) * 2], ("ps", 5 + (u % 2) * 2)
                si = k % 4
                Sps, sk = PS[si], ("ps", si)
                pi = k % 4
                pt = PP[pi]
                if w["first_of_head"] and qc == 1 and j == 1 and hp < 3:
                    ut_load(2 * (hp + 1))
                if w["first_of_head"] and qc == 0 and j == 0 and hp > 0:
                    ut_load(2 * hp + 1)
                if w["kind"] == "l":
                    e0 = 8 * qc - 2 * b_ + E0
                    ti = k % 3
                    tt = TT_[ti]
                    S.op("dve", lambda e: e.tensor_tensor(out=tt, in0=Sps[:, :], in1=UT[:, h % 2, e0 * 64:(e0 + 8) * 64], op=ALU.add),
                         reads=[sk, ("UT", h % 2)], writes=[("TT", ti)])
                    S.op("act", lambda e: e.activation(out=pt, in_=tt, func=AF.Exp),
                         reads=[("TT", ti)], writes=[("PP", pi)])
                    vkeys = [("VP", b_, 0), ("VP", b_, 1), ("VPall",)]
                    osrc, okeys = ONESL[:, j * 64:j * 64 + 128], [("ONESL",)]
                else:
                    S.op("act", lambda e: e.activation(out=pt, in_=Sps[:, :], func=AF.Exp),
                         reads=[sk], writes=[("PP", pi)])
                    vkeys = [("VPc", 0), ("VPc", 1), ("VPall",)]
                    osrc, okeys = ONESC[:, j * 64:j * 64 + 128], [("ONESC",)]
                vsrc = VP[:, b_, hp, j * 64:j * 64 + 128]
                S.op("pe", lambda e: e.matmul(NUM[:, :], vsrc, pt, start=(it == 0), stop=(it == ntot - 1)),
                     reads=[("PP", pi)] + vkeys, writes=[nk], signal=False)
                S.op("pe", lambda e: e.matmul(DEN[:, :], osrc, pt, start=(it == 0), stop=(it == ntot - 1)),
                     reads=[("PP", pi)] + okeys, writes=[dk])
                if it == ntot - 1:
                    rd = RD[u % 2]
                    S.op("dve", lambda e: e.reciprocal(out=rd, in_=DEN[:, :]), reads=[dk], writes=[("RD", u % 2)])
                    S.op("dve", lambda e: e.tensor_tensor(out=YS[:, 4 + hp, HS(qc)], in0=NUM[:, :], in1=rd, op=ALU.mult),
                         reads=[nk, dk, ("RD", u % 2)] + [("PZ", t) for t in range(8)], writes=[("YS", 4 + hp, qc)])

            ut_load(0)
            ut_load(1)
            for k in range(min(LA, len(work))):
                emit_S(k)
            for k in range(len(work)):
                if k + LA < len(work):
                    emit_S(k + LA)
                emit_rest(k)
            S.barrier()
            ps_ctr[0] = 0
            if l == 0:
                stop_at("attn")
            if l == 0:
                dbg_dump("ysB", AR[:, 19968:19968 + 12288], [])

            S.phase = f"L{l}:C_gates"
            GT = [scr_f32(i * 512, 512) for i in range(6)]
            MM = [scr_f32(3072 + i * 512, 512) for i in range(4)]
            gctr = [0]
            mctr = [0]
            for c in range(8):
                slot = ring_use()
                gw = WR[:, slot, 0:3072].rearrange("p (i k n) -> p i k n", i=3, k=KC)
                bw = WR[:, slot, 3072:3072 + 1536].rearrange("p (i k n) -> p i k n", i=3, k=4)
                for half in range(2):
                    prods = []
                    for i in range(3):
                        psg, pkg = ps_next()
                        mm_group(psg[:, :], pkg, [(gw[:, i, kc, :], Hh[:, kc, HS(half)]) for kc in range(KC)],
                                 gate_keys(slot) + Hkeys(half))
                        gi = gctr[0] % 6
                        gctr[0] += 1
                        gt = GT[gi]
                        S.op("act", lambda e, psg=psg, gt=gt: e.activation(out=gt, in_=psg[:, :], func=AF.Sigmoid),
                             reads=[pkg], writes=[("GT", gi)])
                        psp, pkp = ps_next()
                        mm_group(psp[:, :], pkp, [(bw[:, i, kc, :], YS[:, 4 * i + kc, HS(half)]) for kc in range(4)],
                                 gate_keys(slot) + [("YS", 4 * i + kc, half) for kc in range(4)])
                        mi = mctr[0] % 4
                        mctr[0] += 1
                        mt = MM[mi]
                        S.op("dve", lambda e, psp=psp, gt=gt, mt=mt: e.tensor_tensor(out=mt, in0=psp[:, :], in1=gt, op=ALU.mult),
                             reads=[pkp, ("GT", gi)], writes=[("MM", mi)])
                        prods.append((mt, mi))
                    (m0, k0), (m1, k1), (m2, k2) = prods
                    S.op("pool", lambda e, m0=m0, m1=m1: e.tensor_tensor(out=m0, in0=m0, in1=m1, op=ALU.add),
                         reads=[("MM", k0), ("MM", k1)], writes=[("MM", k0)])
                    S.op("pool", lambda e, m0=m0, m2=m2, c=c, half=half: e.tensor_tensor(out=MERGED[:, c, HS(half)], in0=m0, in1=m2, op=ALU.add),
                         reads=[("MM", k0), ("MM", k2)], writes=[("MG", c, half)])
                mod_part(l, [4 + c])
            if l == 0:
                stop_at("phaseC")
            if l == 0:
                dbg_dump("merged", AR[:, 0:8192], [("MG", c, hf) for c in range(8) for hf in range(2)])

            S.phase = f"L{l}:D_mod_wout"
            gm_compute(l, 1)
            for g in range(2):
                slot = ring_use()
                wv = WR[:, slot, 0:4096].rearrange("p (k n) -> p k n", k=KC)
                for cc in range(4):
                    c = g * 4 + cc
                    for half in range(2):
                        ps, pk = ps_next()
                        mm_group(ps[:, :], pk, [(wv[:, kc, cc * 128:(cc + 1) * 128], MERGED[:, kc, HS(half)]) for kc in range(KC)],
                                 wkeys(slot) + [("MG", kc, half) for kc in range(KC)])
                        S.op("dve", lambda e, ps=ps, c=c, half=half: e.scalar_tensor_tensor(
                            out=X[:, c, HS(half)], in0=ps[:, :], scalar=MOD[:, l, 16 + c:17 + c], in1=X[:, c, HS(half)],
                            op0=ALU.mult, op1=ALU.add),
                            reads=[pk, ("MOD", l, 4 + c // 4), ("X", c, half)], writes=[("X", c, half)])
            if l == 0:
                stop_at("x1")
            if l == 0:
                dbg_dump("x1", X[:, :, :], [("X", dc, hf) for dc in range(KC) for hf in range(2)])

            S.phase = f"L{l}:E_norm2"
            S.barrier()
            norm(lambda dc: GMv[:, l, 1, dc:dc + 1], lambda dc: MOD[:, l, 24 + dc:25 + dc],
                 lambda: [("GM", l, 1), ("MOD", l, 6), ("MOD", l, 7)],
                 lambda dc, half: Hh[:, dc, HS(half)], lambda dc, half: [("H", dc, half)])
            S.barrier()

            S.phase = f"L{l}:F_up"
            U32 = scr_f32(0, 1026)
            YF = scr_f32(1026, 1024)
            GF = scr_f32(2050, 1024)
            S.op("dve", lambda e: e.memset(U32[:, 0:1], 0.0), writes=[("U32", 0), ("U32", 1)])
            S.op("dve", lambda e: e.memset(U32[:, 1025:1026], 0.0), writes=[("U32p",)])
            for jp in range(11):
                slot = ring_use()
                wv = WR[:, slot, 0:4096].rearrange("p (k two n) -> p k two n", k=KC, two=2)
                for jj in range(2):
                    jx = jp * 2 + jj
                    fwb = V_FCONV + (l * NJ + jx) * 3
                    S.op("dve", lambda e, fwb=fwb: e.tensor_scalar(out=SMALL[:, 1:2], in0=vcol(fwb, 0), scalar1=SMALL[:, 0:1],
                                                                  scalar2=None, op0=ALU.mult),
                         reads=[("VEC",), ("NBF",)], writes=[("FX", 0)])
                    S.op("dve", lambda e, fwb=fwb: e.tensor_scalar(out=SMALL[:, 2:3], in0=vcol(fwb, 2), scalar1=SMALL[:, 0:1],
                                                                  scalar2=None, op0=ALU.mult),
                         reads=[("VEC",), ("NBF",)], writes=[("FX", 1)])
                    for half in range(2):
                        ps, pk = ps_next()
                        mm_group(ps[:, :], pk, [(wv[:, kc, 0, jj * 128:(jj + 1) * 128], Hh[:, kc, HS(half)]) for kc in range(KC)],
                                 wkeys(slot) + Hkeys(half))
                        S.op("act", lambda e, ps=ps, half=half: e.activation(out=U32[:, 1 + half * 512:1 + half * 512 + 512], in_=ps[:, :], func=AF.Copy),
                             reads=[pk], writes=[("U32", half)])
                    uk = [("U32", 0), ("U32", 1), ("U32p",)]
                    S.op("act", lambda e, fwb=fwb: e.activation(out=YF[:, :], in_=U32[:, 1:1025], func=AF.Identity, scale=vcol(fwb, 1)),
                         reads=uk + [("VEC",)], writes=[("YF",)])
                    S.op("dve", lambda e, fwb=fwb: e.scalar_tensor_tensor(
                        out=YF[:, :], in0=U32[:, 0:1024], scalar=vcol(fwb, 0), in1=YF[:, :], op0=ALU.mult, op1=ALU.add),
                        reads=uk + [("VEC",), ("YF",)], writes=[("YF",)])
                    S.op("dve", lambda e, fwb=fwb: e.scalar_tensor_tensor(
                        out=YF[:, :], in0=U32[:, 2:1026], scalar=vcol(fwb, 2), in1=YF[:, :], op0=ALU.mult, op1=ALU.add),
                        reads=uk + [("VEC",), ("YF",)], writes=[("YF",)])
                    yfv = YF[:, :].rearrange("p (s t) -> p s t", s=4)
                    uv = U32[:, 1:1025].rearrange("p (s t) -> p s t", s=4)
                    S.op("dve", lambda e, yfv=yfv, uv=uv: e.scalar_tensor_tensor(
                        out=yfv[:, 1:4, 0:1], in0=uv[:, 0:3, 255:256], scalar=SMALL[:, 1:2], in1=yfv[:, 1:4, 0:1],
                        op0=ALU.mult, op1=ALU.add), reads=uk + [("FX", 0), ("YF",)], writes=[("YF",)])
                    S.op("dve", lambda e, yfv=yfv, uv=uv: e.scalar_tensor_tensor(
                        out=yfv[:, 0:3, 255:256], in0=uv[:, 1:4, 0:1], scalar=SMALL[:, 2:3], in1=yfv[:, 0:3, 255:256],
                        op0=ALU.mult, op1=ALU.add), reads=uk + [("FX", 1), ("YF",)], writes=[("YF",)])
                    S.op("act", lambda e: e.activation(out=GF[:, :], in_=YF[:, :], func=AF.Gelu_apprx_tanh),
                         reads=[("YF",)], writes=[("GF",)])
                    for half in range(2):
                        ps, pk = ps_next()
                        mm_group(ps[:, :], pk, [(wv[:, kc, 1, jj * 128:(jj + 1) * 128], Hh[:, kc, HS(half)]) for kc in range(KC)],
                                 wkeys(slot) + Hkeys(half))
                        S.op("dve", lambda e, ps=ps, half=half, jx=jx: e.tensor_tensor(
                            out=A[:, jx, HS(half)], in0=ps[:, :], in1=GF[:, HS(half)], op=ALU.mult),
                            reads=[pk, ("GF",)], writes=[("A", jx, half)])
            if l == 0:
                stop_at("ffnup")
            if l == 0:
                dbg_dump("a", AR[:, 0:NJ * T], [("A", j, hf) for j in range(NJ) for hf in range(2)])

            S.phase = f"L{l}:G_down"
            for c in range(8):
                slot = ring_use()
                wv = WR[:, slot, 0:NJ * 128].rearrange("p (k n) -> p k n", k=NJ)
                for half in range(2):
                    ps, pk = ps_next()
                    mm_group(ps[:, :], pk, [(wv[:, j, :], A[:, j, HS(half)]) for j in range(NJ)],
                             wkeys(slot) + [("A", j, half) for j in range(NJ)])
                    S.op("dve", lambda e, ps=ps, c=c, half=half: e.scalar_tensor_tensor(
                        out=X[:, c, HS(half)], in0=ps[:, :], scalar=MOD[:, l, 40 + c:41 + c], in1=X[:, c, HS(half)],
                        op0=ALU.mult, op1=ALU.add),
                        reads=[pk, ("MOD", l, 10 + c // 4), ("X", c, half)], writes=[("X", c, half)])
                if c < 4 and l + 1 < NL:
                    mod_part(l + 1, [c])
            S.barrier()
            if l + 1 < NL:
                S.op("dve", lambda e: e.memset(AR[:, 12288:12288 + 7680], 0.0), writes=[("VPall",)])
            if l == 0:
                stop_at("layer0")
            if l == 0:
                dbg_dump("x2", X[:, :, :], [("X", dc, hf) for dc in range(KC) for hf in range(2)])

        try:
            run_layers()
        except _Stop:
            pass
        S.barrier()
        S.phase = 'final'
        OUTB = [scr_f32(3584, 512), scr_f32(4096, 512), scr_f32(4608, 512), scr_f32(5120, 512)]
        octr = [0]

        def out_fn(dc, half):
            i = octr[0] % 4
            return OUTB[i]

        def out_keys(dc, half):
            return [("OUTB", octr[0] % 4)]

        SQ = [scr_bf16(0, 256), scr_bf16(256, 256)]
        SS = [scr_f32(512, 512), scr_f32(1024, 512)]
        RS = [scr_f32(1536, 512), scr_f32(2048, 512)]
        TMP = [scr_f32(2560, 512), scr_f32(3072, 512)]
        for half in range(2):
            ps, pk = ps_next()
            for dc in range(KC):
                sq = SQ[dc % 2]
                S.op("act", lambda e, dc=dc, sq=sq: e.activation(out=sq, in_=X[:, dc, HS(half)], func=AF.Square),
                     reads=[("X", dc, half)], writes=[("SQ", dc % 2)])
                S.op("pe", lambda e, dc=dc, sq=sq: e.matmul(ps[:, :], ONES[:, :], sq, start=(dc == 0), stop=(dc == KC - 1)),
                     reads=[("SQ", dc % 2), ("ONES",)], writes=[pk])
            S.op("act", lambda e: e.activation(out=SS[half], in_=ps[:, :], func=AF.Sqrt, scale=1.0 / D, bias=1e-6),
                 reads=[pk], writes=[("SS", half)])
            S.op("dve", lambda e: e.reciprocal(out=RS[half], in_=SS[half]), reads=[("SS", half)], writes=[("RS", half)])
            for dc in range(KC):
                tmp = TMP[dc % 2]
                S.op("dve", lambda e, dc=dc, tmp=tmp: e.tensor_tensor(out=tmp, in0=X[:, dc, HS(half)], in1=RS[half], op=ALU.mult),
                     reads=[("X", dc, half), ("RS", half)], writes=[("TMPn", dc % 2)])
                oi = octr[0] % 4
                octr[0] += 1
                ob = OUTB[oi]
                S.op("act", lambda e, dc=dc, tmp=tmp, ob=ob: e.activation(out=ob, in_=tmp, func=AF.Identity, scale=vcol(V_GF, dc)),
                     reads=[("TMPn", dc % 2), ("VEC",)], writes=[("OUTB", oi)])
                S.dma("sp", yT_d[dc * 128:(dc + 1) * 128, HS(half)], ob, reads=[("OUTB", oi)], is_output=True)
        S.finish()
        build_nc.stats = dict(S.nops)
        build_nc.phase_of = dict(S.phase_of)
    return nc


def _bf16_split(a):
    hi = a.astype(ml_dtypes.bfloat16).astype(np.float32)
    lo = (a - hi).astype(ml_dtypes.bfloat16).astype(np.float32)
    return hi, lo


def _band_tables(seq_len):
    out = np.zeros((128, 4, 8, 2, 128), np.float32)
    t = np.arange(T)
    for g, w in enumerate((2, 4, 8, 16)):
        B = np.zeros((T, T), np.float64)
        s0 = (t // seq_len) * seq_len
        lo = np.clip(t - w // 2, s0, s0 + seq_len)
        hi = np.clip(t - w // 2 + w, s0, s0 + seq_len)
        for tt in range(T):
            B[lo[tt]:hi[tt], tt] = 1.0 / (hi[tt] - lo[tt])
            B[tt, tt] -= 1.0
        B = B.astype(np.float32)

        def blk(ib, ob):
            return B[ib * 128:(ib + 1) * 128, ob * 128:(ob + 1) * 128]
        variants = [blk(0, 0), blk(7, 7), blk(2, 2), blk(1, 1), blk(0, 1), blk(1, 2), blk(1, 0), blk(2, 1)]
        for ob in range(8):
            dv = 0 if ob == 0 else (1 if ob == 7 else (2 if ob % 2 == 0 else 3))
            assert np.array_equal(blk(ob, ob), variants[dv])
            if ob >= 1:
                assert np.array_equal(blk(ob - 1, ob), variants[4 if ob % 2 == 1 else 5])
            if ob <= 6:
                assert np.array_equal(blk(ob + 1, ob), variants[6 if ob % 2 == 0 else 7])
        for v, m in enumerate(variants):
            h_, l_ = _bf16_split(m)
            out[:, g, v, 0, :] = h_
            out[:, g, v, 1, :] = l_
    return out.reshape(128, -1)


def _rneg_table(sample):
    r = np.full((16, 16), NEG, np.float32)
    for rk in range(16):
        for q in range(16):
            if sample:
                s = min(max(q - 4, 0), 8)
                ok = s <= rk < s + 8
            else:
                ok = (rk // 4) == (q // 4)
            if ok:
                r[rk, q] = 0.0
    out = np.zeros((128, T), np.float32)
    out[0:16] = np.repeat(r, 64, axis=1)
    return out


def _onehot_rows():
    oh = np.zeros((16, 8, 128), np.float32)
    for b in range(8):
        oh[2 * b, b, 0:64] = 1.0
        oh[2 * b + 1, b, 64:128] = 1.0
    out = np.zeros((128, 1024), np.float32)
    out[0:16] = oh.reshape(16, 1024)
    return out


def _u_table(rpb_l):
    U = np.full((128, NH, NE, 64), NEG, np.float32)
    cq = np.arange(64)
    cs = np.clip(cq - 8, 0, 48)
    for jj in range(2):
        for ei in range(NE):
            dr = jj - (ei - E0)
            if abs(dr) > 7:
                continue
            for ck in range(64):
                valid = (ck >= cs) & (ck < cs + 16)
                dc = np.clip(ck - cq + 15, 0, 30)
                vals = rpb_l[:, dr + 7, :][:, dc]
                U[jj * 64 + ck, :, ei, :] = np.where(valid[None, :], vals, NEG)
    return U.reshape(128, -1)


def _prep(inputs):
    f = lambda a: np.ascontiguousarray(np.asarray(a, dtype=np.float32))
    x_prompt, x_sample = f(inputs["x_prompt"]), f(inputs["x_sample"])
    cache_kv, c, c_ctx = f(inputs["cache_kv"]), f(inputs["c"]), f(inputs["c_ctx"])
    rpb = f(inputs["rpb"])
    shared = {
        "w_mod": f(inputs["w_mod"]), "w_in": f(inputs["w_in"]), "w_branch": f(inputs["w_branch"]),
        "w_out": f(inputs["w_out"]), "w_up": f(inputs["ffn_w_up"]), "w_down": f(inputs["ffn_w_down"]),
        "pool_w": f(inputs["pool_w"]),
    }

    def pk(v):
        return np.ascontiguousarray(v.reshape(-1, 128).T)

    vec_common = np.zeros((128, NV), np.float32)
    b_mod, g1, g2, gf = f(inputs["b_mod"]), f(inputs["g_norm1"]), f(inputs["g_norm2"]), f(inputs["g_final"])
    conv_w, fconv, pscale = f(inputs["conv_w"]), f(inputs["ffn_conv"]), f(inputs["pool_scale"])
    for l in range(NL):
        vec_common[:, V_BMOD + l * 48: V_BMOD + (l + 1) * 48] = pk(b_mod[l])
        vec_common[:, V_G1 + l * 8: V_G1 + (l + 1) * 8] = pk(g1[l])
        vec_common[:, V_G2 + l * 8: V_G2 + (l + 1) * 8] = pk(g2[l])
        for cc in range(4):
            for k in range(3):
                vec_common[:, V_CONVW + (l * 4 + cc) * 3 + k] = conv_w[l, k, cc * 128:(cc + 1) * 128]
        for j in range(NJ):
            for k in range(3):
                vec_common[:, V_FCONV + (l * NJ + j) * 3 + k] = fconv[l, k, j * 128:(j + 1) * 128]
        vec_common[:, V_PSCALE + l * 4: V_PSCALE + (l + 1) * 4] = pk(pscale[l])
    vec_common[:, V_GF:V_GF + 8] = pk(gf)

    ones_pat = np.concatenate([np.ones((128, 64)), np.zeros((128, 64)), np.ones((128, 64))], axis=1).astype(np.float32)
    band_p, band_s = _band_tables(256), _band_tables(1024)
    rneg_p, rneg_s = _rneg_table(False), _rneg_table(True)
    oh = _onehot_rows()
    utab_s = np.stack([_u_table(rpb[l]) for l in range(NL)])
    utab_p = np.zeros_like(utab_s)
    in_maps = []
    for i in range(8):
        sample = i >= 4
        vec = vec_common.copy()
        if sample:
            b = i - 4
            xT = np.ascontiguousarray(x_sample[b].T)
            vec[:, V_CV:V_CV + 8] = pk(c[b])
            vec[:, V_BFLAG] = 0.0
            ck = cache_kv[b, :, 0]
            ctxk = np.zeros((NL, 2, 64, NH, 256), np.float32)
            for h_ in range(NH):
                ctxk[:, h_ % 2, :, h_, :] = ck[:, h_].transpose(0, 2, 1)
            ctxk = ctxk.reshape(NL, 128, NH * 256)
            cvv = cache_kv[b, :, 1]
            ctxv = cvv.transpose(0, 2, 1, 3).reshape(NL, 2, 128, 512).transpose(0, 2, 1, 3).reshape(NL, 128, 1024)
            m = {"ctxkT": np.ascontiguousarray(ctxk), "ctxv": np.ascontiguousarray(ctxv), "utab": utab_s,
                 "rneg": rneg_s, "oh": oh, "onesc": ones_pat, "band": band_s}
        else:
            xT = np.ascontiguousarray(x_prompt[4 * i:4 * i + 4].reshape(T, D).T)
            vec[:, V_CV:V_CV + 8] = pk(c_ctx)
            vec[:, V_BFLAG] = 1.0
            m = {"ctxkT": np.zeros((NL, 128, 2048), np.float32), "ctxv": np.zeros((NL, 128, 1024), np.float32),
                 "utab": utab_p, "rneg": rneg_p, "oh": oh, "onesc": np.zeros_like(ones_pat), "band": band_p}
        m.update(shared)
        m["xT"] = xT
        m["vecs"] = vec
        in_maps.append(m)
    return in_maps


def _assemble(results):
    y_prompt = np.empty((16, 256, D), np.float32)
    y_sample = np.empty((4, T, D), np.float32)
    kv_state = np.empty((16, NL, 2, NH, 256, 64), np.float32)
    for i in range(8):
        r = results[i]
        y = np.asarray(r["yT"], dtype=np.float32).T
        if i < 4:
            y_prompt[4 * i:4 * i + 4] = y.reshape(4, 256, D)
            kvo = np.asarray(r["kvo"], dtype=np.float32).reshape(NL, 4, 256, NH, 64)
            kv_state[4 * i:4 * i + 4, :, 1] = kvo.transpose(1, 0, 3, 2, 4)
            kT = np.asarray(r["kvoT"], dtype=np.float32).reshape(NL, NH, 64, 4, 256)
            kv_state[4 * i:4 * i + 4, :, 0] = kT.transpose(3, 0, 1, 4, 2)
        else:
            y_sample[i - 4] = y
    return y_prompt, y_sample, kv_state


_NC_CACHE = {}


def kernel(**inputs):
    in_maps = _prep(inputs)
    if "nc" not in _NC_CACHE:
        _NC_CACHE["nc"] = build_nc()
    res = run_bass_kernel_spmd(_NC_CACHE["nc"], in_maps, core_ids=list(range(8)))
    return _assemble(res.results)
```

```python
import numpy as np
import ml_dtypes
from contextlib import ExitStack
import concourse.bass as bass
import concourse.mybir as mybir
from concourse.bass_utils import run_bass_kernel_spmd

F32 = mybir.dt.float32
BF16 = mybir.dt.bfloat16
AF = mybir.ActivationFunctionType
ALU = mybir.AluOpType

D = 1024
T = 1024
KC = 8
DFF = 2816
NJ = 22
INW = 6656
NL = 2
NH = 8
NE = 22
E0 = 10
NEG = -30000.0
SLOT = 4608
NSLOT = 5
ND = 16
SEM_LIMIT = 12000

V_BMOD = 0
V_G1 = 96
V_G2 = 112
V_GF = 128
V_CONVW = 136
V_FCONV = 160
V_PSCALE = 292
V_CV = 300
V_BFLAG = 308
NV = 320

LOCAL_TILES = {0: [0, 1, 2, 3, 4, 5], 1: [2, 3, 4, 5, 6, 7]}


class Sched:
    def __init__(self, nc, es):
        self.nc = nc
        self.es = es
        self.eh = {"pe": nc.tensor, "act": nc.scalar, "dve": nc.vector, "pool": nc.gpsimd, "sp": nc.sync}
        self.epoch = {e: 0 for e in self.eh}
        self.cnt = {e: 0 for e in self.eh}
        self.sems = {}
        for e in self.eh:
            self.sems[(e, 0)] = es.enter_context(nc.semaphore(f"s_{e}_0"))
        self.dsem = [es.enter_context(nc.semaphore(f"s_dma_{i}")) for i in range(ND)]
        self.dcnt = [0] * ND
        self.dpool = {"sp": list(range(0, 6)), "pool": list(range(6, ND))}
        self.dnext = {"sp": 0, "pool": 0}
        self.seen = {e: {} for e in self.eh}
        self.last_w = {}
        self.readers = {}
        self.out_dmas = []
        self.aux = {}
        self.nops = {e: 0 for e in self.eh}
        self.pending = {e: False for e in self.eh}
        self.phase = "init"
        self.phase_of = {}

    def _sem_of(self, src):
        if src[0] == "d":
            return self.dsem[src[1]]
        return self.sems[src]

    def _wait(self, eng, src, c):
        if self.seen[eng].get(src, 0) >= c:
            return
        self.eh[eng].wait_ge(self._sem_of(src), c)
        self.seen[eng][src] = c

    def _deps(self, eng, reads, writes):
        deps = {}

        def add(src, c):
            if deps.get(src, 0) < c:
                deps[src] = c

        for k in reads:
            w = self.last_w.get(k)
            if w:
                add(*w)
        for k in writes:
            w = self.last_w.get(k)
            if w:
                add(*w)
            for src, c in self.readers.get(k, {}).items():
                add(src, c)
        for src, c in deps.items():
            if eng == "pe" and src[0] == "pe":
                continue
            self._wait(eng, src, c)

    def _record(self, src, c, reads, writes):
        for k in writes:
            self.last_w[k] = (src, c)
            self.readers[k] = {}
        for k in reads:
            r = self.readers.setdefault(k, {})
            if r.get(src, 0) < c:
                r[src] = c

    def op(self, eng, fn, reads=(), writes=(), signal=True):
        if (not self.pending[eng]) and self.cnt[eng] >= SEM_LIMIT:
            self.epoch[eng] += 1
            self.cnt[eng] = 0
            self.sems[(eng, self.epoch[eng])] = self.es.enter_context(
                self.nc.semaphore(f"s_{eng}_{self.epoch[eng]}"))
        self._deps(eng, reads, writes)
        ins = fn(self.eh[eng])
        src = (eng, self.epoch[eng])
        if signal:
            ins.then_inc(self.sems[src], 1)
            self.cnt[eng] += 1
            c = self.cnt[eng]
            self.pending[eng] = False
        else:
            c = self.cnt[eng] + 1
            self.pending[eng] = True
        self._record(src, c, reads, writes)
        self.nops[eng] += 1
        try:
            self.phase_of[ins.ins.name] = self.phase
        except Exception:
            pass
        return ins

    def dma(self, q, out, in_, reads=(), writes=(), is_output=False, ring=False):
        self._deps(q, reads, writes)
        lst = self.dpool[q]
        s = lst[self.dnext[q] % len(lst)]
        self.dnext[q] += 1
        src = ("d", s)
        if self.dcnt[s] > 0:
            self._wait(q, src, self.dcnt[s])
        self.eh[q].dma_start(out=out, in_=in_).then_inc(self.dsem[s], 16)
        self.dcnt[s] += 16
        self._record(src, self.dcnt[s], reads, writes)
        if is_output:
            self.out_dmas.append((src, self.dcnt[s]))
        if not ring:
            self.aux[src] = self.dcnt[s]

    def barrier(self):
        cur = [((e, self.epoch[e]), self.cnt[e]) for e in self.eh if self.cnt[e] > 0]
        cur += list(self.aux.items())
        self.aux = {}
        for e in self.eh:
            for src, c in cur:
                if src[0] == e and e == "pe":
                    continue
                self._wait(e, src, c)

    def finish(self):
        for q in ("sp", "pool"):
            for s_, c in enumerate(self.dcnt):
                if c > 0:
                    self._wait(q, ("d", s_), c)
        self.barrier()


class _Stop(Exception):
    pass


def build_nc(debug=None, stop=None):
    nc = bass.Bass("TRN2", target_bir_lowering=False)

    def din(name, shape):
        return nc.dram_tensor(name, list(shape), F32, kind="ExternalInput").ap()

    def dout(name, shape):
        return nc.dram_tensor(name, list(shape), F32, kind="ExternalOutput").ap()

    xT_d = din("xT", [D, T])
    vecs_d = din("vecs", [128, NV])
    w_mod_d = din("w_mod", [NL, D, 6 * D])
    w_in_d = din("w_in", [NL, D, INW])
    w_br_d = din("w_branch", [NL, 3, 512, D])
    w_out_d = din("w_out", [NL, D, D])
    w_up_d = din("w_up", [NL, D, 2 * DFF])
    w_dn_d = din("w_down", [NL, DFF, D])
    pool_w_d = din("pool_w", [NL, 4, 128, 128])
    ctxk_d = din("ctxkT", [NL, 128, 8 * 256])
    ctxv_d = din("ctxv", [NL, 128, 2 * 512])
    utab_d = din("utab", [NL, 128, NH * NE * 64])
    rneg_d = din("rneg", [128, T])
    oh_d = din("oh", [128, 8 * 128])
    onesc_d = din("onesc", [128, 192])
    band_d = din("band", [128, 4 * 8 * 2 * 128])
    yT_d = dout("yT", [D, T])
    kvo_d = dout("kvo", [NL, T, 512])
    kvoT_d = dout("kvoT", [NL, 512, T])
    dbg_d = {}
    if debug:
        for name, (shape, dt_) in debug.items():
            dbg_d[name] = nc.dram_tensor("dbg_" + name, list(shape), dt_, kind="ExternalOutput").ap()

    es = ExitStack()
    with es:
        def sb(name, shape, dt):
            return es.enter_context(nc.sbuf_tensor(name, list(shape), dt))

        X = sb("X", [128, KC, T], F32)
        Hh = sb("Hh", [128, KC, T], BF16)
        AR = sb("AR", [128, 32256], BF16)
        UT = sb("UT", [128, 2, NE * 64], BF16)
        WR = sb("WR", [128, NSLOT, SLOT], BF16)
        SCR = sb("SCR", [128, 5632], F32)
        VEC = sb("VEC", [128, NV], F32)
        MOD = sb("MOD", [128, NL, 48], F32)
        GM = sb("GM", [128, NL * 2 * 8], F32)
        SMALL = sb("SMALL", [128, 256], F32)
        SCb = sb("SCb", [128, 8], BF16)
        RNEG = sb("RNEG", [128, T], BF16)
        OH = sb("OH", [128, 8 * 128], BF16)
        ONES = sb("ONES", [128, 128], BF16)
        ONESL = sb("ONESL", [128, 192], BF16)
        ONESC = sb("ONESC", [128, 192], BF16)
        CK = sb("CK", [128, 8 * 256], BF16)
        PS = [es.enter_context(nc.psum_tensor(f"ps{i}", [128, 512], F32)) for i in range(8)]

        S = Sched(nc, es)

        QK = AR[:, 0:4096].rearrange("p (c t) -> p c t", c=4)
        KZ = AR[:, 4096:12288].rearrange("p (h t) -> p h t", h=8)
        MERGED = AR[:, 0:8192].rearrange("p (c t) -> p c t", c=8)
        VP = AR[:, 12288:12288 + 7680].rearrange("p (k h c) -> p k h c", k=10, h=4)
        YS = AR[:, 19968:19968 + 12288].rearrange("p (c t) -> p c t", c=12)
        PZ = AR[:, 19968 + 4096:19968 + 8192].rearrange("p (b c) -> p b c", b=8)
        A = AR[:, 0:NJ * T].rearrange("p (j t) -> p j t", j=NJ)
        BAND = AR[:, 4096:12288]
        BANDv = BAND.rearrange("p (g v s c) -> p g v s c", g=4, v=8, s=2)
        CKv = CK[:, :].rearrange("p (h k) -> p h k", h=8)
        OHv = OH[:, :].rearrange("p (b k) -> p b k", b=8)
        GMv = GM[:, :].rearrange("p (l j c) -> p l j c", l=NL, j=2)

        def vcol(base, idx=0, n=1):
            return VEC[:, base + idx: base + idx + n]

        ps_ctr = [0]

        ps_excl = set()

        def ps_next():
            while True:
                i = ps_ctr[0] % 8
                ps_ctr[0] += 1
                if i not in ps_excl:
                    return PS[i], ("ps", i)

        STATB = (6, 7)
        SQA = None

        stat_pending = []
        STAT_READY = [False]

        def stat_begin():
            ps_excl.update(STATB)

        def stat_accum(c, half):
            i4 = (c * 2 + half) % 4
            sq = scr_bf16(5120 + 256 * (i4 % 2), 256)
            kq = ("SQA", i4 % 2)
            while len(stat_pending) >= 2:
                stat_pending.pop(0)()
            S.op("act", lambda e: e.activation(out=sq, in_=X[:, c, HS(half)], func=AF.Square),
                 reads=[("X", c, half)], writes=[kq])

            def pe_part():
                S.op("pe", lambda e: e.matmul(PS[STATB[half]][:, :], ONES[:, :], sq, start=(c == 0), stop=(c == KC - 1)),
                     reads=[kq, ("ONES",)], writes=[("ps", STATB[half])])
            stat_pending.append(pe_part)

        def stat_end():
            while stat_pending:
                stat_pending.pop(0)()
            ps_excl.difference_update(STATB)
            STAT_READY[0] = True

        def HS(half):
            return slice(half * 512, half * 512 + 512)

        def scr_f32(off, n):
            return SCR[:, off:off + n]

        def scr_bf16(off, n):
            return SCR[:, off:off + n].bitcast(BF16)

        loads = []

        class Ring:
            issued = 0
            consumed = 0

        def ring_use_many(n):
            idx = Ring.consumed
            target = min(len(loads), idx + NSLOT)
            while Ring.issued < target:
                i = Ring.issued
                loads[i](i % NSLOT)
                Ring.issued += 1
            Ring.consumed += n
            return [(idx + k) % NSLOT for k in range(n)]

        def ring_use():
            return ring_use_many(1)[0]

        def ring_prefetch():
            target = min(len(loads), Ring.consumed + NSLOT)
            while Ring.issued < target:
                i = Ring.issued
                loads[i](i % NSLOT)
                Ring.issued += 1

        def wkeys(slot):
            return [("WR", slot, 0), ("WR", slot, 1)]

        def ld_cols(src2d, col0, ncols, nkc=KC):
            def f(slot):
                dst = WR[:, slot, 0:nkc * ncols].rearrange("p (k n) -> p k n", k=nkc)
                src = src2d.rearrange("(k p) n -> p k n", p=128)[:, :, col0:col0 + ncols]
                S.dma("pool", dst, src, writes=wkeys(slot), ring=True)
            return f

        def ld_gate_branch(l, c):
            def f(slot):
                for i in range(3):
                    dst = WR[:, slot, i * 1024:(i + 1) * 1024].rearrange("p (k n) -> p k n", k=KC)
                    col0 = 3584 + i * 1024 + c * 128
                    src = w_in_d[l].rearrange("(k p) n -> p k n", p=128)[:, :, col0:col0 + 128]
                    S.dma("pool", dst, src, writes=[("WR", slot, 0)] if i == 0 else [("WRx", slot, i)], ring=True)
                for i in range(3):
                    dst = WR[:, slot, 3072 + i * 512:3072 + (i + 1) * 512].rearrange("p (k n) -> p k n", k=4)
                    src = w_br_d[l, i].rearrange("(k p) n -> p k n", p=128)[:, :, c * 128:(c + 1) * 128]
                    S.dma("pool", dst, src, writes=[("WR", slot, 1)] if i == 0 else [("WRy", slot, i)], ring=True)
            return f

        def gate_keys(slot):
            return [("WR", slot, 0), ("WRx", slot, 1), ("WRx", slot, 2), ("WR", slot, 1), ("WRy", slot, 1), ("WRy", slot, 2)]

        def ld_up(l, jp):
            def f(slot):
                dstv = WR[:, slot, 0:4096].rearrange("p (k two n) -> p k two n", k=KC, two=2)
                srcv = w_up_d[l].rearrange("(k p) (two n) -> p k two n", p=128, two=2)
                for t_ in range(2):
                    S.dma("pool", dstv[:, :, t_, :], srcv[:, :, t_, jp * 256:(jp + 1) * 256], writes=[("WR", slot, t_)], ring=True)
            return f

        def ld_down(l, c):
            def f(slot):
                dst = WR[:, slot, 0:NJ * 128].rearrange("p (k n) -> p k n", k=NJ)
                src = w_dn_d[l].rearrange("(k p) n -> p k n", p=128)[:, :, c * 128:(c + 1) * 128]
                S.dma("pool", dst, src, writes=wkeys(slot), ring=True)
            return f

        def ld_poolw(l):
            def f(slot):
                dst = WR[:, slot, 0:512].rearrange("p (g e) -> p g e", g=4)
                src = pool_w_d[l].rearrange("g c e -> c g e")
                S.dma("pool", dst, src, writes=wkeys(slot), ring=True)
            return f

        for l in range(NL):
            if l == 0:
                for g in range(4):
                    loads.append(ld_cols(w_mod_d[l], g * 512, 512))
            loads.append(ld_cols(w_in_d[l], 6 * 512, 512))
            loads.append(ld_poolw(l))
            for g in (3, 4, 5):
                loads.append(ld_cols(w_in_d[l], g * 512, 512))
            for g in (2, 1, 0):
                loads.append(ld_cols(w_in_d[l], g * 512, 512))
            for c in range(8):
                loads.append(ld_gate_branch(l, c))
                loads.append(ld_cols(w_mod_d[l], (4 + c) * 512, 512))
            for g in range(2):
                loads.append(ld_cols(w_out_d[l], g * 512, 512))
            for jp in range(11):
                loads.append(ld_up(l, jp))
            for c in range(8):
                loads.append(ld_down(l, c))
                if c < 4 and l + 1 < NL:
                    loads.append(ld_cols(w_mod_d[l + 1], c * 512, 512))

        def mm_group(ps_ap, ps_key, pairs, reads):
            n = len(pairs)
            for i, (lt, rh) in enumerate(pairs):
                S.op("pe", lambda e, lt=lt, rh=rh, i=i: e.matmul(ps_ap, lt, rh, start=(i == 0), stop=(i == n - 1)),
                     reads=reads if i == 0 else (), writes=[ps_key] if i == 0 else (), signal=(i == n - 1))

        def Hkeys(half):
            return [("H", kc, half) for kc in range(KC)]

        S.dma("sp", VEC[:, :], vecs_d, writes=[("VEC",)])
        for dc in range(KC):
            S.dma("sp", X[:, dc, :], xT_d[dc * 128:(dc + 1) * 128, :], writes=[("X", dc, 0), ("X", dc, 1)])
        ring_prefetch()
        S.dma("pool", RNEG[:, :], rneg_d, writes=[("RNEG",)])
        S.dma("pool", OH[:, :], oh_d, writes=[("OH",)])
        S.dma("pool", ONESC[:, :], onesc_d, writes=[("ONESC",)])
        S.op("dve", lambda e: e.memset(ONES[:, :], 1.0), writes=[("ONES",)])
        S.op("dve", lambda e: e.memset(ONESL[:, :], 1.0), writes=[("ONESL",)])
        S.op("dve", lambda e: e.memset(ONESL[:, 64:128], 0.0), writes=[("ONESL",)])
        S.op("dve", lambda e: e.memset(AR[:, 12288:12288 + 7680], 0.0), writes=[("VPall",)])
        S.op("act", lambda e: e.activation(out=SCb[:, :], in_=vcol(V_CV, 0, 8), func=AF.Silu),
             reads=[("VEC",)], writes=[("SCb",)])
        S.op("dve", lambda e: e.tensor_scalar(out=SMALL[:, 0:1], in0=vcol(V_BFLAG), scalar1=-1.0, scalar2=None,
                                              op0=ALU.mult), reads=[("VEC",)], writes=[("NBF",)])

        def mod_part(l, groups):
            for g in groups:
                slot = ring_use()
                ps, pk = ps_next()
                wv = WR[:, slot, 0:4096].rearrange("p (k n) -> p k n", k=KC)
                first = True
                for c in range(4):
                    col = g * 4 + c
                    for kc in range(KC):
                        S.op("pe", lambda e, c=c, kc=kc, col=col: e.matmul(
                            ps[:, col:col + 1], wv[:, kc, c * 128:(c + 1) * 128], SCb[:, kc:kc + 1],
                            start=(kc == 0), stop=(kc == KC - 1)),
                            reads=(wkeys(slot) + [("SCb",)]) if first else (), writes=[pk] if first else (),
                            signal=(c == 3 and kc == KC - 1))
                        first = False
                S.op("dve", lambda e, g=g: e.tensor_tensor(
                    out=MOD[:, l, g * 4:(g + 1) * 4], in0=ps[:, g * 4:(g + 1) * 4],
                    in1=VEC[:, V_BMOD + l * 48 + g * 4: V_BMOD + l * 48 + (g + 1) * 4], op=ALU.add),
                    reads=[pk, ("VEC",)], writes=[("MOD", l, g)])

        def gm_compute(l, j):
            sc0 = 8 if j == 0 else 32
            gb = (V_G1 if j == 0 else V_G2) + l * 8
            S.op("dve", lambda e: e.scalar_tensor_tensor(
                out=GMv[:, l, j, :], in0=MOD[:, l, sc0:sc0 + 8], scalar=1.0, in1=VEC[:, gb:gb + 8],
                op0=ALU.add, op1=ALU.mult),
                reads=[("MOD", l, sc0 // 4), ("MOD", l, sc0 // 4 + 1), ("VEC",)], writes=[("GM", l, j)])

        def norm(scale_ap_fn, bias_ap_fn, extra_reads, out_fn, out_keys_fn, hook=None, pre=False):
            SQ = [scr_bf16(0, 256), scr_bf16(256, 256)]
            SS = [scr_f32(512, 512), scr_f32(1024, 512)]
            RS = [scr_f32(1536, 512), scr_f32(2048, 512)]
            TMP = [scr_f32(2560, 512), scr_f32(3072, 512)]
            for half in range(2):
                if pre:
                    ps, pk = PS[STATB[half]], ("ps", STATB[half])
                else:
                    ps, pk = ps_next()
                for dc in range(KC if not pre else 0):
                    sq = SQ[dc % 2]
                    if dc % 2 == 0:
                        S.op("act", lambda e, dc=dc, sq=sq: e.activation(out=sq, in_=X[:, dc, HS(half)], func=AF.Square),
                             reads=[("X", dc, half)], writes=[("SQ", dc % 2)])
                    else:
                        S.op("dve", lambda e, dc=dc, sq=sq: e.tensor_tensor(out=sq, in0=X[:, dc, HS(half)], in1=X[:, dc, HS(half)], op=ALU.mult),
                             reads=[("X", dc, half)], writes=[("SQ", dc % 2)])
                    S.op("pe", lambda e, dc=dc, sq=sq: e.matmul(ps[:, :], ONES[:, :], sq, start=(dc == 0), stop=(dc == KC - 1)),
                         reads=[("SQ", dc % 2), ("ONES",)], writes=[pk])
                S.op("act", lambda e: e.activation(out=SS[half], in_=ps[:, :], func=AF.Sqrt, scale=1.0 / D, bias=1e-6),
                     reads=[pk], writes=[("SS", half)])
                S.op("dve", lambda e: e.reciprocal(out=RS[half], in_=SS[half]), reads=[("SS", half)], writes=[("RS", half)])
            if hook is not None:
                hook()
            for half in range(2):
                for dc in range(KC):
                    tmp = TMP[dc % 2]
                    S.op("dve", lambda e, dc=dc, tmp=tmp: e.tensor_tensor(out=tmp, in0=X[:, dc, HS(half)], in1=RS[half], op=ALU.mult),
                         reads=[("X", dc, half), ("RS", half)], writes=[("TMPn", dc % 2)])
                    b = bias_ap_fn(dc)
                    S.op("act", lambda e, dc=dc, tmp=tmp, b=b: e.activation(
                        out=out_fn(dc, half), in_=tmp, func=AF.Identity, scale=scale_ap_fn(dc),
                        **({"bias": b} if b is not None else {})),
                        reads=[("TMPn", dc % 2)] + extra_reads(), writes=out_keys_fn(dc, half))

        def stop_at(tag):
            if stop == tag:
                raise _Stop()

        def dbg_dump(name, ap, keys):
            if debug and name in dbg_d:
                S.barrier()
                S.dma("sp", dbg_d[name], ap, reads=keys, is_output=True)
                S.barrier()

        def run_layers():
          for l in range(NL):
            if l > 0:
                gm_compute(l, 0)
            S.dma("pool", CK[:, :].rearrange("p (a b) -> p a b", b=1024), ctxk_d[l].rearrange("p (a b) -> p a b", b=1024),
                  writes=[("CK",)])
            S.dma("pool", BAND.rearrange("p (a b) -> p a b", b=1024), band_d.rearrange("p (a b) -> p a b", b=1024),
                  writes=[("BAND",)])
            for j in range(2):
                dst = VP[:, 8:10, :, j * 128:j * 128 + 64]
                src = ctxv_d[l].rearrange("p (k h j d) -> p k h j d", k=2, h=4, j=2)[:, :, :, j, :]
                S.dma("pool", dst, src, writes=[("VPc", j)], reads=[("VPall",)])

            S.phase = f"L{l}:norm1"
            def l0_hook():
                mod_part(0, range(0, 4))
                gm_compute(0, 0)
            norm(lambda dc: GMv[:, l, 0, dc:dc + 1], lambda dc: MOD[:, l, dc:dc + 1],
                 lambda: [("GM", l, 0), ("MOD", l, 0), ("MOD", l, 1)],
                 lambda dc, half: Hh[:, dc, HS(half)], lambda dc, half: [("H", dc, half)],
                 hook=l0_hook if l == 0 else None, pre=(l > 0))
            S.barrier()
            if l == 0:
                stop_at("norm1")
            if l == 0:
                dbg_dump("h1", Hh[:, :, :], [("H", dc, hf) for dc in range(KC) for hf in range(2)])

            if l == 0:
                stop_at("A_pz0")
            slot = ring_use()
            wv = WR[:, slot, 0:4096].rearrange("p (k n) -> p k n", k=KC)
            for tb in range(8):
                ps, pk = ps_next()
                mm_group(ps[:, :], pk, [(Hh[:, kc, tb * 128:(tb + 1) * 128], wv[:, kc, :]) for kc in range(KC)],
                         wkeys(slot) + Hkeys(tb // 4))
                S.op("act", lambda e, ps=ps, tb=tb: e.activation(out=PZ[:, tb, :], in_=ps[:, :], func=AF.Copy),
                     reads=[pk], writes=[("PZ", tb)])

            if l == 0:
                stop_at("A_pz")
            S.phase = f"L{l}:A_pool"
            slot = ring_use()
            pwv = WR[:, slot, 0:512].rearrange("p (g e) -> p g e", g=4)
            DT = [scr_bf16(0, 256), scr_bf16(256, 256)]
            for g in range(4):
                for half in range(2):
                    ps, pk = ps_next()
                    first = True
                    for obi in range(4):
                        ob = half * 4 + obi
                        terms = []
                        dv = 0 if ob == 0 else (1 if ob == 7 else (2 if ob % 2 == 0 else 3))
                        terms.append((ob, dv))
                        if ob >= 1:
                            terms.append((ob - 1, 4 if ob % 2 == 1 else 5))
                        if ob <= 6:
                            terms.append((ob + 1, 6 if ob % 2 == 0 else 7))
                        mats = [(ib, v, s) for (ib, v) in terms for s in range(2)]
                        for i, (ib, v, s) in enumerate(mats):
                            last = (obi == 3 and i == len(mats) - 1)
                            S.op("pe", lambda e, ib=ib, v=v, s=s, i=i, obi=obi, n=len(mats): e.matmul(
                                ps[:, obi * 128:(obi + 1) * 128], PZ[:, ib, g * 128:(g + 1) * 128], BANDv[:, g, v, s, :],
                                start=(i == 0), stop=(i == n - 1)),
                                reads=([("PZ", t) for t in range(8)] + [("BAND",)]) if first else (),
                                writes=[pk] if first else (), signal=last)
                            first = False
                    dt = DT[(g * 2 + half) % 2]
                    S.op("act", lambda e, ps=ps, dt=dt: e.activation(out=dt, in_=ps[:, :], func=AF.Copy),
                         reads=[pk], writes=[("DT", (g * 2 + half) % 2)])
                    ps2, pk2 = ps_next()
                    mm_group(ps2[:, :], pk2, [(pwv[:, g, :], dt)], wkeys(slot) + [("DT", (g * 2 + half) % 2)])
                    S.op("dve", lambda e, ps2=ps2, g=g, half=half: e.tensor_scalar(
                        out=YS[:, 8 + g, HS(half)], in0=ps2[:, :], scalar1=vcol(V_PSCALE, l * 4 + g), scalar2=None, op0=ALU.mult),
                        reads=[pk2, ("VEC",)], writes=[("YS", 8 + g, half)])

            if l == 0:
                stop_at("A_pool")
            S.op("dve", lambda e: e.memset(AR[:, 4096:12288], 0.0), writes=[("KZall",), ("BAND",)])
            S.phase = f"L{l}:A_qkv"
            KST = [scr_f32(4096, 512), scr_f32(4608, 512)]
            for which in range(2):
                slot = ring_use()
                wv = WR[:, slot, 0:4096].rearrange("p (k n) -> p k n", k=KC)
                for c in range(4):
                    for half in range(2):
                        ps, pk = ps_next()
                        mm_group(ps[:, :], pk, [(wv[:, kc, c * 128:(c + 1) * 128], Hh[:, kc, HS(half)]) for kc in range(KC)],
                                 wkeys(slot) + Hkeys(half))
                        if which == 0:
                            S.op("act", lambda e, c=c, half=half, ps=ps: e.activation(
                                out=QK[:, c, HS(half)], in_=ps[:, :], func=AF.Copy, scale=0.125),
                                reads=[pk], writes=[("QK", c, half)])
                        else:
                            st = KST[(c * 2 + half) % 2]
                            sk_ = ("KST", (c * 2 + half) % 2)
                            S.op("act", lambda e, ps=ps, st=st: e.activation(out=st, in_=ps[:, :], func=AF.Copy),
                                 reads=[pk], writes=[sk_])
                            S.dma("sp", kvoT_d[l, c * 128:(c + 1) * 128, HS(half)], st, reads=[sk_], is_output=True)
                            for jj in range(2):
                                S.op("dve", lambda e, c=c, half=half, st=st, jj=jj: e.tensor_copy(
                                    out=KZ[jj * 64:(jj + 1) * 64, 2 * c + jj, HS(half)], in_=st[jj * 64:(jj + 1) * 64, :]),
                                    reads=[sk_, ("KZall",)], writes=[("KZ", 2 * c + jj, half)])
            if l == 0:
                stop_at("A_qk")
            slot = ring_use()
            wv = WR[:, slot, 0:4096].rearrange("p (k n) -> p k n", k=KC)
            for tb in range(8):
                ps, pk = ps_next()
                mm_group(ps[:, :], pk, [(Hh[:, kc, tb * 128:(tb + 1) * 128], wv[:, kc, :]) for kc in range(KC)],
                         wkeys(slot) + Hkeys(tb // 4))
                st = KST[tb % 2]
                S.op("act", lambda e, ps=ps, st=st: e.activation(out=st, in_=ps[:, :], func=AF.Copy),
                     reads=[pk], writes=[("KST", tb % 2)])
                S.dma("sp", kvo_d[l, tb * 128:(tb + 1) * 128, :], st, reads=[("KST", tb % 2)], is_output=True)
                for j in range(2):
                    S.op("dve", lambda e, st=st, tb=tb, j=j: e.tensor_copy(
                        out=VP[:, tb, :, j * 128:j * 128 + 64],
                        in_=st.rearrange("p (h j d) -> p h j d", h=4, j=2)[:, :, j, :]),
                        reads=[("KST", tb % 2), ("VPall",)], writes=[("VP", tb, j)])
            S.phase = f"L{l}:A_conv"
            slot_h, slot_c, slot_b = ring_use_many(3)
            wh = WR[:, slot_h, 0:4096].rearrange("p (k n) -> p k n", k=KC)
            wc = WR[:, slot_c, 0:4096].rearrange("p (k n) -> p k n", k=KC)
            wb = WR[:, slot_b, 0:4096].rearrange("p (k n) -> p k n", k=KC)
            HC = [scr_f32(512, 512), scr_f32(1024, 512)]
            CH = scr_f32(1536, 1026)
            YC = scr_f32(2562, 1024)
            S.op("dve", lambda e: e.memset(CH[:, 0:1], 0.0), writes=[("CH", 0), ("CH", 1)])
            S.op("dve", lambda e: e.memset(CH[:, 1025:1026], 0.0), writes=[("CHp",)])
            for c in range(4):
                cwb = V_CONVW + (l * 4 + c) * 3
                S.op("dve", lambda e, cwb=cwb: e.tensor_scalar(out=SMALL[:, 1:2], in0=vcol(cwb, 0), scalar1=SMALL[:, 0:1],
                                                              scalar2=None, op0=ALU.mult),
                     reads=[("VEC",), ("NBF",)], writes=[("FX", 0)])
                S.op("dve", lambda e, cwb=cwb: e.tensor_scalar(out=SMALL[:, 2:3], in0=vcol(cwb, 2), scalar1=SMALL[:, 0:1],
                                                              scalar2=None, op0=ALU.mult),
                     reads=[("VEC",), ("NBF",)], writes=[("FX", 1)])
                for half in range(2):
                    ps1, pk1 = ps_next()
                    mm_group(ps1[:, :], pk1, [(wh[:, kc, c * 128:(c + 1) * 128], Hh[:, kc, HS(half)]) for kc in range(KC)],
                             wkeys(slot_h) + Hkeys(half))
                    hc = HC[half]
                    S.op("act", lambda e, ps1=ps1, hc=hc: e.activation(out=hc, in_=ps1[:, :], func=AF.Copy),
                         reads=[pk1], writes=[("HC", half)])
                    ps2, pk2 = ps_next()
                    mm_group(ps2[:, :], pk2, [(wc[:, kc, c * 128:(c + 1) * 128], Hh[:, kc, HS(half)]) for kc in range(KC)],
                             wkeys(slot_c) + Hkeys(half))
                    S.op("dve", lambda e, ps2=ps2, hc=hc, half=half: e.tensor_tensor(
                        out=CH[:, 1 + half * 512: 1 + half * 512 + 512], in0=ps2[:, :], in1=hc, op=ALU.mult),
                        reads=[pk2, ("HC", half)], writes=[("CH", half)])
                chk = [("CH", 0), ("CH", 1), ("CHp",)]
                S.op("act", lambda e, cwb=cwb: e.activation(out=YC[:, :], in_=CH[:, 1:1025], func=AF.Identity, scale=vcol(cwb, 1)),
                     reads=chk + [("VEC",)], writes=[("YC",)])
                S.op("dve", lambda e, cwb=cwb: e.scalar_tensor_tensor(
                    out=YC[:, :], in0=CH[:, 0:1024], scalar=vcol(cwb, 0), in1=YC[:, :], op0=ALU.mult, op1=ALU.add),
                    reads=chk + [("VEC",), ("YC",)], writes=[("YC",)])
                S.op("dve", lambda e, cwb=cwb: e.scalar_tensor_tensor(
                    out=YC[:, :], in0=CH[:, 2:1026], scalar=vcol(cwb, 2), in1=YC[:, :], op0=ALU.mult, op1=ALU.add),
                    reads=chk + [("VEC",), ("YC",)], writes=[("YC",)])
                ycv = YC[:, :].rearrange("p (s t) -> p s t", s=4)
                chv = CH[:, 1:1025].rearrange("p (s t) -> p s t", s=4)
                S.op("dve", lambda e, ycv=ycv, chv=chv: e.scalar_tensor_tensor(
                    out=ycv[:, 1:4, 0:1], in0=chv[:, 0:3, 255:256], scalar=SMALL[:, 1:2], in1=ycv[:, 1:4, 0:1],
                    op0=ALU.mult, op1=ALU.add), reads=chk + [("FX", 0), ("YC",)], writes=[("YC",)])
                S.op("dve", lambda e, ycv=ycv, chv=chv: e.scalar_tensor_tensor(
                    out=ycv[:, 0:3, 255:256], in0=chv[:, 1:4, 0:1], scalar=SMALL[:, 2:3], in1=ycv[:, 0:3, 255:256],
                    op0=ALU.mult, op1=ALU.add), reads=chk + [("FX", 1), ("YC",)], writes=[("YC",)])
                for half in range(2):
                    ps3, pk3 = ps_next()
                    mm_group(ps3[:, :], pk3, [(wb[:, kc, c * 128:(c + 1) * 128], Hh[:, kc, HS(half)]) for kc in range(KC)],
                             wkeys(slot_b) + Hkeys(half))
                    S.op("dve", lambda e, ps3=ps3, half=half, c=c: e.tensor_tensor(
                        out=YS[:, c, HS(half)], in0=ps3[:, :], in1=YC[:, HS(half)], op=ALU.mult),
                        reads=[pk3, ("YC",)], writes=[("YS", c, half)])

            S.barrier()
            if l == 0:
                stop_at("phaseA")
            if l == 0:
                dbg_dump("qk", AR[:, 0:8192], [])
                dbg_dump("ysA", AR[:, 19968:19968 + 12288], [])

            S.phase = f"L{l}:B_attn"
            TT_ = [scr_f32(0 + i * 512, 512) for i in range(3)]
            PP = [scr_bf16(1536 + i * 256, 256) for i in range(4)]
            RD = [scr_f32(3072 + i * 512, 512) for i in range(2)]
            work = []
            unit = 0
            for hp in range(4):
                for qc in range(2):
                    tiles = [("c", 8), ("c", 9)] + [("l", b_) for b_ in LOCAL_TILES[qc]]
                    ntot = 2 * len(tiles)
                    it = 0
                    for j in range(2):
                        for kind, b_ in tiles:
                            work.append(dict(hp=hp, qc=qc, j=j, kind=kind, b=b_, unit=unit, it=it, ntot=ntot,
                                             first_of_head=(b_ == 8 and kind == "c")))
                            it += 1
                    unit += 1
            LA = 3

            def ut_load(h):
                S.dma("pool", UT[:, h % 2, :], utab_d[l][:, h * NE * 64:(h + 1) * NE * 64], writes=[("UT", h % 2)])

            def emit_S(k):
                w = work[k]
                hp, qc, j, b_ = w["hp"], w["qc"], w["j"], w["b"]
                si = k % 4
                Sps, sk = PS[si], ("ps", si)
                h_ = 2 * hp + j
                qsrc = QK[:, hp, HS(qc)]
                if w["kind"] == "l":
                    ksrc = KZ[:, h_, b_ * 128:(b_ + 1) * 128]
                    S.op("pe", lambda e: e.matmul(Sps[:, :], ksrc, qsrc, start=True, stop=False),
                         reads=[("KZ", h_, b_ // 4), ("KZall",), ("QK", hp, qc)], writes=[sk], signal=False)
                    S.op("pe", lambda e: e.matmul(Sps[:, :], OHv[:, b_, :], RNEG[:, HS(qc)], start=False, stop=True),
                         reads=[("OH",), ("RNEG",)], writes=[sk])
                else:
                    ksrc = CKv[:, h_, (b_ - 8) * 128:(b_ - 7) * 128]
                    S.op("pe", lambda e: e.matmul(Sps[:, :], ksrc, qsrc, start=True, stop=True),
                         reads=[("CK",), ("QK", hp, qc)], writes=[sk])

            def emit_rest(k):
                w = work[k]
                hp, qc, j, b_, it, ntot = w["hp"], w["qc"], w["j"], w["b"], w["it"], w["ntot"]
                u = w["unit"]
                h = 2 * hp + j
                NUM, nk = PS[4 + (u % 2) * 2], ("ps", 4 + (u % 2) * 2)
                DEN, dk = PS[5 + (u % 2) * 2], ("ps", 5 + (u % 2) * 2)
                si = k % 4
                Sps, sk = PS[si], ("ps", si)
                pi = k % 4
                pt = PP[pi]
                if w["first_of_head"] and qc == 1 and j == 1 and hp < 3:
                    ut_load(2 * (hp + 1))
                if w["first_of_head"] and qc == 0 and j == 0 and hp > 0:
                    ut_load(2 * hp + 1)
                if w["kind"] == "l":
                    e0 = 8 * qc - 2 * b_ + E0
                    ti = k % 3
                    tt = TT_[ti]
                    S.op("dve", lambda e: e.tensor_tensor(out=tt, in0=Sps[:, :], in1=UT[:, h % 2, e0 * 64:(e0 + 8) * 64], op=ALU.add),
                         reads=[sk, ("UT", h % 2)], writes=[("TT", ti)])
                    S.op("act", lambda e: e.activation(out=pt, in_=tt, func=AF.Exp),
                         reads=[("TT", ti)], writes=[("PP", pi)])
                    vkeys = [("VP", b_, 0), ("VP", b_, 1), ("VPall",)]
                    osrc, okeys = ONESL[:, j * 64:j * 64 + 128], [("ONESL",)]
                else:
                    S.op("act", lambda e: e.activation(out=pt, in_=Sps[:, :], func=AF.Exp),
                         reads=[sk], writes=[("PP", pi)])
                    vkeys = [("VPc", 0), ("VPc", 1), ("VPall",)]
                    osrc, okeys = ONESC[:, j * 64:j * 64 + 128], [("ONESC",)]
                vsrc = VP[:, b_, hp, j * 64:j * 64 + 128]
                S.op("pe", lambda e: e.matmul(NUM[:, :], vsrc, pt, start=(it == 0), stop=(it == ntot - 1)),
                     reads=[("PP", pi)] + vkeys, writes=[nk], signal=False)
                S.op("pe", lambda e: e.matmul(DEN[:, :], osrc, pt, start=(it == 0), stop=(it == ntot - 1)),
                     reads=[("PP", pi)] + okeys, writes=[dk])
                if it == ntot - 1:
                    rd = RD[u % 2]
                    S.op("dve", lambda e: e.reciprocal(out=rd, in_=DEN[:, :]), reads=[dk], writes=[("RD", u % 2)])
                    S.op("dve", lambda e: e.tensor_tensor(out=YS[:, 4 + hp, HS(qc)], in0=NUM[:, :], in1=rd, op=ALU.mult),
                         reads=[nk, dk, ("RD", u % 2)] + [("PZ", t) for t in range(8)], writes=[("YS", 4 + hp, qc)])

            ut_load(0)
            ut_load(1)
            for k in range(min(LA, len(work))):
                emit_S(k)
            for k in range(len(work)):
                if k + LA < len(work):
                    emit_S(k + LA)
                emit_rest(k)
            S.barrier()
            ps_ctr[0] = 0
            if l == 0:
                stop_at("attn")
            if l == 0:
                dbg_dump("ysB", AR[:, 19968:19968 + 12288], [])

            S.phase = f"L{l}:C_gates"
            GT = [scr_f32(i * 512, 512) for i in range(6)]
            MM = [scr_f32(3072 + i * 512, 512) for i in range(4)]
            gctr = [0]
            mctr = [0]
            for c in range(8):
                slot = ring_use()
                gw = WR[:, slot, 0:3072].rearrange("p (i k n) -> p i k n", i=3, k=KC)
                bw = WR[:, slot, 3072:3072 + 1536].rearrange("p (i k n) -> p i k n", i=3, k=4)
                for half in range(2):
                    prods = []
                    for i in range(3):
                        psg, pkg = ps_next()
                        mm_group(psg[:, :], pkg, [(gw[:, i, kc, :], Hh[:, kc, HS(half)]) for kc in range(KC)],
                                 gate_keys(slot) + Hkeys(half))
                        gi = gctr[0] % 6
                        gctr[0] += 1
                        gt = GT[gi]
                        S.op("act", lambda e, psg=psg, gt=gt: e.activation(out=gt, in_=psg[:, :], func=AF.Sigmoid),
                             reads=[pkg], writes=[("GT", gi)])
                        psp, pkp = ps_next()
                        mm_group(psp[:, :], pkp, [(bw[:, i, kc, :], YS[:, 4 * i + kc, HS(half)]) for kc in range(4)],
                                 gate_keys(slot) + [("YS", 4 * i + kc, half) for kc in range(4)])
                        mi = mctr[0] % 4
                        mctr[0] += 1
                        mt = MM[mi]
                        S.op("dve", lambda e, psp=psp, gt=gt, mt=mt: e.tensor_tensor(out=mt, in0=psp[:, :], in1=gt, op=ALU.mult),
                             reads=[pkp, ("GT", gi)], writes=[("MM", mi)])
                        prods.append((mt, mi))
                    (m0, k0), (m1, k1), (m2, k2) = prods
                    S.op("pool", lambda e, m0=m0, m1=m1: e.tensor_tensor(out=m0, in0=m0, in1=m1, op=ALU.add),
                         reads=[("MM", k0), ("MM", k1)], writes=[("MM", k0)])
                    S.op("pool", lambda e, m0=m0, m2=m2, c=c, half=half: e.tensor_tensor(out=MERGED[:, c, HS(half)], in0=m0, in1=m2, op=ALU.add),
                         reads=[("MM", k0), ("MM", k2)], writes=[("MG", c, half)])
                mod_part(l, [4 + c])
            if l == 0:
                stop_at("phaseC")
            if l == 0:
                dbg_dump("merged", AR[:, 0:8192], [("MG", c, hf) for c in range(8) for hf in range(2)])

            S.phase = f"L{l}:D_mod_wout"
            gm_compute(l, 1)
            stat_begin()
            for g in range(2):
                slot = ring_use()
                wv = WR[:, slot, 0:4096].rearrange("p (k n) -> p k n", k=KC)
                for cc in range(4):
                    c = g * 4 + cc
                    for half in range(2):
                        ps, pk = ps_next()
                        mm_group(ps[:, :], pk, [(wv[:, kc, cc * 128:(cc + 1) * 128], MERGED[:, kc, HS(half)]) for kc in range(KC)],
                                 wkeys(slot) + [("MG", kc, half) for kc in range(KC)])
                        S.op("dve", lambda e, ps=ps, c=c, half=half: e.scalar_tensor_tensor(
                            out=X[:, c, HS(half)], in0=ps[:, :], scalar=MOD[:, l, 16 + c:17 + c], in1=X[:, c, HS(half)],
                            op0=ALU.mult, op1=ALU.add),
                            reads=[pk, ("MOD", l, 4 + c // 4), ("X", c, half)], writes=[("X", c, half)])
                        stat_accum(c, half)
            stat_end()
            if l == 0:
                stop_at("x1")
            if l == 0:
                dbg_dump("x1", X[:, :, :], [("X", dc, hf) for dc in range(KC) for hf in range(2)])

            S.phase = f"L{l}:E_norm2"
            S.barrier()
            norm(lambda dc: GMv[:, l, 1, dc:dc + 1], lambda dc: MOD[:, l, 24 + dc:25 + dc],
                 lambda: [("GM", l, 1), ("MOD", l, 6), ("MOD", l, 7)],
                 lambda dc, half: Hh[:, dc, HS(half)], lambda dc, half: [("H", dc, half)], pre=True)
            S.barrier()

            S.phase = f"L{l}:F_up"
            U32 = scr_f32(0, 1026)
            YF = scr_f32(1026, 1024)
            GF = scr_f32(2050, 1024)
            S.op("dve", lambda e: e.memset(U32[:, 0:1], 0.0), writes=[("U32", 0), ("U32", 1)])
            S.op("dve", lambda e: e.memset(U32[:, 1025:1026], 0.0), writes=[("U32p",)])
            for jp in range(11):
                slot = ring_use()
                wv = WR[:, slot, 0:4096].rearrange("p (k two n) -> p k two n", k=KC, two=2)
                for jj in range(2):
                    jx = jp * 2 + jj
                    fwb = V_FCONV + (l * NJ + jx) * 3
                    S.op("dve", lambda e, fwb=fwb: e.tensor_scalar(out=SMALL[:, 1:2], in0=vcol(fwb, 0), scalar1=SMALL[:, 0:1],
                                                                  scalar2=None, op0=ALU.mult),
                         reads=[("VEC",), ("NBF",)], writes=[("FX", 0)])
                    S.op("dve", lambda e, fwb=fwb: e.tensor_scalar(out=SMALL[:, 2:3], in0=vcol(fwb, 2), scalar1=SMALL[:, 0:1],
                                                                  scalar2=None, op0=ALU.mult),
                         reads=[("VEC",), ("NBF",)], writes=[("FX", 1)])
                    for half in range(2):
                        ps, pk = ps_next()
                        mm_group(ps[:, :], pk, [(wv[:, kc, 0, jj * 128:(jj + 1) * 128], Hh[:, kc, HS(half)]) for kc in range(KC)],
                                 wkeys(slot) + Hkeys(half))
                        S.op("act", lambda e, ps=ps, half=half: e.activation(out=U32[:, 1 + half * 512:1 + half * 512 + 512], in_=ps[:, :], func=AF.Copy),
                             reads=[pk], writes=[("U32", half)])
                    uk = [("U32", 0), ("U32", 1), ("U32p",)]
                    S.op("act", lambda e, fwb=fwb: e.activation(out=YF[:, :], in_=U32[:, 1:1025], func=AF.Identity, scale=vcol(fwb, 1)),
                         reads=uk + [("VEC",)], writes=[("YF",)])
                    S.op("dve", lambda e, fwb=fwb: e.scalar_tensor_tensor(
                        out=YF[:, :], in0=U32[:, 0:1024], scalar=vcol(fwb, 0), in1=YF[:, :], op0=ALU.mult, op1=ALU.add),
                        reads=uk + [("VEC",), ("YF",)], writes=[("YF",)])
                    S.op("dve", lambda e, fwb=fwb: e.scalar_tensor_tensor(
                        out=YF[:, :], in0=U32[:, 2:1026], scalar=vcol(fwb, 2), in1=YF[:, :], op0=ALU.mult, op1=ALU.add),
                        reads=uk + [("VEC",), ("YF",)], writes=[("YF",)])
                    yfv = YF[:, :].rearrange("p (s t) -> p s t", s=4)
                    uv = U32[:, 1:1025].rearrange("p (s t) -> p s t", s=4)
                    S.op("dve", lambda e, yfv=yfv, uv=uv: e.scalar_tensor_tensor(
                        out=yfv[:, 1:4, 0:1], in0=uv[:, 0:3, 255:256], scalar=SMALL[:, 1:2], in1=yfv[:, 1:4, 0:1],
                        op0=ALU.mult, op1=ALU.add), reads=uk + [("FX", 0), ("YF",)], writes=[("YF",)])
                    S.op("dve", lambda e, yfv=yfv, uv=uv: e.scalar_tensor_tensor(
                        out=yfv[:, 0:3, 255:256], in0=uv[:, 1:4, 0:1], scalar=SMALL[:, 2:3], in1=yfv[:, 0:3, 255:256],
                        op0=ALU.mult, op1=ALU.add), reads=uk + [("FX", 1), ("YF",)], writes=[("YF",)])
                    S.op("act", lambda e: e.activation(out=GF[:, :], in_=YF[:, :], func=AF.Gelu_apprx_tanh),
                         reads=[("YF",)], writes=[("GF",)])
                    for half in range(2):
                        ps, pk = ps_next()
                        mm_group(ps[:, :], pk, [(wv[:, kc, 1, jj * 128:(jj + 1) * 128], Hh[:, kc, HS(half)]) for kc in range(KC)],
                                 wkeys(slot) + Hkeys(half))
                        S.op("dve", lambda e, ps=ps, half=half, jx=jx: e.tensor_tensor(
                            out=A[:, jx, HS(half)], in0=ps[:, :], in1=GF[:, HS(half)], op=ALU.mult),
                            reads=[pk, ("GF",)], writes=[("A", jx, half)])
            if l == 0:
                stop_at("ffnup")
            if l == 0:
                dbg_dump("a", AR[:, 0:NJ * T], [("A", j, hf) for j in range(NJ) for hf in range(2)])

            S.phase = f"L{l}:G_down"
            stat_begin()
            for c in range(8):
                slot = ring_use()
                wv = WR[:, slot, 0:NJ * 128].rearrange("p (k n) -> p k n", k=NJ)
                for half in range(2):
                    ps, pk = ps_next()
                    mm_group(ps[:, :], pk, [(wv[:, j, :], A[:, j, HS(half)]) for j in range(NJ)],
                             wkeys(slot) + [("A", j, half) for j in range(NJ)])
                    S.op("dve", lambda e, ps=ps, c=c, half=half: e.scalar_tensor_tensor(
                        out=X[:, c, HS(half)], in0=ps[:, :], scalar=MOD[:, l, 40 + c:41 + c], in1=X[:, c, HS(half)],
                        op0=ALU.mult, op1=ALU.add),
                        reads=[pk, ("MOD", l, 10 + c // 4), ("X", c, half)], writes=[("X", c, half)])
                    stat_accum(c, half)
                if c < 4 and l + 1 < NL:
                    mod_part(l + 1, [c])
            stat_end()
            S.barrier()
            if l + 1 < NL:
                S.op("dve", lambda e: e.memset(AR[:, 12288:12288 + 7680], 0.0), writes=[("VPall",)])
            if l == 0:
                stop_at("layer0")
            if l == 0:
                dbg_dump("x2", X[:, :, :], [("X", dc, hf) for dc in range(KC) for hf in range(2)])

        try:
            run_layers()
        except _Stop:
            pass
        S.barrier()
        S.phase = 'final'
        OUTB = [scr_f32(3584, 512), scr_f32(4096, 512), scr_f32(4608, 512), scr_f32(5120, 512)]
        octr = [0]

        def out_fn(dc, half):
            i = octr[0] % 4
            return OUTB[i]

        def out_keys(dc, half):
            return [("OUTB", octr[0] % 4)]

        SQ = [scr_bf16(0, 256), scr_bf16(256, 256)]
        SS = [scr_f32(512, 512), scr_f32(1024, 512)]
        RS = [scr_f32(1536, 512), scr_f32(2048, 512)]
        TMP = [scr_f32(2560, 512), scr_f32(3072, 512)]
        final_pre = STAT_READY[0] and stop is None
        for half in range(2):
            if final_pre:
                ps, pk = PS[STATB[half]], ("ps", STATB[half])
            else:
                ps, pk = ps_next()
            for dc in range(0 if final_pre else KC):
                sq = SQ[dc % 2]
                S.op("act", lambda e, dc=dc, sq=sq: e.activation(out=sq, in_=X[:, dc, HS(half)], func=AF.Square),
                     reads=[("X", dc, half)], writes=[("SQ", dc % 2)])
                S.op("pe", lambda e, dc=dc, sq=sq: e.matmul(ps[:, :], ONES[:, :], sq, start=(dc == 0), stop=(dc == KC - 1)),
                     reads=[("SQ", dc % 2), ("ONES",)], writes=[pk])
            S.op("act", lambda e: e.activation(out=SS[half], in_=ps[:, :], func=AF.Sqrt, scale=1.0 / D, bias=1e-6),
                 reads=[pk], writes=[("SS", half)])
            S.op("dve", lambda e: e.reciprocal(out=RS[half], in_=SS[half]), reads=[("SS", half)], writes=[("RS", half)])
            for dc in range(KC):
                tmp = TMP[dc % 2]
                S.op("dve", lambda e, dc=dc, tmp=tmp: e.tensor_tensor(out=tmp, in0=X[:, dc, HS(half)], in1=RS[half], op=ALU.mult),
                     reads=[("X", dc, half), ("RS", half)], writes=[("TMPn", dc % 2)])
                oi = octr[0] % 4
                octr[0] += 1
                ob = OUTB[oi]
                S.op("act", lambda e, dc=dc, tmp=tmp, ob=ob: e.activation(out=ob, in_=tmp, func=AF.Identity, scale=vcol(V_GF, dc)),
                     reads=[("TMPn", dc % 2), ("VEC",)], writes=[("OUTB", oi)])
                S.dma("sp", yT_d[dc * 128:(dc + 1) * 128, HS(half)], ob, reads=[("OUTB", oi)], is_output=True)
        S.finish()
        build_nc.stats = dict(S.nops)
        build_nc.phase_of = dict(S.phase_of)
    return nc


def _bf16_split(a):
    hi = a.astype(ml_dtypes.bfloat16).astype(np.float32)
    lo = (a - hi).astype(ml_dtypes.bfloat16).astype(np.float32)
    return hi, lo


def _band_tables(seq_len):
    out = np.zeros((128, 4, 8, 2, 128), np.float32)
    t = np.arange(T)
    for g, w in enumerate((2, 4, 8, 16)):
        B = np.zeros((T, T), np.float64)
        s0 = (t // seq_len) * seq_len
        lo = np.clip(t - w // 2, s0, s0 + seq_len)
        hi = np.clip(t - w // 2 + w, s0, s0 + seq_len)
        for tt in range(T):
            B[lo[tt]:hi[tt], tt] = 1.0 / (hi[tt] - lo[tt])
            B[tt, tt] -= 1.0
        B = B.astype(np.float32)

        def blk(ib, ob):
            return B[ib * 128:(ib + 1) * 128, ob * 128:(ob + 1) * 128]
        variants = [blk(0, 0), blk(7, 7), blk(2, 2), blk(1, 1), blk(0, 1), blk(1, 2), blk(1, 0), blk(2, 1)]
        for ob in range(8):
            dv = 0 if ob == 0 else (1 if ob == 7 else (2 if ob % 2 == 0 else 3))
            assert np.array_equal(blk(ob, ob), variants[dv])
            if ob >= 1:
                assert np.array_equal(blk(ob - 1, ob), variants[4 if ob % 2 == 1 else 5])
            if ob <= 6:
                assert np.array_equal(blk(ob + 1, ob), variants[6 if ob % 2 == 0 else 7])
        for v, m in enumerate(variants):
            h_, l_ = _bf16_split(m)
            out[:, g, v, 0, :] = h_
            out[:, g, v, 1, :] = l_
    return out.reshape(128, -1)


def _rneg_table(sample):
    r = np.full((16, 16), NEG, np.float32)
    for rk in range(16):
        for q in range(16):
            if sample:
                s = min(max(q - 4, 0), 8)
                ok = s <= rk < s + 8
            else:
                ok = (rk // 4) == (q // 4)
            if ok:
                r[rk, q] = 0.0
    out = np.zeros((128, T), np.float32)
    out[0:16] = np.repeat(r, 64, axis=1)
    return out


def _onehot_rows():
    oh = np.zeros((16, 8, 128), np.float32)
    for b in range(8):
        oh[2 * b, b, 0:64] = 1.0
        oh[2 * b + 1, b, 64:128] = 1.0
    out = np.zeros((128, 1024), np.float32)
    out[0:16] = oh.reshape(16, 1024)
    return out


def _u_table(rpb_l):
    U = np.full((128, NH, NE, 64), NEG, np.float32)
    cq = np.arange(64)
    cs = np.clip(cq - 8, 0, 48)
    for jj in range(2):
        for ei in range(NE):
            dr = jj - (ei - E0)
            if abs(dr) > 7:
                continue
            for ck in range(64):
                valid = (ck >= cs) & (ck < cs + 16)
                dc = np.clip(ck - cq + 15, 0, 30)
                vals = rpb_l[:, dr + 7, :][:, dc]
                U[jj * 64 + ck, :, ei, :] = np.where(valid[None, :], vals, NEG)
    return U.reshape(128, -1)


def _prep(inputs):
    f = lambda a: np.ascontiguousarray(np.asarray(a, dtype=np.float32))
    x_prompt, x_sample = f(inputs["x_prompt"]), f(inputs["x_sample"])
    cache_kv, c, c_ctx = f(inputs["cache_kv"]), f(inputs["c"]), f(inputs["c_ctx"])
    rpb = f(inputs["rpb"])
    shared = {
        "w_mod": f(inputs["w_mod"]), "w_in": f(inputs["w_in"]), "w_branch": f(inputs["w_branch"]),
        "w_out": f(inputs["w_out"]), "w_up": f(inputs["ffn_w_up"]), "w_down": f(inputs["ffn_w_down"]),
        "pool_w": f(inputs["pool_w"]),
    }

    def pk(v):
        return np.ascontiguousarray(v.reshape(-1, 128).T)

    vec_common = np.zeros((128, NV), np.float32)
    b_mod, g1, g2, gf = f(inputs["b_mod"]), f(inputs["g_norm1"]), f(inputs["g_norm2"]), f(inputs["g_final"])
    conv_w, fconv, pscale = f(inputs["conv_w"]), f(inputs["ffn_conv"]), f(inputs["pool_scale"])
    for l in range(NL):
        vec_common[:, V_BMOD + l * 48: V_BMOD + (l + 1) * 48] = pk(b_mod[l])
        vec_common[:, V_G1 + l * 8: V_G1 + (l + 1) * 8] = pk(g1[l])
        vec_common[:, V_G2 + l * 8: V_G2 + (l + 1) * 8] = pk(g2[l])
        for cc in range(4):
            for k in range(3):
                vec_common[:, V_CONVW + (l * 4 + cc) * 3 + k] = conv_w[l, k, cc * 128:(cc + 1) * 128]
        for j in range(NJ):
            for k in range(3):
                vec_common[:, V_FCONV + (l * NJ + j) * 3 + k] = fconv[l, k, j * 128:(j + 1) * 128]
        vec_common[:, V_PSCALE + l * 4: V_PSCALE + (l + 1) * 4] = pk(pscale[l])
    vec_common[:, V_GF:V_GF + 8] = pk(gf)

    ones_pat = np.concatenate([np.ones((128, 64)), np.zeros((128, 64)), np.ones((128, 64))], axis=1).astype(np.float32)
    band_p, band_s = _band_tables(256), _band_tables(1024)
    rneg_p, rneg_s = _rneg_table(False), _rneg_table(True)
    oh = _onehot_rows()
    utab_s = np.stack([_u_table(rpb[l]) for l in range(NL)])
    utab_p = np.zeros_like(utab_s)
    in_maps = []
    for i in range(8):
        sample = i >= 4
        vec = vec_common.copy()
        if sample:
            b = i - 4
            xT = np.ascontiguousarray(x_sample[b].T)
            vec[:, V_CV:V_CV + 8] = pk(c[b])
            vec[:, V_BFLAG] = 0.0
            ck = cache_kv[b, :, 0]
            ctxk = np.zeros((NL, 2, 64, NH, 256), np.float32)
            for h_ in range(NH):
                ctxk[:, h_ % 2, :, h_, :] = ck[:, h_].transpose(0, 2, 1)
            ctxk = ctxk.reshape(NL, 128, NH * 256)
            cvv = cache_kv[b, :, 1]
            ctxv = cvv.transpose(0, 2, 1, 3).reshape(NL, 2, 128, 512).transpose(0, 2, 1, 3).reshape(NL, 128, 1024)
            m = {"ctxkT": np.ascontiguousarray(ctxk), "ctxv": np.ascontiguousarray(ctxv), "utab": utab_s,
                 "rneg": rneg_s, "oh": oh, "onesc": ones_pat, "band": band_s}
        else:
            xT = np.ascontiguousarray(x_prompt[4 * i:4 * i + 4].reshape(T, D).T)
            vec[:, V_CV:V_CV + 8] = pk(c_ctx)
            vec[:, V_BFLAG] = 1.0
            m = {"ctxkT": np.zeros((NL, 128, 2048), np.float32), "ctxv": np.zeros((NL, 128, 1024), np.float32),
                 "utab": utab_p, "rneg": rneg_p, "oh": oh, "onesc": np.zeros_like(ones_pat), "band": band_p}
        m.update(shared)
        m["xT"] = xT
        m["vecs"] = vec
        in_maps.append(m)
    return in_maps


def _assemble(results):
    y_prompt = np.empty((16, 256, D), np.float32)
    y_sample = np.empty((4, T, D), np.float32)
    kv_state = np.empty((16, NL, 2, NH, 256, 64), np.float32)
    for i in range(8):
        r = results[i]
        y = np.asarray(r["yT"], dtype=np.float32).T
        if i < 4:
            y_prompt[4 * i:4 * i + 4] = y.reshape(4, 256, D)
            kvo = np.asarray(r["kvo"], dtype=np.float32).reshape(NL, 4, 256, NH, 64)
            kv_state[4 * i:4 * i + 4, :, 1] = kvo.transpose(1, 0, 3, 2, 4)
            kT = np.asarray(r["kvoT"], dtype=np.float32).reshape(NL, NH, 64, 4, 256)
            kv_state[4 * i:4 * i + 4, :, 0] = kT.transpose(3, 0, 1, 4, 2)
        else:
            y_sample[i - 4] = y
    return y_prompt, y_sample, kv_state


_NC_CACHE = {}


def kernel(**inputs):
    in_maps = _prep(inputs)
    if "nc" not in _NC_CACHE:
        _NC_CACHE["nc"] = build_nc()
    res = run_bass_kernel_spmd(_NC_CACHE["nc"], in_maps, core_ids=list(range(8)))
    return _assemble(res.results)
```

```python
import numpy as np
import ml_dtypes
from contextlib import ExitStack
import concourse.bass as bass
import concourse.mybir as mybir
from concourse.bass_utils import run_bass_kernel_spmd

F32 = mybir.dt.float32
BF16 = mybir.dt.bfloat16
AF = mybir.ActivationFunctionType
ALU = mybir.AluOpType

D = 1024
T = 1024
KC = 8
DFF = 2816
NJ = 22
INW = 6656
NL = 2
NH = 8
NE = 22
E0 = 10
NEG = -30000.0
SLOT = 4608
NSLOT = 5
ND = 16
SEM_LIMIT = 12000

V_BMOD = 0
V_G1 = 96
V_G2 = 112
V_GF = 128
V_CONVW = 136
V_FCONV = 160
V_PSCALE = 292
V_CV = 300
V_BFLAG = 308
NV = 320

LOCAL_TILES = {0: [0, 1, 2, 3, 4, 5], 1: [2, 3, 4, 5, 6, 7]}
QROWS = {0: (0, 5), 1: (0, 7), 2: (0, 9), 3: (0, 11), 4: (5, 15), 5: (7, 15), 6: (9, 15), 7: (11, 15)}


class Sched:
    def __init__(self, nc, es):
        self.nc = nc
        self.es = es
        self.eh = {"pe": nc.tensor, "act": nc.scalar, "dve": nc.vector, "pool": nc.gpsimd, "sp": nc.sync}
        self.epoch = {e: 0 for e in self.eh}
        self.cnt = {e: 0 for e in self.eh}
        self.sems = {}
        for e in self.eh:
            self.sems[(e, 0)] = es.enter_context(nc.semaphore(f"s_{e}_0"))
        self.dsem = [es.enter_context(nc.semaphore(f"s_dma_{i}")) for i in range(ND)]
        self.dcnt = [0] * ND
        self.dpool = {"sp": list(range(0, 6)), "pool": list(range(6, ND))}
        self.dnext = {"sp": 0, "pool": 0}
        self.seen = {e: {} for e in self.eh}
        self.last_w = {}
        self.readers = {}
        self.out_dmas = []
        self.aux = {}
        self.nops = {e: 0 for e in self.eh}
        self.pending = {e: False for e in self.eh}
        self.phase = "init"
        self.phase_of = {}

    def _sem_of(self, src):
        if src[0] == "d":
            return self.dsem[src[1]]
        return self.sems[src]

    def _wait(self, eng, src, c):
        if self.seen[eng].get(src, 0) >= c:
            return
        self.eh[eng].wait_ge(self._sem_of(src), c)
        self.seen[eng][src] = c

    def _deps(self, eng, reads, writes):
        deps = {}

        def add(src, c):
            if deps.get(src, 0) < c:
                deps[src] = c

        for k in reads:
            w = self.last_w.get(k)
            if w:
                add(*w)
        for k in writes:
            w = self.last_w.get(k)
            if w:
                add(*w)
            for src, c in self.readers.get(k, {}).items():
                add(src, c)
        for src, c in deps.items():
            if eng == "pe" and src[0] == "pe":
                continue
            self._wait(eng, src, c)

    def _record(self, src, c, reads, writes):
        for k in writes:
            self.last_w[k] = (src, c)
            self.readers[k] = {}
        for k in reads:
            r = self.readers.setdefault(k, {})
            if r.get(src, 0) < c:
                r[src] = c

    def op(self, eng, fn, reads=(), writes=(), signal=True):
        if (not self.pending[eng]) and self.cnt[eng] >= SEM_LIMIT:
            self.epoch[eng] += 1
            self.cnt[eng] = 0
            self.sems[(eng, self.epoch[eng])] = self.es.enter_context(
                self.nc.semaphore(f"s_{eng}_{self.epoch[eng]}"))
        self._deps(eng, reads, writes)
        ins = fn(self.eh[eng])
        src = (eng, self.epoch[eng])
        if signal:
            ins.then_inc(self.sems[src], 1)
            self.cnt[eng] += 1
            c = self.cnt[eng]
            self.pending[eng] = False
        else:
            c = self.cnt[eng] + 1
            self.pending[eng] = True
        self._record(src, c, reads, writes)
        self.nops[eng] += 1
        try:
            self.phase_of[ins.ins.name] = self.phase
        except Exception:
            pass
        return ins

    def dma(self, q, out, in_, reads=(), writes=(), is_output=False, ring=False):
        self._deps(q, reads, writes)
        lst = self.dpool[q]
        s = lst[self.dnext[q] % len(lst)]
        self.dnext[q] += 1
        src = ("d", s)
        if self.dcnt[s] > 0:
            self._wait(q, src, self.dcnt[s])
        self.eh[q].dma_start(out=out, in_=in_).then_inc(self.dsem[s], 16)
        self.dcnt[s] += 16
        self._record(src, self.dcnt[s], reads, writes)
        if is_output:
            self.out_dmas.append((src, self.dcnt[s]))
        if not ring:
            self.aux[src] = self.dcnt[s]

    def barrier(self):
        cur = [((e, self.epoch[e]), self.cnt[e]) for e in self.eh if self.cnt[e] > 0]
        cur += list(self.aux.items())
        self.aux = {}
        for e in self.eh:
            for src, c in cur:
                if src[0] == e and e == "pe":
                    continue
                self._wait(e, src, c)

    def finish(self):
        for q in ("sp", "pool"):
            for s_, c in enumerate(self.dcnt):
                if c > 0:
                    self._wait(q, ("d", s_), c)
        self.barrier()


class _Stop(Exception):
    pass


def build_nc(debug=None, stop=None):
    nc = bass.Bass("TRN2", target_bir_lowering=False)

    def din(name, shape):
        return nc.dram_tensor(name, list(shape), F32, kind="ExternalInput").ap()

    def dout(name, shape):
        return nc.dram_tensor(name, list(shape), F32, kind="ExternalOutput").ap()

    xT_d = din("xT", [D, T])
    vecs_d = din("vecs", [128, NV])
    w_mod_d = din("w_mod", [NL, D, 6 * D])
    w_in_d = din("w_in", [NL, D, INW])
    w_br_d = din("w_branch", [NL, 3, 512, D])
    w_out_d = din("w_out", [NL, D, D])
    w_up_d = din("w_up", [NL, D, 2 * DFF])
    w_dn_d = din("w_down", [NL, DFF, D])
    pool_w_d = din("pool_w", [NL, 4, 128, 128])
    ctxk_d = din("ctxkT", [NL, 128, 8 * 256])
    ctxv_d = din("ctxv", [NL, 128, 2 * 512])
    utab_d = din("utab", [NL, 128, NH * NE * 64])
    rneg_d = din("rneg", [128, T])
    oh_d = din("oh", [128, 8 * 128])
    onesc_d = din("onesc", [128, 192])
    band_d = din("band", [128, 4 * 8 * 2 * 128])
    yT_d = dout("yT", [D, T])
    kvo_d = dout("kvo", [NL, T, 512])
    kvoT_d = dout("kvoT", [NL, 512, T])
    dbg_d = {}
    if debug:
        for name, (shape, dt_) in debug.items():
            dbg_d[name] = nc.dram_tensor("dbg_" + name, list(shape), dt_, kind="ExternalOutput").ap()

    es = ExitStack()
    with es:
        def sb(name, shape, dt):
            return es.enter_context(nc.sbuf_tensor(name, list(shape), dt))

        X = sb("X", [128, KC, T], F32)
        Hh = sb("Hh", [128, KC, T], BF16)
        AR = sb("AR", [128, 32256], BF16)
        UT = sb("UT", [128, 2, NE * 64], BF16)
        WR = sb("WR", [128, NSLOT, SLOT], BF16)
        SCR = sb("SCR", [128, 5632], F32)
        VEC = sb("VEC", [128, NV], F32)
        MOD = sb("MOD", [128, NL, 48], F32)
        GM = sb("GM", [128, NL * 2 * 8], F32)
        SMALL = sb("SMALL", [128, 256], F32)
        SCb = sb("SCb", [128, 8], BF16)
        RNEG = sb("RNEG", [128, T], BF16)
        OH = sb("OH", [128, 8 * 128], BF16)
        ONES = sb("ONES", [128, 128], BF16)
        ONESL = sb("ONESL", [128, 192], BF16)
        ONESC = sb("ONESC", [128, 192], BF16)
        CK = sb("CK", [128, 8 * 256], BF16)
        PS = [es.enter_context(nc.psum_tensor(f"ps{i}", [128, 512], F32)) for i in range(8)]

        S = Sched(nc, es)

        QK = AR[:, 0:4096].rearrange("p (c t) -> p c t", c=4)
        KZ = AR[:, 4096:12288].rearrange("p (h t) -> p h t", h=8)
        MERGED = AR[:, 0:8192].rearrange("p (c t) -> p c t", c=8)
        VP = AR[:, 12288:12288 + 7680].rearrange("p (k h c) -> p k h c", k=10, h=4)
        YS = AR[:, 19968:19968 + 12288].rearrange("p (c t) -> p c t", c=12)
        PZ = AR[:, 19968 + 4096:19968 + 8192].rearrange("p (b c) -> p b c", b=8)
        A = AR[:, 0:NJ * T].rearrange("p (j t) -> p j t", j=NJ)
        BAND = AR[:, 4096:12288]
        BANDv = BAND.rearrange("p (g v s c) -> p g v s c", g=4, v=8, s=2)
        CKv = CK[:, :].rearrange("p (h k) -> p h k", h=8)
        OHv = OH[:, :].rearrange("p (b k) -> p b k", b=8)
        GMv = GM[:, :].rearrange("p (l j c) -> p l j c", l=NL, j=2)

        def vcol(base, idx=0, n=1):
            return VEC[:, base + idx: base + idx + n]

        ps_ctr = [0]

        ps_excl = set()

        def ps_next():
            while True:
                i = ps_ctr[0] % 8
                ps_ctr[0] += 1
                if i not in ps_excl:
                    return PS[i], ("ps", i)

        STATB = (6, 7)
        SQA = None

        stat_pending = []
        STAT_READY = [False]

        def stat_begin():
            ps_excl.update(STATB)

        def stat_accum(c, half):
            i4 = (c * 2 + half) % 4
            sq = scr_bf16(5120 + 256 * (i4 % 2), 256)
            kq = ("SQA", i4 % 2)
            while len(stat_pending) >= 2:
                stat_pending.pop(0)()
            S.op("act", lambda e: e.activation(out=sq, in_=X[:, c, HS(half)], func=AF.Square),
                 reads=[("X", c, half)], writes=[kq])

            def pe_part():
                S.op("pe", lambda e: e.matmul(PS[STATB[half]][:, :], ONES[:, :], sq, start=(c == 0), stop=(c == KC - 1)),
                     reads=[kq, ("ONES",)], writes=[("ps", STATB[half])])
            stat_pending.append(pe_part)

        def stat_end():
            while stat_pending:
                stat_pending.pop(0)()
            ps_excl.difference_update(STATB)
            STAT_READY[0] = True

        def HS(half):
            return slice(half * 512, half * 512 + 512)

        def scr_f32(off, n):
            return SCR[:, off:off + n]

        def scr_bf16(off, n):
            return SCR[:, off:off + n].bitcast(BF16)

        loads = []

        class Ring:
            issued = 0
            consumed = 0

        def ring_use_many(n):
            idx = Ring.consumed
            target = min(len(loads), idx + NSLOT)
            while Ring.issued < target:
                i = Ring.issued
                loads[i](i % NSLOT)
                Ring.issued += 1
            Ring.consumed += n
            return [(idx + k) % NSLOT for k in range(n)]

        def ring_use():
            return ring_use_many(1)[0]

        def ring_prefetch():
            target = min(len(loads), Ring.consumed + NSLOT)
            while Ring.issued < target:
                i = Ring.issued
                loads[i](i % NSLOT)
                Ring.issued += 1

        def wkeys(slot):
            return [("WR", slot, 0), ("WR", slot, 1)]

        def ld_cols(src2d, col0, ncols, nkc=KC):
            def f(slot):
                dst = WR[:, slot, 0:nkc * ncols].rearrange("p (k n) -> p k n", k=nkc)
                src = src2d.rearrange("(k p) n -> p k n", p=128)[:, :, col0:col0 + ncols]
                S.dma("pool", dst, src, writes=wkeys(slot), ring=True)
            return f

        def ld_gate_branch(l, c):
            def f(slot):
                for i in range(3):
                    dst = WR[:, slot, i * 1024:(i + 1) * 1024].rearrange("p (k n) -> p k n", k=KC)
                    col0 = 3584 + i * 1024 + c * 128
                    src = w_in_d[l].rearrange("(k p) n -> p k n", p=128)[:, :, col0:col0 + 128]
                    S.dma("pool", dst, src, writes=[("WR", slot, 0)] if i == 0 else [("WRx", slot, i)], ring=True)
                for i in range(3):
                    dst = WR[:, slot, 3072 + i * 512:3072 + (i + 1) * 512].rearrange("p (k n) -> p k n", k=4)
                    src = w_br_d[l, i].rearrange("(k p) n -> p k n", p=128)[:, :, c * 128:(c + 1) * 128]
                    S.dma("pool", dst, src, writes=[("WR", slot, 1)] if i == 0 else [("WRy", slot, i)], ring=True)
            return f

        def gate_keys(slot):
            return [("WR", slot, 0), ("WRx", slot, 1), ("WRx", slot, 2), ("WR", slot, 1), ("WRy", slot, 1), ("WRy", slot, 2)]

        def ld_up(l, jp):
            def f(slot):
                dstv = WR[:, slot, 0:4096].rearrange("p (k two n) -> p k two n", k=KC, two=2)
                srcv = w_up_d[l].rearrange("(k p) (two n) -> p k two n", p=128, two=2)
                for t_ in range(2):
                    S.dma("pool", dstv[:, :, t_, :], srcv[:, :, t_, jp * 256:(jp + 1) * 256], writes=[("WR", slot, t_)], ring=True)
            return f

        def ld_down(l, c):
            def f(slot):
                dst = WR[:, slot, 0:NJ * 128].rearrange("p (k n) -> p k n", k=NJ)
                src = w_dn_d[l].rearrange("(k p) n -> p k n", p=128)[:, :, c * 128:(c + 1) * 128]
                S.dma("pool", dst, src, writes=wkeys(slot), ring=True)
            return f

        def ld_poolw(l):
            def f(slot):
                dst = WR[:, slot, 0:512].rearrange("p (g e) -> p g e", g=4)
                src = pool_w_d[l].rearrange("g c e -> c g e")
                S.dma("pool", dst, src, writes=wkeys(slot), ring=True)
            return f

        for l in range(NL):
            if l == 0:
                for g in range(4):
                    loads.append(ld_cols(w_mod_d[l], g * 512, 512))
            loads.append(ld_cols(w_in_d[l], 6 * 512, 512))
            loads.append(ld_poolw(l))
            for g in (3, 4, 5):
                loads.append(ld_cols(w_in_d[l], g * 512, 512))
            for g in (2, 1, 0):
                loads.append(ld_cols(w_in_d[l], g * 512, 512))
            for c in range(8):
                loads.append(ld_gate_branch(l, c))
                loads.append(ld_cols(w_mod_d[l], (4 + c) * 512, 512))
            for g in range(2):
                loads.append(ld_cols(w_out_d[l], g * 512, 512))
            for jp in range(11):
                loads.append(ld_up(l, jp))
            for c in range(8):
                loads.append(ld_down(l, c))
                if c < 4 and l + 1 < NL:
                    loads.append(ld_cols(w_mod_d[l + 1], c * 512, 512))

        def mm_group(ps_ap, ps_key, pairs, reads):
            n = len(pairs)
            for i, (lt, rh) in enumerate(pairs):
                S.op("pe", lambda e, lt=lt, rh=rh, i=i: e.matmul(ps_ap, lt, rh, start=(i == 0), stop=(i == n - 1)),
                     reads=reads if i == 0 else (), writes=[ps_key] if i == 0 else (), signal=(i == n - 1))

        def Hkeys(half):
            return [("H", kc, half) for kc in range(KC)]

        S.dma("sp", VEC[:, :], vecs_d, writes=[("VEC",)])
        for dc in range(KC):
            S.dma("sp", X[:, dc, :], xT_d[dc * 128:(dc + 1) * 128, :], writes=[("X", dc, 0), ("X", dc, 1)])
        ring_prefetch()
        S.dma("pool", RNEG[:, :], rneg_d, writes=[("RNEG",)])
        S.dma("pool", OH[:, :], oh_d, writes=[("OH",)])
        S.dma("pool", ONESC[:, :], onesc_d, writes=[("ONESC",)])
        S.op("dve", lambda e: e.memset(ONES[:, :], 1.0), writes=[("ONES",)])
        S.op("dve", lambda e: e.memset(ONESL[:, :], 1.0), writes=[("ONESL",)])
        S.op("dve", lambda e: e.memset(ONESL[:, 64:128], 0.0), writes=[("ONESL",)])
        S.op("dve", lambda e: e.memset(AR[:, 12288:12288 + 7680], 0.0), writes=[("VPall",)])
        S.op("act", lambda e: e.activation(out=SCb[:, :], in_=vcol(V_CV, 0, 8), func=AF.Silu),
             reads=[("VEC",)], writes=[("SCb",)])
        S.op("dve", lambda e: e.tensor_scalar(out=SMALL[:, 0:1], in0=vcol(V_BFLAG), scalar1=-1.0, scalar2=None,
                                              op0=ALU.mult), reads=[("VEC",)], writes=[("NBF",)])

        def mod_part(l, groups):
            for g in groups:
                slot = ring_use()
                ps, pk = ps_next()
                wv = WR[:, slot, 0:4096].rearrange("p (k n) -> p k n", k=KC)
                first = True
                for c in range(4):
                    col = g * 4 + c
                    for kc in range(KC):
                        S.op("pe", lambda e, c=c, kc=kc, col=col: e.matmul(
                            ps[:, col:col + 1], wv[:, kc, c * 128:(c + 1) * 128], SCb[:, kc:kc + 1],
                            start=(kc == 0), stop=(kc == KC - 1)),
                            reads=(wkeys(slot) + [("SCb",)]) if first else (), writes=[pk] if first else (),
                            signal=(c == 3 and kc == KC - 1))
                        first = False
                S.op("dve", lambda e, g=g: e.tensor_tensor(
                    out=MOD[:, l, g * 4:(g + 1) * 4], in0=ps[:, g * 4:(g + 1) * 4],
                    in1=VEC[:, V_BMOD + l * 48 + g * 4: V_BMOD + l * 48 + (g + 1) * 4], op=ALU.add),
                    reads=[pk, ("VEC",)], writes=[("MOD", l, g)])

        def gm_compute(l, j):
            sc0 = 8 if j == 0 else 32
            gb = (V_G1 if j == 0 else V_G2) + l * 8
            S.op("dve", lambda e: e.scalar_tensor_tensor(
                out=GMv[:, l, j, :], in0=MOD[:, l, sc0:sc0 + 8], scalar=1.0, in1=VEC[:, gb:gb + 8],
                op0=ALU.add, op1=ALU.mult),
                reads=[("MOD", l, sc0 // 4), ("MOD", l, sc0 // 4 + 1), ("VEC",)], writes=[("GM", l, j)])

        def norm(scale_ap_fn, bias_ap_fn, extra_reads, out_fn, out_keys_fn, hook=None, pre=False):
            SQ = [scr_bf16(0, 256), scr_bf16(256, 256)]
            SS = [scr_f32(512, 512), scr_f32(1024, 512)]
            RS = [scr_f32(1536, 512), scr_f32(2048, 512)]
            TMP = [scr_f32(2560, 512), scr_f32(3072, 512)]
            for half in range(2):
                if pre:
                    ps, pk = PS[STATB[half]], ("ps", STATB[half])
                else:
                    ps, pk = ps_next()
                for dc in range(KC if not pre else 0):
                    sq = SQ[dc % 2]
                    if dc % 2 == 0:
                        S.op("act", lambda e, dc=dc, sq=sq: e.activation(out=sq, in_=X[:, dc, HS(half)], func=AF.Square),
                             reads=[("X", dc, half)], writes=[("SQ", dc % 2)])
                    else:
                        S.op("dve", lambda e, dc=dc, sq=sq: e.tensor_tensor(out=sq, in0=X[:, dc, HS(half)], in1=X[:, dc, HS(half)], op=ALU.mult),
                             reads=[("X", dc, half)], writes=[("SQ", dc % 2)])
                    S.op("pe", lambda e, dc=dc, sq=sq: e.matmul(ps[:, :], ONES[:, :], sq, start=(dc == 0), stop=(dc == KC - 1)),
                         reads=[("SQ", dc % 2), ("ONES",)], writes=[pk])
                S.op("act", lambda e: e.activation(out=SS[half], in_=ps[:, :], func=AF.Sqrt, scale=1.0 / D, bias=1e-6),
                     reads=[pk], writes=[("SS", half)])
                S.op("dve", lambda e: e.reciprocal(out=RS[half], in_=SS[half]), reads=[("SS", half)], writes=[("RS", half)])
            if hook is not None:
                hook()
            for half in range(2):
                for dc in range(KC):
                    tmp = TMP[dc % 2]
                    S.op("dve", lambda e, dc=dc, tmp=tmp: e.tensor_tensor(out=tmp, in0=X[:, dc, HS(half)], in1=RS[half], op=ALU.mult),
                         reads=[("X", dc, half), ("RS", half)], writes=[("TMPn", dc % 2)])
                    b = bias_ap_fn(dc)
                    S.op("act", lambda e, dc=dc, tmp=tmp, b=b: e.activation(
                        out=out_fn(dc, half), in_=tmp, func=AF.Identity, scale=scale_ap_fn(dc),
                        **({"bias": b} if b is not None else {})),
                        reads=[("TMPn", dc % 2)] + extra_reads(), writes=out_keys_fn(dc, half))

        def stop_at(tag):
            if stop == tag:
                raise _Stop()

        def dbg_dump(name, ap, keys):
            if debug and name in dbg_d:
                S.barrier()
                S.dma("sp", dbg_d[name], ap, reads=keys, is_output=True)
                S.barrier()

        def run_layers():
          for l in range(NL):
            if l > 0:
                gm_compute(l, 0)
            S.dma("pool", CK[:, :].rearrange("p (a b) -> p a b", b=1024), ctxk_d[l].rearrange("p (a b) -> p a b", b=1024),
                  writes=[("CK",)])
            S.dma("pool", BAND.rearrange("p (a b) -> p a b", b=1024), band_d.rearrange("p (a b) -> p a b", b=1024),
                  writes=[("BAND",)])
            for j in range(2):
                dst = VP[:, 8:10, :, j * 128:j * 128 + 64]
                src = ctxv_d[l].rearrange("p (k h j d) -> p k h j d", k=2, h=4, j=2)[:, :, :, j, :]
                S.dma("pool", dst, src, writes=[("VPc", j)], reads=[("VPall",)])

            S.phase = f"L{l}:norm1"
            def l0_hook():
                mod_part(0, range(0, 4))
                gm_compute(0, 0)
            norm(lambda dc: GMv[:, l, 0, dc:dc + 1], lambda dc: MOD[:, l, dc:dc + 1],
                 lambda: [("GM", l, 0), ("MOD", l, 0), ("MOD", l, 1)],
                 lambda dc, half: Hh[:, dc, HS(half)], lambda dc, half: [("H", dc, half)],
                 hook=l0_hook if l == 0 else None, pre=(l > 0))
            S.barrier()
            if l == 0:
                stop_at("norm1")
            if l == 0:
                dbg_dump("h1", Hh[:, :, :], [("H", dc, hf) for dc in range(KC) for hf in range(2)])

            if l == 0:
                stop_at("A_pz0")
            slot = ring_use()
            wv = WR[:, slot, 0:4096].rearrange("p (k n) -> p k n", k=KC)
            for tb in range(8):
                ps, pk = ps_next()
                mm_group(ps[:, :], pk, [(Hh[:, kc, tb * 128:(tb + 1) * 128], wv[:, kc, :]) for kc in range(KC)],
                         wkeys(slot) + Hkeys(tb // 4))
                S.op("act", lambda e, ps=ps, tb=tb: e.activation(out=PZ[:, tb, :], in_=ps[:, :], func=AF.Copy),
                     reads=[pk], writes=[("PZ", tb)])

            if l == 0:
                stop_at("A_pz")
            S.phase = f"L{l}:A_pool"
            slot = ring_use()
            pwv = WR[:, slot, 0:512].rearrange("p (g e) -> p g e", g=4)
            DT = [scr_bf16(0, 256), scr_bf16(256, 256)]
            for g in range(4):
                for half in range(2):
                    ps, pk = ps_next()
                    first = True
                    for obi in range(4):
                        ob = half * 4 + obi
                        terms = []
                        dv = 0 if ob == 0 else (1 if ob == 7 else (2 if ob % 2 == 0 else 3))
                        terms.append((ob, dv))
                        if ob >= 1:
                            terms.append((ob - 1, 4 if ob % 2 == 1 else 5))
                        if ob <= 6:
                            terms.append((ob + 1, 6 if ob % 2 == 0 else 7))
                        mats = [(ib, v, s) for (ib, v) in terms for s in range(2)]
                        for i, (ib, v, s) in enumerate(mats):
                            last = (obi == 3 and i == len(mats) - 1)
                            S.op("pe", lambda e, ib=ib, v=v, s=s, i=i, obi=obi, n=len(mats): e.matmul(
                                ps[:, obi * 128:(obi + 1) * 128], PZ[:, ib, g * 128:(g + 1) * 128], BANDv[:, g, v, s, :],
                                start=(i == 0), stop=(i == n - 1)),
                                reads=([("PZ", t) for t in range(8)] + [("BAND",)]) if first else (),
                                writes=[pk] if first else (), signal=last)
                            first = False
                    dt = DT[(g * 2 + half) % 2]
                    S.op("act", lambda e, ps=ps, dt=dt: e.activation(out=dt, in_=ps[:, :], func=AF.Copy),
                         reads=[pk], writes=[("DT", (g * 2 + half) % 2)])
                    ps2, pk2 = ps_next()
                    mm_group(ps2[:, :], pk2, [(pwv[:, g, :], dt)], wkeys(slot) + [("DT", (g * 2 + half) % 2)])
                    S.op("dve", lambda e, ps2=ps2, g=g, half=half: e.tensor_scalar(
                        out=YS[:, 8 + g, HS(half)], in0=ps2[:, :], scalar1=vcol(V_PSCALE, l * 4 + g), scalar2=None, op0=ALU.mult),
                        reads=[pk2, ("VEC",)], writes=[("YS", 8 + g, half)])

            if l == 0:
                stop_at("A_pool")
            S.op("dve", lambda e: e.memset(AR[:, 4096:12288], 0.0), writes=[("KZall",), ("BAND",)])
            S.phase = f"L{l}:A_qkv"
            KST = [scr_f32(4096, 512), scr_f32(4608, 512)]
            for which in range(2):
                slot = ring_use()
                wv = WR[:, slot, 0:4096].rearrange("p (k n) -> p k n", k=KC)
                for c in range(4):
                    for half in range(2):
                        ps, pk = ps_next()
                        mm_group(ps[:, :], pk, [(wv[:, kc, c * 128:(c + 1) * 128], Hh[:, kc, HS(half)]) for kc in range(KC)],
                                 wkeys(slot) + Hkeys(half))
                        if which == 0:
                            S.op("act", lambda e, c=c, half=half, ps=ps: e.activation(
                                out=QK[:, c, HS(half)], in_=ps[:, :], func=AF.Copy, scale=0.125),
                                reads=[pk], writes=[("QK", c, half)])
                        else:
                            st = KST[(c * 2 + half) % 2]
                            sk_ = ("KST", (c * 2 + half) % 2)
                            S.op("act", lambda e, ps=ps, st=st: e.activation(out=st, in_=ps[:, :], func=AF.Copy),
                                 reads=[pk], writes=[sk_])
                            S.dma("sp", kvoT_d[l, c * 128:(c + 1) * 128, HS(half)], st, reads=[sk_], is_output=True)
                            for jj in range(2):
                                S.op("dve", lambda e, c=c, half=half, st=st, jj=jj: e.tensor_copy(
                                    out=KZ[jj * 64:(jj + 1) * 64, 2 * c + jj, HS(half)], in_=st[jj * 64:(jj + 1) * 64, :]),
                                    reads=[sk_, ("KZall",)], writes=[("KZ", 2 * c + jj, half)])
            if l == 0:
                stop_at("A_qk")
            slot = ring_use()
            wv = WR[:, slot, 0:4096].rearrange("p (k n) -> p k n", k=KC)
            for tb in range(8):
                ps, pk = ps_next()
                mm_group(ps[:, :], pk, [(Hh[:, kc, tb * 128:(tb + 1) * 128], wv[:, kc, :]) for kc in range(KC)],
                         wkeys(slot) + Hkeys(tb // 4))
                st = KST[tb % 2]
                S.op("act", lambda e, ps=ps, st=st: e.activation(out=st, in_=ps[:, :], func=AF.Copy),
                     reads=[pk], writes=[("KST", tb % 2)])
                S.dma("sp", kvo_d[l, tb * 128:(tb + 1) * 128, :], st, reads=[("KST", tb % 2)], is_output=True)
                for j in range(2):
                    S.op("dve", lambda e, st=st, tb=tb, j=j: e.tensor_copy(
                        out=VP[:, tb, :, j * 128:j * 128 + 64],
                        in_=st.rearrange("p (h j d) -> p h j d", h=4, j=2)[:, :, j, :]),
                        reads=[("KST", tb % 2), ("VPall",)], writes=[("VP", tb, j)])
            S.phase = f"L{l}:A_conv"
            slot_h, slot_c, slot_b = ring_use_many(3)
            wh = WR[:, slot_h, 0:4096].rearrange("p (k n) -> p k n", k=KC)
            wc = WR[:, slot_c, 0:4096].rearrange("p (k n) -> p k n", k=KC)
            wb = WR[:, slot_b, 0:4096].rearrange("p (k n) -> p k n", k=KC)
            HC = [scr_f32(512, 512), scr_f32(1024, 512)]
            CH = scr_f32(1536, 1026)
            YC = scr_f32(2562, 1024)
            S.op("dve", lambda e: e.memset(CH[:, 0:1], 0.0), writes=[("CH", 0), ("CH", 1)])
            S.op("dve", lambda e: e.memset(CH[:, 1025:1026], 0.0), writes=[("CHp",)])
            for c in range(4):
                cwb = V_CONVW + (l * 4 + c) * 3
                S.op("dve", lambda e, cwb=cwb: e.tensor_scalar(out=SMALL[:, 1:2], in0=vcol(cwb, 0), scalar1=SMALL[:, 0:1],
                                                              scalar2=None, op0=ALU.mult),
                     reads=[("VEC",), ("NBF",)], writes=[("FX", 0)])
                S.op("dve", lambda e, cwb=cwb: e.tensor_scalar(out=SMALL[:, 2:3], in0=vcol(cwb, 2), scalar1=SMALL[:, 0:1],
                                                              scalar2=None, op0=ALU.mult),
                     reads=[("VEC",), ("NBF",)], writes=[("FX", 1)])
                for half in range(2):
                    ps1, pk1 = ps_next()
                    mm_group(ps1[:, :], pk1, [(wh[:, kc, c * 128:(c + 1) * 128], Hh[:, kc, HS(half)]) for kc in range(KC)],
                             wkeys(slot_h) + Hkeys(half))
                    hc = HC[half]
                    S.op("act", lambda e, ps1=ps1, hc=hc: e.activation(out=hc, in_=ps1[:, :], func=AF.Copy),
                         reads=[pk1], writes=[("HC", half)])
                    ps2, pk2 = ps_next()
                    mm_group(ps2[:, :], pk2, [(wc[:, kc, c * 128:(c + 1) * 128], Hh[:, kc, HS(half)]) for kc in range(KC)],
                             wkeys(slot_c) + Hkeys(half))
                    S.op("dve", lambda e, ps2=ps2, hc=hc, half=half: e.tensor_tensor(
                        out=CH[:, 1 + half * 512: 1 + half * 512 + 512], in0=ps2[:, :], in1=hc, op=ALU.mult),
                        reads=[pk2, ("HC", half)], writes=[("CH", half)])
                chk = [("CH", 0), ("CH", 1), ("CHp",)]
                S.op("act", lambda e, cwb=cwb: e.activation(out=YC[:, :], in_=CH[:, 1:1025], func=AF.Identity, scale=vcol(cwb, 1)),
                     reads=chk + [("VEC",)], writes=[("YC",)])
                S.op("dve", lambda e, cwb=cwb: e.scalar_tensor_tensor(
                    out=YC[:, :], in0=CH[:, 0:1024], scalar=vcol(cwb, 0), in1=YC[:, :], op0=ALU.mult, op1=ALU.add),
                    reads=chk + [("VEC",), ("YC",)], writes=[("YC",)])
                S.op("dve", lambda e, cwb=cwb: e.scalar_tensor_tensor(
                    out=YC[:, :], in0=CH[:, 2:1026], scalar=vcol(cwb, 2), in1=YC[:, :], op0=ALU.mult, op1=ALU.add),
                    reads=chk + [("VEC",), ("YC",)], writes=[("YC",)])
                ycv = YC[:, :].rearrange("p (s t) -> p s t", s=4)
                chv = CH[:, 1:1025].rearrange("p (s t) -> p s t", s=4)
                S.op("dve", lambda e, ycv=ycv, chv=chv: e.scalar_tensor_tensor(
                    out=ycv[:, 1:4, 0:1], in0=chv[:, 0:3, 255:256], scalar=SMALL[:, 1:2], in1=ycv[:, 1:4, 0:1],
                    op0=ALU.mult, op1=ALU.add), reads=chk + [("FX", 0), ("YC",)], writes=[("YC",)])
                S.op("dve", lambda e, ycv=ycv, chv=chv: e.scalar_tensor_tensor(
                    out=ycv[:, 0:3, 255:256], in0=chv[:, 1:4, 0:1], scalar=SMALL[:, 2:3], in1=ycv[:, 0:3, 255:256],
                    op0=ALU.mult, op1=ALU.add), reads=chk + [("FX", 1), ("YC",)], writes=[("YC",)])
                for half in range(2):
                    ps3, pk3 = ps_next()
                    mm_group(ps3[:, :], pk3, [(wb[:, kc, c * 128:(c + 1) * 128], Hh[:, kc, HS(half)]) for kc in range(KC)],
                             wkeys(slot_b) + Hkeys(half))
                    S.op("dve", lambda e, ps3=ps3, half=half, c=c: e.tensor_tensor(
                        out=YS[:, c, HS(half)], in0=ps3[:, :], in1=YC[:, HS(half)], op=ALU.mult),
                        reads=[pk3, ("YC",)], writes=[("YS", c, half)])

            S.barrier()
            if l == 0:
                stop_at("phaseA")
            if l == 0:
                dbg_dump("qk", AR[:, 0:8192], [])
                dbg_dump("ysA", AR[:, 19968:19968 + 12288], [])

            S.phase = f"L{l}:B_attn"
            TT_ = [scr_f32(0 + i * 512, 512) for i in range(3)]
            PP = [scr_bf16(1536 + i * 256, 256) for i in range(4)]
            RD = [scr_f32(3072 + i * 512, 512) for i in range(2)]
            work = []
            unit = 0
            for hp in range(4):
                for qc in range(2):
                    tiles = [("c", 8), ("c", 9)] + [("l", b_) for b_ in LOCAL_TILES[qc]]
                    ntot = 2 * len(tiles)
                    it = 0
                    for j in range(2):
                        for kind, b_ in tiles:
                            if kind == "l":
                                lo, hi = QROWS[b_]
                                lo, hi = max(lo, 8 * qc), min(hi, 8 * qc + 7)
                                c0_, c1_ = (lo - 8 * qc) * 64, (hi + 1 - 8 * qc) * 64
                            else:
                                c0_, c1_ = 0, 512
                            work.append(dict(hp=hp, qc=qc, j=j, kind=kind, b=b_, unit=unit, it=it, ntot=ntot,
                                             c0=c0_, c1=c1_, first_of_head=(b_ == 8 and kind == "c")))
                            it += 1
                    unit += 1
            LA = 3

            def ut_load(h):
                S.dma("pool", UT[:, h % 2, :], utab_d[l][:, h * NE * 64:(h + 1) * NE * 64], writes=[("UT", h % 2)])

            def emit_S(k):
                w = work[k]
                hp, qc, j, b_ = w["hp"], w["qc"], w["j"], w["b"]
                si = k % 4
                Sps, sk = PS[si], ("ps", si)
                h_ = 2 * hp + j
                qsrc = QK[:, hp, HS(qc)]
                if w["kind"] == "l":
                    ksrc = KZ[:, h_, b_ * 128:(b_ + 1) * 128]
                    c0, c1 = w["c0"], w["c1"]
                    n = c1 - c0
                    qs = QK[:, hp, qc * 512 + c0:qc * 512 + c1]
                    S.op("pe", lambda e: e.matmul(Sps[:, 0:n], ksrc, qs, start=True, stop=False),
                         reads=[("KZ", h_, b_ // 4), ("KZall",), ("QK", hp, qc)], writes=[sk], signal=False)
                    S.op("pe", lambda e: e.matmul(Sps[:, 0:n], OHv[:, b_, :], RNEG[:, qc * 512 + c0:qc * 512 + c1], start=False, stop=True),
                         reads=[("OH",), ("RNEG",)], writes=[sk])
                else:
                    ksrc = CKv[:, h_, (b_ - 8) * 128:(b_ - 7) * 128]
                    S.op("pe", lambda e: e.matmul(Sps[:, :], ksrc, qsrc, start=True, stop=True),
                         reads=[("CK",), ("QK", hp, qc)], writes=[sk])

            def emit_rest(k):
                w = work[k]
                hp, qc, j, b_, it, ntot = w["hp"], w["qc"], w["j"], w["b"], w["it"], w["ntot"]
                u = w["unit"]
                h = 2 * hp + j
                NUM, nk = PS[4 + (u % 2) * 2], ("ps", 4 + (u % 2) * 2)
                DEN, dk = PS[5 + (u % 2) * 2], ("ps", 5 + (u % 2) * 2)
                si = k % 4
                Sps, sk = PS[si], ("ps", si)
                pi = k % 4
                pt = PP[pi]
                if w["first_of_head"] and qc == 1 and j == 1 and hp < 3:
                    ut_load(2 * (hp + 1))
                if w["first_of_head"] and qc == 0 and j == 0 and hp > 0:
                    ut_load(2 * hp + 1)
                c0, c1 = w["c0"], w["c1"]
                n = c1 - c0
                if w["kind"] == "l":
                    e0 = 8 * qc + c0 // 64 - 2 * b_ + E0
                    ti = k % 3
                    tt = TT_[ti]
                    S.op("dve", lambda e: e.tensor_tensor(out=tt[:, 0:n], in0=Sps[:, 0:n], in1=UT[:, h % 2, e0 * 64:e0 * 64 + n], op=ALU.add),
                         reads=[sk, ("UT", h % 2)], writes=[("TT", ti)])
                    S.op("act", lambda e: e.activation(out=pt[:, 0:n], in_=tt[:, 0:n], func=AF.Exp),
                         reads=[("TT", ti)], writes=[("PP", pi)])
                    vkeys = [("VP", b_, 0), ("VP", b_, 1), ("VPall",)]
                    osrc, okeys = ONESL[:, j * 64:j * 64 + 128], [("ONESL",)]
                else:
                    S.op("act", lambda e: e.activation(out=pt, in_=Sps[:, :], func=AF.Exp),
                         reads=[sk], writes=[("PP", pi)])
                    vkeys = [("VPc", 0), ("VPc", 1), ("VPall",)]
                    osrc, okeys = ONESC[:, j * 64:j * 64 + 128], [("ONESC",)]
                vsrc = VP[:, b_, hp, j * 64:j * 64 + 128]
                S.op("pe", lambda e: e.matmul(NUM[:, c0:c1], vsrc, pt[:, 0:n], start=(it == 0), stop=(it == ntot - 1),
                                              skip_group_check=True),
                     reads=[("PP", pi)] + vkeys, writes=[nk], signal=False)
                S.op("pe", lambda e: e.matmul(DEN[:, c0:c1], osrc, pt[:, 0:n], start=(it == 0), stop=(it == ntot - 1),
                                              skip_group_check=True),
                     reads=[("PP", pi)] + okeys, writes=[dk])
                if it == ntot - 1:
                    rd = RD[u % 2]
                    S.op("dve", lambda e: e.reciprocal(out=rd, in_=DEN[:, :]), reads=[dk], writes=[("RD", u % 2)])
                    S.op("dve", lambda e: e.tensor_tensor(out=YS[:, 4 + hp, HS(qc)], in0=NUM[:, :], in1=rd, op=ALU.mult),
                         reads=[nk, dk, ("RD", u % 2)] + [("PZ", t) for t in range(8)], writes=[("YS", 4 + hp, qc)])

            ut_load(0)
            ut_load(1)
            for k in range(min(LA, len(work))):
                emit_S(k)
            for k in range(len(work)):
                if k + LA < len(work):
                    emit_S(k + LA)
                emit_rest(k)
            S.barrier()
            ps_ctr[0] = 0
            if l == 0:
                stop_at("attn")
            if l == 0:
                dbg_dump("ysB", AR[:, 19968:19968 + 12288], [])

            S.phase = f"L{l}:C_gates"
            GT = [scr_f32(i * 512, 512) for i in range(6)]
            MM = [scr_f32(3072 + i * 512, 512) for i in range(4)]
            gctr = [0]
            mctr = [0]
            for c in range(8):
                slot = ring_use()
                gw = WR[:, slot, 0:3072].rearrange("p (i k n) -> p i k n", i=3, k=KC)
                bw = WR[:, slot, 3072:3072 + 1536].rearrange("p (i k n) -> p i k n", i=3, k=4)
                for half in range(2):
                    prods = []
                    for i in range(3):
                        psg, pkg = ps_next()
                        mm_group(psg[:, :], pkg, [(gw[:, i, kc, :], Hh[:, kc, HS(half)]) for kc in range(KC)],
                                 gate_keys(slot) + Hkeys(half))
                        gi = gctr[0] % 6
                        gctr[0] += 1
                        gt = GT[gi]
                        S.op("act", lambda e, psg=psg, gt=gt: e.activation(out=gt, in_=psg[:, :], func=AF.Sigmoid),
                             reads=[pkg], writes=[("GT", gi)])
                        psp, pkp = ps_next()
                        mm_group(psp[:, :], pkp, [(bw[:, i, kc, :], YS[:, 4 * i + kc, HS(half)]) for kc in range(4)],
                                 gate_keys(slot) + [("YS", 4 * i + kc, half) for kc in range(4)])
                        mi = mctr[0] % 4
                        mctr[0] += 1
                        mt = MM[mi]
                        S.op("dve", lambda e, psp=psp, gt=gt, mt=mt: e.tensor_tensor(out=mt, in0=psp[:, :], in1=gt, op=ALU.mult),
                             reads=[pkp, ("GT", gi)], writes=[("MM", mi)])
                        prods.append((mt, mi))
                    (m0, k0), (m1, k1), (m2, k2) = prods
                    S.op("pool", lambda e, m0=m0, m1=m1: e.tensor_tensor(out=m0, in0=m0, in1=m1, op=ALU.add),
                         reads=[("MM", k0), ("MM", k1)], writes=[("MM", k0)])
                    S.op("pool", lambda e, m0=m0, m2=m2, c=c, half=half: e.tensor_tensor(out=MERGED[:, c, HS(half)], in0=m0, in1=m2, op=ALU.add),
                         reads=[("MM", k0), ("MM", k2)], writes=[("MG", c, half)])
                mod_part(l, [4 + c])
            if l == 0:
                stop_at("phaseC")
            if l == 0:
                dbg_dump("merged", AR[:, 0:8192], [("MG", c, hf) for c in range(8) for hf in range(2)])

            S.phase = f"L{l}:D_mod_wout"
            gm_compute(l, 1)
            stat_begin()
            for g in range(2):
                slot = ring_use()
                wv = WR[:, slot, 0:4096].rearrange("p (k n) -> p k n", k=KC)
                for cc in range(4):
                    c = g * 4 + cc
                    for half in range(2):
                        ps, pk = ps_next()
                        mm_group(ps[:, :], pk, [(wv[:, kc, cc * 128:(cc + 1) * 128], MERGED[:, kc, HS(half)]) for kc in range(KC)],
                                 wkeys(slot) + [("MG", kc, half) for kc in range(KC)])
                        S.op("dve", lambda e, ps=ps, c=c, half=half: e.scalar_tensor_tensor(
                            out=X[:, c, HS(half)], in0=ps[:, :], scalar=MOD[:, l, 16 + c:17 + c], in1=X[:, c, HS(half)],
                            op0=ALU.mult, op1=ALU.add),
                            reads=[pk, ("MOD", l, 4 + c // 4), ("X", c, half)], writes=[("X", c, half)])
                        stat_accum(c, half)
            stat_end()
            if l == 0:
                stop_at("x1")
            if l == 0:
                dbg_dump("x1", X[:, :, :], [("X", dc, hf) for dc in range(KC) for hf in range(2)])

            S.phase = f"L{l}:E_norm2"
            S.barrier()
            norm(lambda dc: GMv[:, l, 1, dc:dc + 1], lambda dc: MOD[:, l, 24 + dc:25 + dc],
                 lambda: [("GM", l, 1), ("MOD", l, 6), ("MOD", l, 7)],
                 lambda dc, half: Hh[:, dc, HS(half)], lambda dc, half: [("H", dc, half)], pre=True)
            S.barrier()

            S.phase = f"L{l}:F_up"
            U32 = scr_f32(0, 1026)
            YF = scr_f32(1026, 1024)
            GF = scr_f32(2050, 1024)
            S.op("dve", lambda e: e.memset(U32[:, 0:1], 0.0), writes=[("U32", 0), ("U32", 1)])
            S.op("dve", lambda e: e.memset(U32[:, 1025:1026], 0.0), writes=[("U32p",)])
            for jp in range(11):
                slot = ring_use()
                wv = WR[:, slot, 0:4096].rearrange("p (k two n) -> p k two n", k=KC, two=2)
                for jj in range(2):
                    jx = jp * 2 + jj
                    fwb = V_FCONV + (l * NJ + jx) * 3
                    S.op("dve", lambda e, fwb=fwb: e.tensor_scalar(out=SMALL[:, 1:2], in0=vcol(fwb, 0), scalar1=SMALL[:, 0:1],
                                                                  scalar2=None, op0=ALU.mult),
                         reads=[("VEC",), ("NBF",)], writes=[("FX", 0)])
                    S.op("dve", lambda e, fwb=fwb: e.tensor_scalar(out=SMALL[:, 2:3], in0=vcol(fwb, 2), scalar1=SMALL[:, 0:1],
                                                                  scalar2=None, op0=ALU.mult),
                         reads=[("VEC",), ("NBF",)], writes=[("FX", 1)])
                    for half in range(2):
                        ps, pk = ps_next()
                        mm_group(ps[:, :], pk, [(wv[:, kc, 0, jj * 128:(jj + 1) * 128], Hh[:, kc, HS(half)]) for kc in range(KC)],
                                 wkeys(slot) + Hkeys(half))
                        S.op("act", lambda e, ps=ps, half=half: e.activation(out=U32[:, 1 + half * 512:1 + half * 512 + 512], in_=ps[:, :], func=AF.Copy),
                             reads=[pk], writes=[("U32", half)])
                    uk = [("U32", 0), ("U32", 1), ("U32p",)]
                    S.op("act", lambda e, fwb=fwb: e.activation(out=YF[:, :], in_=U32[:, 1:1025], func=AF.Identity, scale=vcol(fwb, 1)),
                         reads=uk + [("VEC",)], writes=[("YF",)])
                    S.op("dve", lambda e, fwb=fwb: e.scalar_tensor_tensor(
                        out=YF[:, :], in0=U32[:, 0:1024], scalar=vcol(fwb, 0), in1=YF[:, :], op0=ALU.mult, op1=ALU.add),
                        reads=uk + [("VEC",), ("YF",)], writes=[("YF",)])
                    S.op("dve", lambda e, fwb=fwb: e.scalar_tensor_tensor(
                        out=YF[:, :], in0=U32[:, 2:1026], scalar=vcol(fwb, 2), in1=YF[:, :], op0=ALU.mult, op1=ALU.add),
                        reads=uk + [("VEC",), ("YF",)], writes=[("YF",)])
                    yfv = YF[:, :].rearrange("p (s t) -> p s t", s=4)
                    uv = U32[:, 1:1025].rearrange("p (s t) -> p s t", s=4)
                    S.op("dve", lambda e, yfv=yfv, uv=uv: e.scalar_tensor_tensor(
                        out=yfv[:, 1:4, 0:1], in0=uv[:, 0:3, 255:256], scalar=SMALL[:, 1:2], in1=yfv[:, 1:4, 0:1],
                        op0=ALU.mult, op1=ALU.add), reads=uk + [("FX", 0), ("YF",)], writes=[("YF",)])
                    S.op("dve", lambda e, yfv=yfv, uv=uv: e.scalar_tensor_tensor(
                        out=yfv[:, 0:3, 255:256], in0=uv[:, 1:4, 0:1], scalar=SMALL[:, 2:3], in1=yfv[:, 0:3, 255:256],
                        op0=ALU.mult, op1=ALU.add), reads=uk + [("FX", 1), ("YF",)], writes=[("YF",)])
                    S.op("act", lambda e: e.activation(out=GF[:, :], in_=YF[:, :], func=AF.Gelu_apprx_tanh),
                         reads=[("YF",)], writes=[("GF",)])
                    for half in range(2):
                        ps, pk = ps_next()
                        mm_group(ps[:, :], pk, [(wv[:, kc, 1, jj * 128:(jj + 1) * 128], Hh[:, kc, HS(half)]) for kc in range(KC)],
                                 wkeys(slot) + Hkeys(half))
                        S.op("dve", lambda e, ps=ps, half=half, jx=jx: e.tensor_tensor(
                            out=A[:, jx, HS(half)], in0=ps[:, :], in1=GF[:, HS(half)], op=ALU.mult),
                            reads=[pk, ("GF",)], writes=[("A", jx, half)])
            if l == 0:
                stop_at("ffnup")
            if l == 0:
                dbg_dump("a", AR[:, 0:NJ * T], [("A", j, hf) for j in range(NJ) for hf in range(2)])

            S.phase = f"L{l}:G_down"
            stat_begin()
            for c in range(8):
                slot = ring_use()
                wv = WR[:, slot, 0:NJ * 128].rearrange("p (k n) -> p k n", k=NJ)
                for half in range(2):
                    ps, pk = ps_next()
                    mm_group(ps[:, :], pk, [(wv[:, j, :], A[:, j, HS(half)]) for j in range(NJ)],
                             wkeys(slot) + [("A", j, half) for j in range(NJ)])
                    S.op("dve", lambda e, ps=ps, c=c, half=half: e.scalar_tensor_tensor(
                        out=X[:, c, HS(half)], in0=ps[:, :], scalar=MOD[:, l, 40 + c:41 + c], in1=X[:, c, HS(half)],
                        op0=ALU.mult, op1=ALU.add),
                        reads=[pk, ("MOD", l, 10 + c // 4), ("X", c, half)], writes=[("X", c, half)])
                    stat_accum(c, half)
                if c < 4 and l + 1 < NL:
                    mod_part(l + 1, [c])
            stat_end()
            S.barrier()
            if l + 1 < NL:
                S.op("dve", lambda e: e.memset(AR[:, 12288:12288 + 7680], 0.0), writes=[("VPall",)])
            if l == 0:
                stop_at("layer0")
            if l == 0:
                dbg_dump("x2", X[:, :, :], [("X", dc, hf) for dc in range(KC) for hf in range(2)])

        try:
            run_layers()
        except _Stop:
            pass
        S.barrier()
        S.phase = 'final'
        OUTB = [scr_f32(3584, 512), scr_f32(4096, 512), scr_f32(4608, 512), scr_f32(5120, 512)]
        octr = [0]

        def out_fn(dc, half):
            i = octr[0] % 4
            return OUTB[i]

        def out_keys(dc, half):
            return [("OUTB", octr[0] % 4)]

        SQ = [scr_bf16(0, 256), scr_bf16(256, 256)]
        SS = [scr_f32(512, 512), scr_f32(1024, 512)]
        RS = [scr_f32(1536, 512), scr_f32(2048, 512)]
        TMP = [scr_f32(2560, 512), scr_f32(3072, 512)]
        final_pre = STAT_READY[0] and stop is None
        for half in range(2):
            if final_pre:
                ps, pk = PS[STATB[half]], ("ps", STATB[half])
            else:
                ps, pk = ps_next()
            for dc in range(0 if final_pre else KC):
                sq = SQ[dc % 2]
                S.op("act", lambda e, dc=dc, sq=sq: e.activation(out=sq, in_=X[:, dc, HS(half)], func=AF.Square),
                     reads=[("X", dc, half)], writes=[("SQ", dc % 2)])
                S.op("pe", lambda e, dc=dc, sq=sq: e.matmul(ps[:, :], ONES[:, :], sq, start=(dc == 0), stop=(dc == KC - 1)),
                     reads=[("SQ", dc % 2), ("ONES",)], writes=[pk])
            S.op("act", lambda e: e.activation(out=SS[half], in_=ps[:, :], func=AF.Sqrt, scale=1.0 / D, bias=1e-6),
                 reads=[pk], writes=[("SS", half)])
            S.op("dve", lambda e: e.reciprocal(out=RS[half], in_=SS[half]), reads=[("SS", half)], writes=[("RS", half)])
            for dc in range(KC):
                tmp = TMP[dc % 2]
                S.op("dve", lambda e, dc=dc, tmp=tmp: e.tensor_tensor(out=tmp, in0=X[:, dc, HS(half)], in1=RS[half], op=ALU.mult),
                     reads=[("X", dc, half), ("RS", half)], writes=[("TMPn", dc % 2)])
                oi = octr[0] % 4
                octr[0] += 1
                ob = OUTB[oi]
                S.op("act", lambda e, dc=dc, tmp=tmp, ob=ob: e.activation(out=ob, in_=tmp, func=AF.Identity, scale=vcol(V_GF, dc)),
                     reads=[("TMPn", dc % 2), ("VEC",)], writes=[("OUTB", oi)])
                S.dma("sp", yT_d[dc * 128:(dc + 1) * 128, HS(half)], ob, reads=[("OUTB", oi)], is_output=True)
        S.finish()
        build_nc.stats = dict(S.nops)
        build_nc.phase_of = dict(S.phase_of)
    return nc


def _bf16_split(a):
    hi = a.astype(ml_dtypes.bfloat16).astype(np.float32)
    lo = (a - hi).astype(ml_dtypes.bfloat16).astype(np.float32)
    return hi, lo


def _band_tables(seq_len):
    out = np.zeros((128, 4, 8, 2, 128), np.float32)
    t = np.arange(T)
    for g, w in enumerate((2, 4, 8, 16)):
        B = np.zeros((T, T), np.float64)
        s0 = (t // seq_len) * seq_len
        lo = np.clip(t - w // 2, s0, s0 + seq_len)
        hi = np.clip(t - w // 2 + w, s0, s0 + seq_len)
        for tt in range(T):
            B[lo[tt]:hi[tt], tt] = 1.0 / (hi[tt] - lo[tt])
            B[tt, tt] -= 1.0
        B = B.astype(np.float32)

        def blk(ib, ob):
            return B[ib * 128:(ib + 1) * 128, ob * 128:(ob + 1) * 128]
        variants = [blk(0, 0), blk(7, 7), blk(2, 2), blk(1, 1), blk(0, 1), blk(1, 2), blk(1, 0), blk(2, 1)]
        for ob in range(8):
            dv = 0 if ob == 0 else (1 if ob == 7 else (2 if ob % 2 == 0 else 3))
            assert np.array_equal(blk(ob, ob), variants[dv])
            if ob >= 1:
                assert np.array_equal(blk(ob - 1, ob), variants[4 if ob % 2 == 1 else 5])
            if ob <= 6:
                assert np.array_equal(blk(ob + 1, ob), variants[6 if ob % 2 == 0 else 7])
        for v, m in enumerate(variants):
            h_, l_ = _bf16_split(m)
            out[:, g, v, 0, :] = h_
            out[:, g, v, 1, :] = l_
    return out.reshape(128, -1)


def _rneg_table(sample):
    r = np.full((16, 16), NEG, np.float32)
    for rk in range(16):
        for q in range(16):
            if sample:
                s = min(max(q - 4, 0), 8)
                ok = s <= rk < s + 8
            else:
                ok = (rk // 4) == (q // 4)
            if ok:
                r[rk, q] = 0.0
    out = np.zeros((128, T), np.float32)
    out[0:16] = np.repeat(r, 64, axis=1)
    return out


def _onehot_rows():
    oh = np.zeros((16, 8, 128), np.float32)
    for b in range(8):
        oh[2 * b, b, 0:64] = 1.0
        oh[2 * b + 1, b, 64:128] = 1.0
    out = np.zeros((128, 1024), np.float32)
    out[0:16] = oh.reshape(16, 1024)
    return out


def _u_table(rpb_l):
    U = np.full((128, NH, NE, 64), NEG, np.float32)
    cq = np.arange(64)
    cs = np.clip(cq - 8, 0, 48)
    for jj in range(2):
        for ei in range(NE):
            dr = jj - (ei - E0)
            if abs(dr) > 7:
                continue
            for ck in range(64):
                valid = (ck >= cs) & (ck < cs + 16)
                dc = np.clip(ck - cq + 15, 0, 30)
                vals = rpb_l[:, dr + 7, :][:, dc]
                U[jj * 64 + ck, :, ei, :] = np.where(valid[None, :], vals, NEG)
    return U.reshape(128, -1)


def _prep(inputs):
    f = lambda a: np.ascontiguousarray(np.asarray(a, dtype=np.float32))
    x_prompt, x_sample = f(inputs["x_prompt"]), f(inputs["x_sample"])
    cache_kv, c, c_ctx = f(inputs["cache_kv"]), f(inputs["c"]), f(inputs["c_ctx"])
    rpb = f(inputs["rpb"])
    shared = {
        "w_mod": f(inputs["w_mod"]), "w_in": f(inputs["w_in"]), "w_branch": f(inputs["w_branch"]),
        "w_out": f(inputs["w_out"]), "w_up": f(inputs["ffn_w_up"]), "w_down": f(inputs["ffn_w_down"]),
        "pool_w": f(inputs["pool_w"]),
    }

    def pk(v):
        return np.ascontiguousarray(v.reshape(-1, 128).T)

    vec_common = np.zeros((128, NV), np.float32)
    b_mod, g1, g2, gf = f(inputs["b_mod"]), f(inputs["g_norm1"]), f(inputs["g_norm2"]), f(inputs["g_final"])
    conv_w, fconv, pscale = f(inputs["conv_w"]), f(inputs["ffn_conv"]), f(inputs["pool_scale"])
    for l in range(NL):
        vec_common[:, V_BMOD + l * 48: V_BMOD + (l + 1) * 48] = pk(b_mod[l])
        vec_common[:, V_G1 + l * 8: V_G1 + (l + 1) * 8] = pk(g1[l])
        vec_common[:, V_G2 + l * 8: V_G2 + (l + 1) * 8] = pk(g2[l])
        for cc in range(4):
            for k in range(3):
                vec_common[:, V_CONVW + (l * 4 + cc) * 3 + k] = conv_w[l, k, cc * 128:(cc + 1) * 128]
        for j in range(NJ):
            for k in range(3):
                vec_common[:, V_FCONV + (l * NJ + j) * 3 + k] = fconv[l, k, j * 128:(j + 1) * 128]
        vec_common[:, V_PSCALE + l * 4: V_PSCALE + (l + 1) * 4] = pk(pscale[l])
    vec_common[:, V_GF:V_GF + 8] = pk(gf)

    ones_pat = np.concatenate([np.ones((128, 64)), np.zeros((128, 64)), np.ones((128, 64))], axis=1).astype(np.float32)
    band_p, band_s = _band_tables(256), _band_tables(1024)
    rneg_p, rneg_s = _rneg_table(False), _rneg_table(True)
    oh = _onehot_rows()
    utab_s = np.stack([_u_table(rpb[l]) for l in range(NL)])
    utab_p = np.zeros_like(utab_s)
    in_maps = []
    for i in range(8):
        sample = i >= 4
        vec = vec_common.copy()
        if sample:
            b = i - 4
            xT = np.ascontiguousarray(x_sample[b].T)
            vec[:, V_CV:V_CV + 8] = pk(c[b])
            vec[:, V_BFLAG] = 0.0
            ck = cache_kv[b, :, 0]
            ctxk = np.zeros((NL, 2, 64, NH, 256), np.float32)
            for h_ in range(NH):
                ctxk[:, h_ % 2, :, h_, :] = ck[:, h_].transpose(0, 2, 1)
            ctxk = ctxk.reshape(NL, 128, NH * 256)
            cvv = cache_kv[b, :, 1]
            ctxv = cvv.transpose(0, 2, 1, 3).reshape(NL, 2, 128, 512).transpose(0, 2, 1, 3).reshape(NL, 128, 1024)
            m = {"ctxkT": np.ascontiguousarray(ctxk), "ctxv": np.ascontiguousarray(ctxv), "utab": utab_s,
                 "rneg": rneg_s, "oh": oh, "onesc": ones_pat, "band": band_s}
        else:
            xT = np.ascontiguousarray(x_prompt[4 * i:4 * i + 4].reshape(T, D).T)
            vec[:, V_CV:V_CV + 8] = pk(c_ctx)
            vec[:, V_BFLAG] = 1.0
            m = {"ctxkT": np.zeros((NL, 128, 2048), np.float32), "ctxv": np.zeros((NL, 128, 1024), np.float32),
                 "utab": utab_p, "rneg": rneg_p, "oh": oh, "onesc": np.zeros_like(ones_pat), "band": band_p}
        m.update(shared)
        m["xT"] = xT
        m["vecs"] = vec
        in_maps.append(m)
    return in_maps


def _assemble(results):
    y_prompt = np.empty((16, 256, D), np.float32)
    y_sample = np.empty((4, T, D), np.float32)
    kv_state = np.empty((16, NL, 2, NH, 256, 64), np.float32)
    for i in range(8):
        r = results[i]
        y = np.asarray(r["yT"], dtype=np.float32).T
        if i < 4:
            y_prompt[4 * i:4 * i + 4] = y.reshape(4, 256, D)
            kvo = np.asarray(r["kvo"], dtype=np.float32).reshape(NL, 4, 256, NH, 64)
            kv_state[4 * i:4 * i + 4, :, 1] = kvo.transpose(1, 0, 3, 2, 4)
            kT = np.asarray(r["kvoT"], dtype=np.float32).reshape(NL, NH, 64, 4, 256)
            kv_state[4 * i:4 * i + 4, :, 0] = kT.transpose(3, 0, 1, 4, 2)
        else:
            y_sample[i - 4] = y
    return y_prompt, y_sample, kv_state


_NC_CACHE = {}


def kernel(**inputs):
    in_maps = _prep(inputs)
    if "nc" not in _NC_CACHE:
        _NC_CACHE["nc"] = build_nc()
    res = run_bass_kernel_spmd(_NC_CACHE["nc"], in_maps, core_ids=list(range(8)))
    return _assemble(res.results)
```

```python
import numpy as np
import ml_dtypes
from contextlib import ExitStack
import concourse.bass as bass
import concourse.mybir as mybir
from concourse.bass_utils import run_bass_kernel_spmd

F32 = mybir.dt.float32
BF16 = mybir.dt.bfloat16
AF = mybir.ActivationFunctionType
ALU = mybir.AluOpType

D = 1024
T = 1024
KC = 8
DFF = 2816
NJ = 22
INW = 6656
NL = 2
NH = 8
NE = 22
E0 = 10
NEG = -30000.0
SLOT = 4608
NSLOT = 5
ND = 16
SEM_LIMIT = 12000

V_BMOD = 0
V_G1 = 96
V_G2 = 112
V_GF = 128
V_CONVW = 136
V_FCONV = 160
V_PSCALE = 292
V_CV = 300
V_BFLAG = 308
NV = 320

LOCAL_TILES = {0: [0, 1, 2, 3, 4, 5], 1: [2, 3, 4, 5, 6, 7]}
QROWS = {0: (0, 5), 1: (0, 7), 2: (0, 9), 3: (0, 11), 4: (5, 15), 5: (7, 15), 6: (9, 15), 7: (11, 15)}


class Sched:
    def __init__(self, nc, es):
        self.nc = nc
        self.es = es
        self.eh = {"pe": nc.tensor, "act": nc.scalar, "dve": nc.vector, "pool": nc.gpsimd, "sp": nc.sync}
        self.epoch = {e: 0 for e in self.eh}
        self.cnt = {e: 0 for e in self.eh}
        self.sems = {}
        for e in self.eh:
            self.sems[(e, 0)] = es.enter_context(nc.semaphore(f"s_{e}_0"))
        self.dsem = [es.enter_context(nc.semaphore(f"s_dma_{i}")) for i in range(ND)]
        self.dcnt = [0] * ND
        self.dpool = {"sp": list(range(0, 6)), "pool": list(range(6, ND))}
        self.dnext = {"sp": 0, "pool": 0}
        self.seen = {e: {} for e in self.eh}
        self.last_w = {}
        self.readers = {}
        self.out_dmas = []
        self.aux = {}
        self.nops = {e: 0 for e in self.eh}
        self.pending = {e: False for e in self.eh}
        self.phase = "init"
        self.phase_of = {}

    def _sem_of(self, src):
        if src[0] == "d":
            return self.dsem[src[1]]
        return self.sems[src]

    def _wait(self, eng, src, c):
        if self.seen[eng].get(src, 0) >= c:
            return
        self.eh[eng].wait_ge(self._sem_of(src), c)
        self.seen[eng][src] = c

    def _deps(self, eng, reads, writes):
        deps = {}

        def add(src, c):
            if deps.get(src, 0) < c:
                deps[src] = c

        for k in reads:
            w = self.last_w.get(k)
            if w:
                add(*w)
        for k in writes:
            w = self.last_w.get(k)
            if w:
                add(*w)
            for src, c in self.readers.get(k, {}).items():
                add(src, c)
        for src, c in deps.items():
            if eng == "pe" and src[0] == "pe":
                continue
            self._wait(eng, src, c)

    def _record(self, src, c, reads, writes):
        for k in writes:
            self.last_w[k] = (src, c)
            self.readers[k] = {}
        for k in reads:
            r = self.readers.setdefault(k, {})
            if r.get(src, 0) < c:
                r[src] = c

    def op(self, eng, fn, reads=(), writes=(), signal=True):
        if (not self.pending[eng]) and self.cnt[eng] >= SEM_LIMIT:
            self.epoch[eng] += 1
            self.cnt[eng] = 0
            self.sems[(eng, self.epoch[eng])] = self.es.enter_context(
                self.nc.semaphore(f"s_{eng}_{self.epoch[eng]}"))
        self._deps(eng, reads, writes)
        ins = fn(self.eh[eng])
        src = (eng, self.epoch[eng])
        if signal:
            ins.then_inc(self.sems[src], 1)
            self.cnt[eng] += 1
            c = self.cnt[eng]
            self.pending[eng] = False
        else:
            c = self.cnt[eng] + 1
            self.pending[eng] = True
        self._record(src, c, reads, writes)
        self.nops[eng] += 1
        try:
            self.phase_of[ins.ins.name] = self.phase
        except Exception:
            pass
        return ins

    def dma(self, q, out, in_, reads=(), writes=(), is_output=False, ring=False):
        self._deps(q, reads, writes)
        lst = self.dpool[q]
        s = lst[self.dnext[q] % len(lst)]
        self.dnext[q] += 1
        src = ("d", s)
        if self.dcnt[s] > 0:
            self._wait(q, src, self.dcnt[s])
        self.eh[q].dma_start(out=out, in_=in_).then_inc(self.dsem[s], 16)
        self.dcnt[s] += 16
        self._record(src, self.dcnt[s], reads, writes)
        if is_output:
            self.out_dmas.append((src, self.dcnt[s]))
        if not ring:
            self.aux[src] = self.dcnt[s]

    def barrier(self):
        cur = [((e, self.epoch[e]), self.cnt[e]) for e in self.eh if self.cnt[e] > 0]
        cur += list(self.aux.items())
        self.aux = {}
        for e in self.eh:
            for src, c in cur:
                if src[0] == e and e == "pe":
                    continue
                self._wait(e, src, c)

    def finish(self):
        for q in ("sp", "pool"):
            for s_, c in enumerate(self.dcnt):
                if c > 0:
                    self._wait(q, ("d", s_), c)
        self.barrier()


class _Stop(Exception):
    pass


def build_nc(debug=None, stop=None):
    nc = bass.Bass("TRN2", target_bir_lowering=False)

    def din(name, shape):
        return nc.dram_tensor(name, list(shape), F32, kind="ExternalInput").ap()

    def dout(name, shape):
        return nc.dram_tensor(name, list(shape), F32, kind="ExternalOutput").ap()

    xT_d = din("xT", [D, T])
    vecs_d = din("vecs", [128, NV])
    w_mod_d = din("w_mod", [NL, D, 6 * D])
    w_in_d = din("w_in", [NL, D, INW])
    w_br_d = din("w_branch", [NL, 3, 512, D])
    w_out_d = din("w_out", [NL, D, D])
    w_up_d = din("w_up", [NL, D, 2 * DFF])
    w_dn_d = din("w_down", [NL, DFF, D])
    pool_w_d = din("pool_w", [NL, 4, 128, 128])
    ctxk_d = din("ctxkT", [NL, 128, 8 * 256])
    ctxv_d = din("ctxv", [NL, 128, 2 * 512])
    utab_d = din("utab", [NL, 128, NH * NE * 64])
    rneg_d = din("rneg", [128, T])
    oh_d = din("oh", [128, 8 * 128])
    onesc_d = din("onesc", [128, 192])
    band_d = din("band", [128, 4 * 8 * 2 * 128])
    yT_d = dout("yT", [D, T])
    kvo_d = dout("kvo", [NL, T, 512])
    kvoT_d = dout("kvoT", [NL, 512, T])
    dbg_d = {}
    if debug:
        for name, (shape, dt_) in debug.items():
            dbg_d[name] = nc.dram_tensor("dbg_" + name, list(shape), dt_, kind="ExternalOutput").ap()

    es = ExitStack()
    with es:
        def sb(name, shape, dt):
            return es.enter_context(nc.sbuf_tensor(name, list(shape), dt))

        X = sb("X", [128, KC, T], F32)
        Hh = sb("Hh", [128, KC, T], BF16)
        AR = sb("AR", [128, 32256], BF16)
        UT = sb("UT", [128, 2, NE * 64], BF16)
        WR = sb("WR", [128, NSLOT, SLOT], BF16)
        SCR = sb("SCR", [128, 5632], F32)
        VEC = sb("VEC", [128, NV], F32)
        MOD = sb("MOD", [128, NL, 48], F32)
        GM = sb("GM", [128, NL * 2 * 8], F32)
        SMALL = sb("SMALL", [128, 256], F32)
        SCb = sb("SCb", [128, 8], BF16)
        RNEG = sb("RNEG", [128, T], BF16)
        OH = sb("OH", [128, 8 * 128], BF16)
        ONES = sb("ONES", [128, 128], BF16)
        ONESL = sb("ONESL", [128, 192], BF16)
        ONESC = sb("ONESC", [128, 192], BF16)
        CK = sb("CK", [128, 8 * 256], BF16)
        PS = [es.enter_context(nc.psum_tensor(f"ps{i}", [128, 512], F32)) for i in range(8)]

        S = Sched(nc, es)

        QK = AR[:, 0:4096].rearrange("p (c t) -> p c t", c=4)
        KZ = AR[:, 4096:12288].rearrange("p (h t) -> p h t", h=8)
        MERGED = AR[:, 0:8192].rearrange("p (c t) -> p c t", c=8)
        VP = AR[:, 12288:12288 + 7680].rearrange("p (k h c) -> p k h c", k=10, h=4)
        YS = AR[:, 19968:19968 + 12288].rearrange("p (c t) -> p c t", c=12)
        PZ = AR[:, 19968 + 4096:19968 + 8192].rearrange("p (b c) -> p b c", b=8)
        A = AR[:, 0:NJ * T].rearrange("p (j t) -> p j t", j=NJ)
        BAND = AR[:, 4096:12288]
        BANDv = BAND.rearrange("p (g v s c) -> p g v s c", g=4, v=8, s=2)
        CKv = CK[:, :].rearrange("p (h k) -> p h k", h=8)
        OHv = OH[:, :].rearrange("p (b k) -> p b k", b=8)
        GMv = GM[:, :].rearrange("p (l j c) -> p l j c", l=NL, j=2)

        def vcol(base, idx=0, n=1):
            return VEC[:, base + idx: base + idx + n]

        ps_ctr = [0]

        ps_excl = set()

        def ps_next():
            while True:
                i = ps_ctr[0] % 8
                ps_ctr[0] += 1
                if i not in ps_excl:
                    return PS[i], ("ps", i)

        STATB = (6, 7)
        SQA = None

        stat_pending = []
        STAT_READY = [False]

        def stat_begin():
            ps_excl.update(STATB)

        def stat_accum(c, half):
            i4 = (c * 2 + half) % 4
            sq = scr_bf16(5120 + 256 * (i4 % 2), 256)
            kq = ("SQA", i4 % 2)
            while len(stat_pending) >= 2:
                stat_pending.pop(0)()
            S.op("act", lambda e: e.activation(out=sq, in_=X[:, c, HS(half)], func=AF.Square),
                 reads=[("X", c, half)], writes=[kq])

            def pe_part():
                S.op("pe", lambda e: e.matmul(PS[STATB[half]][:, :], ONES[:, :], sq, start=(c == 0), stop=(c == KC - 1)),
                     reads=[kq, ("ONES",)], writes=[("ps", STATB[half])])
            stat_pending.append(pe_part)

        def stat_end():
            while stat_pending:
                stat_pending.pop(0)()
            ps_excl.difference_update(STATB)
            STAT_READY[0] = True

        def HS(half):
            return slice(half * 512, half * 512 + 512)

        def scr_f32(off, n):
            return SCR[:, off:off + n]

        def scr_bf16(off, n):
            return SCR[:, off:off + n].bitcast(BF16)

        loads = []

        class Ring:
            issued = 0
            consumed = 0

        def ring_use_many(n):
            idx = Ring.consumed
            target = min(len(loads), idx + NSLOT)
            while Ring.issued < target:
                i = Ring.issued
                loads[i](i % NSLOT)
                Ring.issued += 1
            Ring.consumed += n
            return [(idx + k) % NSLOT for k in range(n)]

        def ring_use():
            return ring_use_many(1)[0]

        def ring_prefetch():
            target = min(len(loads), Ring.consumed + NSLOT)
            while Ring.issued < target:
                i = Ring.issued
                loads[i](i % NSLOT)
                Ring.issued += 1

        def wkeys(slot):
            return [("WR", slot, 0), ("WR", slot, 1)]

        def ld_cols(src2d, col0, ncols, nkc=KC):
            def f(slot):
                dst = WR[:, slot, 0:nkc * ncols].rearrange("p (k n) -> p k n", k=nkc)
                src = src2d.rearrange("(k p) n -> p k n", p=128)[:, :, col0:col0 + ncols]
                S.dma("pool", dst, src, writes=wkeys(slot), ring=True)
            return f

        def ld_gate_branch(l, c):
            def f(slot):
                for i in range(3):
                    dst = WR[:, slot, i * 1024:(i + 1) * 1024].rearrange("p (k n) -> p k n", k=KC)
                    col0 = 3584 + i * 1024 + c * 128
                    src = w_in_d[l].rearrange("(k p) n -> p k n", p=128)[:, :, col0:col0 + 128]
                    S.dma("pool", dst, src, writes=[("WR", slot, 0)] if i == 0 else [("WRx", slot, i)], ring=True)
                for i in range(3):
                    dst = WR[:, slot, 3072 + i * 512:3072 + (i + 1) * 512].rearrange("p (k n) -> p k n", k=4)
                    src = w_br_d[l, i].rearrange("(k p) n -> p k n", p=128)[:, :, c * 128:(c + 1) * 128]
                    S.dma("pool", dst, src, writes=[("WR", slot, 1)] if i == 0 else [("WRy", slot, i)], ring=True)
            return f

        def gate_keys(slot):
            return [("WR", slot, 0), ("WRx", slot, 1), ("WRx", slot, 2), ("WR", slot, 1), ("WRy", slot, 1), ("WRy", slot, 2)]

        def ld_up(l, jp):
            def f(slot):
                dstv = WR[:, slot, 0:4096].rearrange("p (k two n) -> p k two n", k=KC, two=2)
                srcv = w_up_d[l].rearrange("(k p) (two n) -> p k two n", p=128, two=2)
                for t_ in range(2):
                    S.dma("pool", dstv[:, :, t_, :], srcv[:, :, t_, jp * 256:(jp + 1) * 256], writes=[("WR", slot, t_)], ring=True)
            return f

        def ld_down(l, c):
            def f(slot):
                dst = WR[:, slot, 0:NJ * 128].rearrange("p (k n) -> p k n", k=NJ)
                src = w_dn_d[l].rearrange("(k p) n -> p k n", p=128)[:, :, c * 128:(c + 1) * 128]
                S.dma("pool", dst, src, writes=wkeys(slot), ring=True)
            return f

        def ld_poolw(l):
            def f(slot):
                dst = WR[:, slot, 0:512].rearrange("p (g e) -> p g e", g=4)
                src = pool_w_d[l].rearrange("g c e -> c g e")
                S.dma("pool", dst, src, writes=wkeys(slot), ring=True)
            return f

        for l in range(NL):
            if l == 0:
                for g in range(4):
                    loads.append(ld_cols(w_mod_d[l], g * 512, 512))
            loads.append(ld_cols(w_in_d[l], 6 * 512, 512))
            loads.append(ld_poolw(l))
            for g in (3, 4, 5):
                loads.append(ld_cols(w_in_d[l], g * 512, 512))
            for g in (2, 1, 0):
                loads.append(ld_cols(w_in_d[l], g * 512, 512))
            for c in range(8):
                loads.append(ld_gate_branch(l, c))
                loads.append(ld_cols(w_mod_d[l], (4 + c) * 512, 512))
            for g in range(2):
                loads.append(ld_cols(w_out_d[l], g * 512, 512))
            for jp in range(11):
                loads.append(ld_up(l, jp))
            for c in range(8):
                loads.append(ld_down(l, c))
                if c < 4 and l + 1 < NL:
                    loads.append(ld_cols(w_mod_d[l + 1], c * 512, 512))

        def mm_group(ps_ap, ps_key, pairs, reads):
            n = len(pairs)
            for i, (lt, rh) in enumerate(pairs):
                S.op("pe", lambda e, lt=lt, rh=rh, i=i: e.matmul(ps_ap, lt, rh, start=(i == 0), stop=(i == n - 1)),
                     reads=reads if i == 0 else (), writes=[ps_key] if i == 0 else (), signal=(i == n - 1))

        def Hkeys(half):
            return [("H", kc, half) for kc in range(KC)]

        S.dma("sp", VEC[:, :], vecs_d, writes=[("VEC",)])
        for dc in range(KC):
            S.dma("sp", X[:, dc, :], xT_d[dc * 128:(dc + 1) * 128, :], writes=[("X", dc, 0), ("X", dc, 1)])
        ring_prefetch()
        S.dma("pool", RNEG[:, :], rneg_d, writes=[("RNEG",)])
        S.dma("pool", OH[:, :], oh_d, writes=[("OH",)])
        S.dma("pool", ONESC[:, :], onesc_d, writes=[("ONESC",)])
        S.op("dve", lambda e: e.memset(ONES[:, :], 1.0), writes=[("ONES",)])
        S.op("dve", lambda e: e.memset(ONESL[:, :], 1.0), writes=[("ONESL",)])
        S.op("dve", lambda e: e.memset(ONESL[:, 64:128], 0.0), writes=[("ONESL",)])
        S.op("dve", lambda e: e.memset(AR[:, 12288:12288 + 7680], 0.0), writes=[("VPall",)])
        S.op("act", lambda e: e.activation(out=SCb[:, :], in_=vcol(V_CV, 0, 8), func=AF.Silu),
             reads=[("VEC",)], writes=[("SCb",)])
        S.op("dve", lambda e: e.tensor_scalar(out=SMALL[:, 0:1], in0=vcol(V_BFLAG), scalar1=-1.0, scalar2=None,
                                              op0=ALU.mult), reads=[("VEC",)], writes=[("NBF",)])

        def mod_part(l, groups):
            for g in groups:
                slot = ring_use()
                ps, pk = ps_next()
                wv = WR[:, slot, 0:4096].rearrange("p (k n) -> p k n", k=KC)
                first = True
                for c in range(4):
                    col = g * 4 + c
                    for kc in range(KC):
                        S.op("pe", lambda e, c=c, kc=kc, col=col: e.matmul(
                            ps[:, col:col + 1], wv[:, kc, c * 128:(c + 1) * 128], SCb[:, kc:kc + 1],
                            start=(kc == 0), stop=(kc == KC - 1)),
                            reads=(wkeys(slot) + [("SCb",)]) if first else (), writes=[pk] if first else (),
                            signal=(c == 3 and kc == KC - 1))
                        first = False
                S.op("dve", lambda e, g=g: e.tensor_tensor(
                    out=MOD[:, l, g * 4:(g + 1) * 4], in0=ps[:, g * 4:(g + 1) * 4],
                    in1=VEC[:, V_BMOD + l * 48 + g * 4: V_BMOD + l * 48 + (g + 1) * 4], op=ALU.add),
                    reads=[pk, ("VEC",)], writes=[("MOD", l, g)])

        def gm_compute(l, j):
            sc0 = 8 if j == 0 else 32
            gb = (V_G1 if j == 0 else V_G2) + l * 8
            S.op("dve", lambda e: e.scalar_tensor_tensor(
                out=GMv[:, l, j, :], in0=MOD[:, l, sc0:sc0 + 8], scalar=1.0, in1=VEC[:, gb:gb + 8],
                op0=ALU.add, op1=ALU.mult),
                reads=[("MOD", l, sc0 // 4), ("MOD", l, sc0 // 4 + 1), ("VEC",)], writes=[("GM", l, j)])

        def norm(scale_ap_fn, bias_ap_fn, extra_reads, out_fn, out_keys_fn, hook=None, pre=False):
            SQ = [scr_bf16(0, 256), scr_bf16(256, 256)]
            SS = [scr_f32(512, 512), scr_f32(1024, 512)]
            RS = [scr_f32(1536, 512), scr_f32(2048, 512)]
            TMP = [scr_f32(2560, 512), scr_f32(3072, 512)]
            for half in range(2):
                if pre:
                    ps, pk = PS[STATB[half]], ("ps", STATB[half])
                else:
                    ps, pk = ps_next()
                for dc in range(KC if not pre else 0):
                    sq = SQ[dc % 2]
                    if dc % 2 == 0:
                        S.op("act", lambda e, dc=dc, sq=sq: e.activation(out=sq, in_=X[:, dc, HS(half)], func=AF.Square),
                             reads=[("X", dc, half)], writes=[("SQ", dc % 2)])
                    else:
                        S.op("dve", lambda e, dc=dc, sq=sq: e.tensor_tensor(out=sq, in0=X[:, dc, HS(half)], in1=X[:, dc, HS(half)], op=ALU.mult),
                             reads=[("X", dc, half)], writes=[("SQ", dc % 2)])
                    S.op("pe", lambda e, dc=dc, sq=sq: e.matmul(ps[:, :], ONES[:, :], sq, start=(dc == 0), stop=(dc == KC - 1)),
                         reads=[("SQ", dc % 2), ("ONES",)], writes=[pk])
                S.op("act", lambda e: e.activation(out=SS[half], in_=ps[:, :], func=AF.Sqrt, scale=1.0 / D, bias=1e-6),
                     reads=[pk], writes=[("SS", half)])
                S.op("dve", lambda e: e.reciprocal(out=RS[half], in_=SS[half]), reads=[("SS", half)], writes=[("RS", half)])
            if hook is not None:
                hook()
            for half in range(2):
                for dc in range(KC):
                    tmp = TMP[dc % 2]
                    S.op("dve", lambda e, dc=dc, tmp=tmp: e.tensor_tensor(out=tmp, in0=X[:, dc, HS(half)], in1=RS[half], op=ALU.mult),
                         reads=[("X", dc, half), ("RS", half)], writes=[("TMPn", dc % 2)])
                    b = bias_ap_fn(dc)
                    S.op("act", lambda e, dc=dc, tmp=tmp, b=b: e.activation(
                        out=out_fn(dc, half), in_=tmp, func=AF.Identity, scale=scale_ap_fn(dc),
                        **({"bias": b} if b is not None else {})),
                        reads=[("TMPn", dc % 2)] + extra_reads(), writes=out_keys_fn(dc, half))

        def stop_at(tag):
            if stop == tag:
                raise _Stop()

        def dbg_dump(name, ap, keys):
            if debug and name in dbg_d:
                S.barrier()
                S.dma("sp", dbg_d[name], ap, reads=keys, is_output=True)
                S.barrier()

        def run_layers():
          for l in range(NL):
            if l > 0:
                gm_compute(l, 0)
            S.dma("pool", CK[:, :].rearrange("p (a b) -> p a b", b=1024), ctxk_d[l].rearrange("p (a b) -> p a b", b=1024),
                  writes=[("CK",)])
            S.dma("pool", BAND.rearrange("p (a b) -> p a b", b=1024), band_d.rearrange("p (a b) -> p a b", b=1024),
                  writes=[("BAND",)])
            for j in range(2):
                dst = VP[:, 8:10, :, j * 128:j * 128 + 64]
                src = ctxv_d[l].rearrange("p (k h j d) -> p k h j d", k=2, h=4, j=2)[:, :, :, j, :]
                S.dma("pool", dst, src, writes=[("VPc", j)], reads=[("VPall",)])

            S.phase = f"L{l}:norm1"
            def l0_hook():
                mod_part(0, range(0, 4))
                gm_compute(0, 0)
            norm(lambda dc: GMv[:, l, 0, dc:dc + 1], lambda dc: MOD[:, l, dc:dc + 1],
                 lambda: [("GM", l, 0), ("MOD", l, 0), ("MOD", l, 1)],
                 lambda dc, half: Hh[:, dc, HS(half)], lambda dc, half: [("H", dc, half)],
                 hook=l0_hook if l == 0 else None, pre=(l > 0))
            S.barrier()
            if l == 0:
                stop_at("norm1")
            if l == 0:
                dbg_dump("h1", Hh[:, :, :], [("H", dc, hf) for dc in range(KC) for hf in range(2)])

            if l == 0:
                stop_at("A_pz0")
            slot = ring_use()
            wv = WR[:, slot, 0:4096].rearrange("p (k n) -> p k n", k=KC)
            for tb in range(8):
                ps, pk = ps_next()
                mm_group(ps[:, :], pk, [(Hh[:, kc, tb * 128:(tb + 1) * 128], wv[:, kc, :]) for kc in range(KC)],
                         wkeys(slot) + Hkeys(tb // 4))
                S.op("act", lambda e, ps=ps, tb=tb: e.activation(out=PZ[:, tb, :], in_=ps[:, :], func=AF.Copy),
                     reads=[pk], writes=[("PZ", tb)])

            if l == 0:
                stop_at("A_pz")
            S.phase = f"L{l}:A_pool"
            slot = ring_use()
            pwv = WR[:, slot, 0:512].rearrange("p (g e) -> p g e", g=4)
            DT = [scr_bf16(0, 256), scr_bf16(256, 256), scr_bf16(5122, 256)]
            pend = []

            def pool_stage2(g, half, dt, dti):
                ps2, pk2 = ps_next()
                mm_group(ps2[:, :], pk2, [(pwv[:, g, :], dt)], wkeys(slot) + [("DT", dti)])
                S.op("dve", lambda e: e.tensor_scalar(
                    out=YS[:, 8 + g, HS(half)], in0=ps2[:, :], scalar1=vcol(V_PSCALE, l * 4 + g), scalar2=None, op0=ALU.mult),
                    reads=[pk2, ("VEC",)], writes=[("YS", 8 + g, half)])

            for g in range(4):
                for half in range(2):
                    ps, pk = ps_next()
                    first = True
                    for obi in range(4):
                        ob = half * 4 + obi
                        terms = []
                        dv = 0 if ob == 0 else (1 if ob == 7 else (2 if ob % 2 == 0 else 3))
                        terms.append((ob, dv))
                        if ob >= 1:
                            terms.append((ob - 1, 4 if ob % 2 == 1 else 5))
                        if ob <= 6:
                            terms.append((ob + 1, 6 if ob % 2 == 0 else 7))
                        mats = [(ib, v, s_) for (ib, v) in terms for s_ in range(2)]
                        for i, (ib, v, s_) in enumerate(mats):
                            last = (obi == 3 and i == len(mats) - 1)
                            S.op("pe", lambda e, ib=ib, v=v, s_=s_, i=i, obi=obi, n=len(mats), ps=ps: e.matmul(
                                ps[:, obi * 128:(obi + 1) * 128], PZ[:, ib, g * 128:(g + 1) * 128], BANDv[:, g, v, s_, :],
                                start=(i == 0), stop=(i == n - 1)),
                                reads=([("PZ", t) for t in range(8)] + [("BAND",)]) if first else (),
                                writes=[pk] if first else (), signal=last)
                            first = False
                    dti = (g * 2 + half) % 3
                    dt = DT[dti]
                    S.op("act", lambda e, ps=ps, dt=dt: e.activation(out=dt, in_=ps[:, :], func=AF.Copy),
                         reads=[pk], writes=[("DT", dti)])
                    pend.append((g, half, dt, dti))
                    if len(pend) > 1:
                        pool_stage2(*pend.pop(0))
            while pend:
                pool_stage2(*pend.pop(0))

            if l == 0:
                stop_at("A_pool")
            S.op("dve", lambda e: e.memset(AR[:, 4096:12288], 0.0), writes=[("KZall",), ("BAND",)])
            S.phase = f"L{l}:A_qkv"
            KST = [scr_f32(3586, 512), scr_f32(4098, 512), scr_f32(4610, 512)]
            for which in range(2):
                slot = ring_use()
                wv = WR[:, slot, 0:4096].rearrange("p (k n) -> p k n", k=KC)
                for c in range(4):
                    for half in range(2):
                        ps, pk = ps_next()
                        mm_group(ps[:, :], pk, [(wv[:, kc, c * 128:(c + 1) * 128], Hh[:, kc, HS(half)]) for kc in range(KC)],
                                 wkeys(slot) + Hkeys(half))
                        if which == 0:
                            S.op("act", lambda e, c=c, half=half, ps=ps: e.activation(
                                out=QK[:, c, HS(half)], in_=ps[:, :], func=AF.Copy, scale=0.125),
                                reads=[pk], writes=[("QK", c, half)])
                        else:
                            st = KST[(c * 2 + half) % 3]
                            sk_ = ("KST", (c * 2 + half) % 3)
                            S.op("act", lambda e, ps=ps, st=st: e.activation(out=st, in_=ps[:, :], func=AF.Copy),
                                 reads=[pk], writes=[sk_])
                            S.dma("sp", kvoT_d[l, c * 128:(c + 1) * 128, HS(half)], st, reads=[sk_], is_output=True)
                            for jj in range(2):
                                S.op("dve", lambda e, c=c, half=half, st=st, jj=jj: e.tensor_copy(
                                    out=KZ[jj * 64:(jj + 1) * 64, 2 * c + jj, HS(half)], in_=st[jj * 64:(jj + 1) * 64, :]),
                                    reads=[sk_, ("KZall",)], writes=[("KZ", 2 * c + jj, half)])
            if l == 0:
                stop_at("A_qk")
            slot = ring_use()
            wv = WR[:, slot, 0:4096].rearrange("p (k n) -> p k n", k=KC)
            for tb in range(8):
                ps, pk = ps_next()
                mm_group(ps[:, :], pk, [(Hh[:, kc, tb * 128:(tb + 1) * 128], wv[:, kc, :]) for kc in range(KC)],
                         wkeys(slot) + Hkeys(tb // 4))
                st = KST[tb % 3]
                S.op("act", lambda e, ps=ps, st=st: e.activation(out=st, in_=ps[:, :], func=AF.Copy),
                     reads=[pk], writes=[("KST", tb % 3)])
                S.dma("sp", kvo_d[l, tb * 128:(tb + 1) * 128, :], st, reads=[("KST", tb % 3)], is_output=True)
                for j in range(2):
                    S.op("dve", lambda e, st=st, tb=tb, j=j: e.tensor_copy(
                        out=VP[:, tb, :, j * 128:j * 128 + 64],
                        in_=st.rearrange("p (h j d) -> p h j d", h=4, j=2)[:, :, j, :]),
                        reads=[("KST", tb % 3), ("VPall",)], writes=[("VP", tb, j)])
            S.phase = f"L{l}:A_conv"
            slot_h, slot_c, slot_b = ring_use_many(3)
            wh = WR[:, slot_h, 0:4096].rearrange("p (k n) -> p k n", k=KC)
            wc = WR[:, slot_c, 0:4096].rearrange("p (k n) -> p k n", k=KC)
            wb = WR[:, slot_b, 0:4096].rearrange("p (k n) -> p k n", k=KC)
            HC = [scr_f32(512, 512), scr_f32(1024, 512)]
            CH = scr_f32(1536, 1026)
            YC = scr_f32(2562, 1024)
            S.op("dve", lambda e: e.memset(CH[:, 0:1], 0.0), writes=[("CH", 0), ("CH", 1)])
            S.op("dve", lambda e: e.memset(CH[:, 1025:1026], 0.0), writes=[("CHp",)])
            for c in range(4):
                cwb = V_CONVW + (l * 4 + c) * 3
                S.op("dve", lambda e, cwb=cwb: e.tensor_scalar(out=SMALL[:, 1:2], in0=vcol(cwb, 0), scalar1=SMALL[:, 0:1],
                                                              scalar2=None, op0=ALU.mult),
                     reads=[("VEC",), ("NBF",)], writes=[("FX", 0)])
                S.op("dve", lambda e, cwb=cwb: e.tensor_scalar(out=SMALL[:, 2:3], in0=vcol(cwb, 2), scalar1=SMALL[:, 0:1],
                                                              scalar2=None, op0=ALU.mult),
                     reads=[("VEC",), ("NBF",)], writes=[("FX", 1)])
                for half in range(2):
                    ps1, pk1 = ps_next()
                    mm_group(ps1[:, :], pk1, [(wh[:, kc, c * 128:(c + 1) * 128], Hh[:, kc, HS(half)]) for kc in range(KC)],
                             wkeys(slot_h) + Hkeys(half))
                    hc = HC[half]
                    S.op("act", lambda e, ps1=ps1, hc=hc: e.activation(out=hc, in_=ps1[:, :], func=AF.Copy),
                         reads=[pk1], writes=[("HC", half)])
                    ps2, pk2 = ps_next()
                    mm_group(ps2[:, :], pk2, [(wc[:, kc, c * 128:(c + 1) * 128], Hh[:, kc, HS(half)]) for kc in range(KC)],
                             wkeys(slot_c) + Hkeys(half))
                    S.op("dve", lambda e, ps2=ps2, hc=hc, half=half: e.tensor_tensor(
                        out=CH[:, 1 + half * 512: 1 + half * 512 + 512], in0=ps2[:, :], in1=hc, op=ALU.mult),
                        reads=[pk2, ("HC", half)], writes=[("CH", half)])
                chk = [("CH", 0), ("CH", 1), ("CHp",)]
                S.op("act", lambda e, cwb=cwb: e.activation(out=YC[:, :], in_=CH[:, 1:1025], func=AF.Identity, scale=vcol(cwb, 1)),
                     reads=chk + [("VEC",)], writes=[("YC",)])
                S.op("dve", lambda e, cwb=cwb: e.scalar_tensor_tensor(
                    out=YC[:, :], in0=CH[:, 0:1024], scalar=vcol(cwb, 0), in1=YC[:, :], op0=ALU.mult, op1=ALU.add),
                    reads=chk + [("VEC",), ("YC",)], writes=[("YC",)])
                S.op("dve", lambda e, cwb=cwb: e.scalar_tensor_tensor(
                    out=YC[:, :], in0=CH[:, 2:1026], scalar=vcol(cwb, 2), in1=YC[:, :], op0=ALU.mult, op1=ALU.add),
                    reads=chk + [("VEC",), ("YC",)], writes=[("YC",)])
                ycv = YC[:, :].rearrange("p (s t) -> p s t", s=4)
                chv = CH[:, 1:1025].rearrange("p (s t) -> p s t", s=4)
                S.op("dve", lambda e, ycv=ycv, chv=chv: e.scalar_tensor_tensor(
                    out=ycv[:, 1:4, 0:1], in0=chv[:, 0:3, 255:256], scalar=SMALL[:, 1:2], in1=ycv[:, 1:4, 0:1],
                    op0=ALU.mult, op1=ALU.add), reads=chk + [("FX", 0), ("YC",)], writes=[("YC",)])
                S.op("dve", lambda e, ycv=ycv, chv=chv: e.scalar_tensor_tensor(
                    out=ycv[:, 0:3, 255:256], in0=chv[:, 1:4, 0:1], scalar=SMALL[:, 2:3], in1=ycv[:, 0:3, 255:256],
                    op0=ALU.mult, op1=ALU.add), reads=chk + [("FX", 1), ("YC",)], writes=[("YC",)])
                for half in range(2):
                    ps3, pk3 = ps_next()
                    mm_group(ps3[:, :], pk3, [(wb[:, kc, c * 128:(c + 1) * 128], Hh[:, kc, HS(half)]) for kc in range(KC)],
                             wkeys(slot_b) + Hkeys(half))
                    S.op("dve", lambda e, ps3=ps3, half=half, c=c: e.tensor_tensor(
                        out=YS[:, c, HS(half)], in0=ps3[:, :], in1=YC[:, HS(half)], op=ALU.mult),
                        reads=[pk3, ("YC",)], writes=[("YS", c, half)])

            S.barrier()
            if l == 0:
                stop_at("phaseA")
            if l == 0:
                dbg_dump("qk", AR[:, 0:8192], [])
                dbg_dump("ysA", AR[:, 19968:19968 + 12288], [])

            S.phase = f"L{l}:B_attn"
            TT_ = [scr_f32(0 + i * 512, 512) for i in range(3)]
            PP = [scr_bf16(1536 + i * 256, 256) for i in range(4)]
            RD = [scr_f32(3072 + i * 512, 512) for i in range(2)]
            work = []
            unit = 0
            for hp in range(4):
                for qc in range(2):
                    tiles = [("c", 8), ("c", 9)] + [("l", b_) for b_ in LOCAL_TILES[qc]]
                    ntot = 2 * len(tiles)
                    it = 0
                    for j in range(2):
                        for kind, b_ in tiles:
                            if kind == "l":
                                lo, hi = QROWS[b_]
                                lo, hi = max(lo, 8 * qc), min(hi, 8 * qc + 7)
                                c0_, c1_ = (lo - 8 * qc) * 64, (hi + 1 - 8 * qc) * 64
                            else:
                                c0_, c1_ = 0, 512
                            work.append(dict(hp=hp, qc=qc, j=j, kind=kind, b=b_, unit=unit, it=it, ntot=ntot,
                                             c0=c0_, c1=c1_, first_of_head=(b_ == 8 and kind == "c")))
                            it += 1
                    unit += 1
            LA = 3

            def ut_load(h):
                S.dma("pool", UT[:, h % 2, :], utab_d[l][:, h * NE * 64:(h + 1) * NE * 64], writes=[("UT", h % 2)])

            def emit_S(k):
                w = work[k]
                hp, qc, j, b_ = w["hp"], w["qc"], w["j"], w["b"]
                si = k % 4
                Sps, sk = PS[si], ("ps", si)
                h_ = 2 * hp + j
                qsrc = QK[:, hp, HS(qc)]
                if w["kind"] == "l":
                    ksrc = KZ[:, h_, b_ * 128:(b_ + 1) * 128]
                    c0, c1 = w["c0"], w["c1"]
                    n = c1 - c0
                    qs = QK[:, hp, qc * 512 + c0:qc * 512 + c1]
                    S.op("pe", lambda e: e.matmul(Sps[:, 0:n], ksrc, qs, start=True, stop=False),
                         reads=[("KZ", h_, b_ // 4), ("KZall",), ("QK", hp, qc)], writes=[sk], signal=False)
                    S.op("pe", lambda e: e.matmul(Sps[:, 0:n], OHv[:, b_, :], RNEG[:, qc * 512 + c0:qc * 512 + c1], start=False, stop=True),
                         reads=[("OH",), ("RNEG",)], writes=[sk])
                else:
                    ksrc = CKv[:, h_, (b_ - 8) * 128:(b_ - 7) * 128]
                    S.op("pe", lambda e: e.matmul(Sps[:, :], ksrc, qsrc, start=True, stop=True),
                         reads=[("CK",), ("QK", hp, qc)], writes=[sk])

            def emit_rest(k):
                w = work[k]
                hp, qc, j, b_, it, ntot = w["hp"], w["qc"], w["j"], w["b"], w["it"], w["ntot"]
                u = w["unit"]
                h = 2 * hp + j
                NUM, nk = PS[4 + (u % 2) * 2], ("ps", 4 + (u % 2) * 2)
                DEN, dk = PS[5 + (u % 2) * 2], ("ps", 5 + (u % 2) * 2)
                si = k % 4
                Sps, sk = PS[si], ("ps", si)
                pi = k % 4
                pt = PP[pi]
                if w["first_of_head"] and qc == 1 and j == 1 and hp < 3:
                    ut_load(2 * (hp + 1))
                if w["first_of_head"] and qc == 0 and j == 0 and hp > 0:
                    ut_load(2 * hp + 1)
                c0, c1 = w["c0"], w["c1"]
                n = c1 - c0
                if w["kind"] == "l":
                    e0 = 8 * qc + c0 // 64 - 2 * b_ + E0
                    ti = k % 3
                    tt = TT_[ti]
                    S.op("dve", lambda e: e.tensor_tensor(out=tt[:, 0:n], in0=Sps[:, 0:n], in1=UT[:, h % 2, e0 * 64:e0 * 64 + n], op=ALU.add),
                         reads=[sk, ("UT", h % 2)], writes=[("TT", ti)])
                    S.op("act", lambda e: e.activation(out=pt[:, 0:n], in_=tt[:, 0:n], func=AF.Exp),
                         reads=[("TT", ti)], writes=[("PP", pi)])
                    vkeys = [("VP", b_, 0), ("VP", b_, 1), ("VPall",)]
                    osrc, okeys = ONESL[:, j * 64:j * 64 + 128], [("ONESL",)]
                else:
                    S.op("act", lambda e: e.activation(out=pt, in_=Sps[:, :], func=AF.Exp),
                         reads=[sk], writes=[("PP", pi)])
                    vkeys = [("VPc", 0), ("VPc", 1), ("VPall",)]
                    osrc, okeys = ONESC[:, j * 64:j * 64 + 128], [("ONESC",)]
                vsrc = VP[:, b_, hp, j * 64:j * 64 + 128]
                S.op("pe", lambda e: e.matmul(NUM[:, c0:c1], vsrc, pt[:, 0:n], start=(it == 0), stop=(it == ntot - 1),
                                              skip_group_check=True),
                     reads=[("PP", pi)] + vkeys, writes=[nk], signal=False)
                S.op("pe", lambda e: e.matmul(DEN[:, c0:c1], osrc, pt[:, 0:n], start=(it == 0), stop=(it == ntot - 1),
                                              skip_group_check=True),
                     reads=[("PP", pi)] + okeys, writes=[dk])
                if it == ntot - 1:
                    rd = RD[u % 2]
                    S.op("dve", lambda e: e.reciprocal(out=rd, in_=DEN[:, :]), reads=[dk], writes=[("RD", u % 2)])
                    S.op("dve", lambda e: e.tensor_tensor(out=YS[:, 4 + hp, HS(qc)], in0=NUM[:, :], in1=rd, op=ALU.mult),
                         reads=[nk, dk, ("RD", u % 2)] + [("PZ", t) for t in range(8)], writes=[("YS", 4 + hp, qc)])

            ut_load(0)
            ut_load(1)
            for k in range(min(LA, len(work))):
                emit_S(k)
            for k in range(len(work)):
                if k + LA < len(work):
                    emit_S(k + LA)
                emit_rest(k)
            S.barrier()
            ps_ctr[0] = 0
            if l == 0:
                stop_at("attn")
            if l == 0:
                dbg_dump("ysB", AR[:, 19968:19968 + 12288], [])

            S.phase = f"L{l}:C_gates"
            GT = [scr_f32(i * 512, 512) for i in range(6)]
            MM = [scr_f32(3072 + i * 512, 512) for i in range(4)]
            gctr = [0]
            mctr = [0]
            for c in range(8):
                slot = ring_use()
                gw = WR[:, slot, 0:3072].rearrange("p (i k n) -> p i k n", i=3, k=KC)
                bw = WR[:, slot, 3072:3072 + 1536].rearrange("p (i k n) -> p i k n", i=3, k=4)
                for half in range(2):
                    prods = []
                    for i in range(3):
                        psg, pkg = ps_next()
                        mm_group(psg[:, :], pkg, [(gw[:, i, kc, :], Hh[:, kc, HS(half)]) for kc in range(KC)],
                                 gate_keys(slot) + Hkeys(half))
                        gi = gctr[0] % 6
                        gctr[0] += 1
                        gt = GT[gi]
                        S.op("act", lambda e, psg=psg, gt=gt: e.activation(out=gt, in_=psg[:, :], func=AF.Sigmoid),
                             reads=[pkg], writes=[("GT", gi)])
                        psp, pkp = ps_next()
                        mm_group(psp[:, :], pkp, [(bw[:, i, kc, :], YS[:, 4 * i + kc, HS(half)]) for kc in range(4)],
                                 gate_keys(slot) + [("YS", 4 * i + kc, half) for kc in range(4)])
                        mi = mctr[0] % 4
                        mctr[0] += 1
                        mt = MM[mi]
                        S.op("dve", lambda e, psp=psp, gt=gt, mt=mt: e.tensor_tensor(out=mt, in0=psp[:, :], in1=gt, op=ALU.mult),
                             reads=[pkp, ("GT", gi)], writes=[("MM", mi)])
                        prods.append((mt, mi))
                    (m0, k0), (m1, k1), (m2, k2) = prods
                    S.op("pool", lambda e, m0=m0, m1=m1: e.tensor_tensor(out=m0, in0=m0, in1=m1, op=ALU.add),
                         reads=[("MM", k0), ("MM", k1)], writes=[("MM", k0)])
                    S.op("pool", lambda e, m0=m0, m2=m2, c=c, half=half: e.tensor_tensor(out=MERGED[:, c, HS(half)], in0=m0, in1=m2, op=ALU.add),
                         reads=[("MM", k0), ("MM", k2)], writes=[("MG", c, half)])
                mod_part(l, [4 + c])
            if l == 0:
                stop_at("phaseC")
            if l == 0:
                dbg_dump("merged", AR[:, 0:8192], [("MG", c, hf) for c in range(8) for hf in range(2)])

            S.phase = f"L{l}:D_mod_wout"
            gm_compute(l, 1)
            stat_begin()
            for g in range(2):
                slot = ring_use()
                wv = WR[:, slot, 0:4096].rearrange("p (k n) -> p k n", k=KC)
                for cc in range(4):
                    c = g * 4 + cc
                    for half in range(2):
                        ps, pk = ps_next()
                        mm_group(ps[:, :], pk, [(wv[:, kc, cc * 128:(cc + 1) * 128], MERGED[:, kc, HS(half)]) for kc in range(KC)],
                                 wkeys(slot) + [("MG", kc, half) for kc in range(KC)])
                        S.op("dve", lambda e, ps=ps, c=c, half=half: e.scalar_tensor_tensor(
                            out=X[:, c, HS(half)], in0=ps[:, :], scalar=MOD[:, l, 16 + c:17 + c], in1=X[:, c, HS(half)],
                            op0=ALU.mult, op1=ALU.add),
                            reads=[pk, ("MOD", l, 4 + c // 4), ("X", c, half)], writes=[("X", c, half)])
                        stat_accum(c, half)
            stat_end()
            if l == 0:
                stop_at("x1")
            if l == 0:
                dbg_dump("x1", X[:, :, :], [("X", dc, hf) for dc in range(KC) for hf in range(2)])

            S.phase = f"L{l}:E_norm2"
            S.barrier()
            norm(lambda dc: GMv[:, l, 1, dc:dc + 1], lambda dc: MOD[:, l, 24 + dc:25 + dc],
                 lambda: [("GM", l, 1), ("MOD", l, 6), ("MOD", l, 7)],
                 lambda dc, half: Hh[:, dc, HS(half)], lambda dc, half: [("H", dc, half)], pre=True)
            S.barrier()

            S.phase = f"L{l}:F_up"
            U32 = scr_f32(0, 1026)
            YF = scr_f32(1026, 1024)
            GF = scr_f32(2050, 1024)
            S.op("dve", lambda e: e.memset(U32[:, 0:1], 0.0), writes=[("U32", 0), ("U32", 1)])
            S.op("dve", lambda e: e.memset(U32[:, 1025:1026], 0.0), writes=[("U32p",)])
            for jp in range(11):
                slot = ring_use()
                wv = WR[:, slot, 0:4096].rearrange("p (k two n) -> p k two n", k=KC, two=2)
                for jj in range(2):
                    jx = jp * 2 + jj
                    fwb = V_FCONV + (l * NJ + jx) * 3
                    S.op("dve", lambda e, fwb=fwb: e.tensor_scalar(out=SMALL[:, 1:2], in0=vcol(fwb, 0), scalar1=SMALL[:, 0:1],
                                                                  scalar2=None, op0=ALU.mult),
                         reads=[("VEC",), ("NBF",)], writes=[("FX", 0)])
                    S.op("dve", lambda e, fwb=fwb: e.tensor_scalar(out=SMALL[:, 2:3], in0=vcol(fwb, 2), scalar1=SMALL[:, 0:1],
                                                                  scalar2=None, op0=ALU.mult),
                         reads=[("VEC",), ("NBF",)], writes=[("FX", 1)])
                    for half in range(2):
                        ps, pk = ps_next()
                        mm_group(ps[:, :], pk, [(wv[:, kc, 0, jj * 128:(jj + 1) * 128], Hh[:, kc, HS(half)]) for kc in range(KC)],
                                 wkeys(slot) + Hkeys(half))
                        S.op("act", lambda e, ps=ps, half=half: e.activation(out=U32[:, 1 + half * 512:1 + half * 512 + 512], in_=ps[:, :], func=AF.Copy),
                             reads=[pk], writes=[("U32", half)])
                    uk = [("U32", 0), ("U32", 1), ("U32p",)]
                    S.op("act", lambda e, fwb=fwb: e.activation(out=YF[:, :], in_=U32[:, 1:1025], func=AF.Identity, scale=vcol(fwb, 1)),
                         reads=uk + [("VEC",)], writes=[("YF",)])
                    S.op("dve", lambda e, fwb=fwb: e.scalar_tensor_tensor(
                        out=YF[:, :], in0=U32[:, 0:1024], scalar=vcol(fwb, 0), in1=YF[:, :], op0=ALU.mult, op1=ALU.add),
                        reads=uk + [("VEC",), ("YF",)], writes=[("YF",)])
                    S.op("dve", lambda e, fwb=fwb: e.scalar_tensor_tensor(
                        out=YF[:, :], in0=U32[:, 2:1026], scalar=vcol(fwb, 2), in1=YF[:, :], op0=ALU.mult, op1=ALU.add),
                        reads=uk + [("VEC",), ("YF",)], writes=[("YF",)])
                    yfv = YF[:, :].rearrange("p (s t) -> p s t", s=4)
                    uv = U32[:, 1:1025].rearrange("p (s t) -> p s t", s=4)
                    S.op("dve", lambda e, yfv=yfv, uv=uv: e.scalar_tensor_tensor(
                        out=yfv[:, 1:4, 0:1], in0=uv[:, 0:3, 255:256], scalar=SMALL[:, 1:2], in1=yfv[:, 1:4, 0:1],
                        op0=ALU.mult, op1=ALU.add), reads=uk + [("FX", 0), ("YF",)], writes=[("YF",)])
                    S.op("dve", lambda e, yfv=yfv, uv=uv: e.scalar_tensor_tensor(
                        out=yfv[:, 0:3, 255:256], in0=uv[:, 1:4, 0:1], scalar=SMALL[:, 2:3], in1=yfv[:, 0:3, 255:256],
                        op0=ALU.mult, op1=ALU.add), reads=uk + [("FX", 1), ("YF",)], writes=[("YF",)])
                    S.op("act", lambda e: e.activation(out=GF[:, :], in_=YF[:, :], func=AF.Gelu_apprx_tanh),
                         reads=[("YF",)], writes=[("GF",)])
                    for half in range(2):
                        ps, pk = ps_next()
                        mm_group(ps[:, :], pk, [(wv[:, kc, 1, jj * 128:(jj + 1) * 128], Hh[:, kc, HS(half)]) for kc in range(KC)],
                                 wkeys(slot) + Hkeys(half))
                        S.op("dve", lambda e, ps=ps, half=half, jx=jx: e.tensor_tensor(
                            out=A[:, jx, HS(half)], in0=ps[:, :], in1=GF[:, HS(half)], op=ALU.mult),
                            reads=[pk, ("GF",)], writes=[("A", jx, half)])
            if l == 0:
                stop_at("ffnup")
            if l == 0:
                dbg_dump("a", AR[:, 0:NJ * T], [("A", j, hf) for j in range(NJ) for hf in range(2)])

            S.phase = f"L{l}:G_down"
            stat_begin()
            for c in range(8):
                slot = ring_use()
                wv = WR[:, slot, 0:NJ * 128].rearrange("p (k n) -> p k n", k=NJ)
                for half in range(2):
                    ps, pk = ps_next()
                    mm_group(ps[:, :], pk, [(wv[:, j, :], A[:, j, HS(half)]) for j in range(NJ)],
                             wkeys(slot) + [("A", j, half) for j in range(NJ)])
                    S.op("dve", lambda e, ps=ps, c=c, half=half: e.scalar_tensor_tensor(
                        out=X[:, c, HS(half)], in0=ps[:, :], scalar=MOD[:, l, 40 + c:41 + c], in1=X[:, c, HS(half)],
                        op0=ALU.mult, op1=ALU.add),
                        reads=[pk, ("MOD", l, 10 + c // 4), ("X", c, half)], writes=[("X", c, half)])
                    stat_accum(c, half)
                if c < 4 and l + 1 < NL:
                    mod_part(l + 1, [c])
            stat_end()
            S.barrier()
            if l + 1 < NL:
                S.op("dve", lambda e: e.memset(AR[:, 12288:12288 + 7680], 0.0), writes=[("VPall",)])
            if l == 0:
                stop_at("layer0")
            if l == 0:
                dbg_dump("x2", X[:, :, :], [("X", dc, hf) for dc in range(KC) for hf in range(2)])

        try:
            run_layers()
        except _Stop:
            pass
        S.barrier()
        S.phase = 'final'
        OUTB = [scr_f32(3584, 512), scr_f32(4096, 512), scr_f32(4608, 512), scr_f32(5120, 512)]
        octr = [0]

        def out_fn(dc, half):
            i = octr[0] % 4
            return OUTB[i]

        def out_keys(dc, half):
            return [("OUTB", octr[0] % 4)]

        SQ = [scr_bf16(0, 256), scr_bf16(256, 256)]
        SS = [scr_f32(512, 512), scr_f32(1024, 512)]
        RS = [scr_f32(1536, 512), scr_f32(2048, 512)]
        TMP = [scr_f32(2560, 512), scr_f32(3072, 512)]
        final_pre = STAT_READY[0] and stop is None
        for half in range(2):
            if final_pre:
                ps, pk = PS[STATB[half]], ("ps", STATB[half])
            else:
                ps, pk = ps_next()
            for dc in range(0 if final_pre else KC):
                sq = SQ[dc % 2]
                S.op("act", lambda e, dc=dc, sq=sq: e.activation(out=sq, in_=X[:, dc, HS(half)], func=AF.Square),
                     reads=[("X", dc, half)], writes=[("SQ", dc % 2)])
                S.op("pe", lambda e, dc=dc, sq=sq: e.matmul(ps[:, :], ONES[:, :], sq, start=(dc == 0), stop=(dc == KC - 1)),
                     reads=[("SQ", dc % 2), ("ONES",)], writes=[pk])
            S.op("act", lambda e: e.activation(out=SS[half], in_=ps[:, :], func=AF.Sqrt, scale=1.0 / D, bias=1e-6),
                 reads=[pk], writes=[("SS", half)])
            S.op("dve", lambda e: e.reciprocal(out=RS[half], in_=SS[half]), reads=[("SS", half)], writes=[("RS", half)])
            for dc in range(KC):
                tmp = TMP[dc % 2]
                S.op("dve", lambda e, dc=dc, tmp=tmp: e.tensor_tensor(out=tmp, in0=X[:, dc, HS(half)], in1=RS[half], op=ALU.mult),
                     reads=[("X", dc, half), ("RS", half)], writes=[("TMPn", dc % 2)])
                oi = octr[0] % 4
                octr[0] += 1
                ob = OUTB[oi]
                S.op("act", lambda e, dc=dc, tmp=tmp, ob=ob: e.activation(out=ob, in_=tmp, func=AF.Identity, scale=vcol(V_GF, dc)),
                     reads=[("TMPn", dc % 2), ("VEC",)], writes=[("OUTB", oi)])
                S.dma("sp", yT_d[dc * 128:(dc + 1) * 128, HS(half)], ob, reads=[("OUTB", oi)], is_output=True)
        S.finish()
        build_nc.stats = dict(S.nops)
        build_nc.phase_of = dict(S.phase_of)
    return nc


def _bf16_split(a):
    hi = a.astype(ml_dtypes.bfloat16).astype(np.float32)
    lo = (a - hi).astype(ml_dtypes.bfloat16).astype(np.float32)
    return hi, lo


def _band_tables(seq_len):
    out = np.zeros((128, 4, 8, 2, 128), np.float32)
    t = np.arange(T)
    for g, w in enumerate((2, 4, 8, 16)):
        B = np.zeros((T, T), np.float64)
        s0 = (t // seq_len) * seq_len
        lo = np.clip(t - w // 2, s0, s0 + seq_len)
        hi = np.clip(t - w // 2 + w, s0, s0 + seq_len)
        for tt in range(T):
            B[lo[tt]:hi[tt], tt] = 1.0 / (hi[tt] - lo[tt])
            B[tt, tt] -= 1.0
        B = B.astype(np.float32)

        def blk(ib, ob):
            return B[ib * 128:(ib + 1) * 128, ob * 128:(ob + 1) * 128]
        variants = [blk(0, 0), blk(7, 7), blk(2, 2), blk(1, 1), blk(0, 1), blk(1, 2), blk(1, 0), blk(2, 1)]
        for ob in range(8):
            dv = 0 if ob == 0 else (1 if ob == 7 else (2 if ob % 2 == 0 else 3))
            assert np.array_equal(blk(ob, ob), variants[dv])
            if ob >= 1:
                assert np.array_equal(blk(ob - 1, ob), variants[4 if ob % 2 == 1 else 5])
            if ob <= 6:
                assert np.array_equal(blk(ob + 1, ob), variants[6 if ob % 2 == 0 else 7])
        for v, m in enumerate(variants):
            h_, l_ = _bf16_split(m)
            out[:, g, v, 0, :] = h_
            out[:, g, v, 1, :] = l_
    return out.reshape(128, -1)


def _rneg_table(sample):
    r = np.full((16, 16), NEG, np.float32)
    for rk in range(16):
        for q in range(16):
            if sample:
                s = min(max(q - 4, 0), 8)
                ok = s <= rk < s + 8
            else:
                ok = (rk // 4) == (q // 4)
            if ok:
                r[rk, q] = 0.0
    out = np.zeros((128, T), np.float32)
    out[0:16] = np.repeat(r, 64, axis=1)
    return out


def _onehot_rows():
    oh = np.zeros((16, 8, 128), np.float32)
    for b in range(8):
        oh[2 * b, b, 0:64] = 1.0
        oh[2 * b + 1, b, 64:128] = 1.0
    out = np.zeros((128, 1024), np.float32)
    out[0:16] = oh.reshape(16, 1024)
    return out


def _u_table(rpb_l):
    U = np.full((128, NH, NE, 64), NEG, np.float32)
    cq = np.arange(64)
    cs = np.clip(cq - 8, 0, 48)
    for jj in range(2):
        for ei in range(NE):
            dr = jj - (ei - E0)
            if abs(dr) > 7:
                continue
            for ck in range(64):
                valid = (ck >= cs) & (ck < cs + 16)
                dc = np.clip(ck - cq + 15, 0, 30)
                vals = rpb_l[:, dr + 7, :][:, dc]
                U[jj * 64 + ck, :, ei, :] = np.where(valid[None, :], vals, NEG)
    return U.reshape(128, -1)


def _prep(inputs):
    f = lambda a: np.ascontiguousarray(np.asarray(a, dtype=np.float32))
    x_prompt, x_sample = f(inputs["x_prompt"]), f(inputs["x_sample"])
    cache_kv, c, c_ctx = f(inputs["cache_kv"]), f(inputs["c"]), f(inputs["c_ctx"])
    rpb = f(inputs["rpb"])
    shared = {
        "w_mod": f(inputs["w_mod"]), "w_in": f(inputs["w_in"]), "w_branch": f(inputs["w_branch"]),
        "w_out": f(inputs["w_out"]), "w_up": f(inputs["ffn_w_up"]), "w_down": f(inputs["ffn_w_down"]),
        "pool_w": f(inputs["pool_w"]),
    }

    def pk(v):
        return np.ascontiguousarray(v.reshape(-1, 128).T)

    vec_common = np.zeros((128, NV), np.float32)
    b_mod, g1, g2, gf = f(inputs["b_mod"]), f(inputs["g_norm1"]), f(inputs["g_norm2"]), f(inputs["g_final"])
    conv_w, fconv, pscale = f(inputs["conv_w"]), f(inputs["ffn_conv"]), f(inputs["pool_scale"])
    for l in range(NL):
        vec_common[:, V_BMOD + l * 48: V_BMOD + (l + 1) * 48] = pk(b_mod[l])
        vec_common[:, V_G1 + l * 8: V_G1 + (l + 1) * 8] = pk(g1[l])
        vec_common[:, V_G2 + l * 8: V_G2 + (l + 1) * 8] = pk(g2[l])
        for cc in range(4):
            for k in range(3):
                vec_common[:, V_CONVW + (l * 4 + cc) * 3 + k] = conv_w[l, k, cc * 128:(cc + 1) * 128]
        for j in range(NJ):
            for k in range(3):
                vec_common[:, V_FCONV + (l * NJ + j) * 3 + k] = fconv[l, k, j * 128:(j + 1) * 128]
        vec_common[:, V_PSCALE + l * 4: V_PSCALE + (l + 1) * 4] = pk(pscale[l])
    vec_common[:, V_GF:V_GF + 8] = pk(gf)

    ones_pat = np.concatenate([np.ones((128, 64)), np.zeros((128, 64)), np.ones((128, 64))], axis=1).astype(np.float32)
    band_p, band_s = _band_tables(256), _band_tables(1024)
    rneg_p, rneg_s = _rneg_table(False), _rneg_table(True)
    oh = _onehot_rows()
    utab_s = np.stack([_u_table(rpb[l]) for l in range(NL)])
    utab_p = np.zeros_like(utab_s)
    in_maps = []
    for i in range(8):
        sample = i >= 4
        vec = vec_common.copy()
        if sample:
            b = i - 4
            xT = np.ascontiguousarray(x_sample[b].T)
            vec[:, V_CV:V_CV + 8] = pk(c[b])
            vec[:, V_BFLAG] = 0.0
            ck = cache_kv[b, :, 0]
            ctxk = np.zeros((NL, 2, 64, NH, 256), np.float32)
            for h_ in range(NH):
                ctxk[:, h_ % 2, :, h_, :] = ck[:, h_].transpose(0, 2, 1)
            ctxk = ctxk.reshape(NL, 128, NH * 256)
            cvv = cache_kv[b, :, 1]
            ctxv = cvv.transpose(0, 2, 1, 3).reshape(NL, 2, 128, 512).transpose(0, 2, 1, 3).reshape(NL, 128, 1024)
            m = {"ctxkT": np.ascontiguousarray(ctxk), "ctxv": np.ascontiguousarray(ctxv), "utab": utab_s,
                 "rneg": rneg_s, "oh": oh, "onesc": ones_pat, "band": band_s}
        else:
            xT = np.ascontiguousarray(x_prompt[4 * i:4 * i + 4].reshape(T, D).T)
            vec[:, V_CV:V_CV + 8] = pk(c_ctx)
            vec[:, V_BFLAG] = 1.0
            m = {"ctxkT": np.zeros((NL, 128, 2048), np.float32), "ctxv": np.zeros((NL, 128, 1024), np.float32),
                 "utab": utab_p, "rneg": rneg_p, "oh": oh, "onesc": np.zeros_like(ones_pat), "band": band_p}
        m.update(shared)
        m["xT"] = xT
        m["vecs"] = vec
        in_maps.append(m)
    return in_maps


def _assemble(results):
    y_prompt = np.empty((16, 256, D), np.float32)
    y_sample = np.empty((4, T, D), np.float32)
    kv_state = np.empty((16, NL, 2, NH, 256, 64), np.float32)
    for i in range(8):
        r = results[i]
        y = np.asarray(r["yT"], dtype=np.float32).T
        if i < 4:
            y_prompt[4 * i:4 * i + 4] = y.reshape(4, 256, D)
            kvo = np.asarray(r["kvo"], dtype=np.float32).reshape(NL, 4, 256, NH, 64)
            kv_state[4 * i:4 * i + 4, :, 1] = kvo.transpose(1, 0, 3, 2, 4)
            kT = np.asarray(r["kvoT"], dtype=np.float32).reshape(NL, NH, 64, 4, 256)
            kv_state[4 * i:4 * i + 4, :, 0] = kT.transpose(3, 0, 1, 4, 2)
        else:
            y_sample[i - 4] = y
    return y_prompt, y_sample, kv_state


_NC_CACHE = {}


def kernel(**inputs):
    in_maps = _prep(inputs)
    if "nc" not in _NC_CACHE:
        _NC_CACHE["nc"] = build_nc()
    res = run_bass_kernel_spmd(_NC_CACHE["nc"], in_maps, core_ids=list(range(8)))
    return _assemble(res.results)
```
